# Optimizing a Trainium2 kernel written in Bass

```python
import numpy as np
import jax
import jax.numpy as jnp
from jax import lax

D_MODEL = 2048
BATCH = 4
SEQ = 4096
DEPTH = 4

N_MIXERS = 3
PLE_DIM = 256
FFN_DIM = ((8 * D_MODEL + 3 * 256 - 1) // (3 * 256)) * 256
RMS_EPS = 1e-6
ROPE_THETA = 500000.0
NEG_INF = -1e30
FORCED_SCORE = 1e9

SSD_INNER = 2 * D_MODEL
SSD_HEAD_DIM = 64
SSD_HEADS = SSD_INNER // SSD_HEAD_DIM
SSD_GROUPS = 8
SSD_STATE = 128
SSD_CONV = 4
SSD_CHUNK = 128
SSD_CONV_DIM = SSD_INNER + 2 * SSD_GROUPS * SSD_STATE
SSD_PROJ = SSD_INNER + SSD_CONV_DIM + SSD_HEADS

LRU_WIDTH = D_MODEL
LRU_BLOCK_DIM = 256
LRU_BLOCKS = LRU_WIDTH // LRU_BLOCK_DIM
LRU_CONV = 4
LRU_C = 8.0

NSA_HEAD_DIM = 128
NSA_HEADS = D_MODEL // NSA_HEAD_DIM
NSA_KV_GROUPS = 4
NSA_HPG = NSA_HEADS // NSA_KV_GROUPS
NSA_QDIM = NSA_HEADS * NSA_HEAD_DIM
NSA_KVDIM = NSA_KV_GROUPS * NSA_HEAD_DIM
NSA_PROJ = NSA_QDIM + 6 * NSA_KVDIM + 3 * NSA_HEADS
ROT_DIM = NSA_HEAD_DIM // 4
CMP_LEN = 32
CMP_STRIDE = 16
SEL_LEN = 64
SEL_TOPK = 16
WINDOW = 512
Q_BLOCK = 128

N_SSD = (DEPTH + N_MIXERS - 1) // N_MIXERS
N_LRU = (DEPTH + N_MIXERS - 2) // N_MIXERS
N_NSA = DEPTH // N_MIXERS

kernel_name = "hybrid_ssd_rglru_nsa_trunk"


def rms_norm(x, w):
    x32 = x.astype(jnp.float32)
    y = x32 * lax.rsqrt(jnp.mean(x32 * x32, axis=-1, keepdims=True) + RMS_EPS)
    return (y * w.astype(jnp.float32)).astype(x.dtype)


def causal_dwconv(x, w, b):
    k = w.shape[0]
    y = lax.conv_general_dilated(x, w[:, None, :].astype(x.dtype), (1,), [(k - 1, 0)],
                                 dimension_numbers=("NWC", "WIO", "NWC"),
                                 feature_group_count=x.shape[-1])
    return y + b.astype(x.dtype)


def partial_rope(x, pos):
    half = ROT_DIM // 2
    inv = ROPE_THETA ** (-jnp.arange(half, dtype=jnp.float32) * 2.0 / ROT_DIM)
    ang = pos.astype(jnp.float32)[..., None] * inv
    cos = jnp.cos(ang)[:, :, None, :]
    sin = jnp.sin(ang)[:, :, None, :]
    xr = x[..., :ROT_DIM].astype(jnp.float32)
    x1, x2 = xr[..., :half], xr[..., half:]
    rot = jnp.concatenate([x1 * cos - x2 * sin, x2 * cos + x1 * sin], axis=-1).astype(x.dtype)
    return jnp.concatenate([rot, x[..., ROT_DIM:]], axis=-1)


def ssd_chunked_scan(xdt, adt, bm, cm):
    b, s = xdt.shape[:2]
    nc, q, r = s // SSD_CHUNK, SSD_CHUNK, SSD_HEADS // SSD_GROUPS
    x = xdt.reshape(b, nc, q, SSD_GROUPS, r, SSD_HEAD_DIM)
    a = adt.reshape(b, nc, q, SSD_GROUPS, r).transpose(0, 3, 4, 1, 2)
    bm = bm.reshape(b, nc, q, SSD_GROUPS, SSD_STATE)
    cm = cm.reshape(b, nc, q, SSD_GROUPS, SSD_STATE)
    a_cs = jnp.cumsum(a, axis=-1)
    causal = jnp.tril(jnp.ones((q, q), dtype=bool))
    seg = a_cs[..., :, None] - a_cs[..., None, :]
    decay_in = jnp.exp(jnp.where(causal, seg, -jnp.inf))
    cb = jnp.einsum("bclgn,bcsgn->bcgls", cm, bm)
    y_diag = jnp.einsum("bcgls,bgrcls,bcsgrp->bclgrp", cb, decay_in, x)
    decay_to_end = jnp.exp(a_cs[..., -1:] - a_cs)
    states = jnp.einsum("bclgn,bgrcl,bclgrp->bcgrpn", bm, decay_to_end, x)
    chunk_decay = jnp.exp(a_cs[..., -1])

    def step(carry, inp):
        st, dec = inp
        return carry * dec[..., None, None] + st, carry

    init = jnp.zeros_like(states[:, 0])
    _, prev = lax.scan(step, init, (jnp.moveaxis(states, 1, 0), jnp.moveaxis(chunk_decay, -1, 0)))
    prev = jnp.moveaxis(prev, 0, 1)
    y_off = jnp.einsum("bclgn,bcgrpn,bgrcl->bclgrp", cm, prev, jnp.exp(a_cs))
    return (y_diag + y_off).reshape(b, s, SSD_HEADS, SSD_HEAD_DIM)


def ssd_mixer(u, in_proj, conv_w, conv_b, dt_bias, a_log, d_skip, norm_w, out_proj):
    b, s, _ = u.shape
    f32 = jnp.float32
    z, xbc, dt = jnp.split(u @ in_proj, [SSD_INNER, SSD_INNER + SSD_CONV_DIM], axis=-1)
    xbc = jax.nn.silu(causal_dwconv(xbc, conv_w, conv_b))
    xs, bm, cm = jnp.split(xbc, [SSD_INNER, SSD_INNER + SSD_GROUPS * SSD_STATE], axis=-1)
    xs = xs.reshape(b, s, SSD_HEADS, SSD_HEAD_DIM).astype(f32)
    bm = bm.reshape(b, s, SSD_GROUPS, SSD_STATE).astype(f32)
    cm = cm.reshape(b, s, SSD_GROUPS, SSD_STATE).astype(f32)
    dt = jax.nn.softplus((dt + dt_bias).astype(f32))
    a = -jnp.exp(a_log.astype(f32))
    y = ssd_chunked_scan(xs * dt[..., None], dt * a, bm, cm) + d_skip.astype(f32)[:, None] * xs
    y = y.reshape(b, s, SSD_INNER) * jax.nn.silu(z.astype(f32))
    yg = y.reshape(b, s, SSD_GROUPS, SSD_INNER // SSD_GROUPS)
    yg = yg * lax.rsqrt(jnp.mean(yg * yg, axis=-1, keepdims=True) + RMS_EPS)
    y = yg.reshape(b, s, SSD_INNER) * norm_w.astype(f32)
    return y.astype(u.dtype) @ out_proj


def rglru_mixer(u, in_proj, conv_w, conv_b, wa, ba, wx, bx, a_param, out_proj):
    b, s, _ = u.shape
    f32 = jnp.float32
    gate, xr = jnp.split(u @ in_proj, 2, axis=-1)
    xr = causal_dwconv(xr, conv_w, conv_b)
    xb = xr.reshape(b, s, LRU_BLOCKS, LRU_BLOCK_DIM)
    r_t = jax.nn.sigmoid((jnp.einsum("bskc,kcd->bskd", xb, wa).reshape(b, s, LRU_WIDTH) + ba).astype(f32))
    i_t = jax.nn.sigmoid((jnp.einsum("bskc,kcd->bskd", xb, wx).reshape(b, s, LRU_WIDTH) + bx).astype(f32))
    log_a = -LRU_C * r_t * jax.nn.softplus(-a_param.astype(f32))
    a_t = jnp.exp(log_a)
    b_t = jnp.sqrt(-jnp.expm1(2.0 * log_a)) * (i_t * xr.astype(f32))

    def combine(left, right):
        a1, b1 = left
        a2, b2 = right
        return a1 * a2, a2 * b1 + b2

    _, h = lax.associative_scan(combine, (a_t, b_t), axis=1)
    y = jax.nn.gelu(gate.astype(f32)) * h
    return y.astype(u.dtype) @ out_proj


def nsa_mixer(u, positions, in_proj, cmp_pe, cmp_w1, cmp_w2, out_proj):
    b, s, _ = u.shape
    f32 = jnp.float32
    G, R, HD = NSA_KV_GROUPS, NSA_HPG, NSA_HEAD_DIM
    offs = [NSA_QDIM + k * NSA_KVDIM for k in range(7)]
    q, kc, vc, ks, vs, kw, vw, g = jnp.split(u @ in_proj, offs, axis=-1)
    kv_shape = (b, s, G, HD)
    q = partial_rope(q.reshape(b, s, NSA_HEADS, HD), positions)
    kc, vc, vs, vw = (t.reshape(kv_shape) for t in (kc, vc, vs, vw))
    ks = partial_rope(ks.reshape(kv_shape), positions)
    kw = partial_rope(kw.reshape(kv_shape), positions)
    gates = jax.nn.sigmoid(g.astype(f32)).reshape(b, s, G, R, 3)

    n_cmp = (s - CMP_LEN) // CMP_STRIDE + 1
    cmp_idx = np.arange(n_cmp)[:, None] * CMP_STRIDE + np.arange(CMP_LEN)[None, :]
    cmp_end = cmp_idx[:, -1]

    def compress(tok, pe, w1, w2):
        blk = tok[:, cmp_idx] + pe[:, None, :]
        flat = jnp.moveaxis(blk, 3, 2).reshape(b, n_cmp, G, CMP_LEN * HD)
        return jax.nn.gelu(flat @ w1) @ w2

    k_cmp = partial_rope(compress(kc, cmp_pe[0], cmp_w1[0], cmp_w2[0]), positions[:, cmp_end])
    v_cmp = compress(vc, cmp_pe[1], cmp_w1[1], cmp_w2[1]).astype(f32)
    cmp_end_j = jnp.asarray(cmp_end, dtype=jnp.int32)

    n_sel = s // SEL_LEN
    top_n = min(SEL_TOPK, n_sel)
    c_start = np.arange(n_cmp)[:, None] * CMP_STRIDE
    s_start = np.arange(n_sel)[None, :] * SEL_LEN
    overlap = np.clip(np.minimum(c_start + CMP_LEN, s_start + SEL_LEN) - np.maximum(c_start, s_start), 0, None)
    cmp_to_sel = jnp.asarray(overlap / CMP_STRIDE, dtype=f32)
    k_sel_blk = jnp.moveaxis(ks.reshape(b, n_sel, SEL_LEN, G, HD), 3, 1)
    v_sel_blk = jnp.moveaxis(vs.reshape(b, n_sel, SEL_LEN, G, HD), 3, 1)
    k_win = jnp.pad(kw, ((0, 0), (WINDOW, 0), (0, 0), (0, 0)))
    v_win = jnp.pad(vw, ((0, 0), (WINDOW, 0), (0, 0), (0, 0)))
    scale = HD ** -0.5
    bi = jnp.arange(b)[:, None, None, None]
    gi = jnp.arange(G)[None, :, None, None]
    jsel = jnp.arange(n_sel)

    def attend_block(blk):
        t0 = blk * Q_BLOCK
        t = t0 + jnp.arange(Q_BLOCK)
        qb = lax.dynamic_slice_in_dim(q, t0, Q_BLOCK, 1).reshape(b, Q_BLOCK, G, R, HD)
        gb = lax.dynamic_slice_in_dim(gates, t0, Q_BLOCK, 1)
        s_c = jnp.einsum("btgrd,bngd->bgrtn", qb, k_cmp).astype(f32) * scale
        ok_c = cmp_end_j[None, :] <= t[:, None]
        p_c = jax.nn.softmax(jnp.where(ok_c, s_c, NEG_INF), axis=-1) * ok_c
        o_c = jnp.einsum("bgrtn,bngd->btgrd", p_c, v_cmp)
        imp = jnp.einsum("bgrtn,nj->bgtj", p_c, cmp_to_sel)
        cur = t // SEL_LEN
        forced = (jsel[None, :] == 0) | (jsel[None, :] == cur[:, None]) | (jsel[None, :] == cur[:, None] - 1)
        imp = jnp.where(forced, FORCED_SCORE, imp)
        imp = jnp.where(jsel[None, :] <= cur[:, None], imp, -jnp.inf)
        _, sel = lax.top_k(imp, top_n)
        kg = k_sel_blk[bi, gi, sel]
        vg = v_sel_blk[bi, gi, sel].astype(f32)
        s_s = jnp.einsum("btgrd,bgtnld->bgrtnl", qb, kg).astype(f32) * scale
        kpos = sel[..., None] * SEL_LEN + jnp.arange(SEL_LEN)
        ok_s = (kpos <= t[:, None, None])[:, :, None]
        s_s = jnp.where(ok_s, s_s, NEG_INF).reshape(b, G, R, Q_BLOCK, top_n * SEL_LEN)
        p_s = jax.nn.softmax(s_s, axis=-1).reshape(b, G, R, Q_BLOCK, top_n, SEL_LEN)
        o_s = jnp.einsum("bgrtnl,bgtnld->btgrd", p_s, vg)
        kwb = lax.dynamic_slice_in_dim(k_win, t0, Q_BLOCK + WINDOW, 1)
        vwb = lax.dynamic_slice_in_dim(v_win, t0, Q_BLOCK + WINDOW, 1).astype(f32)
        kp = t0 - WINDOW + jnp.arange(Q_BLOCK + WINDOW)
        ok_w = (kp[None, :] <= t[:, None]) & (kp[None, :] > t[:, None] - WINDOW) & (kp[None, :] >= 0)
        s_w = jnp.einsum("btgrd,bkgd->bgrtk", qb, kwb).astype(f32) * scale
        p_w = jax.nn.softmax(jnp.where(ok_w, s_w, NEG_INF), axis=-1)
        o_w = jnp.einsum("bgrtk,bkgd->btgrd", p_w, vwb)
        o = gb[..., 0:1] * o_c + gb[..., 1:2] * o_s + gb[..., 2:3] * o_w
        return o.reshape(b, Q_BLOCK, NSA_QDIM).astype(u.dtype)

    out = lax.map(attend_block, jnp.arange(s // Q_BLOCK))
    out = jnp.moveaxis(out, 0, 1).reshape(b, s, NSA_QDIM)
    return out @ out_proj


def swiglu(v, w_in, w_out):
    gate, up = jnp.split(v @ w_in, 2, axis=-1)
    return (jax.nn.silu(gate) * up) @ w_out


def setup_inputs(seed: int = 0) -> dict:
    key = jax.random.key(seed)
    keys = iter(jax.random.split(key, 48))
    f32 = jnp.float32

    def nrm(shape, scale):
        return jax.random.normal(next(keys), shape, f32) * scale

    def unif(shape, lo, hi):
        return jax.random.uniform(next(keys), shape, f32, lo, hi)

    def gain(shape):
        return 1.0 + nrm(shape, 0.02)

    x = nrm((BATCH, SEQ, D_MODEL), 1.0)
    p = nrm((DEPTH, BATCH, SEQ, PLE_DIM), 1.0)
    offset = jax.random.randint(next(keys), (BATCH, 1), 0, 1024, jnp.int32)
    positions = offset + jnp.arange(SEQ, dtype=jnp.int32)[None, :]
    dt0 = jnp.exp(unif((N_SSD, SSD_HEADS), float(np.log(1e-3)), float(np.log(1e-1))))
    a0 = unif((N_LRU, LRU_WIDTH), 0.9, 0.999)
    s0 = a0 ** (1.0 / LRU_C)
    return {
        "x": x,
        "p": p,
        "positions": positions,
        "norm_mix": gain((DEPTH, D_MODEL)),
        "norm_ffn": gain((DEPTH, D_MODEL)),
        "norm_ple": gain((DEPTH, D_MODEL)),
        "w_ple_up": nrm((DEPTH, PLE_DIM, D_MODEL), PLE_DIM ** -0.5),
        "w_ple_gate": nrm((DEPTH, D_MODEL, D_MODEL), D_MODEL ** -0.5),
        "w_ffn_in": nrm((DEPTH, D_MODEL, 2 * FFN_DIM), D_MODEL ** -0.5),
        "w_ffn_out": nrm((DEPTH, FFN_DIM, D_MODEL), FFN_DIM ** -0.5),
        "norm_final": gain((D_MODEL,)),
        "ssd_in_proj": nrm((N_SSD, D_MODEL, SSD_PROJ), D_MODEL ** -0.5),
        "ssd_conv_w": nrm((N_SSD, SSD_CONV, SSD_CONV_DIM), SSD_CONV ** -0.5),
        "ssd_conv_b": nrm((N_SSD, SSD_CONV_DIM), 0.01),
        "ssd_dt_bias": dt0 + jnp.log(-jnp.expm1(-dt0)),
        "ssd_a_log": jnp.log(unif((N_SSD, SSD_HEADS), 1.0, 16.0)),
        "ssd_d": gain((N_SSD, SSD_HEADS)),
        "ssd_norm": gain((N_SSD, SSD_INNER)),
        "ssd_out_proj": nrm((N_SSD, SSD_INNER, D_MODEL), SSD_INNER ** -0.5),
        "lru_in_proj": nrm((N_LRU, D_MODEL, 2 * LRU_WIDTH), D_MODEL ** -0.5),
        "lru_conv_w": nrm((N_LRU, LRU_CONV, LRU_WIDTH), LRU_CONV ** -0.5),
        "lru_conv_b": nrm((N_LRU, LRU_WIDTH), 0.01),
        "lru_wa": nrm((N_LRU, LRU_BLOCKS, LRU_BLOCK_DIM, LRU_BLOCK_DIM), LRU_BLOCK_DIM ** -0.5),
        "lru_ba": nrm((N_LRU, LRU_WIDTH), 0.01),
        "lru_wx": nrm((N_LRU, LRU_BLOCKS, LRU_BLOCK_DIM, LRU_BLOCK_DIM), LRU_BLOCK_DIM ** -0.5),
        "lru_bx": nrm((N_LRU, LRU_WIDTH), 0.01),
        "lru_a_param": jnp.log(s0) - jnp.log1p(-s0),
        "lru_out_proj": nrm((N_LRU, LRU_WIDTH, D_MODEL), LRU_WIDTH ** -0.5),
        "nsa_in_proj": nrm((N_NSA, D_MODEL, NSA_PROJ), D_MODEL ** -0.5),
        "nsa_cmp_pe": nrm((N_NSA, 2, CMP_LEN, NSA_HEAD_DIM), 0.02),
        "nsa_cmp_w1": nrm((N_NSA, 2, CMP_LEN * NSA_HEAD_DIM, NSA_HEAD_DIM), (CMP_LEN * NSA_HEAD_DIM) ** -0.5),
        "nsa_cmp_w2": nrm((N_NSA, 2, NSA_HEAD_DIM, NSA_HEAD_DIM), NSA_HEAD_DIM ** -0.5),
        "nsa_out_proj": nrm((N_NSA, NSA_QDIM, D_MODEL), NSA_QDIM ** -0.5),
    }


def reference(x, p, positions, norm_mix, norm_ffn, norm_ple, w_ple_up, w_ple_gate, w_ffn_in, w_ffn_out,
              norm_final, ssd_in_proj, ssd_conv_w, ssd_conv_b, ssd_dt_bias, ssd_a_log, ssd_d, ssd_norm,
              ssd_out_proj, lru_in_proj, lru_conv_w, lru_conv_b, lru_wa, lru_ba, lru_wx, lru_bx, lru_a_param,
              lru_out_proj, nsa_in_proj, nsa_cmp_pe, nsa_cmp_w1, nsa_cmp_w2, nsa_out_proj):
    h = x
    for i in range(DEPTH):
        kind, j = i % N_MIXERS, i // N_MIXERS
        u = rms_norm(h, norm_mix[i])
        if kind == 0:
            mix = ssd_mixer(u, ssd_in_proj[j], ssd_conv_w[j], ssd_conv_b[j], ssd_dt_bias[j], ssd_a_log[j],
                            ssd_d[j], ssd_norm[j], ssd_out_proj[j])
        elif kind == 1:
            mix = rglru_mixer(u, lru_in_proj[j], lru_conv_w[j], lru_conv_b[j], lru_wa[j], lru_ba[j], lru_wx[j],
                              lru_bx[j], lru_a_param[j], lru_out_proj[j])
        else:
            mix = nsa_mixer(u, positions, nsa_in_proj[j], nsa_cmp_pe[j], nsa_cmp_w1[j], nsa_cmp_w2[j],
                            nsa_out_proj[j])
        h = h + mix
        h = h + swiglu(rms_norm(h, norm_ffn[i]), w_ffn_in[i], w_ffn_out[i])
        ple_gate = jax.nn.sigmoid(rms_norm(h, norm_ple[i]) @ w_ple_gate[i])
        h = h + ple_gate * (p[i].astype(h.dtype) @ w_ple_up[i])
    return rms_norm(h, norm_final)
```

```python
import math, os


import numpy as np
import concourse.bass as bass
import concourse.mybir as mybir
from concourse.bass_utils import run_bass_kernel_spmd

F32 = mybir.dt.float32; BF16 = mybir.dt.bfloat16; I32 = mybir.dt.int32
AF = mybir.ActivationFunctionType; ALU = mybir.AluOpType
AX = mybir.AxisListType


class D:
    __slots__ = ("w", "r", "excl")
    def __init__(s, excl=False): s.w = []; s.r = []; s.excl = excl


class Eng:
    def __init__(s, ctx, name, e):
        s.e = e; s.name = name; s.sem = ctx.nc.alloc_semaphore("s_" + name); s.cnt = 0; s.seen = {}
        s.dsems = []; s.dcnt = []; s.di = 0


class Ctx:
    def __init__(s, nc, ndma=8):
        s.nc = nc
        s.pe = Eng(s, "pe", nc.tensor); s.act = Eng(s, "act", nc.scalar); s.dve = Eng(s, "dve", nc.vector)
        s.pool = Eng(s, "pool", nc.gpsimd); s.sp = Eng(s, "sp", nc.sync)
        for q in (s.sp, s.pool, s.act):
            n = ndma
            q.dsems = [nc.alloc_semaphore(f"d_{q.name}{i}") for i in range(n)]; q.dcnt = [0] * n
        s.nbank = 0
        import contextlib
        s._cl = contextlib
        s.es = contextlib.ExitStack(); s.pes = contextlib.ExitStack(); s.uid = 0; s.pfx = ""

    def psb(s, name, shape, dtype):
        s.uid += 1
        return s.pes.enter_context(s.nc.sbuf_tensor(f"{name}_{s.uid}", shape, dtype))

    def dram(s, name, shape, dtype=F32, kind="ExternalInput"):
        if kind is None: return s.nc.dram_tensor(s.pfx + name, shape, dtype).ap()
        return s.nc.dram_tensor(s.pfx + name, shape, dtype, kind=kind).ap()

    def new_phase(s):
        s.barrier(); s.es.close(); s.pes.close(); s.es = s._cl.ExitStack(); s.pes = s._cl.ExitStack()

    def allgather(s, src, src_deps, dst, dst_dep):
        q = s.pool
        s._deps(q, src_deps, [dst_dep])
        s.uid += 1
        sem = s.nc.alloc_semaphore(f"cc_{s.uid}")
        s.nc.gpsimd.collective_compute("AllGather", ALU.bypass, replica_groups=[[0, 1], [2, 3], [4, 5], [6, 7]], ins=[src.opt()], outs=[dst.opt()]).then_inc(sem)
        s._mark((sem, 1), src_deps, [dst_dep])

    def sb(s, name, shape, dtype):
        s.uid += 1
        return s.es.enter_context(s.nc.sbuf_tensor(f"{name}_{s.uid}", shape, dtype))

    def ps(s, name, shape, dtype=None):
        s.uid += 1
        return s.es.enter_context(s.nc.psum_tensor(f"{name}_{s.uid}", shape, dtype or F32))

    def barrier(s):
        engs = [s.pe, s.act, s.dve, s.pool, s.sp]
        for e in engs:
            for o in engs:
                if o is not e and o.cnt > 0: s._wait(e, o.sem, o.cnt)
            for q in (s.sp, s.pool, s.act):
                for sem, cnt in zip(q.dsems, q.dcnt):
                    if cnt > 0: s._wait(e, sem, cnt)

    def new_stage(s):
        s.barrier(); s.es.close(); s.es = s._cl.ExitStack()

    def _wait(s, eng, sem, val):
        key = id(sem)
        if eng.seen.get(key, 0) < val:
            eng.e.wait_ge(sem, val); eng.seen[key] = val

    def _deps(s, eng, reads, writes, acc=()):
        for t in reads:
            for w in t.w: s._wait(eng, *w)
            if t.excl:
                for (sem, val) in t.r:
                    if sem is not eng.sem: s._wait(eng, sem, val)
        for t in writes:
            for w in t.w: s._wait(eng, *w)
        for t in list(writes) + list(acc):
            for (sem, val) in t.r:
                if sem is eng.sem: continue
                s._wait(eng, sem, val)

    def _mark(s, tag, reads, writes, acc=()):
        for t in writes: t.w = [tag]; t.r = []
        for t in acc: t.w.append(tag)
        for t in reads: t.r.append(tag)

    def op(s, eng, fn, reads=(), writes=()):
        s._deps(eng, reads, writes)
        inst = fn()
        eng.cnt += 1
        inst.then_inc(eng.sem, 1)
        s._mark((eng.sem, eng.cnt), reads, writes)

    def group(s, eng, fns, reads=(), writes=()):
        s._deps(eng, reads, writes)
        inst = None
        for fn in fns: inst = fn()
        eng.cnt += 1
        inst.then_inc(eng.sem, 1)
        s._mark((eng.sem, eng.cnt), reads, writes)

    def dma(s, q, out, in_, reads=(), writes=(), acc=(), **kw):
        s._deps(q, reads, writes, acc)
        i = q.di; q.di = (q.di + 1) % len(q.dsems)
        sem = q.dsems[i]
        if q.dcnt[i] > 0: s._wait(q, sem, q.dcnt[i])
        q.dcnt[i] += 16
        q.e.dma_start(out=out, in_=in_, **kw).then_inc(sem, 16)
        s._mark((sem, q.dcnt[i]), reads, writes, acc)

    def finish(s, deps):
        for t in deps:
            for w in t.w: s._wait(s.sp, *w)


class Pool:
    def __init__(s, c, name, n, shape, dtype, psum=False, nd=0):
        alloc = c.ps if psum else c.sb
        s.t = [alloc(f"{name}{i}", shape, dtype) for i in range(n)]
        s.d = [(D(psum) if nd == 0 else [D(psum) for _ in range(nd)]) for _ in range(n)]; s.i = 0; s.n = n
    def next(s):
        i = s.i; s.i = (s.i + 1) % s.n
        return s.t[i], s.d[i]


class WStream:
    def __init__(s, c, nbuf, nelem, name="wbuf", live=1):
        s.c = c; s.nelem = nelem; s.ahead = nbuf - live
        s.t = [c.sb(f"{name}{i}", [128, nelem], BF16) for i in range(nbuf)]
        s.d = [D() for _ in range(nbuf)]
        s.plan = []; s.issued = 0; s.pos = 0
    def _issue(s):
        i = s.issued
        if i >= len(s.plan): return
        src, kc, nw = s.plan[i]
        b = i % len(s.t)
        dst = s.t[b][:, 0:kc * nw].rearrange("p (k n) -> p k n", k=kc)
        s.c.dma(s.c.pool, dst, src, writes=[s.d[b]])
        s.issued += 1
    def next(s):
        while s.issued <= min(s.pos + s.ahead, len(s.plan) - 1): s._issue()
        src, kc, nw = s.plan[s.pos]
        b = s.pos % len(s.t); s.pos += 1
        return s.t[b][:, 0:kc * nw].rearrange("p (k n) -> p k n", k=kc), s.d[b]


def wtiles(W, K, n0, n1, nw):
    kc = K // 128
    Wv = W.rearrange("(k p) n -> p k n", p=128)
    return [(Wv[:, :, a:min(a + nw, n1)], kc, min(a + nw, n1) - a) for a in range(n0, n1, nw)]


def gemm(c, ws, pp, spec, rhs, rhs_deps, T, epi, mchunk=128):
    nc = c.nc
    col = 0
    for (src, kc, nw) in spec:
        wt, wd = ws.next()
        for m0 in range(0, nw, mchunk):
            m1 = min(m0 + mchunk, nw)
            for t0 in range(0, T, 512):
                t1 = min(t0 + 512, T)
                pt, pd = pp.next()
                out = pt[0:m1 - m0, 0:t1 - t0]
                fns = [(lambda k=k: nc.tensor.matmul(out, lhsT=wt[:, k, m0:m1], rhs=rhs(k, t0, t1), start=(k == 0), stop=(k == kc - 1))) for k in range(kc)]
                deps = [wd]
                for k in range(kc): deps += rhs_deps(k)
                c.group(c.pe, fns, reads=deps, writes=[pd])
                epi(col + m0, t0, t1, out, pd)
        col += nw


DM = 2048; FF = 5632; PLE = 256; TPC = 2048; TT = 512; EPS = 1e-6

class Weave:
    def __init__(s): s.prev = None
    def push(s, g):
        a_done = False; b_done = (s.prev is None)
        while not (a_done and b_done):
            if not a_done:
                if next(g) == 'S': a_done = True
            if not b_done:
                try: next(s.prev)
                except StopIteration: b_done = True
        s.prev = g
    def flush(s):
        if s.prev is not None:
            for _ in s.prev: pass
        s.prev = None


def rmsnorm_tile(c, nc, pp, h, hd, wcol, v, vd, ones, sqp, misc, T=TT, out_f32=None):
    pt, pd = pp.next()
    for ch in range(16):
        sq, sd = sqp.next()
        c.op(c.act, lambda: nc.scalar.activation(out=sq[:, 0:T], in_=h[:, ch, :], func=AF.Square), reads=[hd[ch]], writes=[sd])
        c.group(c.pe, [lambda: nc.tensor.matmul(pt[:, 0:T], lhsT=ones[:], rhs=sq[:, 0:T], start=(ch == 0), stop=(ch == 15), skip_group_check=True)], reads=[sd], writes=[pd])
    rs, rd = misc.next()
    c.op(c.act, lambda: nc.scalar.activation(out=rs[:, 0:T], in_=pt[:, 0:T], func=AF.Sqrt, scale=1.0 / DM, bias=c.eps[:]), reads=[pd], writes=[rd])
    c.op(c.dve, lambda: nc.vector.reciprocal(out=rs[:, 0:T], in_=rs[:, 0:T]), reads=[rd], writes=[rd])
    for ch in range(16):
        o = v[:, ch, :] if out_f32 is None else out_f32[:, ch, :]
        c.op(c.dve, lambda: nc.vector.scalar_tensor_tensor(out=o, in0=h[:, ch, :], scalar=wcol[:, ch:ch + 1], in1=rs[:, 0:T], op0=ALU.mult, op1=ALU.mult),
             reads=[hd[ch], rd], writes=[vd[ch]])


def emit_B(nc, c, inner, final, hin, yall, yad, hout, hdone):
    IC = inner // 128
    dt = c.dram
    pT = dt("pT", [PLE, TPC]); sel_d = dt("sel", [128, 2])
    w_o = dt("w_o", [inner, DM]); w_in = dt("w_in", [DM, 2 * FF]); w_out = dt("w_out", [FF, DM])
    w_g = dt("w_g", [DM, DM]); w_u = dt("w_u", [PLE, DM])
    nrm = dt("nrm", [128, 48])
    ones = c.sb("ones", [128, 128], BF16); c.eps = c.sb("eps", [128, 1], F32)
    nw = c.sb("nw", [128, 48], F32)
    cd = D()
    c.op(c.dve, lambda: nc.vector.memset(ones[:], 1.0), writes=[cd])
    c.op(c.dve, lambda: nc.vector.memset(c.eps[:], EPS), writes=[cd])
    c.dma(c.sp, nw[:], nrm, writes=[cd])
    sel = c.sb("sel", [128, 2], F32)
    c.dma(c.sp, sel[:], sel_d, writes=[cd])
    ytp = Pool(c, "ytmp", 2, [128, 8, TT], BF16)
    h = c.sb("h", [128, 16, TT], F32); hd = [D() for _ in range(16)]
    big = c.sb("big", [128, 44, TT], BF16); bd = [D() for _ in range(44)]
    v = c.sb("v", [128, 16, TT], BF16); vd = [D() for _ in range(16)]
    pb = c.sb("pb", [128, 2, TT], BF16); pbd = [D(), D()]
    pp = Pool(c, "ps", 8, [128, 512], F32, psum=True)
    sqp = Pool(c, "sq", 3, [128, TT], BF16)
    misc = Pool(c, "misc", 4, [128, TT], F32)
    ws = WStream(c, 2, 44 * 256)
    NT = TPC // TT
    g_o = wtiles(w_o, inner, 0, DM, 256 if IC > 16 else 512)
    g_in = []
    for j in range(0, FF, 512):
        g_in += wtiles(w_in, DM, j, j + 512, 512) + wtiles(w_in, DM, FF + j, FF + j + 512, 512)
    g_out = wtiles(w_out, FF, 0, DM, 256)
    g_g = wtiles(w_g, DM, 0, DM, 512)
    g_u = wtiles(w_u, PLE, 0, DM, 2048)
    for _ in range(NT): ws.plan += g_o + g_in + g_out + g_u + g_g
    pTv = pT.rearrange("(c p) t -> p c t", p=128)
    for ti in range(NT):
        ts = slice(ti * TT, (ti + 1) * TT)
        for q in range(4):
            hap, hdeps = hin(ti, q)
            c.dma(c.sp, h[:, 4 * q:4 * q + 4, :], hap, reads=hdeps, writes=hd[4 * q:4 * q + 4])
        y0v = yall[ti].rearrange("(c p) t -> p c t", p=128); y1v = yall[4 + ti].rearrange("(c p) t -> p c t", p=128)
        for q in range(0, IC, 8):
            yt_, ytd_ = ytp.next()
            c.dma(c.sp, big[:, q:q + 8, :], y0v[:, q:q + 8, :], reads=[yad[ti]], writes=bd[q:q + 8])
            c.dma(c.sp, yt_[:], y1v[:, q:q + 8, :], reads=[yad[4 + ti]], writes=[ytd_])
            c.op(c.dve, lambda: nc.vector.tensor_scalar(out=yt_[:], in0=yt_[:], scalar1=sel[:, 1:2], scalar2=None, op0=ALU.mult), reads=[ytd_, cd], writes=[ytd_])
            c.op(c.dve, lambda: nc.vector.scalar_tensor_tensor(out=big[:, q:q + 8, :], in0=big[:, q:q + 8, :], scalar=sel[:, 0:1], in1=yt_[:], op0=ALU.mult, op1=ALU.add), reads=bd[q:q + 8] + [ytd_, cd], writes=bd[q:q + 8])
        c.dma(c.pool, pb[:], pTv[:, :, ts], writes=pbd)
        def epi_add(col, t0, t1, ps, pd):
            ch = col // 128
            c.op(c.dve, lambda: nc.vector.tensor_tensor(out=h[:, ch, t0:t1], in0=h[:, ch, t0:t1], in1=ps, op=ALU.add), reads=[pd, hd[ch]], writes=[hd[ch]])
        gemm(c, ws, pp, g_o, lambda k, t0, t1: big[:, k, t0:t1], lambda k: [bd[k]], TT, epi_add)
        rmsnorm_tile(c, nc, pp, h, hd, nw[:, 0:16], v, vd, ones, sqp, misc)
        st = {}
        def epi_ffn(col, t0, t1, ps, pd):
            j = col % 1024; grp = col // 1024; isup = j >= 512; ch = grp * 4 + (j % 512) // 128
            if not isup:
                sg, sd = misc.next()
                c.op(c.act, lambda: nc.scalar.activation(out=sg[:, 0:t1 - t0], in_=ps, func=AF.Silu), reads=[pd], writes=[sd])
                st[ch] = (sg, sd)
            else:
                sg, sd = st.pop(ch)
                c.op(c.dve, lambda: nc.vector.tensor_tensor(out=big[:, ch, t0:t1], in0=sg[:, 0:t1 - t0], in1=ps, op=ALU.mult), reads=[pd, sd], writes=[bd[ch]])
        gemm(c, ws, pp, g_in, lambda k, t0, t1: v[:, k, t0:t1], lambda k: [vd[k]], TT, epi_ffn)
        gemm(c, ws, pp, g_out, lambda k, t0, t1: big[:, k, t0:t1], lambda k: [bd[k]], TT, epi_add)
        rmsnorm_tile(c, nc, pp, h, hd, nw[:, 16:32], v, vd, ones, sqp, misc)
        pu = big
        pus = {}
        def epi_up(col, t0, t1, ps, pd):
            ch = col // 128
            c.op(c.act, lambda: nc.scalar.activation(out=big[:, ch, t0:t1], in_=ps, func=AF.Copy), reads=[pd], writes=[bd[ch]])
        gemm(c, ws, pp, g_u, lambda k, t0, t1: pb[:, k, t0:t1], lambda k: [pbd[k]], TT, epi_up)
        def epi_gate(col, t0, t1, ps, pd):
            ch = col // 128
            sg, sd = misc.next()
            c.op(c.act, lambda: nc.scalar.activation(out=sg[:, 0:t1 - t0], in_=ps, func=AF.Sigmoid), reads=[pd], writes=[sd])
            c.op(c.dve, lambda: nc.vector.tensor_tensor(out=sg[:, 0:t1 - t0], in0=sg[:, 0:t1 - t0], in1=big[:, ch, t0:t1], op=ALU.mult), reads=[sd, bd[ch]], writes=[sd])
            c.op(c.dve, lambda: nc.vector.tensor_tensor(out=h[:, ch, t0:t1], in0=h[:, ch, t0:t1], in1=sg[:, 0:t1 - t0], op=ALU.add), reads=[sd, hd[ch]], writes=[hd[ch]])
        gemm(c, ws, pp, g_g, lambda k, t0, t1: v[:, k, t0:t1], lambda k: [vd[k]], TT, epi_gate)
        if final:
            rmsnorm_tile(c, nc, pp, h, hd, nw[:, 32:48], v, hd, ones, sqp, misc, out_f32=h)
        for q in range(4):
            oap, odp = hout(ti, q)
            c.dma(c.sp, oap, h[:, 4 * q:4 * q + 4, :], reads=hd[4 * q:4 * q + 4], acc=[odp])
        hdone(ti)


S = 4096; TT = 512

class A1:
    def __init__(s, nc, c, ncols, htile, wbuf_elems=16 * 512):
        s.nc = nc; s.c = c; s.htile = htile
        dt = c.dram
        s.dt = dt
        s.w = dt("w_in", [DM, ncols]); s.nrm = dt("nrm", [128, 16])
        s.ones = c.sb("ones", [128, 128], BF16); c.eps = c.sb("eps", [128, 1], F32)
        s.nw = c.sb("nw", [128, 16], F32)
        s.cd = D()
        c.op(c.dve, lambda: nc.vector.memset(s.ones[:], 1.0), writes=[s.cd])
        c.op(c.dve, lambda: nc.vector.memset(c.eps[:], EPS), writes=[s.cd])
        c.dma(c.sp, s.nw[:], s.nrm, writes=[s.cd])
        s.h = c.sb("h", [128, 16, TT], F32); s.hd = [D() for _ in range(16)]
        s.v = c.sb("v", [128, 16, TT], BF16); s.vd = [D() for _ in range(16)]
        s.pp = Pool(c, "ps", 8, [128, 512], F32, psum=True)
        s.sqp = Pool(c, "sq", 3, [128, TT], BF16)
        s.misc = Pool(c, "misc", 4, [128, TT], F32)
        s.stg = Pool(c, "stg", 4, [128, 512], F32)
        s.ws = WStream(c, 2, wbuf_elems)

    def run(s, groups):
        nc, c = s.nc, s.c
        plan = []
        for g in groups: plan += wtiles(s.w, DM, g["n0"], g["n1"], 512)
        NT = S // TT
        for _ in range(NT): s.ws.plan += plan
        for ti in range(NT):
            tb = ti * TT
            for q in range(4):
                hap, hdeps = s.htile(ti, q)
                c.dma(c.sp, s.h[:, 4 * q:4 * q + 4, :], hap, reads=hdeps, writes=s.hd[4 * q:4 * q + 4])
            rmsnorm_tile(c, nc, s.pp, s.h, s.hd, s.nw[:, 0:16], s.v, s.vd, s.ones, s.sqp, s.misc)
            for g in groups:
                spec = wtiles(s.w, DM, g["n0"], g["n1"], 512)
                if not g.get("tm"):
                    gemm(c, s.ws, s.pp, spec, lambda k, t0, t1: s.v[:, k, t0:t1], lambda k: [s.vd[k]], TT,
                         lambda col, t0, t1, ps, pd, g=g: g["epi"](col, tb + t0, tb + t1, ps, pd))
                else:
                    col = 0
                    for (src, kc, nw) in spec:
                        wt, wd = s.ws.next()
                        for t0 in range(0, TT, 128):
                            pt, pd = s.pp.next()
                            out = pt[:, 0:nw]
                            fns = [(lambda k=k: nc.tensor.matmul(out, lhsT=s.v[:, k, t0:t0 + 128], rhs=wt[:, k, :], start=(k == 0), stop=(k == kc - 1))) for k in range(kc)]
                            c.group(c.pe, fns, reads=[wd] + s.vd, writes=[pd])
                            g["epi"](col, tb + t0, nw, out, pd)
                        col += nw


def emit_lru(nc, c, htile, ywrite, ydone):
    a1 = A1(nc, c, 2048, htile)
    dt = a1.dt
    wa = dt("wa", [4, 256, 256]); wx = dt("wx", [4, 256, 256])
    cv = dt("cv", [128, 8, 9])
    GT = dt("GT", [1024, S], BF16, "Internal")
    XT = dt("XT", [1024, S], F32, "Internal")
    gd = [D() for _ in range(8)]; xd = [D() for _ in range(8)]
    cvs = c.sb("cvs", [128, 8, 9], F32)
    cA = c.sb("cA", [128, 8], F32); cA2 = c.sb("cA2", [128, 8], F32)
    cvd = D()
    c.dma(c.sp, cvs[:], cv, writes=[cvd])
    c.op(c.act, lambda: nc.scalar.activation(out=cA[:], in_=cvs[:, :, 7], func=AF.Softplus, scale=-1.0), reads=[cvd], writes=[cvd])
    c.op(c.dve, lambda: nc.vector.tensor_scalar(out=cA2[:], in0=cA[:], scalar1=-16.0, scalar2=None, op0=ALU.mult), reads=[cvd], writes=[cvd])
    c.op(c.dve, lambda: nc.vector.tensor_scalar(out=cA[:], in0=cA[:], scalar1=-8.0, scalar2=None, op0=ALU.mult), reads=[cvd], writes=[cvd])
    stg = a1.stg
    def epi_g(col, t0, t1, ps, pd):
        ch = col // 128
        st, sd = stg.next(); o = st[:].bitcast(BF16)[:, 0:t1 - t0] if False else st[:, 0:(t1 - t0) // 2].bitcast(BF16)
        c.op(c.act, lambda: nc.scalar.activation(out=o, in_=ps, func=AF.Gelu_apprx_tanh), reads=[pd], writes=[sd])
        c.dma(c.sp, GT[col:col + 128, t0:t1], o, reads=[sd], writes=[gd[ch]])
    def epi_x(col, t0, t1, ps, pd):
        ch = col // 128
        st, sd = stg.next()
        c.op(c.dve, lambda: nc.vector.tensor_copy(out=st[:, 0:t1 - t0], in_=ps), reads=[pd], writes=[sd])
        c.dma(c.sp, XT[col:col + 128, t0:t1], st[:, 0:t1 - t0], reads=[sd], writes=[xd[ch]])
    a1.run([dict(n0=0, n1=1024, epi=epi_g), dict(n0=1024, n1=2048, epi=lambda col, t0, t1, ps, pd: epi_x(col, t0, t1, ps, pd))])
    TL = 512
    ws2 = WStream(c, 4, 2 * 256, name="wb2_", live=2)
    for tl in range(S // TL):
        for kb in range(4):
            ws2.plan += [(wa[kb].rearrange("(k p) n -> p k n", p=128), 2, 256), (wx[kb].rearrange("(k p) n -> p k n", p=128), 2, 256)]
    xrp = Pool(c, "xr", 2, [128, 2, TL + 3], F32); xcp = Pool(c, "xc", 2, [128, 2, TL], F32)
    xbp = Pool(c, "xb", 2, [128, 2, TL], BF16); gp = Pool(c, "gg", 2, [128, 2, TL], BF16)
    ap_ = Pool(c, "aa", 2, [128, 2, TL], F32); bp = Pool(c, "bb", 2, [128, 2, TL], F32)
    hp = Pool(c, "hs", 2, [128, 2, TL], F32); yp = Pool(c, "yy", 2, [128, 2, TL], BF16)
    tp = Pool(c, "tmp", 4, [128, 512], F32)
    carry = c.sb("carry", [128, 8], F32); cyd = [D() for _ in range(8)]
    XTv = XT.rearrange("(c p) t -> p c t", p=128); GTv = GT.rearrange("(c p) t -> p c t", p=128)
    pp = a1.pp
    for tl in range(S // TL):
        for kb in range(4):
            t0 = tl * TL
            xr, xrd = xrp.next(); xc, xcd = xcp.next(); xb, xbd = xbp.next(); gg, ggd = gp.next()
            aa, aad = ap_.next(); bb, bbd = bp.next(); hs, hsd = hp.next(); yy, yyd = yp.next()
            if tl == 0:
                c.op(c.dve, lambda: nc.vector.memset(xr[:, :, 0:3], 0.0), writes=[xrd])
                c.dma(c.sp, xr[:, :, 3:], XTv[:, 2 * kb:2 * kb + 2, 0:TL], reads=xd[2 * kb:2 * kb + 2], writes=[xrd])
            else:
                c.dma(c.sp, xr[:], XTv[:, 2 * kb:2 * kb + 2, t0 - 3:t0 + TL], reads=xd[2 * kb:2 * kb + 2], writes=[xrd])
            c.dma(c.sp, gg[:], GTv[:, 2 * kb:2 * kb + 2, t0:t0 + TL], reads=gd[2 * kb:2 * kb + 2], writes=[ggd])
            for d in range(2):
                ch = 2 * kb + d
                c.op(c.dve, lambda: nc.vector.tensor_scalar(out=xc[:, d, :], in0=xr[:, d, 0:TL], scalar1=cvs[:, ch, 0:1], scalar2=cvs[:, ch, 4:5], op0=ALU.mult, op1=ALU.add), reads=[xrd, cvd], writes=[xcd])
                for k in range(1, 4):
                    c.op(c.dve, lambda: nc.vector.scalar_tensor_tensor(out=xc[:, d, :], in0=xr[:, d, k:k + TL], scalar=cvs[:, ch, k:k + 1], in1=xc[:, d, :], op0=ALU.mult, op1=ALU.add), reads=[xrd, xcd], writes=[xcd])
                c.op(c.act, lambda: nc.scalar.activation(out=xb[:, d, :], in_=xc[:, d, :], func=AF.Copy), reads=[xcd], writes=[xbd])
            wat, wad = ws2.next(); wxt, wxd = ws2.next()
            for d in range(2):
                ch = 2 * kb + d
                for b0 in range(0, TL, 512):
                    pa, pad = pp.next(); px, pxd = pp.next()
                    c.group(c.pe, [(lambda k=k: nc.tensor.matmul(pa[:], lhsT=wat[:, k, d * 128:(d + 1) * 128], rhs=xb[:, k, b0:b0 + 512], start=(k == 0), stop=(k == 1))) for k in range(2)], reads=[wad, xbd], writes=[pad])
                    c.group(c.pe, [(lambda k=k: nc.tensor.matmul(px[:], lhsT=wxt[:, k, d * 128:(d + 1) * 128], rhs=xb[:, k, b0:b0 + 512], start=(k == 0), stop=(k == 1))) for k in range(2)], reads=[wxd, xbd], writes=[pxd])
                    r, rd = tp.next(); a2, a2d = tp.next()
                    asl = aa[:, d, b0:b0 + 512]; bsl = bb[:, d, b0:b0 + 512]
                    c.op(c.act, lambda: nc.scalar.activation(out=r[:], in_=pa[:], func=AF.Sigmoid, bias=cvs[:, ch, 5:6]), reads=[pad, cvd], writes=[rd])
                    c.op(c.act, lambda: nc.scalar.activation(out=asl, in_=r[:], func=AF.Exp, scale=cA[:, ch:ch + 1]), reads=[rd, cvd], writes=[aad])
                    c.op(c.act, lambda: nc.scalar.activation(out=a2[:], in_=r[:], func=AF.Exp, scale=cA2[:, ch:ch + 1]), reads=[rd, cvd], writes=[a2d])
                    c.op(c.act, lambda: nc.scalar.activation(out=a2[:], in_=a2[:], func=AF.Sqrt, scale=-1.0, bias=1.0), reads=[a2d], writes=[a2d])
                    c.op(c.act, lambda: nc.scalar.activation(out=r[:], in_=px[:], func=AF.Sigmoid, bias=cvs[:, ch, 6:7]), reads=[pxd, cvd, aad], writes=[rd])
                    c.op(c.dve, lambda: nc.vector.tensor_tensor(out=bsl, in0=r[:], in1=xc[:, d, b0:b0 + 512], op=ALU.mult), reads=[rd, xcd], writes=[bbd])
                    c.op(c.dve, lambda: nc.vector.tensor_tensor(out=bsl, in0=bsl, in1=a2[:], op=ALU.mult), reads=[a2d, bbd], writes=[bbd])
                init = 0.0 if tl == 0 else carry[:, ch:ch + 1]
                c.op(c.dve, lambda: nc.vector.tensor_tensor_scan(out=hs[:, d, :], data0=aa[:, d, :], data1=bb[:, d, :], initial=init, op0=ALU.mult, op1=ALU.add), reads=[aad, bbd, cyd[ch]], writes=[hsd])
                c.op(c.dve, lambda: nc.vector.tensor_copy(out=carry[:, ch:ch + 1], in_=hs[:, d, TL - 1:TL]), reads=[hsd], writes=[cyd[ch]])
                c.op(c.dve, lambda: nc.vector.tensor_tensor(out=yy[:, d, :], in0=hs[:, d, :], in1=gg[:, d, :], op=ALU.mult), reads=[hsd, ggd], writes=[yyd])
            ywrite(yy[:], 2 * kb, 2, t0, TL, [yyd])
        ydone(tl)


class Cut(Exception): pass

def emit_ssd(nc, c, htile, ywrite, ydone, stop=9, lim=(8, 4, 4), cut=99):
    def CUT(k): pass
    NCOL = 2048 + 3072 + 32
    dt = c.dram
    cvx = dt("cvx", [128, 24, 5]); hv = dt("hv", [128, 2, 32]); dn = dt("dn", [128, 16, 2])
    cf32 = dt("cf32", [128, 128 + 1024 + 128 + 128]); c8 = dt("c8", [8, 1024 + 128]); identb_d = dt("identb", [128, 128], BF16)
    ZT = dt("ZT", [2048, S], BF16, "Internal")
    XT = dt("XT", [3072, S], F32, "Internal")
    XC = dt("XC", [3072, S], BF16, "Internal")
    DT = dt("DT", [S, 32], F32, "Internal")
    zd = [D() for _ in range(16)]; xd = [D() for _ in range(24)]; xcd = [D() for _ in range(24)]; dtd = D()
    P = c.psb
    cvs = P("cvs", [128, 24, 5], F32); hvs = P("hvs", [128, 2, 32], F32); dns = P("dns", [128, 16, 2], F32)
    cf = P("cf", [128, 1408], F32); c8s = P("c8s", [8, 1152], F32); identb = P("identb_s", [128, 128], BF16)
    Abc = P("Abc", [128, 32], F32)
    a1 = A1(nc, c, NCOL, htile)
    kd = D()
    for (o, i) in ((cvs, cvx), (hvs, hv), (dns, dn), (cf, cf32), (c8s, c8), (identb, identb_d)):
        c.dma(c.sp, o[:], i, writes=[kd])
    tri = cf[:, 0:128]; nm8 = cf[:, 128:1152]; identf = cf[:, 1152:1280]; mask01 = cf[:, 1280:1408]
    delta = c8s[:, 0:1024]; ones8 = c8s[:, 1024:1152]
    c.op(c.act, lambda: nc.scalar.activation(out=Abc[:], in_=hvs[:, 1, :], func=AF.Exp), reads=[kd], writes=[kd])
    c.op(c.dve, lambda: nc.vector.tensor_scalar(out=Abc[:], in0=Abc[:], scalar1=-1.0, scalar2=None, op0=ALU.mult), reads=[kd], writes=[kd])
    stg = a1.stg
    def epi_z(col, t0, t1, ps, pd):
        st, sd = stg.next(); o = st[:, 0:(t1 - t0) // 2].bitcast(BF16)
        c.op(c.act, lambda: nc.scalar.activation(out=o, in_=ps, func=AF.Silu), reads=[pd], writes=[sd])
        c.dma(c.sp, ZT[col:col + 128, t0:t1], o, reads=[sd], writes=[zd[col // 128]])
    def epi_x(col, t0, t1, ps, pd):
        st, sd = stg.next()
        c.op(c.dve, lambda: nc.vector.tensor_copy(out=st[:, 0:t1 - t0], in_=ps), reads=[pd], writes=[sd])
        c.dma(c.sp, XT[col:col + 128, t0:t1], st[:, 0:t1 - t0], reads=[sd], writes=[xd[col // 128]])
    def epi_dt(col, t0, nw, ps, pd):
        st, sd = stg.next()
        c.op(c.dve, lambda: nc.vector.tensor_tensor(out=st[:, 0:32], in0=ps, in1=hvs[:, 0, :], op=ALU.add), reads=[pd, kd], writes=[sd])
        c.op(c.act, lambda: nc.scalar.activation(out=st[:, 0:32], in_=st[:, 0:32], func=AF.Softplus), reads=[sd], writes=[sd])
        c.dma(c.sp, DT[t0:t0 + 128, :], st[:, 0:32], reads=[sd], writes=[dtd])
    a1.run([dict(n0=0, n1=2048, epi=epi_z), dict(n0=2048, n1=5120, epi=epi_x), dict(n0=5120, n1=5152, tm=True, epi=epi_dt)])
    if stop == 1:
        c.finish(zd + xd + [dtd]); return nc
    c.new_stage()
    TL = 512
    XTv = XT.rearrange("(c p) t -> p c t", p=128); XCv = XC.rearrange("(c p) t -> p c t", p=128); ZTv = ZT.rearrange("(c p) t -> p c t", p=128)
    xrp = Pool(c, "xr", 2, [128, 4, TL + 3], F32); xcp = Pool(c, "xc", 2, [128, 4, TL], F32); xbp = Pool(c, "xb", 2, [128, 4, TL], BF16)
    for cg in range(6):
        for tl in range(S // TL):
            t0 = tl * TL
            xr, xrd = xrp.next(); xc, xcd_ = xcp.next(); xb, xbd = xbp.next()
            if tl == 0:
                c.op(c.dve, lambda: nc.vector.memset(xr[:, :, 0:3], 0.0), writes=[xrd])
                c.dma(c.sp, xr[:, :, 3:], XTv[:, 4 * cg:4 * cg + 4, 0:TL], reads=xd[4 * cg:4 * cg + 4], writes=[xrd])
            else:
                c.dma(c.sp, xr[:], XTv[:, 4 * cg:4 * cg + 4, t0 - 3:t0 + TL], reads=xd[4 * cg:4 * cg + 4], writes=[xrd])
            for d in range(4):
                ch = 4 * cg + d
                eng, E = (c.dve, nc.vector)
                c.op(eng, lambda: E.tensor_scalar(out=xc[:, d, :], in0=xr[:, d, 0:TL], scalar1=cvs[:, ch, 0:1], scalar2=cvs[:, ch, 4:5], op0=ALU.mult, op1=ALU.add), reads=[xrd, kd], writes=[xcd_])
                for k in range(1, 4):
                    c.op(eng, lambda: E.scalar_tensor_tensor(out=xc[:, d, :], in0=xr[:, d, k:k + TL], scalar=cvs[:, ch, k:k + 1], in1=xc[:, d, :], op0=ALU.mult, op1=ALU.add), reads=[xrd, xcd_], writes=[xcd_])
            c.op(c.act, lambda: nc.scalar.activation(out=xb[:], in_=xc[:], func=AF.Silu), reads=[xcd_], writes=[xbd])
            c.dma(c.sp, XCv[:, 4 * cg:4 * cg + 4, t0:t0 + TL], xb[:], reads=[xbd], writes=xcd[4 * cg:4 * cg + 4])
    if stop == 2:
        c.finish(xcd); return nc
    c.new_stage()
    SC = 512
    xsp = Pool(c, "xs", 2, [128, 16, SC], BF16, nd=4); bsp = Pool(c, "bs", 2, [128, 4, SC], BF16); csp = Pool(c, "cs", 2, [128, 4, SC], BF16)
    zsp = Pool(c, "zs", 2, [128, 16, SC], BF16, nd=4); dtp = Pool(c, "dts", 2, [128, 4, 32], F32); yop = Pool(c, "yo", 2, [128, 16, SC], BF16, nd=4)
    pbc = Pool(c, "pbc", 1, [128, 1024], F32, psum=True); pg = Pool(c, "pgA", 3, [128, 512], F32, psum=True); pgB = Pool(c, "pgB", 3, [128, 512], F32, psum=True)
    dtap = Pool(c, "dta", 2, [128, 32], F32); cstp = Pool(c, "cst", 2, [128, 32], F32)
    Rp = Pool(c, "R", 2, [8, 1024], F32); difp = Pool(c, "dif", 2, [128, 1024], F32); Lmp = Pool(c, "Lm", 2, [128, 1024], BF16)
    ECp = Pool(c, "EC", 2, [128, 1024], F32); CBp = Pool(c, "CBm", 2, [128, 128], BF16); Mp = Pool(c, "M", 2, [128, 1024], BF16)
    Cdp = Pool(c, "Cd", 2, [128, 1024], BF16); Xtp = Pool(c, "Xdt", 2, [128, 512], BF16); Xdp = Pool(c, "Xd", 2, [128, 512], BF16)
    dep = Pool(c, "de", 2, [128, 8], F32); Btp = Pool(c, "Btm", 2, [128, 128], BF16)
    YGp = Pool(c, "YG", 2, [128, 4, 128], F32); sqp = Pool(c, "sq2", 2, [128, 128], BF16); rsp = Pool(c, "rs", 2, [128, 128], F32)
    ytp = Pool(c, "yt", 3, [128, 128], F32); tmpS = Pool(c, "tmpS", 2, [128, 512], F32)
    Sf = [c.sb(f"Sf{g}", [128, 512], F32) for g in range(4)]; Sb = [c.sb(f"Sb{g}", [128, 512], BF16) for g in range(4)]
    Sd = [D() for _ in range(4)]; Sbd = [D() for _ in range(4)]
    ones_b = c.sb("ones_b", [128, 128], BF16); epsg = c.sb("epsg", [128, 1], F32); od = D()
    c.op(c.dve, lambda: nc.vector.memset(ones_b[:], 1.0), writes=[od])
    c.op(c.dve, lambda: nc.vector.memset(epsg[:], EPS), writes=[od])
    DTv = DT.rearrange("(j p) h -> p j h", p=128)
    def ssd_body(j, g, first, tk, dta, dtad, cst, cstd, xs, xsd, bs, bsd, cs_, csd, zs, zsd, dts, dtsd, yo, yod):
        hs_ = slice(8 * g, 8 * g + 8)
        g2, g2d = pg.next()
        c.group(c.pe, [lambda: nc.tensor.matmul(g2[0:8, 0:128], lhsT=dta[:, hs_], rhs=tri, start=True, stop=True)], reads=[dtad, kd], writes=[g2d])
        R, Rd = Rp.next()
        c.op(c.dve, lambda: nc.vector.tensor_tensor(out=R[:].rearrange("k (h l) -> k h l", h=8), in0=g2[0:8, 0:128].unsqueeze(1).to_broadcast([8, 8, 128]),
                                                      in1=delta.rearrange("k (h l) -> k h l", h=8), op=ALU.mult), reads=[g2d, kd], writes=[Rd])
        yield 'A'
        bc, bcd = pbc.next()
        c.group(c.pe, [(lambda hh=hh: nc.tensor.matmul(bc[:, 512 * hh:512 * hh + 512], lhsT=ones8, rhs=R[:, 512 * hh:512 * hh + 512], start=True, stop=True)) for hh in range(2)], reads=[Rd, kd], writes=[bcd])
        yield 'A'
        bc3 = bc[:].rearrange("p (h l) -> p h l", h=8)
        dif, difd = difp.next(); Lm, Lmd = Lmp.next(); EC, ECd = ECp.next()
        c.op(c.dve, lambda: nc.vector.tensor_tensor(out=dif[:].rearrange("p (h l) -> p h l", h=8), in0=bc3, in1=cst[:, hs_].unsqueeze(2).to_broadcast([128, 8, 128]), op=ALU.subtract), reads=[bcd, cstd], writes=[difd])
        c.op(c.pool, lambda: nc.gpsimd.tensor_tensor(out=dif[:], in0=dif[:], in1=nm8, op=ALU.add), reads=[difd, kd], writes=[difd])
        yield 'A'
        c.op(c.act, lambda: nc.scalar.activation(out=Lm[:], in_=dif[:], func=AF.Exp), reads=[difd], writes=[Lmd])
        c.op(c.act, lambda: nc.scalar.activation(out=EC[:], in_=bc[:], func=AF.Exp), reads=[bcd], writes=[ECd])
        yield 'A'
        EC3 = EC[:].rearrange("p (h l) -> p h l", h=8)
        g3, g3d = pg.next()
        c.group(c.pe, [lambda: nc.tensor.matmul(g3[:, 0:128], lhsT=bs[:, g, tk], rhs=cs_[:, g, tk], start=True, stop=True)], reads=[bsd, csd], writes=[g3d])
        CBm, CBd = CBp.next()
        c.op(c.dve, lambda: nc.vector.tensor_tensor(out=CBm[:], in0=g3[:, 0:128], in1=mask01, op=ALU.mult), reads=[g3d, kd], writes=[CBd])
        M, Md = Mp.next()
        c.op(c.dve, lambda: nc.vector.tensor_tensor(out=M[:].rearrange("p (h l) -> p h l", h=8), in0=Lm[:].rearrange("p (h l) -> p h l", h=8), in1=CBm[:].unsqueeze(1).to_broadcast([128, 8, 128]), op=ALU.mult), reads=[Lmd, CBd], writes=[Md])
        yield 'A'
        Cd, Cdd = Cdp.next()
        if not first:
            c.op(c.pool, lambda: nc.gpsimd.tensor_tensor(out=Cd[:].rearrange("p (h l) -> p h l", h=8), in0=EC3, in1=cs_[:, g, tk].unsqueeze(1).to_broadcast([128, 8, 128]), op=ALU.mult), reads=[ECd, csd], writes=[Cdd])
        g4, g4d = pg.next(); g4B, g4Bd = pg.next()
        fns = [(lambda q=q: nc.tensor.matmul(g4[:, 128 * q:128 * q + 128], lhsT=xs[:, 4 * g + q, tk], rhs=identb[:], start=True, stop=True)) for q in range(4)]
        c.group(c.pe, fns, reads=[xsd[g], kd], writes=[g4d])
        c.group(c.pe, [lambda: nc.tensor.matmul(g4B[:, 0:128], lhsT=bs[:, g, tk], rhs=identb[:], start=True, stop=True)], reads=[bsd, kd], writes=[g4Bd])
        Xdt, Xdtd = Xtp.next(); Xd, Xdd = Xdp.next(); de, ded = dep.next(); Btm, Btd = Btp.next()
        c.op(c.dve, lambda: nc.vector.tensor_tensor(out=Xdt[:].rearrange("p (h q) -> p h q", h=8), in0=g4[:].rearrange("p (h q) -> p h q", h=8), in1=dts[:, j, hs_].unsqueeze(2).to_broadcast([128, 8, 64]), op=ALU.mult), reads=[g4d, dtsd], writes=[Xdtd])
        c.op(c.act, lambda: nc.scalar.activation(out=Btm[:], in_=g4B[:, 0:128], func=AF.Copy), reads=[g4Bd], writes=[Btd])
        yield 'A'
        c.op(c.dve, lambda: nc.vector.tensor_tensor(out=de[:], in0=bc3[:, :, 127], in1=cst[:, hs_], op=ALU.subtract), reads=[bcd, cstd], writes=[ded])
        c.op(c.act, lambda: nc.scalar.activation(out=de[:], in_=de[:], func=AF.Exp), reads=[ded], writes=[ded])
        c.op(c.dve, lambda: nc.vector.tensor_tensor(out=Xd[:].rearrange("p (h q) -> p h q", h=8), in0=Xdt[:].rearrange("p (h q) -> p h q", h=8), in1=de[:].unsqueeze(2).to_broadcast([128, 8, 64]), op=ALU.mult), reads=[Xdtd, ded], writes=[Xdd])
        yield 'S'
        g5, g5d = pgB.next()
        fns = []
        for hh in range(8):
            o = g5[64 * (hh % 2):64 * (hh % 2) + 64, 128 * (hh // 2):128 * (hh // 2) + 128]
            fns.append(lambda o=o, hh=hh: nc.tensor.matmul(o, lhsT=Xdt[:, 64 * hh:64 * hh + 64], rhs=M[:, 128 * hh:128 * hh + 128], start=True, stop=first))
            if not first:
                fns.append(lambda o=o, hh=hh: nc.tensor.matmul(o, lhsT=Sb[g][:, 64 * hh:64 * hh + 64], rhs=Cd[:, 128 * hh:128 * hh + 128], start=False, stop=True))
        c.group(c.pe, fns, reads=[Xdtd, Md, Sbd[g], Cdd], writes=[g5d])
        yield 'B'
        YG, YGd = YGp.next()
        g6, g6d = pgB.next()
        for q in range(4):
            ch = 4 * g + q
            yt, ytd = ytp.next()
            c.op(c.dve, lambda: nc.vector.scalar_tensor_tensor(out=yt[:], in0=xs[:, ch, tk], scalar=dns[:, ch, 0:1], in1=g5[:, 128 * q:128 * q + 128], op0=ALU.mult, op1=ALU.add), reads=[xsd[g], g5d, kd], writes=[ytd])
            c.op(c.dve, lambda: nc.vector.tensor_tensor(out=YG[:, q, :], in0=yt[:], in1=zs[:, ch, tk], op=ALU.mult), reads=[ytd, zsd[g]], writes=[YGd])
            sq, sqd = sqp.next()
            c.op(c.act, lambda: nc.scalar.activation(out=sq[:], in_=YG[:, q, :], func=AF.Square), reads=[YGd], writes=[sqd])
            c.group(c.pe, [lambda: nc.tensor.matmul(g6[:, 0:128], lhsT=ones_b[:], rhs=sq[:], start=(q == 0), stop=(q == 3), skip_group_check=True)], reads=[sqd, od], writes=[g6d])
            yield 'B'
        rs, rsd = rsp.next()
        c.op(c.act, lambda: nc.scalar.activation(out=rs[:], in_=g6[:, 0:128], func=AF.Sqrt, scale=1.0 / 512, bias=epsg[:]), reads=[g6d, od], writes=[rsd])
        c.op(c.dve, lambda: nc.vector.reciprocal(out=rs[:], in_=rs[:]), reads=[rsd], writes=[rsd])
        for q in range(4):
            ch = 4 * g + q
            c.op(c.dve, lambda: nc.vector.scalar_tensor_tensor(out=yo[:, ch, tk], in0=YG[:, q, :], scalar=dns[:, ch, 1:2], in1=rs[:], op0=ALU.mult, op1=ALU.mult), reads=[YGd, rsd, kd], writes=[yod[g]])
        yield 'B'
        g7, g7d = pgB.next()
        c.group(c.pe, [lambda: nc.tensor.matmul(g7[:], lhsT=Btm[:], rhs=Xd[:], start=True, stop=True)], reads=[Btd, Xdd], writes=[g7d])
        if first:
            c.op(c.dve, lambda: nc.vector.tensor_copy(out=Sf[g][:], in_=g7[:]), reads=[g7d], writes=[Sd[g]])
        else:
            ts_, tsd = tmpS.next()
            c.op(c.dve, lambda: nc.vector.tensor_tensor(out=ts_[:].rearrange("p (h q) -> p h q", h=8), in0=Sf[g][:].rearrange("p (h q) -> p h q", h=8), in1=EC3[:, :, 127:128].to_broadcast([128, 8, 64]), op=ALU.mult), reads=[Sd[g], ECd], writes=[tsd])
            c.op(c.dve, lambda: nc.vector.tensor_tensor(out=Sf[g][:], in0=ts_[:], in1=g7[:], op=ALU.add), reads=[tsd, g7d], writes=[Sd[g]])
        c.op(c.act, lambda: nc.scalar.activation(out=Sb[g][:], in_=Sf[g][:], func=AF.Copy), reads=[Sd[g]], writes=[Sbd[g]])

    wv = Weave()
    for sc in range(min(S // SC, lim[0])):
        s0 = sc * SC
        xs, xsd = xsp.next(); bs, bsd = bsp.next(); cs_, csd = csp.next(); zs, zsd = zsp.next(); dts, dtsd = dtp.next(); yo, yod = yop.next()
        for q in range(4): c.dma(c.sp, xs[:, 4 * q:4 * q + 4, :], XCv[:, 4 * q:4 * q + 4, s0:s0 + SC], reads=xcd[4 * q:4 * q + 4], writes=[xsd[q]])
        c.dma(c.sp, bs[:], XCv[:, 16:20, s0:s0 + SC], reads=xcd[16:20], writes=[bsd])
        c.dma(c.sp, cs_[:], XCv[:, 20:24, s0:s0 + SC], reads=xcd[20:24], writes=[csd])
        for q in range(4): c.dma(c.sp, zs[:, 4 * q:4 * q + 4, :], ZTv[:, 4 * q:4 * q + 4, s0:s0 + SC], reads=zd[4 * q:4 * q + 4], writes=[zsd[q]])
        c.dma(c.sp, dts[:], DTv[:, 4 * sc:4 * sc + 4, :], reads=[dtd], writes=[dtsd])
        for j in range(lim[1]):
            first = (sc == 0 and j == 0)
            tk = slice(j * 128, (j + 1) * 128)
            dta, dtad = dtap.next(); cst, cstd = cstp.next()
            c.op(c.dve, lambda: nc.vector.tensor_tensor(out=dta[:], in0=dts[:, j, :], in1=Abc[:], op=ALU.mult), reads=[dtsd, kd], writes=[dtad])
            g1, g1d = pg.next()
            c.group(c.pe, [lambda: nc.tensor.matmul(g1[:, 0:32], lhsT=tri, rhs=dta[:], start=True, stop=True)], reads=[dtad, kd], writes=[g1d])
            c.op(c.act, lambda: nc.scalar.activation(out=cst[:], in_=g1[:, 0:32], func=AF.Copy), reads=[g1d], writes=[cstd])
            for g in range(lim[2]):
                wv.push(ssd_body(j, g, first, tk, dta, dtad, cst, cstd, xs, xsd, bs, bsd, cs_, csd, zs, zsd, dts, dtsd, yo, yod))
        wv.flush()
        for q in range(4): ywrite(yo[:, 4 * q:4 * q + 4, :], 4 * q, 4, s0, SC, [yod[q]])
        ydone(sc)


import math
SCALE = 128 ** -0.5
NCOLN = 1024 + 6 * 256 + 32

def emit_nsa(nc, c, htile, ywrite, ydone, stop=9, lim=(2, 32), ngrp=5):
    dt = c.dram
    pos_d = dt("pos", [1, S], I32); invf_d = dt("invf", [128, 1]); sgn_d = dt("sgn", [128, 1])
    pm_d = dt("pm", [128, 128], BF16); identb_d = dt("identb", [128, 128], BF16)
    pe_d = dt("peT", [128, 2, 32]); w1_d = dt("w1", [2, 4096, 128]); w2_d = dt("w2", [2, 128, 128])
    cmask_d = dt("cmask", [32, 128, 2, 128], BF16); c2s_d = dt("c2s", [128, 2, 64])
    tkk_d = dt("tkk", [32, 128, 2, 64]); ex_d = dt("ex", [64, 32 * 128], BF16)
    tri_d = dt("tri2", [128, 2, 128], BF16)
    gsel_d = dt("gsel", [24, 6, 512]); ones24_d = dt("ones24", [24, 128])
    QT = dt("QT", [1024, S], BF16, "Internal")
    FT = dt("FT", [1024, S], BF16, "Internal")
    VT = dt("VT", [S, 512], BF16, "Internal")
    GT = dt("GT", [32, S], F32, "Internal")
    qd = [D() for _ in range(8)]; fd = [D() for _ in range(8)]; vtd = D(); gtd = D()
    P = c.psb
    cosf = P("cosf", [128, S], F32); sinf = P("sinf", [128, S], F32)
    invf = P("invf_s", [128, 1], F32); sgn = P("sgn_s", [128, 1], F32); pm = P("pm_s", [128, 128], BF16); identb = P("identb_s", [128, 128], BF16)
    kd = D(); tabd = D()
    for (o, i) in ((invf, invf_d), (sgn, sgn_d), (pm, pm_d), (identb, identb_d)): c.dma(c.sp, o[:], i, writes=[kd])
    posi = c.sb("posi", [128, S], I32); u = c.sb("u", [128, S], F32); kf = c.sb("kf", [128, S], F32); ki = c.sb("ki", [128, S], I32)
    pd_ = D(); ud = D(); kfd = D()
    c.dma(c.sp, posi[:], pos_d[0, :].partition_broadcast(128), writes=[pd_])
    c.op(c.dve, lambda: nc.vector.tensor_copy(out=u[:], in_=posi[:]), reads=[pd_], writes=[ud])
    c.op(c.dve, lambda: nc.vector.tensor_scalar(out=u[:], in0=u[:], scalar1=invf[:, 0:1], scalar2=float(1.0 / (2 * math.pi)), op0=ALU.mult, op1=ALU.mult), reads=[ud, kd], writes=[ud])
    def wrap(dst, src, sd_):
        c.op(c.dve, lambda: nc.vector.tensor_copy(out=ki[:], in_=src), reads=[sd_], writes=[kfd])
        c.op(c.dve, lambda: nc.vector.tensor_copy(out=kf[:], in_=ki[:]), reads=[kfd], writes=[kfd])
        c.op(c.dve, lambda: nc.vector.tensor_tensor(out=dst, in0=src, in1=kf[:], op=ALU.subtract), reads=[sd_, kfd], writes=[tabd])
        c.op(c.dve, lambda: nc.vector.tensor_scalar(out=kf[:], in0=dst, scalar1=0.5, scalar2=None, op0=ALU.is_gt), reads=[tabd], writes=[kfd])
        c.op(c.dve, lambda: nc.vector.tensor_tensor(out=dst, in0=dst, in1=kf[:], op=ALU.subtract), reads=[tabd, kfd], writes=[tabd])
        c.op(c.dve, lambda: nc.vector.tensor_scalar(out=kf[:], in0=dst, scalar1=-0.5, scalar2=None, op0=ALU.is_lt), reads=[tabd], writes=[kfd])
        c.op(c.dve, lambda: nc.vector.tensor_tensor(out=dst, in0=dst, in1=kf[:], op=ALU.add), reads=[tabd, kfd], writes=[tabd])
    wrap(sinf[:], u[:], ud)
    c.op(c.dve, lambda: nc.vector.tensor_scalar(out=u[:], in0=u[:], scalar1=0.25, scalar2=None, op0=ALU.add), reads=[ud, tabd], writes=[ud])
    wrap(cosf[:], u[:], ud)
    c.op(c.act, lambda: nc.scalar.activation(out=sinf[:], in_=sinf[:], func=AF.Sin, scale=float(2 * math.pi)), reads=[tabd], writes=[tabd])
    c.op(c.act, lambda: nc.scalar.activation(out=cosf[:], in_=cosf[:], func=AF.Sin, scale=float(2 * math.pi)), reads=[tabd], writes=[tabd])
    c.op(c.dve, lambda: nc.vector.tensor_scalar(out=sinf[:], in0=sinf[:], scalar1=sgn[:, 0:1], scalar2=None, op0=ALU.mult), reads=[tabd, kd], writes=[tabd])
    if stop == 0:
        dt('hT', [DM, S]); dt('w_in', [DM, NCOLN]); dt('nrm', [128, 16])
        tabo = dt('tabo', [128, 2, S], F32, 'ExternalOutput'); tod = D()
        c.dma(c.sp, tabo[:, 0, :], cosf[:], reads=[tabd], acc=[tod]); c.dma(c.sp, tabo[:, 1, :], sinf[:], reads=[tabd], acc=[tod])
        c.finish([tod]); c.barrier(); return nc
    c.new_stage()
    a1 = A1(nc, c, NCOLN, htile)
    stg = a1.stg; pp = a1.pp
    xbp = Pool(c, "xbr", 2, [128, 512], BF16); t1p = Pool(c, "t1r", 2, [128, 512], F32); t2p = Pool(c, "t2r", 2, [128, 512], F32)
    import os
    RM = 3
    def rope_epi(dst, dd):
        def epi(col, t0, t1, ps, pd):
            n = t1 - t0
            if RM == 0:
                st, sd = stg.next(); o = st[:, 0:n // 2].bitcast(BF16)
                c.op(c.act, lambda: nc.scalar.activation(out=o, in_=ps, func=AF.Copy), reads=[pd], writes=[sd])
                c.dma(c.sp, dst[col:col + 128, t0:t1], o, reads=[sd], acc=[dd[col // 128]])
                return
            xb, xbd = xbp.next(); t1_, t1d = t1p.next(); t2_, t2d = t2p.next()
            c.op(c.act, lambda: nc.scalar.activation(out=xb[:, 0:n], in_=ps, func=AF.Copy), reads=[pd], writes=[xbd])
            p2, p2d = pp.next()
            c.group(c.pe, [lambda: nc.tensor.matmul(p2[:, 0:n], lhsT=pm[:], rhs=xb[:, 0:n], start=True, stop=True)], reads=[xbd, kd], writes=[p2d])
            ch = col // 128
            st, sd = stg.next(); o = st[:, 0:n // 2].bitcast(BF16)
            if RM == 1:
                c.op(c.act, lambda: nc.scalar.activation(out=o, in_=p2[:, 0:n], func=AF.Copy), reads=[pd, p2d], writes=[sd])
                c.dma(c.sp, dst[col:col + 128, t0:t1], o, reads=[sd], acc=[dd[ch]])
                return
            c.op(c.dve, lambda: nc.vector.tensor_tensor(out=t1_[:, 0:n], in0=ps, in1=cosf[:, t0:t1], op=ALU.mult), reads=[pd, tabd], writes=[t1d])
            if RM == 2:
                c.op(c.act, lambda: nc.scalar.activation(out=o, in_=t1_[:, 0:n], func=AF.Copy), reads=[t1d, p2d], writes=[sd])
                c.dma(c.sp, dst[col:col + 128, t0:t1], o, reads=[sd], acc=[dd[ch]])
                return
            c.op(c.dve, lambda: nc.vector.tensor_tensor(out=t2_[:, 0:n], in0=p2[:, 0:n], in1=sinf[:, t0:t1], op=ALU.mult), reads=[p2d, tabd], writes=[t2d])
            c.op(c.dve, lambda: nc.vector.tensor_tensor(out=o, in0=t1_[:, 0:n], in1=t2_[:, 0:n], op=ALU.add), reads=[t1d, t2d], writes=[sd])
            c.dma(c.sp, dst[col:col + 128, t0:t1], o, reads=[sd], acc=[dd[ch]])
        return epi
    def plain_epi(dst, dd, off):
        def epi(col, t0, t1, ps, pd):
            n = t1 - t0
            st, sd = stg.next(); o = st[:, 0:n // 2].bitcast(BF16)
            c.op(c.act, lambda: nc.scalar.activation(out=o, in_=ps, func=AF.Copy), reads=[pd], writes=[sd])
            c.dma(c.sp, dst[off + col:off + col + 128, t0:t1], o, reads=[sd], acc=[dd[(off + col) // 128]])
        return epi
    def epi_v(col, t0, nw, ps, pd):
        st, sd = stg.next(); o = st[:, 0:256].bitcast(BF16)
        c.op(c.act, lambda: nc.scalar.activation(out=o, in_=ps, func=AF.Copy), reads=[pd], writes=[sd])
        c.dma(c.sp, VT[t0:t0 + 128, :], o, reads=[sd], acc=[vtd])
    def epi_g(col, t0, t1, ps, pd):
        st, sd = stg.next()
        c.op(c.act, lambda: nc.scalar.activation(out=st[0:32, 0:t1 - t0], in_=ps, func=AF.Sigmoid), reads=[pd], writes=[sd])
        c.dma(c.sp, GT[:, t0:t1], st[0:32, 0:t1 - t0], reads=[sd], acc=[gtd])
    rope_f = rope_epi(FT, fd)
    a1.run([dict(n0=0, n1=1024, epi=rope_epi(QT, qd)),
            dict(n0=1024, n1=1536, epi=plain_epi(FT, fd, 0)),
            dict(n0=1536, n1=2048, epi=lambda col, t0, t1, ps, pd: rope_f(col + 512, t0, t1, ps, pd)),
            dict(n0=2048, n1=2560, tm=True, epi=epi_v),
            dict(n0=2560, n1=2592, epi=epi_g)][0:ngrp])
    if stop == 1:
        c.finish(qd + fd + [vtd, gtd]); return nc
    c.new_stage()
    NC_ = 255
    KC = [P(f"KC{g}", [128, 256], BF16) for g in range(2)]; VC = [P(f"VC{g}", [128, 2, 128], BF16) for g in range(2)]
    KCd = [D(), D()]; VCd = [D(), D()]
    w1s = c.sb("w1s", [128, 32, 128], BF16); w2s = c.sb("w2s", [128, 128], BF16); pes = c.sb("pes", [128, 2, 32], F32); peb = c.sb("peb", [128, 2, 32], BF16)
    srcp = Pool(c, "csrc", 2, [128, S], BF16); b1 = c.sb("b1", [128, 2], F32); gTt = c.sb("gTt", [128, 256], BF16); xb2 = c.sb("xb2", [128, 256], BF16)
    t1c = c.sb("t1c", [128, 256], F32); t2c = c.sb("t2c", [128, 256], F32)
    pp2 = Pool(c, "pp2", 4, [128, 512], F32, psum=True)
    w1d = D(); w2d = D(); ped = D(); b1d = D(); gTd = D(); xb2d = D(); tcd = D()
    c.dma(c.sp, pes[:], pe_d, writes=[ped])
    c.op(c.dve, lambda: nc.vector.tensor_copy(out=peb[:], in_=pes[:]), reads=[ped], writes=[ped])
    for g in range(2):
        c.op(c.dve, lambda: nc.vector.memset(KC[g][:], 0.0), writes=[KCd[g]])
        c.op(c.dve, lambda: nc.vector.memset(VC[g][:], 0.0), writes=[VCd[g]])
    FTv = FT.rearrange("(c p) t -> p c t", p=128)
    for kv in range(2):
        c.dma(c.pool, w1s[:], w1_d[kv].rearrange("(l d) f -> d l f", d=128), writes=[w1d])
        c.dma(c.pool, w2s[:], w2_d[kv], writes=[w2d])
        pb, pbd = pp2.next()
        c.group(c.pe, [(lambda l=l: nc.tensor.matmul(pb[:, 0:1], lhsT=w1s[:, l, :], rhs=peb[:, kv, l:l + 1], start=(l == 0), stop=(l == 31))) for l in range(32)], reads=[w1d, ped], writes=[pbd])
        c.op(c.act, lambda: nc.scalar.activation(out=b1[:, kv:kv + 1], in_=pb[:, 0:1], func=AF.Copy), reads=[pbd], writes=[b1d])
        for g in range(2):
            src, srcd = srcp.next()
            ch = 2 * kv + g
            c.dma(c.sp, src[:], FTv[:, ch, :], reads=[fd[ch]], writes=[srcd])
            ph, phd = pp2.next()
            c.group(c.pe, [(lambda l=l: nc.tensor.matmul(ph[:, 0:NC_], lhsT=w1s[:, l, :], rhs=src[:, l:l + 16 * (NC_ - 1) + 1:16], start=(l == 0), stop=(l == 31))) for l in range(32)], reads=[w1d, srcd], writes=[phd])
            c.op(c.act, lambda: nc.scalar.activation(out=gTt[:, 0:NC_], in_=ph[:, 0:NC_], func=AF.Gelu_apprx_tanh, bias=b1[:, kv:kv + 1]), reads=[phd, b1d], writes=[gTd])
            if kv == 0:
                pk, pkd = pp2.next()
                c.group(c.pe, [lambda: nc.tensor.matmul(pk[:, 0:NC_], lhsT=w2s[:], rhs=gTt[:, 0:NC_], start=True, stop=True)], reads=[w2d, gTd], writes=[pkd])
                c.op(c.act, lambda: nc.scalar.activation(out=xb2[:, 0:NC_], in_=pk[:, 0:NC_], func=AF.Copy), reads=[pkd], writes=[xb2d])
                p2, p2d = pp2.next()
                c.group(c.pe, [lambda: nc.tensor.matmul(p2[:, 0:NC_], lhsT=pm[:], rhs=xb2[:, 0:NC_], start=True, stop=True)], reads=[xb2d, kd], writes=[p2d])
                cs_ = cosf[:, 31:31 + 16 * (NC_ - 1) + 1:16]; sn_ = sinf[:, 31:31 + 16 * (NC_ - 1) + 1:16]
                c.op(c.dve, lambda: nc.vector.tensor_tensor(out=t1c[:, 0:NC_], in0=pk[:, 0:NC_], in1=cs_, op=ALU.mult), reads=[pkd, tabd], writes=[tcd])
                c.op(c.dve, lambda: nc.vector.tensor_tensor(out=t2c[:, 0:NC_], in0=p2[:, 0:NC_], in1=sn_, op=ALU.mult), reads=[p2d, tabd, tcd], writes=[tcd])
                c.op(c.dve, lambda: nc.vector.tensor_tensor(out=KC[g][:, 0:NC_], in0=t1c[:, 0:NC_], in1=t2c[:, 0:NC_], op=ALU.add), reads=[tcd], writes=[KCd[g]])
            else:
                for nch in range(2):
                    nn = 128 if nch == 0 else NC_ - 128
                    pv, pvd = pp2.next()
                    c.group(c.pe, [lambda: nc.tensor.matmul(pv[0:nn, 0:128], lhsT=gTt[:, 128 * nch:128 * nch + nn], rhs=w2s[:], start=True, stop=True)], reads=[w2d, gTd], writes=[pvd])
                    c.op(c.act, lambda: nc.scalar.activation(out=VC[g][0:nn, nch, :], in_=pv[0:nn, 0:128], func=AF.Copy), reads=[pvd], writes=[VCd[g]])
    if stop == 2:
        c.barrier(); return nc
    c.new_stage()
    KsT_ = [c.sb(f"KsT{g}", [128, S], BF16) for g in range(2)]; KwT_ = [c.sb(f"KwT{g}", [128, S], BF16) for g in range(2)]
    Vs_ = [c.sb(f"Vs{g}", [128, 32, 128], BF16) for g in range(2)]; Vw_ = [c.sb(f"Vw{g}", [128, 32, 128], BF16) for g in range(2)]
    kvd_ = [D(), D()]
    ex = c.sb("ex", [64, 32 * 128], BF16); tri2 = c.sb("tri2", [128, 2, 128], BF16); c2s = c.sb("c2s", [128, 2, 64], F32)
    gsel = c.sb("gsel", [24, 6, 512], F32); ones24 = c.sb("ones24", [24, 128], F32); ones_b = c.sb("ones_b", [128, 128], BF16)
    k3 = D()
    for (o, i) in ((ex, ex_d), (tri2, tri_d), (c2s, c2s_d), (gsel, gsel_d), (ones24, ones24_d)): c.dma(c.sp, o[:], i, writes=[k3])
    c.op(c.dve, lambda: nc.vector.memset(ones_b[:], 1.0), writes=[k3])
    qsp = Pool(c, "qs", 2, [128, 4, 128], BF16, nd=0); cmp_ = Pool(c, "cm", 2, [128, 2, 128], BF16); tkp = Pool(c, "tk", 2, [128, 2, 64], F32)
    gtp = Pool(c, "gt", 2, [24, 128], F32)
    Ecp = Pool(c, "Ec", 2, [128, 2, 512], F32); Ecbp = Pool(c, "Ecb", 2, [128, 2, 512], BF16)
    Ep = Pool(c, "E", 6, [128, 512], BF16); Emp = Pool(c, "Em", 3, [128, 512], BF16)
    rdp = Pool(c, "rd", 2, [128, 512], F32); tp_ = Pool(c, "tt", 2, [128, 512], F32); accp = Pool(c, "acc", 2, [128, 512], F32)
    impp = Pool(c, "imp", 2, [128, 64], F32); imp2p = Pool(c, "imp2", 2, [128, 64], F32); m8p = Pool(c, "m8", 2, [128, 8], F32)
    selp = Pool(c, "sel", 2, [128, 64], BF16); selTp = Pool(c, "selT", 2, [64, 128], BF16); Rgp = Pool(c, "Rg", 2, [24, 512], F32)
    mdp = Pool(c, "md", 2, [128, 128], BF16); yop = Pool(c, "yo", 2, [128, 512], BF16)
    ps = Pool(c, "ps3", 3, [128, 512], F32, psum=True); psacc = Pool(c, "psacc", 2, [128, 512], F32, psum=True); psA = Pool(c, "psA", 3, [128, 512], F32, psum=True)
    VTv = VT.rearrange("(c p) e -> p c e", p=128)
    QTv = QT.rearrange("(h p) t -> p h t", p=128)
    def finish_branch(o_ps, o_d, den_ps, den_d, gl, k, acc, accd, gt, gtd_, first, pspool=None):
        rd, rdd = rdp.next(); tt_, ttd = tp_.next(); Rg, Rgd = Rgp.next()
        c.op(c.dve, lambda: nc.vector.tensor_scalar(out=rd[:], in0=den_ps[:], scalar1=1e-30, scalar2=None, op0=ALU.max), reads=[den_d], writes=[rdd])
        c.op(c.dve, lambda: nc.vector.reciprocal(out=rd[:], in_=rd[:]), reads=[rdd], writes=[rdd])
        c.op(c.dve, lambda: nc.vector.tensor_tensor(out=tt_[:], in0=o_ps[:], in1=rd[:], op=ALU.mult), reads=[o_d, rdd], writes=[ttd])
        c.op(c.pool, lambda: nc.gpsimd.tensor_tensor(out=Rg[:].rearrange("k (h t) -> k h t", h=4), in0=gsel[:, 3 * gl + k, :].rearrange("k (h t) -> k h t", h=4), in1=gt[:].unsqueeze(1).to_broadcast([24, 4, 128]), op=ALU.mult), reads=[gtd_, k3], writes=[Rgd])
        gb, gbd = (pspool or ps).next()
        c.group(c.pe, [lambda: nc.tensor.matmul(gb[:], lhsT=ones24[:], rhs=Rg[:], start=True, stop=True)], reads=[Rgd, k3], writes=[gbd])
        if first:
            c.op(c.dve, lambda: nc.vector.tensor_tensor(out=acc[:], in0=tt_[:], in1=gb[:], op=ALU.mult), reads=[ttd, gbd], writes=[accd])
        else:
            c.op(c.dve, lambda: nc.vector.tensor_tensor(out=tt_[:], in0=tt_[:], in1=gb[:], op=ALU.mult), reads=[ttd, gbd], writes=[ttd])
            c.op(c.pool, lambda: nc.gpsimd.tensor_tensor(out=acc[:], in0=acc[:], in1=tt_[:], op=ALU.add), reads=[ttd, accd], writes=[accd])
        return rd, rdd
    BRS = '012'
    for gl in range(2):
        kvd = kvd_[gl]
        c.dma(c.sp, KsT_[gl][:], FTv[:, 4 + gl, :], reads=[fd[4 + gl]], writes=[kvd])
        c.dma(c.sp, KwT_[gl][:], FTv[:, 6 + gl, :], reads=[fd[6 + gl]], writes=[kvd])
        c.dma(c.sp, Vs_[gl][:], VTv[:, :, 128 * gl:128 * gl + 128], reads=[vtd], writes=[kvd])
        c.dma(c.sp, Vw_[gl][:], VTv[:, :, 256 + 128 * gl:256 + 128 * gl + 128], reads=[vtd], writes=[kvd])
    def nsa_body(qb, gl):
        kvd = kvd_[gl]; KsT = KsT_[gl]; KwT = KwT_[gl]; Vs = Vs_[gl]; Vw = Vw_[gl]
        t0 = 128 * qb
        qs, qsd = qsp.next(); cm, cmd = cmp_.next(); tk, tkd = tkp.next(); gt, gtd_ = gtp.next()
        c.dma(c.sp, qs[:], QTv[:, 4 * gl:4 * gl + 4, t0:t0 + 128], reads=qd[4 * gl:4 * gl + 4], writes=[qsd])
        c.dma(c.sp, cm[:], cmask_d[qb], writes=[cmd])
        c.dma(c.sp, tk[:], tkk_d[qb], writes=[tkd])
        c.dma(c.sp, gt[:], GT[0:24, t0:t0 + 128], reads=[gtd], writes=[gtd_])
        q2 = qs[:].rearrange("p h t -> p (h t)")
        acc, accd = accp.next()
        yield 'A'
        Ec, Ecd = Ecp.next(); Ecb, Ecbd = Ecbp.next()
        for nch in range(2):
            sp_, spd = psA.next()
            c.group(c.pe, [lambda: nc.tensor.matmul(sp_[:], lhsT=KC[gl][:, 128 * nch:128 * nch + 128], rhs=q2, start=True, stop=True)], reads=[KCd[gl], qsd], writes=[spd])
            E, Ed = Ep.next()
            c.op(c.act, lambda: nc.scalar.activation(out=E[:], in_=sp_[:], func=AF.Exp, scale=SCALE), reads=[spd], writes=[Ed])
            c.op(c.dve, lambda: nc.vector.tensor_tensor(out=Ec[:, nch, :].rearrange("p (h t) -> p h t", h=4), in0=E[:].rearrange("p (h t) -> p h t", h=4), in1=cm[:, nch, :].unsqueeze(1).to_broadcast([128, 4, 128]), op=ALU.mult), reads=[Ed, cmd], writes=[Ecd])
            yield 'A'
        c.op(c.act, lambda: nc.scalar.activation(out=Ecb[:], in_=Ec[:], func=AF.Copy), reads=[Ecd], writes=[Ecbd])
        oc, ocd = psA.next(); dc, dcd = psA.next()
        c.group(c.pe, [(lambda n_=n_: nc.tensor.matmul(oc[:], lhsT=VC[gl][:, n_, :], rhs=Ecb[:, n_, :], start=(n_ == 0), stop=(n_ == 1))) for n_ in range(2)], reads=[VCd[gl], Ecbd], writes=[ocd])
        c.group(c.pe, [(lambda n_=n_: nc.tensor.matmul(dc[:], lhsT=ones_b[:], rhs=Ecb[:, n_, :], start=(n_ == 0), stop=(n_ == 1))) for n_ in range(2)], reads=[k3, Ecbd], writes=[dcd])
        yield 'A'
        inited = ['0' in BRS]
        if '0' not in BRS:
            rd, rdd = rdp.next()
            c.op(c.dve, lambda: nc.vector.tensor_scalar(out=rd[:], in0=dc[:], scalar1=1e-30, scalar2=None, op0=ALU.max), reads=[dcd], writes=[rdd])
            c.op(c.dve, lambda: nc.vector.reciprocal(out=rd[:], in_=rd[:]), reads=[rdd], writes=[rdd])
        else:
            rd, rdd = finish_branch(oc, ocd, dc, dcd, gl, 0, acc, accd, gt, gtd_, True, psA)
        c.op(c.dve, lambda: nc.vector.tensor_tensor(out=Ec[:], in0=Ec[:], in1=rd[:].unsqueeze(1).to_broadcast([128, 2, 512]), op=ALU.mult), reads=[Ecd, rdd], writes=[Ecd])
        ip, ipd = psA.next()
        fns = []
        for n_ in range(2):
            for h in range(4):
                fns.append(lambda n_=n_, h=h: nc.tensor.matmul(ip[:, 0:64], lhsT=Ec[:, n_, 128 * h:128 * h + 128], rhs=c2s[:, n_, :], start=(n_ == 0 and h == 0), stop=(n_ == 1 and h == 3)))
        c.group(c.pe, fns, reads=[Ecd, k3], writes=[ipd])
        imp, impd = impp.next(); imp2, imp2d = imp2p.next(); m8, m8d = m8p.next(); sel, seld = selp.next(); selT, selTd = selTp.next()
        c.op(c.dve, lambda: nc.vector.tensor_tensor(out=imp[:], in0=ip[:, 0:64], in1=tk[:, 0, :], op=ALU.mult), reads=[ipd, tkd], writes=[impd])
        c.op(c.dve, lambda: nc.vector.tensor_tensor(out=imp[:], in0=imp[:], in1=tk[:, 1, :], op=ALU.add), reads=[impd, tkd], writes=[impd])
        yield 'A'
        c.op(c.dve, lambda: nc.vector.max(out=m8[:], in_=imp[:]), reads=[impd], writes=[m8d])
        c.op(c.dve, lambda: nc.vector.match_replace(out=imp2[:], in_to_replace=m8[:], in_values=imp[:], imm_value=-2e30), reads=[impd, m8d], writes=[imp2d])
        c.op(c.dve, lambda: nc.vector.max(out=m8[:], in_=imp2[:]), reads=[imp2d], writes=[m8d])
        c.op(c.dve, lambda: nc.vector.tensor_scalar(out=sel[:], in0=imp[:], scalar1=m8[:, 7:8], scalar2=None, op0=ALU.is_ge), reads=[impd, m8d], writes=[seld])
        yield 'A'
        stp, stpd = psA.next()
        c.group(c.pe, [lambda: nc.tensor.matmul(stp[0:64, 0:128], lhsT=sel[:], rhs=identb[:], start=True, stop=True)], reads=[seld, kd], writes=[stpd])
        c.op(c.act, lambda: nc.scalar.activation(out=selT[:], in_=stp[0:64, 0:128], func=AF.Copy), reads=[stpd], writes=[selTd])
        yield 'S'
        for br in (1, 2):
            K_ = KsT if br == 1 else KwT; V_ = Vs if br == 1 else Vw
            kcs = list(range(0, qb + 1)) if br == 1 else list(range(max(0, qb - 4), qb + 1))
            ob, obd = psacc.next(); db, dbd = psacc.next()
            def stA(ki_, kc):
                sp_, spd = ps.next()
                c.group(c.pe, [lambda: nc.tensor.matmul(sp_[:], lhsT=K_[:, 128 * kc:128 * kc + 128], rhs=q2, start=True, stop=True)], reads=[kvd, qsd], writes=[spd])
                E, Ed = Ep.next()
                c.op(c.act, lambda: nc.scalar.activation(out=E[:], in_=sp_[:], func=AF.Exp, scale=SCALE), reads=[spd], writes=[Ed])
                st = dict(ki=ki_, kc=kc, E=E, Ed=Ed)
                if br == 1:
                    mp, mpd = ps.next()
                    c.group(c.pe, [lambda: nc.tensor.matmul(mp[:, 0:128], lhsT=ex[:, 128 * kc:128 * kc + 128], rhs=selT[:], start=True, stop=True)], reads=[selTd, k3], writes=[mpd])
                    st.update(mp=mp, mpd=mpd)
                return st
            def stB(st):
                ki_, kc, E, Ed = st['ki'], st['kc'], st['E'], st['Ed']
                if br == 1:
                    mp, mpd = st['mp'], st['mpd']
                    Em, Emd = Emp.next()
                    if kc == qb:
                        md, mdd = mdp.next()
                        c.op(c.dve, lambda: nc.vector.tensor_tensor(out=md[:], in0=mp[:, 0:128], in1=tri2[:, 0, :], op=ALU.mult), reads=[mpd, k3], writes=[mdd])
                        msk, mskd = md[:], mdd
                    else:
                        msk, mskd = mp[:, 0:128], mpd
                    c.op(c.dve, lambda: nc.vector.tensor_tensor(out=Em[:].rearrange("p (h t) -> p h t", h=4), in0=E[:].rearrange("p (h t) -> p h t", h=4), in1=msk.unsqueeze(1).to_broadcast([128, 4, 128]), op=ALU.mult), reads=[Ed, mskd], writes=[Emd])
                else:
                    if kc == qb or kc == qb - 4:
                        Em, Emd = Emp.next()
                        msk = tri2[:, 0 if kc == qb else 1, :]
                        c.op(c.dve, lambda: nc.vector.tensor_tensor(out=Em[:].rearrange("p (h t) -> p h t", h=4), in0=E[:].rearrange("p (h t) -> p h t", h=4), in1=msk.unsqueeze(1).to_broadcast([128, 4, 128]), op=ALU.mult), reads=[Ed, k3], writes=[Emd])
                    else:
                        Em, Emd = E, Ed
                st_ = (ki_ == 0); sp2 = (ki_ == len(kcs) - 1)
                c.group(c.pe, [lambda: nc.tensor.matmul(ob[:], lhsT=V_[:, kc, :], rhs=Em[:], start=st_, stop=sp2, skip_group_check=True)], reads=[kvd, Emd], writes=[obd])
                c.group(c.pe, [lambda: nc.tensor.matmul(db[:], lhsT=ones_b[:], rhs=Em[:], start=st_, stop=sp2, skip_group_check=True)], reads=[k3, Emd], writes=[dbd])
            pend = None
            for ki_, kc in enumerate(kcs):
                cur = stA(ki_, kc)
                if pend is not None: stB(pend)
                pend = cur
                yield 'B'
            stB(pend)
            if str(br) in BRS:
                finish_branch(ob, obd, db, dbd, gl, br, acc, accd, gt, gtd_, not inited[0]); inited[0] = True
        yo, yod = yop.next()
        c.op(c.act, lambda: nc.scalar.activation(out=yo[:], in_=acc[:], func=AF.Copy), reads=[accd], writes=[yod])
        ywrite(yo[:].rearrange("p (h t) -> p h t", h=4), 4 * gl, 4, t0, 128, [yod])
        if qb % 4 == 3 and gl == 1: ydone(qb // 4)
    wv = Weave()
    for qb in range(32):
        for gl in range(2):
            wv.push(nsa_body(qb, gl))
    wv.flush()
import ml_dtypes
_BF = ml_dtypes.bfloat16
_arr = lambda n: np.ascontiguousarray(np.asarray(n).reshape(-1, 128).T)
_sl = lambda a, n: np.arange(a, a + n)
_PROGS = {}
_CONST = {}
_KINDS = [0, 1, 2, 0]
_INNER = [4096, 2048, 2048, 4096]


def build_all():
    nc = bass.Bass("TRN2", target_bir_lowering=False)
    c = Ctx(nc)
    xT = c.dram("xT", [DM, S]); hT0 = c.dram("hT0", [DM, TPC])
    oT = c.dram("oT", [DM, TPC], F32, "ExternalOutput")
    xv = xT.rearrange("(c p) t -> p c t", p=128); h0v = hT0.rearrange("(c p) t -> p c t", p=128); oTv = oT.rearrange("(c p) t -> p c t", p=128)
    hall = None; had = None; hloc = None; hlocd = None
    fin = D()
    for i in range(4):
        kind = _KINDS[i]; inner = _INNER[i]; final = (i == 3)
        IC2 = inner // 256
        c.pfx = f"L{i}A_"
        ysrc = [c.dram(f"ysrc{k}", [inner // 2, 512], BF16, None) for k in range(8)]; ysd = [D() for _ in range(8)]
        yall = [c.dram(f"yall{k}", [inner, 512], BF16, None) for k in range(8)]; yad = [D() for _ in range(8)]
        ysv = [y.rearrange("(c p) t -> p c t", p=128) for y in ysrc]
        if i == 0:
            htile = lambda ti, q: (xv[:, 4 * q:4 * q + 4, ti * TT:(ti + 1) * TT], [])
        else:
            def htile(ti, q, hall=hall, had=had):
                sh = ti // 4; tl = ti % 4; fh = q // 2
                r0 = sh * 1024 + (q % 2) * 512
                return hall[tl][fh][r0:r0 + 512, :].rearrange("(c p) t -> p c t", p=128), [had[tl][fh]]
        def ywrite(ap, ch0, n, t0, T, reads, ysv=ysv, ysd=ysd):
            k = t0 // 512; tl0 = t0 % 512
            c.dma(c.sp, ysv[k][:, ch0:ch0 + n, tl0:tl0 + T], ap, reads=reads, acc=[ysd[k]])
        def ydone(k, ysrc=ysrc, ysd=ysd, yall=yall, yad=yad):
            c.allgather(ysrc[k], [ysd[k]], yall[k], yad[k])
        [emit_ssd, emit_lru, emit_nsa][kind](nc, c, htile, ywrite, ydone)
        c.new_phase()
        c.pfx = f"L{i}B_"
        if i == 0:
            hin = lambda ti, q: (h0v[:, 4 * q:4 * q + 4, ti * TT:(ti + 1) * TT], [])
        else:
            def hin(ti, q, hloc=hloc, hlocd=hlocd):
                fh = q // 2; r0 = (q % 2) * 512
                return hloc[ti][fh][r0:r0 + 512, :].rearrange("(c p) t -> p c t", p=128), [hlocd[ti][fh]]
        if final:
            hout = lambda ti, q: (oTv[:, 4 * q:4 * q + 4, ti * TT:(ti + 1) * TT], fin)
            hdone = lambda ti: None
        else:
            nloc = [[c.dram(f"hout{t}_{f}", [1024, 512], F32, None) for f in range(2)] for t in range(4)]
            nlocd = [[D() for f in range(2)] for t in range(4)]
            nall = [[c.dram(f"hall{t}_{f}", [2048, 512], F32, None) for f in range(2)] for t in range(4)]
            nalld = [[D() for f in range(2)] for t in range(4)]
            def hout(ti, q, nloc=nloc, nlocd=nlocd):
                fh = q // 2; r0 = (q % 2) * 512
                return nloc[ti][fh][r0:r0 + 512, :].rearrange("(c p) t -> p c t", p=128), nlocd[ti][fh]
            def hdone(ti, nloc=nloc, nlocd=nlocd, nall=nall, nalld=nalld):
                for f in range(2): c.allgather(nloc[ti][f], [nlocd[ti][f]], nall[ti][f], nalld[ti][f])
        emit_B(nc, c, inner, final, hin, yall, yad, hout, hdone)
        if not final:
            hall, had, hloc, hlocd = nall, nalld, nloc, nlocd
        c.new_phase()
    c.finish([fin])
    return nc


def _ssd_consts():
    if 'ssd' in _CONST: return _CONST['ssd']
    s_ = np.arange(128)
    tri = (s_[:, None] <= s_[None, :]).astype(np.float32)
    nm = np.where(s_[:, None] <= s_[None, :], 0.0, -30000.0).astype(np.float32)
    cf32 = np.ascontiguousarray(np.concatenate([tri, np.tile(nm, (1, 8)), np.eye(128, dtype=np.float32), tri], axis=1))
    delta = np.zeros((8, 8, 128), np.float32)
    for k in range(8): delta[k, k, :] = 1
    c8 = np.ascontiguousarray(np.concatenate([delta.reshape(8, 1024), np.ones((8, 128), np.float32)], axis=1))
    _CONST['ssd'] = dict(cf32=cf32, c8=c8, identb=np.eye(128).astype(_BF))
    return _CONST['ssd']


def _ssd_inputs(d, j, li, hf):
    W = d['ssd_in_proj'][j]
    cols = np.concatenate([_sl(hf * 2048, 2048), _sl(4096 + hf * 2048, 2048), _sl(8192 + hf * 512, 512), _sl(8192 + 1024 + hf * 512, 512), _sl(4096 + 6144 + 32 * hf, 32)])
    Wc = np.ascontiguousarray(W[:, cols])
    xbc = np.concatenate([_sl(hf * 2048, 2048), _sl(4096 + hf * 512, 512), _sl(4096 + 1024 + hf * 512, 512)])
    cvx = np.zeros((128, 24, 5), np.float32)
    for k in range(4): cvx[:, :, k] = _arr(d['ssd_conv_w'][j][k, xbc])
    cvx[:, :, 4] = _arr(d['ssd_conv_b'][j][xbc])
    hs = slice(32 * hf, 32 * hf + 32)
    hv = np.zeros((128, 2, 32), np.float32); hv[:, 0, :] = d['ssd_dt_bias'][j][hs][None]; hv[:, 1, :] = d['ssd_a_log'][j][hs][None]
    dn = np.zeros((128, 16, 2), np.float32)
    dn[:, :, 0] = _arr(np.repeat(d['ssd_d'][j][hs], 64)); dn[:, :, 1] = _arr(d['ssd_norm'][j][hf * 2048:(hf + 1) * 2048])
    ins = dict(w_in=Wc, nrm=_arr(d['norm_mix'][li]), cvx=cvx, hv=hv, dn=dn)
    ins.update(_ssd_consts())
    return ins


def _lru_inputs(d, li, hf):
    W = d['lru_in_proj'][0]
    cs = slice(hf * 1024, (hf + 1) * 1024)
    Wc = np.ascontiguousarray(np.concatenate([W[:, cs], W[:, 2048 + hf * 1024: 2048 + (hf + 1) * 1024]], axis=1))
    cv = np.zeros((128, 8, 9), np.float32)
    for k in range(4): cv[:, :, k] = _arr(d['lru_conv_w'][0][k, cs])
    cv[:, :, 4] = _arr(d['lru_conv_b'][0][cs]); cv[:, :, 5] = _arr(d['lru_ba'][0][cs]); cv[:, :, 6] = _arr(d['lru_bx'][0][cs]); cv[:, :, 7] = _arr(d['lru_a_param'][0][cs])
    return dict(w_in=Wc, nrm=_arr(d['norm_mix'][li]), wa=np.ascontiguousarray(d['lru_wa'][0][4 * hf:4 * hf + 4]), wx=np.ascontiguousarray(d['lru_wx'][0][4 * hf:4 * hf + 4]), cv=cv)


def _nsa_consts():
    if 'nsa' in _CONST: return _CONST['nsa']
    k = {}
    half = 16
    inv = (np.float32(500000.0) ** (-np.arange(half, dtype=np.float32) * np.float32(2.0) / np.float32(32))).astype(np.float32)
    invf = np.zeros((128, 1), np.float32); invf[0:16, 0] = inv; invf[16:32, 0] = inv
    k['invf'] = invf
    sgn = np.zeros((128, 1), np.float32); sgn[0:16] = -1; sgn[16:32] = 1
    k['sgn'] = sgn
    pm = np.zeros((128, 128), np.float32)
    for dd in range(16): pm[dd + 16, dd] = 1; pm[dd, dd + 16] = 1
    k['pm'] = pm.astype(_BF); k['identb'] = np.eye(128).astype(_BF)
    tt = np.arange(128)
    cmask = np.zeros((32, 128, 2, 128), np.float32)
    for qb in range(32):
        t = 128 * qb + tt
        for nch in range(2):
            nn = 128 * nch + np.arange(128)
            cmask[qb, :, nch, :] = ((16 * nn[:, None] + 31 <= t[None, :]) & (nn[:, None] < 255))
    k['cmask'] = cmask.astype(_BF)
    c_start = np.arange(255)[:, None] * 16; s_start = np.arange(64)[None, :] * 64
    ov = np.clip(np.minimum(c_start + 32, s_start + 64) - np.maximum(c_start, s_start), 0, None) / 16.0
    c2s = np.zeros((256, 64), np.float32); c2s[:255] = ov
    k['c2s'] = np.ascontiguousarray(c2s.reshape(2, 128, 64).transpose(1, 0, 2))
    tkk = np.zeros((32, 128, 2, 64), np.float32)
    jj = np.arange(64)
    for qb in range(32):
        cur = (128 * qb + tt) // 64
        forced = (jj[None, :] == 0) | (jj[None, :] == cur[:, None]) | (jj[None, :] == cur[:, None] - 1)
        valid = jj[None, :] <= cur[:, None]
        tkk[qb, :, 0, :] = (~forced) & valid
        tkk[qb, :, 1, :] = np.where(valid, np.where(forced, 1e9, 0.0), -1e30)
    k['tkk'] = tkk
    ex = np.zeros((64, 32, 128), np.float32)
    for kc in range(32):
        for p in range(128): ex[2 * kc + p // 64, kc, p] = 1
    k['ex'] = ex.reshape(64, 32 * 128).astype(_BF)
    p = np.arange(128)
    tri2 = np.zeros((128, 2, 128), np.float32); tri2[:, 0, :] = p[:, None] <= tt[None, :]; tri2[:, 1, :] = p[:, None] > tt[None, :]
    k['tri2'] = tri2.astype(_BF)
    gsel = np.zeros((24, 6, 4, 128), np.float32)
    for gl in range(2):
        for kk in range(3):
            for h in range(4): gsel[(4 * gl + h) * 3 + kk, 3 * gl + kk, h, :] = 1
    k['gsel'] = gsel.reshape(24, 6, 512); k['ones24'] = np.ones((24, 128), np.float32)
    _CONST['nsa'] = k
    return k


def _nsa_inputs(d, li, pos_b, gp):
    W = d['nsa_in_proj'][0]
    g0 = 2 * gp
    parts = [_sl(1024 * gp, 1024)]
    for kidx in (0, 1, 2, 4, 3, 5):
        parts.append(_sl(2048 + kidx * 512 + g0 * 128, 256))
    parts.append(_sl(2048 + 6 * 512 + 24 * gp, 24))
    Wc = np.ascontiguousarray(np.concatenate([W[:, np.concatenate(parts)], np.zeros((2048, 8), np.float32)], axis=1))
    ins = dict(w_in=Wc, nrm=_arr(d['norm_mix'][li]), pos=np.ascontiguousarray(np.asarray(pos_b)[None, :]).astype(np.int32),
               peT=np.ascontiguousarray(d['nsa_cmp_pe'][0].transpose(2, 0, 1)), w1=d['nsa_cmp_w1'][0], w2=d['nsa_cmp_w2'][0])
    ins.update(_nsa_consts())
    return ins


def kernel(**inputs):
    d = {k: np.asarray(v) for k, v in inputs.items()}
    x = d['x']
    NB = 4
    if 'all' not in _PROGS: _PROGS['all'] = build_all()
    nc = _PROGS['all']
    maps = []
    for b in range(NB):
        xTb = np.ascontiguousarray(x[b].T)
        for r in range(2):
            ts = slice(r * 2048, (r + 1) * 2048)
            m = dict(xT=xTb, hT0=np.ascontiguousarray(xTb[:, ts]))
            sel = np.zeros((128, 2), np.float32); sel[:, r] = 1.0
            for i in range(4):
                kind, j = i % 3, i // 3
                if kind == 0: a = _ssd_inputs(d, j, i, r)
                elif kind == 1: a = _lru_inputs(d, i, r)
                else: a = _nsa_inputs(d, i, d['positions'][b], r)
                for k_, v_ in a.items(): m[f"L{i}A_{k_}"] = v_
                nrm = np.ascontiguousarray(np.concatenate([_arr(d['norm_ffn'][i]), _arr(d['norm_ple'][i]), _arr(d['norm_final'])], axis=1))
                w_o = [d['ssd_out_proj'][j], d['lru_out_proj'][0], d['nsa_out_proj'][0]][kind]
                bi = dict(pT=np.ascontiguousarray(d['p'][i][b, ts].T), sel=sel, w_o=w_o, w_in=d['w_ffn_in'][i], w_out=d['w_ffn_out'][i],
                          w_g=d['w_ple_gate'][i], w_u=d['w_ple_up'][i], nrm=nrm)
                for k_, v_ in bi.items(): m[f"L{i}B_{k_}"] = v_
            maps.append(m)
    res = run_bass_kernel_spmd(nc, maps, core_ids=list(range(8)))
    out = np.empty((NB, S, DM), np.float32)
    for b in range(NB):
        for r in range(2):
            out[b, r * 2048:(r + 1) * 2048, :] = np.asarray(res.results[2 * b + r]['oT']).T
    return out
```

```python
import math, os


import numpy as np
import concourse.bass as bass
import concourse.mybir as mybir
from concourse.bass_utils import run_bass_kernel_spmd

F32 = mybir.dt.float32; BF16 = mybir.dt.bfloat16; I32 = mybir.dt.int32
AF = mybir.ActivationFunctionType; ALU = mybir.AluOpType
AX = mybir.AxisListType


class D:
    __slots__ = ("w", "r", "excl")
    def __init__(s, excl=False): s.w = []; s.r = []; s.excl = excl


class Eng:
    def __init__(s, ctx, name, e):
        s.e = e; s.name = name; s.sem = ctx.nc.alloc_semaphore("s_" + name); s.cnt = 0; s.seen = {}
        s.dsems = []; s.dcnt = []; s.di = 0


class Ctx:
    def __init__(s, nc, ndma=8):
        s.nc = nc
        s.pe = Eng(s, "pe", nc.tensor); s.act = Eng(s, "act", nc.scalar); s.dve = Eng(s, "dve", nc.vector)
        s.pool = Eng(s, "pool", nc.gpsimd); s.sp = Eng(s, "sp", nc.sync)
        for q in (s.sp, s.pool, s.act):
            n = ndma
            q.dsems = [nc.alloc_semaphore(f"d_{q.name}{i}") for i in range(n)]; q.dcnt = [0] * n
        s.nbank = 0
        import contextlib
        s._cl = contextlib
        s.es = contextlib.ExitStack(); s.pes = contextlib.ExitStack(); s.uid = 0; s.pfx = ""

    def psb(s, name, shape, dtype):
        s.uid += 1
        return s.pes.enter_context(s.nc.sbuf_tensor(f"{name}_{s.uid}", shape, dtype))

    def dram(s, name, shape, dtype=F32, kind="ExternalInput"):
        if kind is None: return s.nc.dram_tensor(s.pfx + name, shape, dtype).ap()
        return s.nc.dram_tensor(s.pfx + name, shape, dtype, kind=kind).ap()

    def new_phase(s):
        s.barrier(); s.es.close(); s.pes.close(); s.es = s._cl.ExitStack(); s.pes = s._cl.ExitStack()

    def allgather(s, src, src_deps, dst, dst_dep):
        q = s.pool
        s._deps(q, src_deps, [dst_dep])
        s.uid += 1
        sem = s.nc.alloc_semaphore(f"cc_{s.uid}")
        s.nc.gpsimd.collective_compute("AllGather", ALU.bypass, replica_groups=[[0, 1], [2, 3], [4, 5], [6, 7]], ins=[src.opt()], outs=[dst.opt()]).then_inc(sem)
        s._mark((sem, 1), src_deps, [dst_dep])

    def sb(s, name, shape, dtype):
        s.uid += 1
        return s.es.enter_context(s.nc.sbuf_tensor(f"{name}_{s.uid}", shape, dtype))

    def ps(s, name, shape, dtype=None):
        s.uid += 1
        return s.es.enter_context(s.nc.psum_tensor(f"{name}_{s.uid}", shape, dtype or F32))

    def barrier(s):
        engs = [s.pe, s.act, s.dve, s.pool, s.sp]
        for e in engs:
            for o in engs:
                if o is not e and o.cnt > 0: s._wait(e, o.sem, o.cnt)
            for q in (s.sp, s.pool, s.act):
                for sem, cnt in zip(q.dsems, q.dcnt):
                    if cnt > 0: s._wait(e, sem, cnt)

    def new_stage(s):
        s.barrier(); s.es.close(); s.es = s._cl.ExitStack()

    def _wait(s, eng, sem, val):
        key = id(sem)
        if eng.seen.get(key, 0) < val:
            eng.e.wait_ge(sem, val); eng.seen[key] = val

    def _deps(s, eng, reads, writes, acc=()):
        for t in reads:
            for w in t.w: s._wait(eng, *w)
            if t.excl:
                for (sem, val) in t.r:
                    if sem is not eng.sem: s._wait(eng, sem, val)
        for t in writes:
            for w in t.w: s._wait(eng, *w)
        for t in list(writes) + list(acc):
            for (sem, val) in t.r:
                if sem is eng.sem: continue
                s._wait(eng, sem, val)

    def _mark(s, tag, reads, writes, acc=()):
        for t in writes: t.w = [tag]; t.r = []
        for t in acc: t.w.append(tag)
        for t in reads: t.r.append(tag)

    def op(s, eng, fn, reads=(), writes=()):
        s._deps(eng, reads, writes)
        inst = fn()
        eng.cnt += 1
        inst.then_inc(eng.sem, 1)
        s._mark((eng.sem, eng.cnt), reads, writes)

    def group(s, eng, fns, reads=(), writes=()):
        s._deps(eng, reads, writes)
        inst = None
        for fn in fns: inst = fn()
        eng.cnt += 1
        inst.then_inc(eng.sem, 1)
        s._mark((eng.sem, eng.cnt), reads, writes)

    def dma(s, q, out, in_, reads=(), writes=(), acc=(), **kw):
        s._deps(q, reads, writes, acc)
        i = q.di; q.di = (q.di + 1) % len(q.dsems)
        sem = q.dsems[i]
        if q.dcnt[i] > 0: s._wait(q, sem, q.dcnt[i])
        q.dcnt[i] += 16
        q.e.dma_start(out=out, in_=in_, **kw).then_inc(sem, 16)
        s._mark((sem, q.dcnt[i]), reads, writes, acc)

    def finish(s, deps):
        for t in deps:
            for w in t.w: s._wait(s.sp, *w)


class Pool:
    def __init__(s, c, name, n, shape, dtype, psum=False, nd=0):
        alloc = c.ps if psum else c.sb
        s.t = [alloc(f"{name}{i}", shape, dtype) for i in range(n)]
        s.d = [(D(psum) if nd == 0 else [D(psum) for _ in range(nd)]) for _ in range(n)]; s.i = 0; s.n = n
    def next(s):
        i = s.i; s.i = (s.i + 1) % s.n
        return s.t[i], s.d[i]


class WStream:
    def __init__(s, c, nbuf, nelem, name="wbuf", live=1):
        s.c = c; s.nelem = nelem; s.ahead = nbuf - live
        s.t = [c.sb(f"{name}{i}", [128, nelem], BF16) for i in range(nbuf)]
        s.d = [D() for _ in range(nbuf)]
        s.plan = []; s.issued = 0; s.pos = 0
    def _issue(s):
        i = s.issued
        if i >= len(s.plan): return
        src, kc, nw = s.plan[i]
        b = i % len(s.t)
        dst = s.t[b][:, 0:kc * nw].rearrange("p (k n) -> p k n", k=kc)
        s.c.dma(s.c.pool, dst, src, writes=[s.d[b]])
        s.issued += 1
    def next(s):
        while s.issued <= min(s.pos + s.ahead, len(s.plan) - 1): s._issue()
        src, kc, nw = s.plan[s.pos]
        b = s.pos % len(s.t); s.pos += 1
        return s.t[b][:, 0:kc * nw].rearrange("p (k n) -> p k n", k=kc), s.d[b]


def wtiles(W, K, n0, n1, nw):
    kc = K // 128
    Wv = W.rearrange("(k p) n -> p k n", p=128)
    return [(Wv[:, :, a:min(a + nw, n1)], kc, min(a + nw, n1) - a) for a in range(n0, n1, nw)]


def gemm(c, ws, pp, spec, rhs, rhs_deps, T, epi, mchunk=128):
    nc = c.nc
    col = 0
    for (src, kc, nw) in spec:
        wt, wd = ws.next()
        for m0 in range(0, nw, mchunk):
            m1 = min(m0 + mchunk, nw)
            for t0 in range(0, T, 512):
                t1 = min(t0 + 512, T)
                pt, pd = pp.next()
                out = pt[0:m1 - m0, 0:t1 - t0]
                fns = [(lambda k=k: nc.tensor.matmul(out, lhsT=wt[:, k, m0:m1], rhs=rhs(k, t0, t1), start=(k == 0), stop=(k == kc - 1))) for k in range(kc)]
                deps = [wd]
                for k in range(kc): deps += rhs_deps(k, t0)
                c.group(c.pe, fns, reads=deps, writes=[pd])
                epi(col + m0, t0, t1, out, pd)
        col += nw


DM = 2048; FF = 5632; PLE = 256; TPC = 2048; TT = 512; EPS = 1e-6

class Weave:
    def __init__(s): s.prev = None
    def push(s, g):
        a_done = False; b_done = (s.prev is None)
        while not (a_done and b_done):
            if not a_done:
                if next(g) == 'S': a_done = True
            if not b_done:
                try: next(s.prev)
                except StopIteration: b_done = True
        s.prev = g
    def flush(s):
        if s.prev is not None:
            for _ in s.prev: pass
        s.prev = None


def rmsnorm_tile(c, nc, pp, h, hd, wcol, v, vd, ones, sqp, misc, T=TT, out_f32=None):
    pt, pd = pp.next()
    for ch in range(16):
        sq, sd = sqp.next()
        c.op(c.act, lambda: nc.scalar.activation(out=sq[:, 0:T], in_=h[:, ch, :], func=AF.Square), reads=[hd[ch]], writes=[sd])
        c.group(c.pe, [lambda: nc.tensor.matmul(pt[:, 0:T], lhsT=ones[:], rhs=sq[:, 0:T], start=(ch == 0), stop=(ch == 15), skip_group_check=True)], reads=[sd], writes=[pd])
    rs, rd = misc.next()
    c.op(c.act, lambda: nc.scalar.activation(out=rs[:, 0:T], in_=pt[:, 0:T], func=AF.Sqrt, scale=1.0 / DM, bias=c.eps[:]), reads=[pd], writes=[rd])
    c.op(c.dve, lambda: nc.vector.reciprocal(out=rs[:, 0:T], in_=rs[:, 0:T]), reads=[rd], writes=[rd])
    for ch in range(16):
        o = v[:, ch, :] if out_f32 is None else out_f32[:, ch, :]
        c.op(c.dve, lambda: nc.vector.scalar_tensor_tensor(out=o, in0=h[:, ch, :], scalar=wcol[:, ch:ch + 1], in1=rs[:, 0:T], op0=ALU.mult, op1=ALU.mult),
             reads=[hd[ch], rd], writes=[vd[ch]])


def emit_B(nc, c, inner, final, hin, yall, yad, hout, hdone):
    TB = 1024; NS = 2
    IC = inner // 128
    dt = c.dram
    pT = dt("pT", [PLE, TPC]); sel_d = dt("sel", [128, 2])
    w_o = dt("w_o", [inner, DM]); w_in = dt("w_in", [DM, 2 * FF]); w_out = dt("w_out", [FF, DM])
    w_g = dt("w_g", [DM, DM]); w_u = dt("w_u", [PLE, DM])
    nrm = dt("nrm", [128, 48])
    ones = c.sb("ones", [128, 128], BF16); c.eps = c.sb("eps", [128, 1], F32)
    nw = c.sb("nw", [128, 48], F32)
    cd = D()
    c.op(c.dve, lambda: nc.vector.memset(ones[:], 1.0), writes=[cd])
    c.op(c.dve, lambda: nc.vector.memset(c.eps[:], EPS), writes=[cd])
    c.dma(c.sp, nw[:], nrm, writes=[cd])
    sel = c.sb("sel", [128, 2], F32)
    c.dma(c.sp, sel[:], sel_d, writes=[cd])
    h = c.sb("h", [128, 16, TB], F32); hd = [[D() for _ in range(16)] for _ in range(NS)]
    big = c.sb("big", [128, 32, TB], BF16); bd = [[D() for _ in range(32)] for _ in range(NS)]
    v = c.sb("v", [128, 16, TB], BF16); vd = [[D() for _ in range(16)] for _ in range(NS)]
    pb = c.sb("pb", [128, 2, TB], BF16); pbd = [D(), D()]
    pp = Pool(c, "ps", 8, [128, 512], F32, psum=True)
    sqp = Pool(c, "sq", 3, [128, 512], BF16)
    misc = Pool(c, "misc", 4, [128, 512], F32)
    sgp = Pool(c, "sg", 4, [128, 512], BF16)
    ws = WStream(c, 2, 6144)
    NT = TPC // TB
    g_o = wtiles(w_o, inner, 0, DM, 128 if IC > 16 else 256)
    halves = [(0, 3072), (3072, FF)]
    g_in = []; g_out = []
    for (a_, b_) in halves:
        gi = []
        for j in range(a_, b_, 256):
            gi += wtiles(w_in, DM, j, j + 256, 256) + wtiles(w_in, DM, FF + j, FF + j + 256, 256)
        g_in.append(gi)
        g_out.append(wtiles(w_out[a_:b_, :], b_ - a_, 0, DM, 256))
    g_g = wtiles(w_g, DM, 0, DM, 256)
    g_u = wtiles(w_u, PLE, 0, DM, 2048)
    for _ in range(NT): ws.plan += g_o + g_in[0] + g_out[0] + g_in[1] + g_out[1] + g_u + g_g
    pTv = pT.rearrange("(c p) t -> p c t", p=128)
    sb_ = lambda s_: slice(s_ * 512, (s_ + 1) * 512)
    for ti in range(NT):
        for s_ in range(NS):
            u = NS * ti + s_
            for q in range(4):
                hap, hdeps = hin(u, q)
                c.dma(c.sp, h[:, 4 * q:4 * q + 4, sb_(s_)], hap, reads=hdeps, writes=hd[s_][4 * q:4 * q + 4])
            y0v = yall[u].rearrange("(c p) t -> p c t", p=128); y1v = yall[4 + u].rearrange("(c p) t -> p c t", p=128)
            for q in range(0, IC, 8):
                vq = q % 16
                yt_ = v[:, vq:vq + 8, sb_(s_)]; ytd_ = vd[s_][vq:vq + 8]
                c.dma(c.sp, big[:, q:q + 8, sb_(s_)], y0v[:, q:q + 8, :], reads=[yad[u]], writes=bd[s_][q:q + 8])
                c.dma(c.sp, yt_, y1v[:, q:q + 8, :], reads=[yad[4 + u]], writes=ytd_)
                c.op(c.dve, lambda: nc.vector.tensor_scalar(out=yt_, in0=yt_, scalar1=sel[:, 1:2], scalar2=None, op0=ALU.mult), reads=ytd_ + [cd], writes=ytd_)
                c.op(c.dve, lambda: nc.vector.scalar_tensor_tensor(out=big[:, q:q + 8, sb_(s_)], in0=big[:, q:q + 8, sb_(s_)], scalar=sel[:, 0:1], in1=yt_, op0=ALU.mult, op1=ALU.add), reads=bd[s_][q:q + 8] + ytd_ + [cd], writes=bd[s_][q:q + 8])
        c.dma(c.pool, pb[:], pTv[:, :, ti * TB:(ti + 1) * TB], writes=pbd)
        def epi_add(col, t0, t1, ps, pd):
            ch = col // 128; s_ = t0 // 512
            c.op(c.dve, lambda: nc.vector.tensor_tensor(out=h[:, ch, t0:t1], in0=h[:, ch, t0:t1], in1=ps, op=ALU.add), reads=[pd, hd[s_][ch]], writes=[hd[s_][ch]])
        gemm(c, ws, pp, g_o, lambda k, t0, t1: big[:, k, t0:t1], lambda k, t0: [bd[t0 // 512][k]], TB, epi_add)
        for s_ in range(NS):
            rmsnorm_tile(c, nc, pp, h[:, :, sb_(s_)], hd[s_], nw[:, 0:16], v[:, :, sb_(s_)], vd[s_], ones, sqp, misc, T=512)
        for hp in range(2):
            st = {}
            def epi_ffn(col, t0, t1, ps, pd):
                j = col % 512; grp = col // 512; isup = j >= 256; ch = grp * 2 + (j % 256) // 128; s_ = t0 // 512
                if not isup:
                    sg, sd = sgp.next()
                    c.op(c.act, lambda: nc.scalar.activation(out=sg[:, 0:t1 - t0], in_=ps, func=AF.Silu), reads=[pd], writes=[sd])
                    st[(ch, t0)] = (sg, sd)
                else:
                    sg, sd = st.pop((ch, t0))
                    c.op(c.dve, lambda: nc.vector.tensor_tensor(out=big[:, ch, t0:t1], in0=sg[:, 0:t1 - t0], in1=ps, op=ALU.mult), reads=[pd, sd], writes=[bd[s_][ch]])
            gemm(c, ws, pp, g_in[hp], lambda k, t0, t1: v[:, k, t0:t1], lambda k, t0: [vd[t0 // 512][k]], TB, epi_ffn)
            gemm(c, ws, pp, g_out[hp], lambda k, t0, t1: big[:, k, t0:t1], lambda k, t0: [bd[t0 // 512][k]], TB, epi_add)
        for s_ in range(NS):
            rmsnorm_tile(c, nc, pp, h[:, :, sb_(s_)], hd[s_], nw[:, 16:32], v[:, :, sb_(s_)], vd[s_], ones, sqp, misc, T=512)
        def epi_up(col, t0, t1, ps, pd):
            ch = col // 128; s_ = t0 // 512
            c.op(c.act, lambda: nc.scalar.activation(out=big[:, ch, t0:t1], in_=ps, func=AF.Copy), reads=[pd], writes=[bd[s_][ch]])
        gemm(c, ws, pp, g_u, lambda k, t0, t1: pb[:, k, t0:t1], lambda k, t0: [pbd[k]], TB, epi_up)
        def epi_gate(col, t0, t1, ps, pd):
            ch = col // 128; s_ = t0 // 512
            sg, sd = misc.next()
            c.op(c.act, lambda: nc.scalar.activation(out=sg[:, 0:t1 - t0], in_=ps, func=AF.Sigmoid), reads=[pd], writes=[sd])
            c.op(c.dve, lambda: nc.vector.tensor_tensor(out=sg[:, 0:t1 - t0], in0=sg[:, 0:t1 - t0], in1=big[:, ch, t0:t1], op=ALU.mult), reads=[sd, bd[s_][ch]], writes=[sd])
            c.op(c.dve, lambda: nc.vector.tensor_tensor(out=h[:, ch, t0:t1], in0=h[:, ch, t0:t1], in1=sg[:, 0:t1 - t0], op=ALU.add), reads=[sd, hd[s_][ch]], writes=[hd[s_][ch]])
        gemm(c, ws, pp, g_g, lambda k, t0, t1: v[:, k, t0:t1], lambda k, t0: [vd[t0 // 512][k]], TB, epi_gate)
        for s_ in range(NS):
            u = NS * ti + s_
            if final:
                rmsnorm_tile(c, nc, pp, h[:, :, sb_(s_)], hd[s_], nw[:, 32:48], v[:, :, sb_(s_)], hd[s_], ones, sqp, misc, T=512, out_f32=h[:, :, sb_(s_)])
            for q in range(4):
                oap, odp = hout(u, q)
                c.dma(c.sp, oap, h[:, 4 * q:4 * q + 4, sb_(s_)], reads=hd[s_][4 * q:4 * q + 4], acc=[odp])
            hdone(u)


S = 4096; TT = 512


class A1:
    def __init__(s, nc, c, ncols, htile, wbuf_elems=16 * 512):
        s.nc = nc; s.c = c; s.htile = htile
        dt = c.dram
        s.dt = dt
        s.w = dt("w_in", [DM, ncols]); s.nrm = dt("nrm", [128, 16])
        s.ones = c.sb("ones", [128, 128], BF16); c.eps = c.sb("eps", [128, 1], F32)
        s.nw = c.sb("nw", [128, 16], F32)
        s.cd = D()
        c.op(c.dve, lambda: nc.vector.memset(s.ones[:], 1.0), writes=[s.cd])
        c.op(c.dve, lambda: nc.vector.memset(c.eps[:], EPS), writes=[s.cd])
        c.dma(c.sp, s.nw[:], s.nrm, writes=[s.cd])
        s.h = c.sb("h", [128, 16, TT], F32); s.hd = [D() for _ in range(16)]
        s.v = c.sb("v", [128, 16, TT], BF16); s.vd = [D() for _ in range(16)]
        s.pp = Pool(c, "ps", 8, [128, 512], F32, psum=True)
        s.sqp = Pool(c, "sq", 3, [128, TT], BF16)
        s.misc = Pool(c, "misc", 4, [128, TT], F32)
        s.stg = Pool(c, "stg", 4, [128, 512], F32)
        s.ws = WStream(c, 2, wbuf_elems)

    def run(s, groups):
        nc, c = s.nc, s.c
        plan = []
        for g in groups: plan += wtiles(s.w, DM, g["n0"], g["n1"], 512)
        NT = S // TT
        for _ in range(NT): s.ws.plan += plan
        for ti in range(NT):
            tb = ti * TT
            for q in range(4):
                hap, hdeps = s.htile(ti, q)
                c.dma(c.sp, s.h[:, 4 * q:4 * q + 4, :], hap, reads=hdeps, writes=s.hd[4 * q:4 * q + 4])
            rmsnorm_tile(c, nc, s.pp, s.h, s.hd, s.nw[:, 0:16], s.v, s.vd, s.ones, s.sqp, s.misc)
            for g in groups:
                spec = wtiles(s.w, DM, g["n0"], g["n1"], 512)
                if not g.get("tm"):
                    gemm(c, s.ws, s.pp, spec, lambda k, t0, t1: s.v[:, k, t0:t1], lambda k, t0: [s.vd[k]], TT,
                         lambda col, t0, t1, ps, pd, g=g: g["epi"](col, tb + t0, tb + t1, ps, pd))
                else:
                    col = 0
                    for (src, kc, nw) in spec:
                        wt, wd = s.ws.next()
                        for t0 in range(0, TT, 128):
                            pt, pd = s.pp.next()
                            out = pt[:, 0:nw]
                            fns = [(lambda k=k: nc.tensor.matmul(out, lhsT=s.v[:, k, t0:t0 + 128], rhs=wt[:, k, :], start=(k == 0), stop=(k == kc - 1))) for k in range(kc)]
                            c.group(c.pe, fns, reads=[wd] + s.vd, writes=[pd])
                            g["epi"](col, tb + t0, nw, out, pd)
                        col += nw


def emit_lru(nc, c, htile, ywrite, ydone):
    a1 = A1(nc, c, 2048, htile)
    dt = a1.dt
    wa = dt("wa", [4, 256, 256]); wx = dt("wx", [4, 256, 256])
    cv = dt("cv", [128, 8, 9])
    GT = dt("GT", [1024, S], BF16, "Internal")
    XT = dt("XT", [1024, S], F32, "Internal")
    gd = [D() for _ in range(8)]; xd = [D() for _ in range(8)]
    cvs = c.sb("cvs", [128, 8, 9], F32)
    cA = c.sb("cA", [128, 8], F32); cA2 = c.sb("cA2", [128, 8], F32)
    cvd = D()
    c.dma(c.sp, cvs[:], cv, writes=[cvd])
    c.op(c.act, lambda: nc.scalar.activation(out=cA[:], in_=cvs[:, :, 7], func=AF.Softplus, scale=-1.0), reads=[cvd], writes=[cvd])
    c.op(c.dve, lambda: nc.vector.tensor_scalar(out=cA2[:], in0=cA[:], scalar1=-16.0, scalar2=None, op0=ALU.mult), reads=[cvd], writes=[cvd])
    c.op(c.dve, lambda: nc.vector.tensor_scalar(out=cA[:], in0=cA[:], scalar1=-8.0, scalar2=None, op0=ALU.mult), reads=[cvd], writes=[cvd])
    stg = a1.stg
    def epi_g(col, t0, t1, ps, pd):
        ch = col // 128
        st, sd = stg.next(); o = st[:].bitcast(BF16)[:, 0:t1 - t0] if False else st[:, 0:(t1 - t0) // 2].bitcast(BF16)
        c.op(c.act, lambda: nc.scalar.activation(out=o, in_=ps, func=AF.Gelu_apprx_tanh), reads=[pd], writes=[sd])
        c.dma(c.sp, GT[col:col + 128, t0:t1], o, reads=[sd], writes=[gd[ch]])
    def epi_x(col, t0, t1, ps, pd):
        ch = col // 128
        st, sd = stg.next()
        c.op(c.dve, lambda: nc.vector.tensor_copy(out=st[:, 0:t1 - t0], in_=ps), reads=[pd], writes=[sd])
        c.dma(c.sp, XT[col:col + 128, t0:t1], st[:, 0:t1 - t0], reads=[sd], writes=[xd[ch]])
    a1.run([dict(n0=0, n1=1024, epi=epi_g), dict(n0=1024, n1=2048, epi=lambda col, t0, t1, ps, pd: epi_x(col, t0, t1, ps, pd))])
    TL = 512
    ws2 = WStream(c, 4, 2 * 256, name="wb2_", live=2)
    for tl in range(S // TL):
        for kb in range(4):
            ws2.plan += [(wa[kb].rearrange("(k p) n -> p k n", p=128), 2, 256), (wx[kb].rearrange("(k p) n -> p k n", p=128), 2, 256)]
    xrp = Pool(c, "xr", 2, [128, 2, TL + 3], F32); xcp = Pool(c, "xc", 2, [128, 2, TL], F32)
    xbp = Pool(c, "xb", 2, [128, 2, TL], BF16); gp = Pool(c, "gg", 2, [128, 2, TL], BF16)
    ap_ = Pool(c, "aa", 2, [128, 2, TL], F32); bp = Pool(c, "bb", 2, [128, 2, TL], F32)
    hp = Pool(c, "hs", 2, [128, 2, TL], F32); yp = Pool(c, "yy", 2, [128, 2, TL], BF16)
    tp = Pool(c, "tmp", 4, [128, 512], F32)
    carry = c.sb("carry", [128, 8], F32); cyd = [D() for _ in range(8)]
    XTv = XT.rearrange("(c p) t -> p c t", p=128); GTv = GT.rearrange("(c p) t -> p c t", p=128)
    pp = a1.pp
    for tl in range(S // TL):
        for kb in range(4):
            t0 = tl * TL
            xr, xrd = xrp.next(); xc, xcd = xcp.next(); xb, xbd = xbp.next(); gg, ggd = gp.next()
            aa, aad = ap_.next(); bb, bbd = bp.next(); hs, hsd = hp.next(); yy, yyd = yp.next()
            if tl == 0:
                c.op(c.dve, lambda: nc.vector.memset(xr[:, :, 0:3], 0.0), writes=[xrd])
                c.dma(c.sp, xr[:, :, 3:], XTv[:, 2 * kb:2 * kb + 2, 0:TL], reads=xd[2 * kb:2 * kb + 2], writes=[xrd])
            else:
                c.dma(c.sp, xr[:], XTv[:, 2 * kb:2 * kb + 2, t0 - 3:t0 + TL], reads=xd[2 * kb:2 * kb + 2], writes=[xrd])
            c.dma(c.sp, gg[:], GTv[:, 2 * kb:2 * kb + 2, t0:t0 + TL], reads=gd[2 * kb:2 * kb + 2], writes=[ggd])
            for d in range(2):
                ch = 2 * kb + d
                c.op(c.dve, lambda: nc.vector.tensor_scalar(out=xc[:, d, :], in0=xr[:, d, 0:TL], scalar1=cvs[:, ch, 0:1], scalar2=cvs[:, ch, 4:5], op0=ALU.mult, op1=ALU.add), reads=[xrd, cvd], writes=[xcd])
                for k in range(1, 4):
                    c.op(c.dve, lambda: nc.vector.scalar_tensor_tensor(out=xc[:, d, :], in0=xr[:, d, k:k + TL], scalar=cvs[:, ch, k:k + 1], in1=xc[:, d, :], op0=ALU.mult, op1=ALU.add), reads=[xrd, xcd], writes=[xcd])
                c.op(c.act, lambda: nc.scalar.activation(out=xb[:, d, :], in_=xc[:, d, :], func=AF.Copy), reads=[xcd], writes=[xbd])
            wat, wad = ws2.next(); wxt, wxd = ws2.next()
            for d in range(2):
                ch = 2 * kb + d
                for b0 in range(0, TL, 512):
                    pa, pad = pp.next(); px, pxd = pp.next()
                    c.group(c.pe, [(lambda k=k: nc.tensor.matmul(pa[:], lhsT=wat[:, k, d * 128:(d + 1) * 128], rhs=xb[:, k, b0:b0 + 512], start=(k == 0), stop=(k == 1))) for k in range(2)], reads=[wad, xbd], writes=[pad])
                    c.group(c.pe, [(lambda k=k: nc.tensor.matmul(px[:], lhsT=wxt[:, k, d * 128:(d + 1) * 128], rhs=xb[:, k, b0:b0 + 512], start=(k == 0), stop=(k == 1))) for k in range(2)], reads=[wxd, xbd], writes=[pxd])
                    r, rd = tp.next(); a2, a2d = tp.next()
                    asl = aa[:, d, b0:b0 + 512]; bsl = bb[:, d, b0:b0 + 512]
                    c.op(c.act, lambda: nc.scalar.activation(out=r[:], in_=pa[:], func=AF.Sigmoid, bias=cvs[:, ch, 5:6]), reads=[pad, cvd], writes=[rd])
                    c.op(c.act, lambda: nc.scalar.activation(out=asl, in_=r[:], func=AF.Exp, scale=cA[:, ch:ch + 1]), reads=[rd, cvd], writes=[aad])
                    c.op(c.act, lambda: nc.scalar.activation(out=a2[:], in_=r[:], func=AF.Exp, scale=cA2[:, ch:ch + 1]), reads=[rd, cvd], writes=[a2d])
                    c.op(c.act, lambda: nc.scalar.activation(out=a2[:], in_=a2[:], func=AF.Sqrt, scale=-1.0, bias=1.0), reads=[a2d], writes=[a2d])
                    c.op(c.act, lambda: nc.scalar.activation(out=r[:], in_=px[:], func=AF.Sigmoid, bias=cvs[:, ch, 6:7]), reads=[pxd, cvd, aad], writes=[rd])
                    c.op(c.dve, lambda: nc.vector.tensor_tensor(out=bsl, in0=r[:], in1=xc[:, d, b0:b0 + 512], op=ALU.mult), reads=[rd, xcd], writes=[bbd])
                    c.op(c.dve, lambda: nc.vector.tensor_tensor(out=bsl, in0=bsl, in1=a2[:], op=ALU.mult), reads=[a2d, bbd], writes=[bbd])
                init = 0.0 if tl == 0 else carry[:, ch:ch + 1]
                c.op(c.dve, lambda: nc.vector.tensor_tensor_scan(out=hs[:, d, :], data0=aa[:, d, :], data1=bb[:, d, :], initial=init, op0=ALU.mult, op1=ALU.add), reads=[aad, bbd, cyd[ch]], writes=[hsd])
                c.op(c.dve, lambda: nc.vector.tensor_copy(out=carry[:, ch:ch + 1], in_=hs[:, d, TL - 1:TL]), reads=[hsd], writes=[cyd[ch]])
                c.op(c.dve, lambda: nc.vector.tensor_tensor(out=yy[:, d, :], in0=hs[:, d, :], in1=gg[:, d, :], op=ALU.mult), reads=[hsd, ggd], writes=[yyd])
            ywrite(yy[:], 2 * kb, 2, t0, TL, [yyd])
        ydone(tl)


class Cut(Exception): pass

def emit_ssd(nc, c, htile, ywrite, ydone, stop=9, lim=(8, 4, 4), cut=99):
    def CUT(k): pass
    NCOL = 2048 + 3072 + 32
    dt = c.dram
    cvx = dt("cvx", [128, 24, 5]); hv = dt("hv", [128, 2, 32]); dn = dt("dn", [128, 16, 2])
    cf32 = dt("cf32", [128, 128 + 1024 + 128 + 128]); c8 = dt("c8", [8, 1024 + 128]); identb_d = dt("identb", [128, 128], BF16)
    ZT = dt("ZT", [2048, S], BF16, "Internal")
    XT = dt("XT", [3072, S], F32, "Internal")
    XC = dt("XC", [3072, S], BF16, "Internal")
    DT = dt("DT", [S, 32], F32, "Internal")
    zd = [D() for _ in range(16)]; xd = [D() for _ in range(24)]; xcd = [D() for _ in range(24)]; dtd = D()
    P = c.psb
    cvs = P("cvs", [128, 24, 5], F32); hvs = P("hvs", [128, 2, 32], F32); dns = P("dns", [128, 16, 2], F32)
    cf = P("cf", [128, 1408], F32); c8s = P("c8s", [8, 1152], F32); identb = P("identb_s", [128, 128], BF16)
    Abc = P("Abc", [128, 32], F32)
    a1 = A1(nc, c, NCOL, htile)
    kd = D()
    for (o, i) in ((cvs, cvx), (hvs, hv), (dns, dn), (cf, cf32), (c8s, c8), (identb, identb_d)):
        c.dma(c.sp, o[:], i, writes=[kd])
    tri = cf[:, 0:128]; nm8 = cf[:, 128:1152]; identf = cf[:, 1152:1280]; mask01 = cf[:, 1280:1408]
    delta = c8s[:, 0:1024]; ones8 = c8s[:, 1024:1152]
    c.op(c.act, lambda: nc.scalar.activation(out=Abc[:], in_=hvs[:, 1, :], func=AF.Exp), reads=[kd], writes=[kd])
    c.op(c.dve, lambda: nc.vector.tensor_scalar(out=Abc[:], in0=Abc[:], scalar1=-1.0, scalar2=None, op0=ALU.mult), reads=[kd], writes=[kd])
    stg = a1.stg
    def epi_z(col, t0, t1, ps, pd):
        st, sd = stg.next(); o = st[:, 0:(t1 - t0) // 2].bitcast(BF16)
        c.op(c.act, lambda: nc.scalar.activation(out=o, in_=ps, func=AF.Silu), reads=[pd], writes=[sd])
        c.dma(c.sp, ZT[col:col + 128, t0:t1], o, reads=[sd], writes=[zd[col // 128]])
    def epi_x(col, t0, t1, ps, pd):
        st, sd = stg.next()
        c.op(c.dve, lambda: nc.vector.tensor_copy(out=st[:, 0:t1 - t0], in_=ps), reads=[pd], writes=[sd])
        c.dma(c.sp, XT[col:col + 128, t0:t1], st[:, 0:t1 - t0], reads=[sd], writes=[xd[col // 128]])
    def epi_dt(col, t0, nw, ps, pd):
        st, sd = stg.next()
        c.op(c.dve, lambda: nc.vector.tensor_tensor(out=st[:, 0:32], in0=ps, in1=hvs[:, 0, :], op=ALU.add), reads=[pd, kd], writes=[sd])
        c.op(c.act, lambda: nc.scalar.activation(out=st[:, 0:32], in_=st[:, 0:32], func=AF.Softplus), reads=[sd], writes=[sd])
        c.dma(c.sp, DT[t0:t0 + 128, :], st[:, 0:32], reads=[sd], writes=[dtd])
    a1.run([dict(n0=0, n1=2048, epi=epi_z), dict(n0=2048, n1=5120, epi=epi_x), dict(n0=5120, n1=5152, tm=True, epi=epi_dt)])
    if stop == 1:
        c.finish(zd + xd + [dtd]); return nc
    c.new_stage()
    TL = 512
    XTv = XT.rearrange("(c p) t -> p c t", p=128); XCv = XC.rearrange("(c p) t -> p c t", p=128); ZTv = ZT.rearrange("(c p) t -> p c t", p=128)
    xrp = Pool(c, "xr", 2, [128, 4, TL + 3], F32); xcp = Pool(c, "xc", 2, [128, 4, TL], F32); xbp = Pool(c, "xb", 2, [128, 4, TL], BF16)
    for cg in range(6):
        for tl in range(S // TL):
            t0 = tl * TL
            xr, xrd = xrp.next(); xc, xcd_ = xcp.next(); xb, xbd = xbp.next()
            if tl == 0:
                c.op(c.dve, lambda: nc.vector.memset(xr[:, :, 0:3], 0.0), writes=[xrd])
                c.dma(c.sp, xr[:, :, 3:], XTv[:, 4 * cg:4 * cg + 4, 0:TL], reads=xd[4 * cg:4 * cg + 4], writes=[xrd])
            else:
                c.dma(c.sp, xr[:], XTv[:, 4 * cg:4 * cg + 4, t0 - 3:t0 + TL], reads=xd[4 * cg:4 * cg + 4], writes=[xrd])
            for d in range(4):
                ch = 4 * cg + d
                eng, E = (c.dve, nc.vector)
                c.op(eng, lambda: E.tensor_scalar(out=xc[:, d, :], in0=xr[:, d, 0:TL], scalar1=cvs[:, ch, 0:1], scalar2=cvs[:, ch, 4:5], op0=ALU.mult, op1=ALU.add), reads=[xrd, kd], writes=[xcd_])
                for k in range(1, 4):
                    c.op(eng, lambda: E.scalar_tensor_tensor(out=xc[:, d, :], in0=xr[:, d, k:k + TL], scalar=cvs[:, ch, k:k + 1], in1=xc[:, d, :], op0=ALU.mult, op1=ALU.add), reads=[xrd, xcd_], writes=[xcd_])
            c.op(c.act, lambda: nc.scalar.activation(out=xb[:], in_=xc[:], func=AF.Silu), reads=[xcd_], writes=[xbd])
            c.dma(c.sp, XCv[:, 4 * cg:4 * cg + 4, t0:t0 + TL], xb[:], reads=[xbd], writes=xcd[4 * cg:4 * cg + 4])
    if stop == 2:
        c.finish(xcd); return nc
    c.new_stage()
    SC = 512
    xsp = Pool(c, "xs", 2, [128, 16, SC], BF16, nd=4); bsp = Pool(c, "bs", 2, [128, 4, SC], BF16); csp = Pool(c, "cs", 2, [128, 4, SC], BF16)
    zsp = Pool(c, "zs", 2, [128, 16, SC], BF16, nd=4); dtp = Pool(c, "dts", 2, [128, 4, 32], F32); yop = Pool(c, "yo", 2, [128, 16, SC], BF16, nd=4)
    pbc = Pool(c, "pbc", 1, [128, 1024], F32, psum=True); pg = Pool(c, "pgA", 3, [128, 512], F32, psum=True); pgB = Pool(c, "pgB", 3, [128, 512], F32, psum=True)
    dtap = Pool(c, "dta", 2, [128, 32], F32); cstp = Pool(c, "cst", 2, [128, 32], F32)
    Rp = Pool(c, "R", 2, [8, 1024], F32); difp = Pool(c, "dif", 2, [128, 1024], F32); Lmp = Pool(c, "Lm", 2, [128, 1024], BF16)
    ECp = Pool(c, "EC", 2, [128, 1024], F32); CBp = Pool(c, "CBm", 2, [128, 128], BF16); Mp = Pool(c, "M", 2, [128, 1024], BF16)
    Cdp = Pool(c, "Cd", 2, [128, 1024], BF16); Xtp = Pool(c, "Xdt", 2, [128, 512], BF16); Xdp = Pool(c, "Xd", 2, [128, 512], BF16)
    dep = Pool(c, "de", 2, [128, 8], F32); Btp = Pool(c, "Btm", 2, [128, 128], BF16)
    YGp = Pool(c, "YG", 2, [128, 4, 128], F32); sqp = Pool(c, "sq2", 2, [128, 128], BF16); rsp = Pool(c, "rs", 2, [128, 128], F32)
    ytp = Pool(c, "yt", 3, [128, 128], F32); tmpS = Pool(c, "tmpS", 2, [128, 512], F32)
    Sf = [c.sb(f"Sf{g}", [128, 512], F32) for g in range(4)]; Sb = [c.sb(f"Sb{g}", [128, 512], BF16) for g in range(4)]
    Sd = [D() for _ in range(4)]; Sbd = [D() for _ in range(4)]
    ones_b = c.sb("ones_b", [128, 128], BF16); epsg = c.sb("epsg", [128, 1], F32); od = D()
    c.op(c.dve, lambda: nc.vector.memset(ones_b[:], 1.0), writes=[od])
    c.op(c.dve, lambda: nc.vector.memset(epsg[:], EPS), writes=[od])
    DTv = DT.rearrange("(j p) h -> p j h", p=128)
    def ssd_body(j, g, first, tk, dta, dtad, cst, cstd, xs, xsd, bs, bsd, cs_, csd, zs, zsd, dts, dtsd, yo, yod):
        hs_ = slice(8 * g, 8 * g + 8)
        g2, g2d = pg.next()
        c.group(c.pe, [lambda: nc.tensor.matmul(g2[0:8, 0:128], lhsT=dta[:, hs_], rhs=tri, start=True, stop=True)], reads=[dtad, kd], writes=[g2d])
        R, Rd = Rp.next()
        c.op(c.dve, lambda: nc.vector.tensor_tensor(out=R[:].rearrange("k (h l) -> k h l", h=8), in0=g2[0:8, 0:128].unsqueeze(1).to_broadcast([8, 8, 128]),
                                                      in1=delta.rearrange("k (h l) -> k h l", h=8), op=ALU.mult), reads=[g2d, kd], writes=[Rd])
        yield 'A'
        bc, bcd = pbc.next()
        c.group(c.pe, [(lambda hh=hh: nc.tensor.matmul(bc[:, 512 * hh:512 * hh + 512], lhsT=ones8, rhs=R[:, 512 * hh:512 * hh + 512], start=True, stop=True)) for hh in range(2)], reads=[Rd, kd], writes=[bcd])
        yield 'A'
        bc3 = bc[:].rearrange("p (h l) -> p h l", h=8)
        dif, difd = difp.next(); Lm, Lmd = Lmp.next(); EC, ECd = ECp.next()
        c.op(c.dve, lambda: nc.vector.tensor_tensor(out=dif[:].rearrange("p (h l) -> p h l", h=8), in0=bc3, in1=cst[:, hs_].unsqueeze(2).to_broadcast([128, 8, 128]), op=ALU.subtract), reads=[bcd, cstd], writes=[difd])
        c.op(c.pool, lambda: nc.gpsimd.tensor_tensor(out=dif[:], in0=dif[:], in1=nm8, op=ALU.add), reads=[difd, kd], writes=[difd])
        yield 'A'
        c.op(c.act, lambda: nc.scalar.activation(out=Lm[:], in_=dif[:], func=AF.Exp), reads=[difd], writes=[Lmd])
        c.op(c.act, lambda: nc.scalar.activation(out=EC[:], in_=bc[:], func=AF.Exp), reads=[bcd], writes=[ECd])
        yield 'A'
        EC3 = EC[:].rearrange("p (h l) -> p h l", h=8)
        g3, g3d = pg.next()
        c.group(c.pe, [lambda: nc.tensor.matmul(g3[:, 0:128], lhsT=bs[:, g, tk], rhs=cs_[:, g, tk], start=True, stop=True)], reads=[bsd, csd], writes=[g3d])
        CBm, CBd = CBp.next()
        c.op(c.dve, lambda: nc.vector.tensor_tensor(out=CBm[:], in0=g3[:, 0:128], in1=mask01, op=ALU.mult), reads=[g3d, kd], writes=[CBd])
        M, Md = Mp.next()
        c.op(c.dve, lambda: nc.vector.tensor_tensor(out=M[:].rearrange("p (h l) -> p h l", h=8), in0=Lm[:].rearrange("p (h l) -> p h l", h=8), in1=CBm[:].unsqueeze(1).to_broadcast([128, 8, 128]), op=ALU.mult), reads=[Lmd, CBd], writes=[Md])
        yield 'A'
        Cd, Cdd = Cdp.next()
        if not first:
            c.op(c.pool, lambda: nc.gpsimd.tensor_tensor(out=Cd[:].rearrange("p (h l) -> p h l", h=8), in0=EC3, in1=cs_[:, g, tk].unsqueeze(1).to_broadcast([128, 8, 128]), op=ALU.mult), reads=[ECd, csd], writes=[Cdd])
        g4, g4d = pg.next(); g4B, g4Bd = pg.next()
        fns = [(lambda q=q: nc.tensor.matmul(g4[:, 128 * q:128 * q + 128], lhsT=xs[:, 4 * g + q, tk], rhs=identb[:], start=True, stop=True)) for q in range(4)]
        c.group(c.pe, fns, reads=[xsd[g], kd], writes=[g4d])
        c.group(c.pe, [lambda: nc.tensor.matmul(g4B[:, 0:128], lhsT=bs[:, g, tk], rhs=identb[:], start=True, stop=True)], reads=[bsd, kd], writes=[g4Bd])
        Xdt, Xdtd = Xtp.next(); Xd, Xdd = Xdp.next(); de, ded = dep.next(); Btm, Btd = Btp.next()
        c.op(c.dve, lambda: nc.vector.tensor_tensor(out=Xdt[:].rearrange("p (h q) -> p h q", h=8), in0=g4[:].rearrange("p (h q) -> p h q", h=8), in1=dts[:, j, hs_].unsqueeze(2).to_broadcast([128, 8, 64]), op=ALU.mult), reads=[g4d, dtsd], writes=[Xdtd])
        c.op(c.act, lambda: nc.scalar.activation(out=Btm[:], in_=g4B[:, 0:128], func=AF.Copy), reads=[g4Bd], writes=[Btd])
        yield 'A'
        c.op(c.dve, lambda: nc.vector.tensor_tensor(out=de[:], in0=bc3[:, :, 127], in1=cst[:, hs_], op=ALU.subtract), reads=[bcd, cstd], writes=[ded])
        c.op(c.act, lambda: nc.scalar.activation(out=de[:], in_=de[:], func=AF.Exp), reads=[ded], writes=[ded])
        c.op(c.dve, lambda: nc.vector.tensor_tensor(out=Xd[:].rearrange("p (h q) -> p h q", h=8), in0=Xdt[:].rearrange("p (h q) -> p h q", h=8), in1=de[:].unsqueeze(2).to_broadcast([128, 8, 64]), op=ALU.mult), reads=[Xdtd, ded], writes=[Xdd])
        yield 'S'
        g5, g5d = pgB.next()
        fns = []
        for hh in range(8):
            o = g5[64 * (hh % 2):64 * (hh % 2) + 64, 128 * (hh // 2):128 * (hh // 2) + 128]
            fns.append(lambda o=o, hh=hh: nc.tensor.matmul(o, lhsT=Xdt[:, 64 * hh:64 * hh + 64], rhs=M[:, 128 * hh:128 * hh + 128], start=True, stop=first))
            if not first:
                fns.append(lambda o=o, hh=hh: nc.tensor.matmul(o, lhsT=Sb[g][:, 64 * hh:64 * hh + 64], rhs=Cd[:, 128 * hh:128 * hh + 128], start=False, stop=True))
        c.group(c.pe, fns, reads=[Xdtd, Md, Sbd[g], Cdd], writes=[g5d])
        yield 'B'
        YG, YGd = YGp.next()
        g6, g6d = pgB.next()
        for q in range(4):
            ch = 4 * g + q
            yt, ytd = ytp.next()
            c.op(c.dve, lambda: nc.vector.scalar_tensor_tensor(out=yt[:], in0=xs[:, ch, tk], scalar=dns[:, ch, 0:1], in1=g5[:, 128 * q:128 * q + 128], op0=ALU.mult, op1=ALU.add), reads=[xsd[g], g5d, kd], writes=[ytd])
            c.op(c.dve, lambda: nc.vector.tensor_tensor(out=YG[:, q, :], in0=yt[:], in1=zs[:, ch, tk], op=ALU.mult), reads=[ytd, zsd[g]], writes=[YGd])
            sq, sqd = sqp.next()
            c.op(c.act, lambda: nc.scalar.activation(out=sq[:], in_=YG[:, q, :], func=AF.Square), reads=[YGd], writes=[sqd])
            c.group(c.pe, [lambda: nc.tensor.matmul(g6[:, 0:128], lhsT=ones_b[:], rhs=sq[:], start=(q == 0), stop=(q == 3), skip_group_check=True)], reads=[sqd, od], writes=[g6d])
            yield 'B'
        rs, rsd = rsp.next()
        c.op(c.act, lambda: nc.scalar.activation(out=rs[:], in_=g6[:, 0:128], func=AF.Sqrt, scale=1.0 / 512, bias=epsg[:]), reads=[g6d, od], writes=[rsd])
        c.op(c.dve, lambda: nc.vector.reciprocal(out=rs[:], in_=rs[:]), reads=[rsd], writes=[rsd])
        for q in range(4):
            ch = 4 * g + q
            c.op(c.dve, lambda: nc.vector.scalar_tensor_tensor(out=yo[:, ch, tk], in0=YG[:, q, :], scalar=dns[:, ch, 1:2], in1=rs[:], op0=ALU.mult, op1=ALU.mult), reads=[YGd, rsd, kd], writes=[yod[g]])
        yield 'B'
        g7, g7d = pgB.next()
        c.group(c.pe, [lambda: nc.tensor.matmul(g7[:], lhsT=Btm[:], rhs=Xd[:], start=True, stop=True)], reads=[Btd, Xdd], writes=[g7d])
        if first:
            c.op(c.dve, lambda: nc.vector.tensor_copy(out=Sf[g][:], in_=g7[:]), reads=[g7d], writes=[Sd[g]])
        else:
            ts_, tsd = tmpS.next()
            c.op(c.dve, lambda: nc.vector.tensor_tensor(out=ts_[:].rearrange("p (h q) -> p h q", h=8), in0=Sf[g][:].rearrange("p (h q) -> p h q", h=8), in1=EC3[:, :, 127:128].to_broadcast([128, 8, 64]), op=ALU.mult), reads=[Sd[g], ECd], writes=[tsd])
            c.op(c.dve, lambda: nc.vector.tensor_tensor(out=Sf[g][:], in0=ts_[:], in1=g7[:], op=ALU.add), reads=[tsd, g7d], writes=[Sd[g]])
        c.op(c.act, lambda: nc.scalar.activation(out=Sb[g][:], in_=Sf[g][:], func=AF.Copy), reads=[Sd[g]], writes=[Sbd[g]])

    wv = Weave()
    for sc in range(min(S // SC, lim[0])):
        s0 = sc * SC
        xs, xsd = xsp.next(); bs, bsd = bsp.next(); cs_, csd = csp.next(); zs, zsd = zsp.next(); dts, dtsd = dtp.next(); yo, yod = yop.next()
        for q in range(4): c.dma(c.sp, xs[:, 4 * q:4 * q + 4, :], XCv[:, 4 * q:4 * q + 4, s0:s0 + SC], reads=xcd[4 * q:4 * q + 4], writes=[xsd[q]])
        c.dma(c.sp, bs[:], XCv[:, 16:20, s0:s0 + SC], reads=xcd[16:20], writes=[bsd])
        c.dma(c.sp, cs_[:], XCv[:, 20:24, s0:s0 + SC], reads=xcd[20:24], writes=[csd])
        for q in range(4): c.dma(c.sp, zs[:, 4 * q:4 * q + 4, :], ZTv[:, 4 * q:4 * q + 4, s0:s0 + SC], reads=zd[4 * q:4 * q + 4], writes=[zsd[q]])
        c.dma(c.sp, dts[:], DTv[:, 4 * sc:4 * sc + 4, :], reads=[dtd], writes=[dtsd])
        for j in range(lim[1]):
            first = (sc == 0 and j == 0)
            tk = slice(j * 128, (j + 1) * 128)
            dta, dtad = dtap.next(); cst, cstd = cstp.next()
            c.op(c.dve, lambda: nc.vector.tensor_tensor(out=dta[:], in0=dts[:, j, :], in1=Abc[:], op=ALU.mult), reads=[dtsd, kd], writes=[dtad])
            g1, g1d = pg.next()
            c.group(c.pe, [lambda: nc.tensor.matmul(g1[:, 0:32], lhsT=tri, rhs=dta[:], start=True, stop=True)], reads=[dtad, kd], writes=[g1d])
            c.op(c.act, lambda: nc.scalar.activation(out=cst[:], in_=g1[:, 0:32], func=AF.Copy), reads=[g1d], writes=[cstd])
            for g in range(lim[2]):
                wv.push(ssd_body(j, g, first, tk, dta, dtad, cst, cstd, xs, xsd, bs, bsd, cs_, csd, zs, zsd, dts, dtsd, yo, yod))
        wv.flush()
        for q in range(4): ywrite(yo[:, 4 * q:4 * q + 4, :], 4 * q, 4, s0, SC, [yod[q]])
        ydone(sc)


import math
SCALE = 128 ** -0.5
NCOLN = 1024 + 6 * 256 + 32

def emit_nsa(nc, c, htile, ywrite, ydone, stop=9, lim=(2, 32), ngrp=5):
    dt = c.dram
    pos_d = dt("pos", [1, S], I32); invf_d = dt("invf", [128, 1]); sgn_d = dt("sgn", [128, 1])
    pm_d = dt("pm", [128, 128], BF16); identb_d = dt("identb", [128, 128], BF16)
    pe_d = dt("peT", [128, 2, 32]); w1_d = dt("w1", [2, 4096, 128]); w2_d = dt("w2", [2, 128, 128])
    cmask_d = dt("cmask", [32, 128, 2, 128], BF16); c2s_d = dt("c2s", [128, 2, 64])
    tkk_d = dt("tkk", [32, 128, 2, 64]); ex_d = dt("ex", [64, 32 * 128], BF16)
    tri_d = dt("tri2", [128, 2, 128], BF16)
    gsel_d = dt("gsel", [24, 6, 512]); ones24_d = dt("ones24", [24, 128])
    QT = dt("QT", [1024, S], BF16, "Internal")
    FT = dt("FT", [1024, S], BF16, "Internal")
    VT = dt("VT", [S, 512], BF16, "Internal")
    GT = dt("GT", [32, S], F32, "Internal")
    qd = [D() for _ in range(8)]; fd = [D() for _ in range(8)]; vtd = D(); gtd = D()
    P = c.psb
    cosf = P("cosf", [128, S], F32); sinf = P("sinf", [128, S], F32)
    invf = P("invf_s", [128, 1], F32); sgn = P("sgn_s", [128, 1], F32); pm = P("pm_s", [128, 128], BF16); identb = P("identb_s", [128, 128], BF16)
    kd = D(); tabd = D()
    for (o, i) in ((invf, invf_d), (sgn, sgn_d), (pm, pm_d), (identb, identb_d)): c.dma(c.sp, o[:], i, writes=[kd])
    posi = c.sb("posi", [128, S], I32); u = c.sb("u", [128, S], F32); kf = c.sb("kf", [128, S], F32); ki = c.sb("ki", [128, S], I32)
    pd_ = D(); ud = D(); kfd = D()
    c.dma(c.sp, posi[:], pos_d[0, :].partition_broadcast(128), writes=[pd_])
    c.op(c.dve, lambda: nc.vector.tensor_copy(out=u[:], in_=posi[:]), reads=[pd_], writes=[ud])
    c.op(c.dve, lambda: nc.vector.tensor_scalar(out=u[:], in0=u[:], scalar1=invf[:, 0:1], scalar2=float(1.0 / (2 * math.pi)), op0=ALU.mult, op1=ALU.mult), reads=[ud, kd], writes=[ud])
    def wrap(dst, src, sd_):
        c.op(c.dve, lambda: nc.vector.tensor_copy(out=ki[:], in_=src), reads=[sd_], writes=[kfd])
        c.op(c.dve, lambda: nc.vector.tensor_copy(out=kf[:], in_=ki[:]), reads=[kfd], writes=[kfd])
        c.op(c.dve, lambda: nc.vector.tensor_tensor(out=dst, in0=src, in1=kf[:], op=ALU.subtract), reads=[sd_, kfd], writes=[tabd])
        c.op(c.dve, lambda: nc.vector.tensor_scalar(out=kf[:], in0=dst, scalar1=0.5, scalar2=None, op0=ALU.is_gt), reads=[tabd], writes=[kfd])
        c.op(c.dve, lambda: nc.vector.tensor_tensor(out=dst, in0=dst, in1=kf[:], op=ALU.subtract), reads=[tabd, kfd], writes=[tabd])
        c.op(c.dve, lambda: nc.vector.tensor_scalar(out=kf[:], in0=dst, scalar1=-0.5, scalar2=None, op0=ALU.is_lt), reads=[tabd], writes=[kfd])
        c.op(c.dve, lambda: nc.vector.tensor_tensor(out=dst, in0=dst, in1=kf[:], op=ALU.add), reads=[tabd, kfd], writes=[tabd])
    wrap(sinf[:], u[:], ud)
    c.op(c.dve, lambda: nc.vector.tensor_scalar(out=u[:], in0=u[:], scalar1=0.25, scalar2=None, op0=ALU.add), reads=[ud, tabd], writes=[ud])
    wrap(cosf[:], u[:], ud)
    c.op(c.act, lambda: nc.scalar.activation(out=sinf[:], in_=sinf[:], func=AF.Sin, scale=float(2 * math.pi)), reads=[tabd], writes=[tabd])
    c.op(c.act, lambda: nc.scalar.activation(out=cosf[:], in_=cosf[:], func=AF.Sin, scale=float(2 * math.pi)), reads=[tabd], writes=[tabd])
    c.op(c.dve, lambda: nc.vector.tensor_scalar(out=sinf[:], in0=sinf[:], scalar1=sgn[:, 0:1], scalar2=None, op0=ALU.mult), reads=[tabd, kd], writes=[tabd])
    if stop == 0:
        dt('hT', [DM, S]); dt('w_in', [DM, NCOLN]); dt('nrm', [128, 16])
        tabo = dt('tabo', [128, 2, S], F32, 'ExternalOutput'); tod = D()
        c.dma(c.sp, tabo[:, 0, :], cosf[:], reads=[tabd], acc=[tod]); c.dma(c.sp, tabo[:, 1, :], sinf[:], reads=[tabd], acc=[tod])
        c.finish([tod]); c.barrier(); return nc
    c.new_stage()
    a1 = A1(nc, c, NCOLN, htile)
    stg = a1.stg; pp = a1.pp
    xbp = Pool(c, "xbr", 2, [128, 512], BF16); t1p = Pool(c, "t1r", 2, [128, 512], F32); t2p = Pool(c, "t2r", 2, [128, 512], F32)
    import os
    RM = 3
    def rope_epi(dst, dd):
        def epi(col, t0, t1, ps, pd):
            n = t1 - t0
            if RM == 0:
                st, sd = stg.next(); o = st[:, 0:n // 2].bitcast(BF16)
                c.op(c.act, lambda: nc.scalar.activation(out=o, in_=ps, func=AF.Copy), reads=[pd], writes=[sd])
                c.dma(c.sp, dst[col:col + 128, t0:t1], o, reads=[sd], acc=[dd[col // 128]])
                return
            xb, xbd = xbp.next(); t1_, t1d = t1p.next(); t2_, t2d = t2p.next()
            c.op(c.act, lambda: nc.scalar.activation(out=xb[:, 0:n], in_=ps, func=AF.Copy), reads=[pd], writes=[xbd])
            p2, p2d = pp.next()
            c.group(c.pe, [lambda: nc.tensor.matmul(p2[:, 0:n], lhsT=pm[:], rhs=xb[:, 0:n], start=True, stop=True)], reads=[xbd, kd], writes=[p2d])
            ch = col // 128
            st, sd = stg.next(); o = st[:, 0:n // 2].bitcast(BF16)
            if RM == 1:
                c.op(c.act, lambda: nc.scalar.activation(out=o, in_=p2[:, 0:n], func=AF.Copy), reads=[pd, p2d], writes=[sd])
                c.dma(c.sp, dst[col:col + 128, t0:t1], o, reads=[sd], acc=[dd[ch]])
                return
            c.op(c.dve, lambda: nc.vector.tensor_tensor(out=t1_[:, 0:n], in0=ps, in1=cosf[:, t0:t1], op=ALU.mult), reads=[pd, tabd], writes=[t1d])
            if RM == 2:
                c.op(c.act, lambda: nc.scalar.activation(out=o, in_=t1_[:, 0:n], func=AF.Copy), reads=[t1d, p2d], writes=[sd])
                c.dma(c.sp, dst[col:col + 128, t0:t1], o, reads=[sd], acc=[dd[ch]])
                return
            c.op(c.dve, lambda: nc.vector.tensor_tensor(out=t2_[:, 0:n], in0=p2[:, 0:n], in1=sinf[:, t0:t1], op=ALU.mult), reads=[p2d, tabd], writes=[t2d])
            c.op(c.dve, lambda: nc.vector.tensor_tensor(out=o, in0=t1_[:, 0:n], in1=t2_[:, 0:n], op=ALU.add), reads=[t1d, t2d], writes=[sd])
            c.dma(c.sp, dst[col:col + 128, t0:t1], o, reads=[sd], acc=[dd[ch]])
        return epi
    def plain_epi(dst, dd, off):
        def epi(col, t0, t1, ps, pd):
            n = t1 - t0
            st, sd = stg.next(); o = st[:, 0:n // 2].bitcast(BF16)
            c.op(c.act, lambda: nc.scalar.activation(out=o, in_=ps, func=AF.Copy), reads=[pd], writes=[sd])
            c.dma(c.sp, dst[off + col:off + col + 128, t0:t1], o, reads=[sd], acc=[dd[(off + col) // 128]])
        return epi
    def epi_v(col, t0, nw, ps, pd):
        st, sd = stg.next(); o = st[:, 0:256].bitcast(BF16)
        c.op(c.act, lambda: nc.scalar.activation(out=o, in_=ps, func=AF.Copy), reads=[pd], writes=[sd])
        c.dma(c.sp, VT[t0:t0 + 128, :], o, reads=[sd], acc=[vtd])
    def epi_g(col, t0, t1, ps, pd):
        st, sd = stg.next()
        c.op(c.act, lambda: nc.scalar.activation(out=st[0:32, 0:t1 - t0], in_=ps, func=AF.Sigmoid), reads=[pd], writes=[sd])
        c.dma(c.sp, GT[:, t0:t1], st[0:32, 0:t1 - t0], reads=[sd], acc=[gtd])
    rope_f = rope_epi(FT, fd)
    a1.run([dict(n0=0, n1=1024, epi=rope_epi(QT, qd)),
            dict(n0=1024, n1=1536, epi=plain_epi(FT, fd, 0)),
            dict(n0=1536, n1=2048, epi=lambda col, t0, t1, ps, pd: rope_f(col + 512, t0, t1, ps, pd)),
            dict(n0=2048, n1=2560, tm=True, epi=epi_v),
            dict(n0=2560, n1=2592, epi=epi_g)][0:ngrp])
    if stop == 1:
        c.finish(qd + fd + [vtd, gtd]); return nc
    c.new_stage()
    NC_ = 255
    KC = [P(f"KC{g}", [128, 256], BF16) for g in range(2)]; VC = [P(f"VC{g}", [128, 2, 128], BF16) for g in range(2)]
    KCd = [D(), D()]; VCd = [D(), D()]
    w1s = c.sb("w1s", [128, 32, 128], BF16); w2s = c.sb("w2s", [128, 128], BF16); pes = c.sb("pes", [128, 2, 32], F32); peb = c.sb("peb", [128, 2, 32], BF16)
    srcp = Pool(c, "csrc", 2, [128, S], BF16); b1 = c.sb("b1", [128, 2], F32); gTt = c.sb("gTt", [128, 256], BF16); xb2 = c.sb("xb2", [128, 256], BF16)
    t1c = c.sb("t1c", [128, 256], F32); t2c = c.sb("t2c", [128, 256], F32)
    pp2 = Pool(c, "pp2", 4, [128, 512], F32, psum=True)
    w1d = D(); w2d = D(); ped = D(); b1d = D(); gTd = D(); xb2d = D(); tcd = D()
    c.dma(c.sp, pes[:], pe_d, writes=[ped])
    c.op(c.dve, lambda: nc.vector.tensor_copy(out=peb[:], in_=pes[:]), reads=[ped], writes=[ped])
    for g in range(2):
        c.op(c.dve, lambda: nc.vector.memset(KC[g][:], 0.0), writes=[KCd[g]])
        c.op(c.dve, lambda: nc.vector.memset(VC[g][:], 0.0), writes=[VCd[g]])
    FTv = FT.rearrange("(c p) t -> p c t", p=128)
    for kv in range(2):
        c.dma(c.pool, w1s[:], w1_d[kv].rearrange("(l d) f -> d l f", d=128), writes=[w1d])
        c.dma(c.pool, w2s[:], w2_d[kv], writes=[w2d])
        pb, pbd = pp2.next()
        c.group(c.pe, [(lambda l=l: nc.tensor.matmul(pb[:, 0:1], lhsT=w1s[:, l, :], rhs=peb[:, kv, l:l + 1], start=(l == 0), stop=(l == 31))) for l in range(32)], reads=[w1d, ped], writes=[pbd])
        c.op(c.act, lambda: nc.scalar.activation(out=b1[:, kv:kv + 1], in_=pb[:, 0:1], func=AF.Copy), reads=[pbd], writes=[b1d])
        for g in range(2):
            src, srcd = srcp.next()
            ch = 2 * kv + g
            c.dma(c.sp, src[:], FTv[:, ch, :], reads=[fd[ch]], writes=[srcd])
            ph, phd = pp2.next()
            c.group(c.pe, [(lambda l=l: nc.tensor.matmul(ph[:, 0:NC_], lhsT=w1s[:, l, :], rhs=src[:, l:l + 16 * (NC_ - 1) + 1:16], start=(l == 0), stop=(l == 31))) for l in range(32)], reads=[w1d, srcd], writes=[phd])
            c.op(c.act, lambda: nc.scalar.activation(out=gTt[:, 0:NC_], in_=ph[:, 0:NC_], func=AF.Gelu_apprx_tanh, bias=b1[:, kv:kv + 1]), reads=[phd, b1d], writes=[gTd])
            if kv == 0:
                pk, pkd = pp2.next()
                c.group(c.pe, [lambda: nc.tensor.matmul(pk[:, 0:NC_], lhsT=w2s[:], rhs=gTt[:, 0:NC_], start=True, stop=True)], reads=[w2d, gTd], writes=[pkd])
                c.op(c.act, lambda: nc.scalar.activation(out=xb2[:, 0:NC_], in_=pk[:, 0:NC_], func=AF.Copy), reads=[pkd], writes=[xb2d])
                p2, p2d = pp2.next()
                c.group(c.pe, [lambda: nc.tensor.matmul(p2[:, 0:NC_], lhsT=pm[:], rhs=xb2[:, 0:NC_], start=True, stop=True)], reads=[xb2d, kd], writes=[p2d])
                cs_ = cosf[:, 31:31 + 16 * (NC_ - 1) + 1:16]; sn_ = sinf[:, 31:31 + 16 * (NC_ - 1) + 1:16]
                c.op(c.dve, lambda: nc.vector.tensor_tensor(out=t1c[:, 0:NC_], in0=pk[:, 0:NC_], in1=cs_, op=ALU.mult), reads=[pkd, tabd], writes=[tcd])
                c.op(c.dve, lambda: nc.vector.tensor_tensor(out=t2c[:, 0:NC_], in0=p2[:, 0:NC_], in1=sn_, op=ALU.mult), reads=[p2d, tabd, tcd], writes=[tcd])
                c.op(c.dve, lambda: nc.vector.tensor_tensor(out=KC[g][:, 0:NC_], in0=t1c[:, 0:NC_], in1=t2c[:, 0:NC_], op=ALU.add), reads=[tcd], writes=[KCd[g]])
            else:
                for nch in range(2):
                    nn = 128 if nch == 0 else NC_ - 128
                    pv, pvd = pp2.next()
                    c.group(c.pe, [lambda: nc.tensor.matmul(pv[0:nn, 0:128], lhsT=gTt[:, 128 * nch:128 * nch + nn], rhs=w2s[:], start=True, stop=True)], reads=[w2d, gTd], writes=[pvd])
                    c.op(c.act, lambda: nc.scalar.activation(out=VC[g][0:nn, nch, :], in_=pv[0:nn, 0:128], func=AF.Copy), reads=[pvd], writes=[VCd[g]])
    if stop == 2:
        c.barrier(); return nc
    c.new_stage()
    KsT_ = [c.sb(f"KsT{g}", [128, S], BF16) for g in range(2)]; KwT_ = [c.sb(f"KwT{g}", [128, S], BF16) for g in range(2)]
    Vs_ = [c.sb(f"Vs{g}", [128, 32, 128], BF16) for g in range(2)]; Vw_ = [c.sb(f"Vw{g}", [128, 32, 128], BF16) for g in range(2)]
    kvd_ = [D(), D()]
    ex = c.sb("ex", [64, 32 * 128], BF16); tri2 = c.sb("tri2", [128, 2, 128], BF16); c2s = c.sb("c2s", [128, 2, 64], F32)
    gsel = c.sb("gsel", [24, 6, 512], F32); ones24 = c.sb("ones24", [24, 128], F32); ones_b = c.sb("ones_b", [128, 128], BF16)
    k3 = D()
    for (o, i) in ((ex, ex_d), (tri2, tri_d), (c2s, c2s_d), (gsel, gsel_d), (ones24, ones24_d)): c.dma(c.sp, o[:], i, writes=[k3])
    c.op(c.dve, lambda: nc.vector.memset(ones_b[:], 1.0), writes=[k3])
    qsp = Pool(c, "qs", 2, [128, 4, 128], BF16, nd=0); cmp_ = Pool(c, "cm", 2, [128, 2, 128], BF16); tkp = Pool(c, "tk", 2, [128, 2, 64], F32)
    gtp = Pool(c, "gt", 2, [24, 128], F32)
    Ecp = Pool(c, "Ec", 2, [128, 2, 512], F32); Ecbp = Pool(c, "Ecb", 2, [128, 2, 512], BF16)
    Ep = Pool(c, "E", 6, [128, 512], BF16); Emp = Pool(c, "Em", 3, [128, 512], BF16)
    rdp = Pool(c, "rd", 2, [128, 512], F32); tp_ = Pool(c, "tt", 2, [128, 512], F32); accp = Pool(c, "acc", 2, [128, 512], F32)
    impp = Pool(c, "imp", 2, [128, 64], F32); imp2p = Pool(c, "imp2", 2, [128, 64], F32); m8p = Pool(c, "m8", 2, [128, 8], F32)
    selp = Pool(c, "sel", 2, [128, 64], BF16); selTp = Pool(c, "selT", 2, [64, 128], BF16); Rgp = Pool(c, "Rg", 2, [24, 512], F32)
    mdp = Pool(c, "md", 2, [128, 128], BF16); yop = Pool(c, "yo", 2, [128, 512], BF16)
    ps = Pool(c, "ps3", 3, [128, 512], F32, psum=True); psacc = Pool(c, "psacc", 2, [128, 512], F32, psum=True); psA = Pool(c, "psA", 3, [128, 512], F32, psum=True)
    VTv = VT.rearrange("(c p) e -> p c e", p=128)
    QTv = QT.rearrange("(h p) t -> p h t", p=128)
    def finish_branch(o_ps, o_d, den_ps, den_d, gl, k, acc, accd, gt, gtd_, first, pspool=None):
        rd, rdd = rdp.next(); tt_, ttd = tp_.next(); Rg, Rgd = Rgp.next()
        c.op(c.dve, lambda: nc.vector.tensor_scalar(out=rd[:], in0=den_ps[:], scalar1=1e-30, scalar2=None, op0=ALU.max), reads=[den_d], writes=[rdd])
        c.op(c.dve, lambda: nc.vector.reciprocal(out=rd[:], in_=rd[:]), reads=[rdd], writes=[rdd])
        c.op(c.dve, lambda: nc.vector.tensor_tensor(out=tt_[:], in0=o_ps[:], in1=rd[:], op=ALU.mult), reads=[o_d, rdd], writes=[ttd])
        c.op(c.pool, lambda: nc.gpsimd.tensor_tensor(out=Rg[:].rearrange("k (h t) -> k h t", h=4), in0=gsel[:, 3 * gl + k, :].rearrange("k (h t) -> k h t", h=4), in1=gt[:].unsqueeze(1).to_broadcast([24, 4, 128]), op=ALU.mult), reads=[gtd_, k3], writes=[Rgd])
        gb, gbd = (pspool or ps).next()
        c.group(c.pe, [lambda: nc.tensor.matmul(gb[:], lhsT=ones24[:], rhs=Rg[:], start=True, stop=True)], reads=[Rgd, k3], writes=[gbd])
        if first:
            c.op(c.dve, lambda: nc.vector.tensor_tensor(out=acc[:], in0=tt_[:], in1=gb[:], op=ALU.mult), reads=[ttd, gbd], writes=[accd])
        else:
            c.op(c.dve, lambda: nc.vector.tensor_tensor(out=tt_[:], in0=tt_[:], in1=gb[:], op=ALU.mult), reads=[ttd, gbd], writes=[ttd])
            c.op(c.pool, lambda: nc.gpsimd.tensor_tensor(out=acc[:], in0=acc[:], in1=tt_[:], op=ALU.add), reads=[ttd, accd], writes=[accd])
        return rd, rdd
    BRS = '012'
    for gl in range(2):
        kvd = kvd_[gl]
        c.dma(c.sp, KsT_[gl][:], FTv[:, 4 + gl, :], reads=[fd[4 + gl]], writes=[kvd])
        c.dma(c.sp, KwT_[gl][:], FTv[:, 6 + gl, :], reads=[fd[6 + gl]], writes=[kvd])
        c.dma(c.sp, Vs_[gl][:], VTv[:, :, 128 * gl:128 * gl + 128], reads=[vtd], writes=[kvd])
        c.dma(c.sp, Vw_[gl][:], VTv[:, :, 256 + 128 * gl:256 + 128 * gl + 128], reads=[vtd], writes=[kvd])
    def nsa_body(qb, gl):
        kvd = kvd_[gl]; KsT = KsT_[gl]; KwT = KwT_[gl]; Vs = Vs_[gl]; Vw = Vw_[gl]
        t0 = 128 * qb
        qs, qsd = qsp.next(); cm, cmd = cmp_.next(); tk, tkd = tkp.next(); gt, gtd_ = gtp.next()
        c.dma(c.sp, qs[:], QTv[:, 4 * gl:4 * gl + 4, t0:t0 + 128], reads=qd[4 * gl:4 * gl + 4], writes=[qsd])
        c.dma(c.sp, cm[:], cmask_d[qb], writes=[cmd])
        c.dma(c.sp, tk[:], tkk_d[qb], writes=[tkd])
        c.dma(c.sp, gt[:], GT[0:24, t0:t0 + 128], reads=[gtd], writes=[gtd_])
        q2 = qs[:].rearrange("p h t -> p (h t)")
        acc, accd = accp.next()
        yield 'A'
        Ec, Ecd = Ecp.next(); Ecb, Ecbd = Ecbp.next()
        for nch in range(2):
            sp_, spd = psA.next()
            c.group(c.pe, [lambda: nc.tensor.matmul(sp_[:], lhsT=KC[gl][:, 128 * nch:128 * nch + 128], rhs=q2, start=True, stop=True)], reads=[KCd[gl], qsd], writes=[spd])
            E, Ed = Ep.next()
            c.op(c.act, lambda: nc.scalar.activation(out=E[:], in_=sp_[:], func=AF.Exp, scale=SCALE), reads=[spd], writes=[Ed])
            c.op(c.dve, lambda: nc.vector.tensor_tensor(out=Ec[:, nch, :].rearrange("p (h t) -> p h t", h=4), in0=E[:].rearrange("p (h t) -> p h t", h=4), in1=cm[:, nch, :].unsqueeze(1).to_broadcast([128, 4, 128]), op=ALU.mult), reads=[Ed, cmd], writes=[Ecd])
            yield 'A'
        c.op(c.act, lambda: nc.scalar.activation(out=Ecb[:], in_=Ec[:], func=AF.Copy), reads=[Ecd], writes=[Ecbd])
        oc, ocd = psA.next(); dc, dcd = psA.next()
        c.group(c.pe, [(lambda n_=n_: nc.tensor.matmul(oc[:], lhsT=VC[gl][:, n_, :], rhs=Ecb[:, n_, :], start=(n_ == 0), stop=(n_ == 1))) for n_ in range(2)], reads=[VCd[gl], Ecbd], writes=[ocd])
        c.group(c.pe, [(lambda n_=n_: nc.tensor.matmul(dc[:], lhsT=ones_b[:], rhs=Ecb[:, n_, :], start=(n_ == 0), stop=(n_ == 1))) for n_ in range(2)], reads=[k3, Ecbd], writes=[dcd])
        yield 'A'
        inited = ['0' in BRS]
        if '0' not in BRS:
            rd, rdd = rdp.next()
            c.op(c.dve, lambda: nc.vector.tensor_scalar(out=rd[:], in0=dc[:], scalar1=1e-30, scalar2=None, op0=ALU.max), reads=[dcd], writes=[rdd])
            c.op(c.dve, lambda: nc.vector.reciprocal(out=rd[:], in_=rd[:]), reads=[rdd], writes=[rdd])
        else:
            rd, rdd = finish_branch(oc, ocd, dc, dcd, gl, 0, acc, accd, gt, gtd_, True, psA)
        c.op(c.dve, lambda: nc.vector.tensor_tensor(out=Ec[:], in0=Ec[:], in1=rd[:].unsqueeze(1).to_broadcast([128, 2, 512]), op=ALU.mult), reads=[Ecd, rdd], writes=[Ecd])
        ip, ipd = psA.next()
        fns = []
        for n_ in range(2):
            for h in range(4):
                fns.append(lambda n_=n_, h=h: nc.tensor.matmul(ip[:, 0:64], lhsT=Ec[:, n_, 128 * h:128 * h + 128], rhs=c2s[:, n_, :], start=(n_ == 0 and h == 0), stop=(n_ == 1 and h == 3)))
        c.group(c.pe, fns, reads=[Ecd, k3], writes=[ipd])
        imp, impd = impp.next(); imp2, imp2d = imp2p.next(); m8, m8d = m8p.next(); sel, seld = selp.next(); selT, selTd = selTp.next()
        c.op(c.dve, lambda: nc.vector.tensor_tensor(out=imp[:], in0=ip[:, 0:64], in1=tk[:, 0, :], op=ALU.mult), reads=[ipd, tkd], writes=[impd])
        c.op(c.dve, lambda: nc.vector.tensor_tensor(out=imp[:], in0=imp[:], in1=tk[:, 1, :], op=ALU.add), reads=[impd, tkd], writes=[impd])
        yield 'A'
        c.op(c.dve, lambda: nc.vector.max(out=m8[:], in_=imp[:]), reads=[impd], writes=[m8d])
        c.op(c.dve, lambda: nc.vector.match_replace(out=imp2[:], in_to_replace=m8[:], in_values=imp[:], imm_value=-2e30), reads=[impd, m8d], writes=[imp2d])
        c.op(c.dve, lambda: nc.vector.max(out=m8[:], in_=imp2[:]), reads=[imp2d], writes=[m8d])
        c.op(c.dve, lambda: nc.vector.tensor_scalar(out=sel[:], in0=imp[:], scalar1=m8[:, 7:8], scalar2=None, op0=ALU.is_ge), reads=[impd, m8d], writes=[seld])
        yield 'A'
        stp, stpd = psA.next()
        c.group(c.pe, [lambda: nc.tensor.matmul(stp[0:64, 0:128], lhsT=sel[:], rhs=identb[:], start=True, stop=True)], reads=[seld, kd], writes=[stpd])
        c.op(c.act, lambda: nc.scalar.activation(out=selT[:], in_=stp[0:64, 0:128], func=AF.Copy), reads=[stpd], writes=[selTd])
        yield 'S'
        for br in (1, 2):
            K_ = KsT if br == 1 else KwT; V_ = Vs if br == 1 else Vw
            kcs = list(range(0, qb + 1)) if br == 1 else list(range(max(0, qb - 4), qb + 1))
            ob, obd = psacc.next(); db, dbd = psacc.next()
            def stA(ki_, kc):
                sp_, spd = ps.next()
                c.group(c.pe, [lambda: nc.tensor.matmul(sp_[:], lhsT=K_[:, 128 * kc:128 * kc + 128], rhs=q2, start=True, stop=True)], reads=[kvd, qsd], writes=[spd])
                E, Ed = Ep.next()
                c.op(c.act, lambda: nc.scalar.activation(out=E[:], in_=sp_[:], func=AF.Exp, scale=SCALE), reads=[spd], writes=[Ed])
                st = dict(ki=ki_, kc=kc, E=E, Ed=Ed)
                if br == 1:
                    mp, mpd = ps.next()
                    c.group(c.pe, [lambda: nc.tensor.matmul(mp[:, 0:128], lhsT=ex[:, 128 * kc:128 * kc + 128], rhs=selT[:], start=True, stop=True)], reads=[selTd, k3], writes=[mpd])
                    st.update(mp=mp, mpd=mpd)
                return st
            def stB(st):
                ki_, kc, E, Ed = st['ki'], st['kc'], st['E'], st['Ed']
                if br == 1:
                    mp, mpd = st['mp'], st['mpd']
                    Em, Emd = Emp.next()
                    if kc == qb:
                        md, mdd = mdp.next()
                        c.op(c.dve, lambda: nc.vector.tensor_tensor(out=md[:], in0=mp[:, 0:128], in1=tri2[:, 0, :], op=ALU.mult), reads=[mpd, k3], writes=[mdd])
                        msk, mskd = md[:], mdd
                    else:
                        msk, mskd = mp[:, 0:128], mpd
                    c.op(c.dve, lambda: nc.vector.tensor_tensor(out=Em[:].rearrange("p (h t) -> p h t", h=4), in0=E[:].rearrange("p (h t) -> p h t", h=4), in1=msk.unsqueeze(1).to_broadcast([128, 4, 128]), op=ALU.mult), reads=[Ed, mskd], writes=[Emd])
                else:
                    if kc == qb or kc == qb - 4:
                        Em, Emd = Emp.next()
                        msk = tri2[:, 0 if kc == qb else 1, :]
                        c.op(c.dve, lambda: nc.vector.tensor_tensor(out=Em[:].rearrange("p (h t) -> p h t", h=4), in0=E[:].rearrange("p (h t) -> p h t", h=4), in1=msk.unsqueeze(1).to_broadcast([128, 4, 128]), op=ALU.mult), reads=[Ed, k3], writes=[Emd])
                    else:
                        Em, Emd = E, Ed
                st_ = (ki_ == 0); sp2 = (ki_ == len(kcs) - 1)
                c.group(c.pe, [lambda: nc.tensor.matmul(ob[:], lhsT=V_[:, kc, :], rhs=Em[:], start=st_, stop=sp2, skip_group_check=True)], reads=[kvd, Emd], writes=[obd])
                c.group(c.pe, [lambda: nc.tensor.matmul(db[:], lhsT=ones_b[:], rhs=Em[:], start=st_, stop=sp2, skip_group_check=True)], reads=[k3, Emd], writes=[dbd])
            pend = None
            for ki_, kc in enumerate(kcs):
                cur = stA(ki_, kc)
                if pend is not None: stB(pend)
                pend = cur
                yield 'B'
            stB(pend)
            if str(br) in BRS:
                finish_branch(ob, obd, db, dbd, gl, br, acc, accd, gt, gtd_, not inited[0]); inited[0] = True
        yo, yod = yop.next()
        c.op(c.act, lambda: nc.scalar.activation(out=yo[:], in_=acc[:], func=AF.Copy), reads=[accd], writes=[yod])
        ywrite(yo[:].rearrange("p (h t) -> p h t", h=4), 4 * gl, 4, t0, 128, [yod])
        if qb % 4 == 3 and gl == 1: ydone(qb // 4)
    wv = Weave()
    for qb in range(32):
        for gl in range(2):
            wv.push(nsa_body(qb, gl))
    wv.flush()
import ml_dtypes
_BF = ml_dtypes.bfloat16
_arr = lambda n: np.ascontiguousarray(np.asarray(n).reshape(-1, 128).T)
_sl = lambda a, n: np.arange(a, a + n)
_PROGS = {}
_CONST = {}
_KINDS = [0, 1, 2, 0]
_INNER = [4096, 2048, 2048, 4096]


def build_all():
    nc = bass.Bass("TRN2", target_bir_lowering=False)
    c = Ctx(nc)
    xT = c.dram("xT", [DM, S]); hT0 = c.dram("hT0", [DM, TPC])
    oT = c.dram("oT", [DM, TPC], F32, "ExternalOutput")
    xv = xT.rearrange("(c p) t -> p c t", p=128); h0v = hT0.rearrange("(c p) t -> p c t", p=128); oTv = oT.rearrange("(c p) t -> p c t", p=128)
    hall = None; had = None; hloc = None; hlocd = None
    fin = D()
    for i in range(4):
        kind = _KINDS[i]; inner = _INNER[i]; final = (i == 3)
        IC2 = inner // 256
        c.pfx = f"L{i}A_"
        ysrc = [c.dram(f"ysrc{k}", [inner // 2, 512], BF16, None) for k in range(8)]; ysd = [D() for _ in range(8)]
        yall = [c.dram(f"yall{k}", [inner, 512], BF16, None) for k in range(8)]; yad = [D() for _ in range(8)]
        ysv = [y.rearrange("(c p) t -> p c t", p=128) for y in ysrc]
        if i == 0:
            htile = lambda ti, q: (xv[:, 4 * q:4 * q + 4, ti * TT:(ti + 1) * TT], [])
        else:
            def htile(ti, q, hall=hall, had=had):
                sh = ti // 4; tl = ti % 4; fh = q // 2
                r0 = sh * 1024 + (q % 2) * 512
                return hall[tl][fh][r0:r0 + 512, :].rearrange("(c p) t -> p c t", p=128), [had[tl][fh]]
        def ywrite(ap, ch0, n, t0, T, reads, ysv=ysv, ysd=ysd):
            k = t0 // 512; tl0 = t0 % 512
            c.dma(c.sp, ysv[k][:, ch0:ch0 + n, tl0:tl0 + T], ap, reads=reads, acc=[ysd[k]])
        def ydone(k, ysrc=ysrc, ysd=ysd, yall=yall, yad=yad):
            c.allgather(ysrc[k], [ysd[k]], yall[k], yad[k])
        [emit_ssd, emit_lru, emit_nsa][kind](nc, c, htile, ywrite, ydone)
        c.new_phase()
        c.pfx = f"L{i}B_"
        if i == 0:
            hin = lambda ti, q: (h0v[:, 4 * q:4 * q + 4, ti * TT:(ti + 1) * TT], [])
        else:
            def hin(ti, q, hloc=hloc, hlocd=hlocd):
                fh = q // 2; r0 = (q % 2) * 512
                return hloc[ti][fh][r0:r0 + 512, :].rearrange("(c p) t -> p c t", p=128), [hlocd[ti][fh]]
        if final:
            hout = lambda ti, q: (oTv[:, 4 * q:4 * q + 4, ti * TT:(ti + 1) * TT], fin)
            hdone = lambda ti: None
        else:
            nloc = [[c.dram(f"hout{t}_{f}", [1024, 512], F32, None) for f in range(2)] for t in range(4)]
            nlocd = [[D() for f in range(2)] for t in range(4)]
            nall = [[c.dram(f"hall{t}_{f}", [2048, 512], F32, None) for f in range(2)] for t in range(4)]
            nalld = [[D() for f in range(2)] for t in range(4)]
            def hout(ti, q, nloc=nloc, nlocd=nlocd):
                fh = q // 2; r0 = (q % 2) * 512
                return nloc[ti][fh][r0:r0 + 512, :].rearrange("(c p) t -> p c t", p=128), nlocd[ti][fh]
            def hdone(ti, nloc=nloc, nlocd=nlocd, nall=nall, nalld=nalld):
                for f in range(2): c.allgather(nloc[ti][f], [nlocd[ti][f]], nall[ti][f], nalld[ti][f])
        emit_B(nc, c, inner, final, hin, yall, yad, hout, hdone)
        if not final:
            hall, had, hloc, hlocd = nall, nalld, nloc, nlocd
        c.new_phase()
    c.finish([fin])
    return nc


def _ssd_consts():
    if 'ssd' in _CONST: return _CONST['ssd']
    s_ = np.arange(128)
    tri = (s_[:, None] <= s_[None, :]).astype(np.float32)
    nm = np.where(s_[:, None] <= s_[None, :], 0.0, -30000.0).astype(np.float32)
    cf32 = np.ascontiguousarray(np.concatenate([tri, np.tile(nm, (1, 8)), np.eye(128, dtype=np.float32), tri], axis=1))
    delta = np.zeros((8, 8, 128), np.float32)
    for k in range(8): delta[k, k, :] = 1
    c8 = np.ascontiguousarray(np.concatenate([delta.reshape(8, 1024), np.ones((8, 128), np.float32)], axis=1))
    _CONST['ssd'] = dict(cf32=cf32, c8=c8, identb=np.eye(128).astype(_BF))
    return _CONST['ssd']


def _ssd_inputs(d, j, li, hf):
    W = d['ssd_in_proj'][j]
    cols = np.concatenate([_sl(hf * 2048, 2048), _sl(4096 + hf * 2048, 2048), _sl(8192 + hf * 512, 512), _sl(8192 + 1024 + hf * 512, 512), _sl(4096 + 6144 + 32 * hf, 32)])
    Wc = np.ascontiguousarray(W[:, cols])
    xbc = np.concatenate([_sl(hf * 2048, 2048), _sl(4096 + hf * 512, 512), _sl(4096 + 1024 + hf * 512, 512)])
    cvx = np.zeros((128, 24, 5), np.float32)
    for k in range(4): cvx[:, :, k] = _arr(d['ssd_conv_w'][j][k, xbc])
    cvx[:, :, 4] = _arr(d['ssd_conv_b'][j][xbc])
    hs = slice(32 * hf, 32 * hf + 32)
    hv = np.zeros((128, 2, 32), np.float32); hv[:, 0, :] = d['ssd_dt_bias'][j][hs][None]; hv[:, 1, :] = d['ssd_a_log'][j][hs][None]
    dn = np.zeros((128, 16, 2), np.float32)
    dn[:, :, 0] = _arr(np.repeat(d['ssd_d'][j][hs], 64)); dn[:, :, 1] = _arr(d['ssd_norm'][j][hf * 2048:(hf + 1) * 2048])
    ins = dict(w_in=Wc, nrm=_arr(d['norm_mix'][li]), cvx=cvx, hv=hv, dn=dn)
    ins.update(_ssd_consts())
    return ins


def _lru_inputs(d, li, hf):
    W = d['lru_in_proj'][0]
    cs = slice(hf * 1024, (hf + 1) * 1024)
    Wc = np.ascontiguousarray(np.concatenate([W[:, cs], W[:, 2048 + hf * 1024: 2048 + (hf + 1) * 1024]], axis=1))
    cv = np.zeros((128, 8, 9), np.float32)
    for k in range(4): cv[:, :, k] = _arr(d['lru_conv_w'][0][k, cs])
    cv[:, :, 4] = _arr(d['lru_conv_b'][0][cs]); cv[:, :, 5] = _arr(d['lru_ba'][0][cs]); cv[:, :, 6] = _arr(d['lru_bx'][0][cs]); cv[:, :, 7] = _arr(d['lru_a_param'][0][cs])
    return dict(w_in=Wc, nrm=_arr(d['norm_mix'][li]), wa=np.ascontiguousarray(d['lru_wa'][0][4 * hf:4 * hf + 4]), wx=np.ascontiguousarray(d['lru_wx'][0][4 * hf:4 * hf + 4]), cv=cv)


def _nsa_consts():
    if 'nsa' in _CONST: return _CONST['nsa']
    k = {}
    half = 16
    inv = (np.float32(500000.0) ** (-np.arange(half, dtype=np.float32) * np.float32(2.0) / np.float32(32))).astype(np.float32)
    invf = np.zeros((128, 1), np.float32); invf[0:16, 0] = inv; invf[16:32, 0] = inv
    k['invf'] = invf
    sgn = np.zeros((128, 1), np.float32); sgn[0:16] = -1; sgn[16:32] = 1
    k['sgn'] = sgn
    pm = np.zeros((128, 128), np.float32)
    for dd in range(16): pm[dd + 16, dd] = 1; pm[dd, dd + 16] = 1
    k['pm'] = pm.astype(_BF); k['identb'] = np.eye(128).astype(_BF)
    tt = np.arange(128)
    cmask = np.zeros((32, 128, 2, 128), np.float32)
    for qb in range(32):
        t = 128 * qb + tt
        for nch in range(2):
            nn = 128 * nch + np.arange(128)
            cmask[qb, :, nch, :] = ((16 * nn[:, None] + 31 <= t[None, :]) & (nn[:, None] < 255))
    k['cmask'] = cmask.astype(_BF)
    c_start = np.arange(255)[:, None] * 16; s_start = np.arange(64)[None, :] * 64
    ov = np.clip(np.minimum(c_start + 32, s_start + 64) - np.maximum(c_start, s_start), 0, None) / 16.0
    c2s = np.zeros((256, 64), np.float32); c2s[:255] = ov
    k['c2s'] = np.ascontiguousarray(c2s.reshape(2, 128, 64).transpose(1, 0, 2))
    tkk = np.zeros((32, 128, 2, 64), np.float32)
    jj = np.arange(64)
    for qb in range(32):
        cur = (128 * qb + tt) // 64
        forced = (jj[None, :] == 0) | (jj[None, :] == cur[:, None]) | (jj[None, :] == cur[:, None] - 1)
        valid = jj[None, :] <= cur[:, None]
        tkk[qb, :, 0, :] = (~forced) & valid
        tkk[qb, :, 1, :] = np.where(valid, np.where(forced, 1e9, 0.0), -1e30)
    k['tkk'] = tkk
    ex = np.zeros((64, 32, 128), np.float32)
    for kc in range(32):
        for p in range(128): ex[2 * kc + p // 64, kc, p] = 1
    k['ex'] = ex.reshape(64, 32 * 128).astype(_BF)
    p = np.arange(128)
    tri2 = np.zeros((128, 2, 128), np.float32); tri2[:, 0, :] = p[:, None] <= tt[None, :]; tri2[:, 1, :] = p[:, None] > tt[None, :]
    k['tri2'] = tri2.astype(_BF)
    gsel = np.zeros((24, 6, 4, 128), np.float32)
    for gl in range(2):
        for kk in range(3):
            for h in range(4): gsel[(4 * gl + h) * 3 + kk, 3 * gl + kk, h, :] = 1
    k['gsel'] = gsel.reshape(24, 6, 512); k['ones24'] = np.ones((24, 128), np.float32)
    _CONST['nsa'] = k
    return k


def _nsa_inputs(d, li, pos_b, gp):
    W = d['nsa_in_proj'][0]
    g0 = 2 * gp
    parts = [_sl(1024 * gp, 1024)]
    for kidx in (0, 1, 2, 4, 3, 5):
        parts.append(_sl(2048 + kidx * 512 + g0 * 128, 256))
    parts.append(_sl(2048 + 6 * 512 + 24 * gp, 24))
    Wc = np.ascontiguousarray(np.concatenate([W[:, np.concatenate(parts)], np.zeros((2048, 8), np.float32)], axis=1))
    ins = dict(w_in=Wc, nrm=_arr(d['norm_mix'][li]), pos=np.ascontiguousarray(np.asarray(pos_b)[None, :]).astype(np.int32),
               peT=np.ascontiguousarray(d['nsa_cmp_pe'][0].transpose(2, 0, 1)), w1=d['nsa_cmp_w1'][0], w2=d['nsa_cmp_w2'][0])
    ins.update(_nsa_consts())
    return ins


def kernel(**inputs):
    d = {k: np.asarray(v) for k, v in inputs.items()}
    x = d['x']
    NB = 4
    if 'all' not in _PROGS: _PROGS['all'] = build_all()
    nc = _PROGS['all']
    maps = []
    for b in range(NB):
        xTb = np.ascontiguousarray(x[b].T)
        for r in range(2):
            ts = slice(r * 2048, (r + 1) * 2048)
            m = dict(xT=xTb, hT0=np.ascontiguousarray(xTb[:, ts]))
            sel = np.zeros((128, 2), np.float32); sel[:, r] = 1.0
            for i in range(4):
                kind, j = i % 3, i // 3
                if kind == 0: a = _ssd_inputs(d, j, i, r)
                elif kind == 1: a = _lru_inputs(d, i, r)
                else: a = _nsa_inputs(d, i, d['positions'][b], r)
                for k_, v_ in a.items(): m[f"L{i}A_{k_}"] = v_
                nrm = np.ascontiguousarray(np.concatenate([_arr(d['norm_ffn'][i]), _arr(d['norm_ple'][i]), _arr(d['norm_final'])], axis=1))
                w_o = [d['ssd_out_proj'][j], d['lru_out_proj'][0], d['nsa_out_proj'][0]][kind]
                bi = dict(pT=np.ascontiguousarray(d['p'][i][b, ts].T), sel=sel, w_o=w_o, w_in=d['w_ffn_in'][i], w_out=d['w_ffn_out'][i],
                          w_g=d['w_ple_gate'][i], w_u=d['w_ple_up'][i], nrm=nrm)
                for k_, v_ in bi.items(): m[f"L{i}B_{k_}"] = v_
            maps.append(m)
    res = run_bass_kernel_spmd(nc, maps, core_ids=list(range(8)))
    out = np.empty((NB, S, DM), np.float32)
    for b in range(NB):
        for r in range(2):
            out[b, r * 2048:(r + 1) * 2048, :] = np.asarray(res.results[2 * b + r]['oT']).T
    return out
```

```python
import math, os


import numpy as np
import concourse.bass as bass
import concourse.mybir as mybir
from concourse.bass_utils import run_bass_kernel_spmd

F32 = mybir.dt.float32; BF16 = mybir.dt.bfloat16; I32 = mybir.dt.int32
AF = mybir.ActivationFunctionType; ALU = mybir.AluOpType
AX = mybir.AxisListType


class D:
    __slots__ = ("w", "r", "excl")
    def __init__(s, excl=False): s.w = []; s.r = []; s.excl = excl


class Eng:
    def __init__(s, ctx, name, e):
        s.e = e; s.name = name; s.sem = ctx.nc.alloc_semaphore("s_" + name); s.cnt = 0; s.seen = {}
        s.dsems = []; s.dcnt = []; s.di = 0


class Ctx:
    def __init__(s, nc, ndma=8):
        s.nc = nc
        s.pe = Eng(s, "pe", nc.tensor); s.act = Eng(s, "act", nc.scalar); s.dve = Eng(s, "dve", nc.vector)
        s.pool = Eng(s, "pool", nc.gpsimd); s.sp = Eng(s, "sp", nc.sync)
        for q in (s.sp, s.pool, s.act):
            n = ndma
            q.dsems = [nc.alloc_semaphore(f"d_{q.name}{i}") for i in range(n)]; q.dcnt = [0] * n
        s.nbank = 0
        import contextlib
        s._cl = contextlib
        s.es = contextlib.ExitStack(); s.pes = contextlib.ExitStack(); s.uid = 0; s.pfx = ""

    def psb(s, name, shape, dtype):
        s.uid += 1
        return s.pes.enter_context(s.nc.sbuf_tensor(f"{name}_{s.uid}", shape, dtype))

    def dram(s, name, shape, dtype=F32, kind="ExternalInput"):
        if kind is None: return s.nc.dram_tensor(s.pfx + name, shape, dtype).ap()
        return s.nc.dram_tensor(s.pfx + name, shape, dtype, kind=kind).ap()

    def new_phase(s):
        s.barrier(); s.es.close(); s.pes.close(); s.es = s._cl.ExitStack(); s.pes = s._cl.ExitStack()

    def allgather(s, src, src_deps, dst, dst_dep):
        q = s.pool
        s._deps(q, src_deps, [dst_dep])
        s.uid += 1
        sem = s.nc.alloc_semaphore(f"cc_{s.uid}")
        s.nc.gpsimd.collective_compute("AllGather", ALU.bypass, replica_groups=[[0, 1], [2, 3], [4, 5], [6, 7]], ins=[src.opt()], outs=[dst.opt()]).then_inc(sem)
        s._mark((sem, 1), src_deps, [dst_dep])

    def sb(s, name, shape, dtype):
        s.uid += 1
        return s.es.enter_context(s.nc.sbuf_tensor(f"{name}_{s.uid}", shape, dtype))

    def ps(s, name, shape, dtype=None):
        s.uid += 1
        return s.es.enter_context(s.nc.psum_tensor(f"{name}_{s.uid}", shape, dtype or F32))

    def barrier(s):
        engs = [s.pe, s.act, s.dve, s.pool, s.sp]
        for e in engs:
            for o in engs:
                if o is not e and o.cnt > 0: s._wait(e, o.sem, o.cnt)
            for q in (s.sp, s.pool, s.act):
                for sem, cnt in zip(q.dsems, q.dcnt):
                    if cnt > 0: s._wait(e, sem, cnt)

    def new_stage(s):
        s.barrier(); s.es.close(); s.es = s._cl.ExitStack()

    def _wait(s, eng, sem, val):
        key = id(sem)
        if eng.seen.get(key, 0) < val:
            eng.e.wait_ge(sem, val); eng.seen[key] = val

    def _deps(s, eng, reads, writes, acc=()):
        for t in reads:
            for w in t.w: s._wait(eng, *w)
            if t.excl:
                for (sem, val) in t.r:
                    if sem is not eng.sem: s._wait(eng, sem, val)
        for t in writes:
            for w in t.w: s._wait(eng, *w)
        for t in list(writes) + list(acc):
            for (sem, val) in t.r:
                if sem is eng.sem: continue
                s._wait(eng, sem, val)

    def _mark(s, tag, reads, writes, acc=()):
        for t in writes: t.w = [tag]; t.r = []
        for t in acc: t.w.append(tag)
        for t in reads: t.r.append(tag)

    def op(s, eng, fn, reads=(), writes=()):
        s._deps(eng, reads, writes)
        inst = fn()
        eng.cnt += 1
        inst.then_inc(eng.sem, 1)
        s._mark((eng.sem, eng.cnt), reads, writes)

    def group(s, eng, fns, reads=(), writes=()):
        s._deps(eng, reads, writes)
        inst = None
        for fn in fns: inst = fn()
        eng.cnt += 1
        inst.then_inc(eng.sem, 1)
        s._mark((eng.sem, eng.cnt), reads, writes)

    def dma(s, q, out, in_, reads=(), writes=(), acc=(), **kw):
        s._deps(q, reads, writes, acc)
        i = q.di; q.di = (q.di + 1) % len(q.dsems)
        sem = q.dsems[i]
        if q.dcnt[i] > 0: s._wait(q, sem, q.dcnt[i])
        q.dcnt[i] += 16
        q.e.dma_start(out=out, in_=in_, **kw).then_inc(sem, 16)
        s._mark((sem, q.dcnt[i]), reads, writes, acc)

    def finish(s, deps):
        for t in deps:
            for w in t.w: s._wait(s.sp, *w)


class Pool:
    def __init__(s, c, name, n, shape, dtype, psum=False, nd=0):
        alloc = c.ps if psum else c.sb
        s.t = [alloc(f"{name}{i}", shape, dtype) for i in range(n)]
        s.d = [(D(psum) if nd == 0 else [D(psum) for _ in range(nd)]) for _ in range(n)]; s.i = 0; s.n = n
    def next(s):
        i = s.i; s.i = (s.i + 1) % s.n
        return s.t[i], s.d[i]


class WStream:
    def __init__(s, c, nbuf, nelem, name="wbuf", live=1):
        s.c = c; s.nelem = nelem; s.ahead = nbuf - live
        s.t = [c.sb(f"{name}{i}", [128, nelem], BF16) for i in range(nbuf)]
        s.d = [D() for _ in range(nbuf)]
        s.plan = []; s.issued = 0; s.pos = 0
    def _issue(s):
        i = s.issued
        if i >= len(s.plan): return
        src, kc, nw = s.plan[i]
        b = i % len(s.t)
        dst = s.t[b][:, 0:kc * nw].rearrange("p (k n) -> p k n", k=kc)
        s.c.dma(s.c.pool, dst, src, writes=[s.d[b]])
        s.issued += 1
    def next(s):
        while s.issued <= min(s.pos + s.ahead, len(s.plan) - 1): s._issue()
        src, kc, nw = s.plan[s.pos]
        b = s.pos % len(s.t); s.pos += 1
        return s.t[b][:, 0:kc * nw].rearrange("p (k n) -> p k n", k=kc), s.d[b]


def wtiles(W, K, n0, n1, nw):
    kc = K // 128
    Wv = W.rearrange("(k p) n -> p k n", p=128)
    return [(Wv[:, :, a:min(a + nw, n1)], kc, min(a + nw, n1) - a) for a in range(n0, n1, nw)]


def gemm(c, ws, pp, spec, rhs, rhs_deps, T, epi, mchunk=128):
    nc = c.nc
    col = 0
    for (src, kc, nw) in spec:
        wt, wd = ws.next()
        for m0 in range(0, nw, mchunk):
            m1 = min(m0 + mchunk, nw)
            for t0 in range(0, T, 512):
                t1 = min(t0 + 512, T)
                pt, pd = pp.next()
                out = pt[0:m1 - m0, 0:t1 - t0]
                fns = [(lambda k=k: nc.tensor.matmul(out, lhsT=wt[:, k, m0:m1], rhs=rhs(k, t0, t1), start=(k == 0), stop=(k == kc - 1))) for k in range(kc)]
                deps = [wd]
                for k in range(kc): deps += rhs_deps(k, t0)
                c.group(c.pe, fns, reads=deps, writes=[pd])
                epi(col + m0, t0, t1, out, pd)
        col += nw


DM = 2048; FF = 5632; PLE = 256; TPC = 2048; TT = 512; EPS = 1e-6

class Weave:
    def __init__(s): s.prev = None
    def push(s, g):
        a_done = False; b_done = (s.prev is None)
        while not (a_done and b_done):
            if not a_done:
                if next(g) == 'S': a_done = True
            if not b_done:
                try: next(s.prev)
                except StopIteration: b_done = True
        s.prev = g
    def flush(s):
        if s.prev is not None:
            for _ in s.prev: pass
        s.prev = None


def rmsnorm_tile(c, nc, pp, h, hd, wcol, v, vd, ones, sqp, misc, T=TT, out_f32=None):
    pt, pd = pp.next()
    for ch in range(16):
        sq, sd = sqp.next()
        c.op(c.act, lambda: nc.scalar.activation(out=sq[:, 0:T], in_=h[:, ch, :], func=AF.Square), reads=[hd[ch]], writes=[sd])
        c.group(c.pe, [lambda: nc.tensor.matmul(pt[:, 0:T], lhsT=ones[:], rhs=sq[:, 0:T], start=(ch == 0), stop=(ch == 15), skip_group_check=True)], reads=[sd], writes=[pd])
    rs, rd = misc.next()
    c.op(c.act, lambda: nc.scalar.activation(out=rs[:, 0:T], in_=pt[:, 0:T], func=AF.Ln, scale=1.0 / DM, bias=c.eps[:]), reads=[pd], writes=[rd])
    c.op(c.act, lambda: nc.scalar.activation(out=rs[:, 0:T], in_=rs[:, 0:T], func=AF.Exp, scale=-0.5), reads=[rd], writes=[rd])
    for ch in range(16):
        o = v[:, ch, :] if out_f32 is None else out_f32[:, ch, :]
        c.op(c.dve, lambda: nc.vector.scalar_tensor_tensor(out=o, in0=h[:, ch, :], scalar=wcol[:, ch:ch + 1], in1=rs[:, 0:T], op0=ALU.mult, op1=ALU.mult),
             reads=[hd[ch], rd], writes=[vd[ch]])


def emit_B(nc, c, inner, final, hin, yall, yad, hout, hdone):
    TB = 1024; NS = 2
    IC = inner // 128
    dt = c.dram
    pT = dt("pT", [PLE, TPC]); sel_d = dt("sel", [128, 2])
    w_o = dt("w_o", [inner, DM]); w_in = dt("w_in", [DM, 2 * FF]); w_out = dt("w_out", [FF, DM])
    w_g = dt("w_g", [DM, DM]); w_u = dt("w_u", [PLE, DM])
    nrm = dt("nrm", [128, 48])
    ones = c.sb("ones", [128, 128], BF16); c.eps = c.sb("eps", [128, 1], F32)
    nw = c.sb("nw", [128, 48], F32)
    cd = D()
    c.op(c.dve, lambda: nc.vector.memset(ones[:], 1.0), writes=[cd])
    c.op(c.dve, lambda: nc.vector.memset(c.eps[:], EPS), writes=[cd])
    c.dma(c.sp, nw[:], nrm, writes=[cd])
    sel = c.sb("sel", [128, 2], F32)
    c.dma(c.sp, sel[:], sel_d, writes=[cd])
    h = c.sb("h", [128, 16, TB], F32); hd = [[D() for _ in range(16)] for _ in range(NS)]
    big = c.sb("big", [128, 32, TB], BF16); bd = [[D() for _ in range(32)] for _ in range(NS)]
    v = c.sb("v", [128, 16, TB], BF16); vd = [[D() for _ in range(16)] for _ in range(NS)]
    pb = c.sb("pb", [128, 2, TB], BF16); pbd = [D(), D()]
    pp = Pool(c, "ps", 8, [128, 512], F32, psum=True)
    sqp = Pool(c, "sq", 3, [128, 512], BF16)
    misc = Pool(c, "misc", 4, [128, 512], F32)
    sgp = Pool(c, "sg", 4, [128, 512], BF16)
    ws = WStream(c, 2, 6144)
    NT = TPC // TB
    g_o = wtiles(w_o, inner, 0, DM, 128 if IC > 16 else 256)
    halves = [(0, 3072), (3072, FF)]
    g_in = []; g_out = []
    for (a_, b_) in halves:
        gi = []
        for j in range(a_, b_, 256):
            gi += wtiles(w_in, DM, j, j + 256, 256) + wtiles(w_in, DM, FF + j, FF + j + 256, 256)
        g_in.append(gi)
        g_out.append(wtiles(w_out[a_:b_, :], b_ - a_, 0, DM, 256))
    g_g = wtiles(w_g, DM, 0, DM, 256)
    g_u = wtiles(w_u, PLE, 0, DM, 2048)
    for _ in range(NT): ws.plan += g_o + g_in[0] + g_out[0] + g_in[1] + g_out[1] + g_u + g_g
    pTv = pT.rearrange("(c p) t -> p c t", p=128)
    sb_ = lambda s_: slice(s_ * 512, (s_ + 1) * 512)
    for ti in range(NT):
        for s_ in range(NS):
            u = NS * ti + s_
            for q in range(4):
                hap, hdeps = hin(u, q)
                c.dma(c.sp, h[:, 4 * q:4 * q + 4, sb_(s_)], hap, reads=hdeps, writes=hd[s_][4 * q:4 * q + 4])
            y0v = yall[u].rearrange("(c p) t -> p c t", p=128); y1v = yall[4 + u].rearrange("(c p) t -> p c t", p=128)
            for q in range(0, IC, 8):
                vq = q % 16
                yt_ = v[:, vq:vq + 8, sb_(s_)]; ytd_ = vd[s_][vq:vq + 8]
                c.dma(c.sp, big[:, q:q + 8, sb_(s_)], y0v[:, q:q + 8, :], reads=[yad[u]], writes=bd[s_][q:q + 8])
                c.dma(c.sp, yt_, y1v[:, q:q + 8, :], reads=[yad[4 + u]], writes=ytd_)
                c.op(c.dve, lambda: nc.vector.tensor_scalar(out=yt_, in0=yt_, scalar1=sel[:, 1:2], scalar2=None, op0=ALU.mult), reads=ytd_ + [cd], writes=ytd_)
                c.op(c.dve, lambda: nc.vector.scalar_tensor_tensor(out=big[:, q:q + 8, sb_(s_)], in0=big[:, q:q + 8, sb_(s_)], scalar=sel[:, 0:1], in1=yt_, op0=ALU.mult, op1=ALU.add), reads=bd[s_][q:q + 8] + ytd_ + [cd], writes=bd[s_][q:q + 8])
        c.dma(c.pool, pb[:], pTv[:, :, ti * TB:(ti + 1) * TB], writes=pbd)
        def epi_add(col, t0, t1, ps, pd):
            ch = col // 128; s_ = t0 // 512
            c.op(c.dve, lambda: nc.vector.tensor_tensor(out=h[:, ch, t0:t1], in0=h[:, ch, t0:t1], in1=ps, op=ALU.add), reads=[pd, hd[s_][ch]], writes=[hd[s_][ch]])
        gemm(c, ws, pp, g_o, lambda k, t0, t1: big[:, k, t0:t1], lambda k, t0: [bd[t0 // 512][k]], TB, epi_add)
        for s_ in range(NS):
            rmsnorm_tile(c, nc, pp, h[:, :, sb_(s_)], hd[s_], nw[:, 0:16], v[:, :, sb_(s_)], vd[s_], ones, sqp, misc, T=512)
        for hp in range(2):
            st = {}
            def epi_ffn(col, t0, t1, ps, pd):
                j = col % 512; grp = col // 512; isup = j >= 256; ch = grp * 2 + (j % 256) // 128; s_ = t0 // 512
                if not isup:
                    sg, sd = sgp.next()
                    c.op(c.act, lambda: nc.scalar.activation(out=sg[:, 0:t1 - t0], in_=ps, func=AF.Silu), reads=[pd], writes=[sd])
                    st[(ch, t0)] = (sg, sd)
                else:
                    sg, sd = st.pop((ch, t0))
                    c.op(c.dve, lambda: nc.vector.tensor_tensor(out=big[:, ch, t0:t1], in0=sg[:, 0:t1 - t0], in1=ps, op=ALU.mult), reads=[pd, sd], writes=[bd[s_][ch]])
            gemm(c, ws, pp, g_in[hp], lambda k, t0, t1: v[:, k, t0:t1], lambda k, t0: [vd[t0 // 512][k]], TB, epi_ffn)
            gemm(c, ws, pp, g_out[hp], lambda k, t0, t1: big[:, k, t0:t1], lambda k, t0: [bd[t0 // 512][k]], TB, epi_add)
        for s_ in range(NS):
            rmsnorm_tile(c, nc, pp, h[:, :, sb_(s_)], hd[s_], nw[:, 16:32], v[:, :, sb_(s_)], vd[s_], ones, sqp, misc, T=512)
        def epi_up(col, t0, t1, ps, pd):
            ch = col // 128; s_ = t0 // 512
            c.op(c.act, lambda: nc.scalar.activation(out=big[:, ch, t0:t1], in_=ps, func=AF.Copy), reads=[pd], writes=[bd[s_][ch]])
        gemm(c, ws, pp, g_u, lambda k, t0, t1: pb[:, k, t0:t1], lambda k, t0: [pbd[k]], TB, epi_up)
        def epi_gate(col, t0, t1, ps, pd):
            ch = col // 128; s_ = t0 // 512
            sg, sd = misc.next()
            c.op(c.act, lambda: nc.scalar.activation(out=sg[:, 0:t1 - t0], in_=ps, func=AF.Sigmoid), reads=[pd], writes=[sd])
            c.op(c.dve, lambda: nc.vector.tensor_tensor(out=sg[:, 0:t1 - t0], in0=sg[:, 0:t1 - t0], in1=big[:, ch, t0:t1], op=ALU.mult), reads=[sd, bd[s_][ch]], writes=[sd])
            c.op(c.dve, lambda: nc.vector.tensor_tensor(out=h[:, ch, t0:t1], in0=h[:, ch, t0:t1], in1=sg[:, 0:t1 - t0], op=ALU.add), reads=[sd, hd[s_][ch]], writes=[hd[s_][ch]])
        gemm(c, ws, pp, g_g, lambda k, t0, t1: v[:, k, t0:t1], lambda k, t0: [vd[t0 // 512][k]], TB, epi_gate)
        for s_ in range(NS):
            u = NS * ti + s_
            if final:
                rmsnorm_tile(c, nc, pp, h[:, :, sb_(s_)], hd[s_], nw[:, 32:48], v[:, :, sb_(s_)], hd[s_], ones, sqp, misc, T=512, out_f32=h[:, :, sb_(s_)])
            for q in range(4):
                oap, odp = hout(u, q)
                c.dma(c.sp, oap, h[:, 4 * q:4 * q + 4, sb_(s_)], reads=hd[s_][4 * q:4 * q + 4], acc=[odp])
            hdone(u)


S = 4096; TT = 512


class A1:
    def __init__(s, nc, c, ncols, htile, wbuf_elems=16 * 512):
        s.nc = nc; s.c = c; s.htile = htile
        dt = c.dram
        s.dt = dt
        s.w = dt("w_in", [DM, ncols]); s.nrm = dt("nrm", [128, 16])
        s.ones = c.sb("ones", [128, 128], BF16); c.eps = c.sb("eps", [128, 1], F32)
        s.nw = c.sb("nw", [128, 16], F32)
        s.cd = D()
        c.op(c.dve, lambda: nc.vector.memset(s.ones[:], 1.0), writes=[s.cd])
        c.op(c.dve, lambda: nc.vector.memset(c.eps[:], EPS), writes=[s.cd])
        c.dma(c.sp, s.nw[:], s.nrm, writes=[s.cd])
        s.h = c.sb("h", [128, 16, TT], F32); s.hd = [D() for _ in range(16)]
        s.v = c.sb("v", [128, 16, TT], BF16); s.vd = [D() for _ in range(16)]
        s.pp = Pool(c, "ps", 8, [128, 512], F32, psum=True)
        s.sqp = Pool(c, "sq", 3, [128, TT], BF16)
        s.misc = Pool(c, "misc", 4, [128, TT], F32)
        s.stg = Pool(c, "stg", 4, [128, 512], F32)
        s.ws = WStream(c, 2, wbuf_elems)

    def run(s, groups):
        nc, c = s.nc, s.c
        plan = []
        for g in groups: plan += wtiles(s.w, DM, g["n0"], g["n1"], 512)
        NT = S // TT
        for _ in range(NT): s.ws.plan += plan
        for ti in range(NT):
            tb = ti * TT
            for q in range(4):
                hap, hdeps = s.htile(ti, q)
                c.dma(c.sp, s.h[:, 4 * q:4 * q + 4, :], hap, reads=hdeps, writes=s.hd[4 * q:4 * q + 4])
            rmsnorm_tile(c, nc, s.pp, s.h, s.hd, s.nw[:, 0:16], s.v, s.vd, s.ones, s.sqp, s.misc)
            for g in groups:
                spec = wtiles(s.w, DM, g["n0"], g["n1"], 512)
                if not g.get("tm"):
                    gemm(c, s.ws, s.pp, spec, lambda k, t0, t1: s.v[:, k, t0:t1], lambda k, t0: [s.vd[k]], TT,
                         lambda col, t0, t1, ps, pd, g=g: g["epi"](col, tb + t0, tb + t1, ps, pd))
                else:
                    col = 0
                    for (src, kc, nw) in spec:
                        wt, wd = s.ws.next()
                        for t0 in range(0, TT, 128):
                            pt, pd = s.pp.next()
                            out = pt[:, 0:nw]
                            fns = [(lambda k=k: nc.tensor.matmul(out, lhsT=s.v[:, k, t0:t0 + 128], rhs=wt[:, k, :], start=(k == 0), stop=(k == kc - 1))) for k in range(kc)]
                            c.group(c.pe, fns, reads=[wd] + s.vd, writes=[pd])
                            g["epi"](col, tb + t0, nw, out, pd)
                        col += nw


def emit_lru(nc, c, htile, ywrite, ydone):
    a1 = A1(nc, c, 2048, htile)
    dt = a1.dt
    wa = dt("wa", [4, 256, 256]); wx = dt("wx", [4, 256, 256])
    cv = dt("cv", [128, 8, 9])
    GT = dt("GT", [1024, S], BF16, "Internal")
    XT = dt("XT", [1024, S], F32, "Internal")
    gd = [D() for _ in range(8)]; xd = [D() for _ in range(8)]
    cvs = c.sb("cvs", [128, 8, 9], F32)
    cA = c.sb("cA", [128, 8], F32); cA2 = c.sb("cA2", [128, 8], F32)
    cvd = D()
    c.dma(c.sp, cvs[:], cv, writes=[cvd])
    c.op(c.act, lambda: nc.scalar.activation(out=cA[:], in_=cvs[:, :, 7], func=AF.Softplus, scale=-1.0), reads=[cvd], writes=[cvd])
    c.op(c.dve, lambda: nc.vector.tensor_scalar(out=cA2[:], in0=cA[:], scalar1=-16.0, scalar2=None, op0=ALU.mult), reads=[cvd], writes=[cvd])
    c.op(c.dve, lambda: nc.vector.tensor_scalar(out=cA[:], in0=cA[:], scalar1=-8.0, scalar2=None, op0=ALU.mult), reads=[cvd], writes=[cvd])
    stg = a1.stg
    def epi_g(col, t0, t1, ps, pd):
        ch = col // 128
        st, sd = stg.next(); o = st[:].bitcast(BF16)[:, 0:t1 - t0] if False else st[:, 0:(t1 - t0) // 2].bitcast(BF16)
        c.op(c.act, lambda: nc.scalar.activation(out=o, in_=ps, func=AF.Gelu_apprx_tanh), reads=[pd], writes=[sd])
        c.dma(c.sp, GT[col:col + 128, t0:t1], o, reads=[sd], writes=[gd[ch]])
    def epi_x(col, t0, t1, ps, pd):
        ch = col // 128
        st, sd = stg.next()
        c.op(c.dve, lambda: nc.vector.tensor_copy(out=st[:, 0:t1 - t0], in_=ps), reads=[pd], writes=[sd])
        c.dma(c.sp, XT[col:col + 128, t0:t1], st[:, 0:t1 - t0], reads=[sd], writes=[xd[ch]])
    a1.run([dict(n0=0, n1=1024, epi=epi_g), dict(n0=1024, n1=2048, epi=lambda col, t0, t1, ps, pd: epi_x(col, t0, t1, ps, pd))])
    TL = 512
    ws2 = WStream(c, 4, 2 * 256, name="wb2_", live=2)
    for tl in range(S // TL):
        for kb in range(4):
            ws2.plan += [(wa[kb].rearrange("(k p) n -> p k n", p=128), 2, 256), (wx[kb].rearrange("(k p) n -> p k n", p=128), 2, 256)]
    xrp = Pool(c, "xr", 2, [128, 2, TL + 3], F32); xcp = Pool(c, "xc", 2, [128, 2, TL], F32)
    xbp = Pool(c, "xb", 2, [128, 2, TL], BF16); gp = Pool(c, "gg", 2, [128, 2, TL], BF16)
    ap_ = Pool(c, "aa", 2, [128, 2, TL], F32); bp = Pool(c, "bb", 2, [128, 2, TL], F32)
    hp = Pool(c, "hs", 2, [128, 2, TL], F32); yp = Pool(c, "yy", 2, [128, 2, TL], BF16)
    tp = Pool(c, "tmp", 4, [128, 512], F32)
    carry = c.sb("carry", [128, 8], F32); cyd = [D() for _ in range(8)]
    XTv = XT.rearrange("(c p) t -> p c t", p=128); GTv = GT.rearrange("(c p) t -> p c t", p=128)
    pp = a1.pp
    for tl in range(S // TL):
        for kb in range(4):
            t0 = tl * TL
            xr, xrd = xrp.next(); xc, xcd = xcp.next(); xb, xbd = xbp.next(); gg, ggd = gp.next()
            aa, aad = ap_.next(); bb, bbd = bp.next(); hs, hsd = hp.next(); yy, yyd = yp.next()
            if tl == 0:
                c.op(c.dve, lambda: nc.vector.memset(xr[:, :, 0:3], 0.0), writes=[xrd])
                c.dma(c.sp, xr[:, :, 3:], XTv[:, 2 * kb:2 * kb + 2, 0:TL], reads=xd[2 * kb:2 * kb + 2], writes=[xrd])
            else:
                c.dma(c.sp, xr[:], XTv[:, 2 * kb:2 * kb + 2, t0 - 3:t0 + TL], reads=xd[2 * kb:2 * kb + 2], writes=[xrd])
            c.dma(c.sp, gg[:], GTv[:, 2 * kb:2 * kb + 2, t0:t0 + TL], reads=gd[2 * kb:2 * kb + 2], writes=[ggd])
            for d in range(2):
                ch = 2 * kb + d
                c.op(c.dve, lambda: nc.vector.tensor_scalar(out=xc[:, d, :], in0=xr[:, d, 0:TL], scalar1=cvs[:, ch, 0:1], scalar2=cvs[:, ch, 4:5], op0=ALU.mult, op1=ALU.add), reads=[xrd, cvd], writes=[xcd])
                for k in range(1, 4):
                    c.op(c.dve, lambda: nc.vector.scalar_tensor_tensor(out=xc[:, d, :], in0=xr[:, d, k:k + TL], scalar=cvs[:, ch, k:k + 1], in1=xc[:, d, :], op0=ALU.mult, op1=ALU.add), reads=[xrd, xcd], writes=[xcd])
                c.op(c.act, lambda: nc.scalar.activation(out=xb[:, d, :], in_=xc[:, d, :], func=AF.Copy), reads=[xcd], writes=[xbd])
            wat, wad = ws2.next(); wxt, wxd = ws2.next()
            for d in range(2):
                ch = 2 * kb + d
                for b0 in range(0, TL, 512):
                    pa, pad = pp.next(); px, pxd = pp.next()
                    c.group(c.pe, [(lambda k=k: nc.tensor.matmul(pa[:], lhsT=wat[:, k, d * 128:(d + 1) * 128], rhs=xb[:, k, b0:b0 + 512], start=(k == 0), stop=(k == 1))) for k in range(2)], reads=[wad, xbd], writes=[pad])
                    c.group(c.pe, [(lambda k=k: nc.tensor.matmul(px[:], lhsT=wxt[:, k, d * 128:(d + 1) * 128], rhs=xb[:, k, b0:b0 + 512], start=(k == 0), stop=(k == 1))) for k in range(2)], reads=[wxd, xbd], writes=[pxd])
                    r, rd = tp.next(); a2, a2d = tp.next()
                    asl = aa[:, d, b0:b0 + 512]; bsl = bb[:, d, b0:b0 + 512]
                    c.op(c.act, lambda: nc.scalar.activation(out=r[:], in_=pa[:], func=AF.Sigmoid, bias=cvs[:, ch, 5:6]), reads=[pad, cvd], writes=[rd])
                    c.op(c.act, lambda: nc.scalar.activation(out=asl, in_=r[:], func=AF.Exp, scale=cA[:, ch:ch + 1]), reads=[rd, cvd], writes=[aad])
                    c.op(c.act, lambda: nc.scalar.activation(out=a2[:], in_=r[:], func=AF.Exp, scale=cA2[:, ch:ch + 1]), reads=[rd, cvd], writes=[a2d])
                    c.op(c.act, lambda: nc.scalar.activation(out=a2[:], in_=a2[:], func=AF.Sqrt, scale=-1.0, bias=1.0), reads=[a2d], writes=[a2d])
                    c.op(c.act, lambda: nc.scalar.activation(out=r[:], in_=px[:], func=AF.Sigmoid, bias=cvs[:, ch, 6:7]), reads=[pxd, cvd, aad], writes=[rd])
                    c.op(c.dve, lambda: nc.vector.tensor_tensor(out=bsl, in0=r[:], in1=xc[:, d, b0:b0 + 512], op=ALU.mult), reads=[rd, xcd], writes=[bbd])
                    c.op(c.dve, lambda: nc.vector.tensor_tensor(out=bsl, in0=bsl, in1=a2[:], op=ALU.mult), reads=[a2d, bbd], writes=[bbd])
                init = 0.0 if tl == 0 else carry[:, ch:ch + 1]
                c.op(c.dve, lambda: nc.vector.tensor_tensor_scan(out=hs[:, d, :], data0=aa[:, d, :], data1=bb[:, d, :], initial=init, op0=ALU.mult, op1=ALU.add), reads=[aad, bbd, cyd[ch]], writes=[hsd])
                c.op(c.dve, lambda: nc.vector.tensor_copy(out=carry[:, ch:ch + 1], in_=hs[:, d, TL - 1:TL]), reads=[hsd], writes=[cyd[ch]])
                c.op(c.dve, lambda: nc.vector.tensor_tensor(out=yy[:, d, :], in0=hs[:, d, :], in1=gg[:, d, :], op=ALU.mult), reads=[hsd, ggd], writes=[yyd])
            ywrite(yy[:], 2 * kb, 2, t0, TL, [yyd])
        ydone(tl)


class Cut(Exception): pass

def emit_ssd(nc, c, htile, ywrite, ydone, stop=9, lim=(8, 4, 4), cut=99):
    def CUT(k): pass
    NCOL = 2048 + 3072 + 32
    dt = c.dram
    cvx = dt("cvx", [128, 24, 5]); hv = dt("hv", [128, 2, 32]); dn = dt("dn", [128, 16, 2])
    cf32 = dt("cf32", [128, 128 + 1024 + 128 + 128]); c8 = dt("c8", [8, 1024 + 128]); identb_d = dt("identb", [128, 128], BF16)
    ZT = dt("ZT", [2048, S], BF16, "Internal")
    XT = dt("XT", [3072, S], F32, "Internal")
    XC = dt("XC", [3072, S], BF16, "Internal")
    DT = dt("DT", [S, 32], F32, "Internal")
    zd = [D() for _ in range(16)]; xd = [D() for _ in range(24)]; xcd = [D() for _ in range(24)]; dtd = D()
    P = c.psb
    cvs = P("cvs", [128, 24, 5], F32); hvs = P("hvs", [128, 2, 32], F32); dns = P("dns", [128, 16, 2], F32)
    cf = P("cf", [128, 1408], F32); c8s = P("c8s", [8, 1152], F32); identb = P("identb_s", [128, 128], BF16)
    Abc = P("Abc", [128, 32], F32)
    a1 = A1(nc, c, NCOL, htile)
    kd = D()
    for (o, i) in ((cvs, cvx), (hvs, hv), (dns, dn), (cf, cf32), (c8s, c8), (identb, identb_d)):
        c.dma(c.sp, o[:], i, writes=[kd])
    tri = cf[:, 0:128]; nm8 = cf[:, 128:1152]; identf = cf[:, 1152:1280]; mask01 = cf[:, 1280:1408]
    delta = c8s[:, 0:1024]; ones8 = c8s[:, 1024:1152]
    c.op(c.act, lambda: nc.scalar.activation(out=Abc[:], in_=hvs[:, 1, :], func=AF.Exp), reads=[kd], writes=[kd])
    c.op(c.dve, lambda: nc.vector.tensor_scalar(out=Abc[:], in0=Abc[:], scalar1=-1.0, scalar2=None, op0=ALU.mult), reads=[kd], writes=[kd])
    stg = a1.stg
    def epi_z(col, t0, t1, ps, pd):
        st, sd = stg.next(); o = st[:, 0:(t1 - t0) // 2].bitcast(BF16)
        c.op(c.act, lambda: nc.scalar.activation(out=o, in_=ps, func=AF.Silu), reads=[pd], writes=[sd])
        c.dma(c.sp, ZT[col:col + 128, t0:t1], o, reads=[sd], writes=[zd[col // 128]])
    xrp1 = Pool(c, "xr1", 2, [128, 515], F32); xcp1 = Pool(c, "xc1", 2, [128, 512], F32)
    halo = c.sb("halo", [128, 24, 3], F32); halod = [D() for _ in range(24)]
    def epi_x(col, t0, t1, ps, pd):
        ch = col // 128
        xr, xrd = xrp1.next(); xc, xcd_ = xcp1.next()
        if t0 == 0:
            c.op(c.dve, lambda: nc.vector.memset(xr[:, 0:3], 0.0), writes=[xrd])
        else:
            c.op(c.dve, lambda: nc.vector.tensor_copy(out=xr[:, 0:3], in_=halo[:, ch, :]), reads=[halod[ch]], writes=[xrd])
        c.op(c.act, lambda: nc.scalar.activation(out=xr[:, 3:515], in_=ps, func=AF.Copy), reads=[pd, xrd], writes=[xrd])
        c.op(c.dve, lambda: nc.vector.tensor_copy(out=halo[:, ch, :], in_=xr[:, 512:515]), reads=[xrd], writes=[halod[ch]])
        c.op(c.dve, lambda: nc.vector.tensor_scalar(out=xc[:], in0=xr[:, 0:512], scalar1=cvs[:, ch, 0:1], scalar2=cvs[:, ch, 4:5], op0=ALU.mult, op1=ALU.add), reads=[xrd, kd], writes=[xcd_])
        for k in range(1, 4):
            c.op(c.dve, lambda: nc.vector.scalar_tensor_tensor(out=xc[:], in0=xr[:, k:k + 512], scalar=cvs[:, ch, k:k + 1], in1=xc[:], op0=ALU.mult, op1=ALU.add), reads=[xrd, xcd_], writes=[xcd_])
        st, sd = stg.next(); o = st[:, 0:256].bitcast(BF16)
        c.op(c.act, lambda: nc.scalar.activation(out=o, in_=xc[:], func=AF.Silu), reads=[xcd_], writes=[sd])
        c.dma(c.sp, XC[col:col + 128, t0:t1], o, reads=[sd], acc=[xcd[ch]])
    def epi_dt(col, t0, nw, ps, pd):
        st, sd = stg.next()
        c.op(c.dve, lambda: nc.vector.tensor_tensor(out=st[:, 0:32], in0=ps, in1=hvs[:, 0, :], op=ALU.add), reads=[pd, kd], writes=[sd])
        c.op(c.act, lambda: nc.scalar.activation(out=st[:, 0:32], in_=st[:, 0:32], func=AF.Softplus), reads=[sd], writes=[sd])
        c.dma(c.sp, DT[t0:t0 + 128, :], st[:, 0:32], reads=[sd], writes=[dtd])
    a1.run([dict(n0=0, n1=2048, epi=epi_z), dict(n0=2048, n1=5120, epi=epi_x), dict(n0=5120, n1=5152, tm=True, epi=epi_dt)])
    if stop == 1:
        c.finish(zd + xd + [dtd]); return nc
    c.new_stage()
    XCv = XC.rearrange("(c p) t -> p c t", p=128); ZTv = ZT.rearrange("(c p) t -> p c t", p=128)
    if stop == 2:
        c.finish(xcd); return nc
    c.new_stage()
    SC = 512
    xsp = Pool(c, "xs", 2, [128, 16, SC], BF16, nd=4); bsp = Pool(c, "bs", 2, [128, 4, SC], BF16); csp = Pool(c, "cs", 2, [128, 4, SC], BF16)
    zsp = Pool(c, "zs", 2, [128, 16, SC], BF16, nd=4); dtp = Pool(c, "dts", 2, [128, 4, 32], F32); yop = Pool(c, "yo", 2, [128, 16, SC], BF16, nd=4)
    pbc = Pool(c, "pbc", 1, [128, 1024], F32, psum=True); pg = Pool(c, "pgA", 3, [128, 512], F32, psum=True); pgB = Pool(c, "pgB", 3, [128, 512], F32, psum=True)
    dtap = Pool(c, "dta", 2, [128, 32], F32); cstp = Pool(c, "cst", 2, [128, 32], F32)
    Rp = Pool(c, "R", 2, [8, 1024], F32); difp = Pool(c, "dif", 2, [128, 1024], F32); Lmp = Pool(c, "Lm", 2, [128, 1024], BF16)
    ECp = Pool(c, "EC", 2, [128, 1024], F32); CBp = Pool(c, "CBm", 2, [128, 128], BF16); Mp = Pool(c, "M", 2, [128, 1024], BF16)
    Cdp = Pool(c, "Cd", 2, [128, 1024], BF16); Xtp = Pool(c, "Xdt", 2, [128, 512], BF16); Xdp = Pool(c, "Xd", 2, [128, 512], BF16)
    dep = Pool(c, "de", 2, [128, 8], F32); Btp = Pool(c, "Btm", 2, [128, 128], BF16)
    YGp = Pool(c, "YG", 2, [128, 4, 128], F32); sqp = Pool(c, "sq2", 2, [128, 128], BF16); rsp = Pool(c, "rs", 2, [128, 128], F32)
    ytp = Pool(c, "yt", 3, [128, 128], F32); tmpS = Pool(c, "tmpS", 2, [128, 512], F32)
    Sf = [c.sb(f"Sf{g}", [128, 512], F32) for g in range(4)]; Sb = [c.sb(f"Sb{g}", [128, 512], BF16) for g in range(4)]
    Sd = [D() for _ in range(4)]; Sbd = [D() for _ in range(4)]
    ones_b = c.sb("ones_b", [128, 128], BF16); epsg = c.sb("epsg", [128, 1], F32); od = D()
    c.op(c.dve, lambda: nc.vector.memset(ones_b[:], 1.0), writes=[od])
    c.op(c.dve, lambda: nc.vector.memset(epsg[:], EPS), writes=[od])
    DTv = DT.rearrange("(j p) h -> p j h", p=128)
    def ssd_body(j, g, first, tk, dta, dtad, cst, cstd, xs, xsd, bs, bsd, cs_, csd, zs, zsd, dts, dtsd, yo, yod):
        hs_ = slice(8 * g, 8 * g + 8)
        g2, g2d = pg.next()
        c.group(c.pe, [lambda: nc.tensor.matmul(g2[0:8, 0:128], lhsT=dta[:, hs_], rhs=tri, start=True, stop=True)], reads=[dtad, kd], writes=[g2d])
        R, Rd = Rp.next()
        c.op(c.dve, lambda: nc.vector.tensor_tensor(out=R[:].rearrange("k (h l) -> k h l", h=8), in0=g2[0:8, 0:128].unsqueeze(1).to_broadcast([8, 8, 128]),
                                                      in1=delta.rearrange("k (h l) -> k h l", h=8), op=ALU.mult), reads=[g2d, kd], writes=[Rd])
        yield 'A'
        bc, bcd = pbc.next()
        c.group(c.pe, [(lambda hh=hh: nc.tensor.matmul(bc[:, 512 * hh:512 * hh + 512], lhsT=ones8, rhs=R[:, 512 * hh:512 * hh + 512], start=True, stop=True)) for hh in range(2)], reads=[Rd, kd], writes=[bcd])
        yield 'A'
        bc3 = bc[:].rearrange("p (h l) -> p h l", h=8)
        dif, difd = difp.next(); Lm, Lmd = Lmp.next(); EC, ECd = ECp.next()
        c.op(c.dve, lambda: nc.vector.tensor_tensor(out=dif[:].rearrange("p (h l) -> p h l", h=8), in0=bc3, in1=cst[:, hs_].unsqueeze(2).to_broadcast([128, 8, 128]), op=ALU.subtract), reads=[bcd, cstd], writes=[difd])
        c.op(c.pool, lambda: nc.gpsimd.tensor_tensor(out=dif[:], in0=dif[:], in1=nm8, op=ALU.add), reads=[difd, kd], writes=[difd])
        yield 'A'
        c.op(c.act, lambda: nc.scalar.activation(out=Lm[:], in_=dif[:], func=AF.Exp), reads=[difd], writes=[Lmd])
        c.op(c.act, lambda: nc.scalar.activation(out=EC[:], in_=bc[:], func=AF.Exp), reads=[bcd], writes=[ECd])
        yield 'A'
        EC3 = EC[:].rearrange("p (h l) -> p h l", h=8)
        g3, g3d = pg.next()
        c.group(c.pe, [lambda: nc.tensor.matmul(g3[:, 0:128], lhsT=bs[:, g, tk], rhs=cs_[:, g, tk], start=True, stop=True)], reads=[bsd, csd], writes=[g3d])
        CBm, CBd = CBp.next()
        c.op(c.dve, lambda: nc.vector.tensor_tensor(out=CBm[:], in0=g3[:, 0:128], in1=mask01, op=ALU.mult), reads=[g3d, kd], writes=[CBd])
        M, Md = Mp.next()
        c.op(c.dve, lambda: nc.vector.tensor_tensor(out=M[:].rearrange("p (h l) -> p h l", h=8), in0=Lm[:].rearrange("p (h l) -> p h l", h=8), in1=CBm[:].unsqueeze(1).to_broadcast([128, 8, 128]), op=ALU.mult), reads=[Lmd, CBd], writes=[Md])
        yield 'A'
        Cd, Cdd = Cdp.next()
        if not first:
            c.op(c.pool, lambda: nc.gpsimd.tensor_tensor(out=Cd[:].rearrange("p (h l) -> p h l", h=8), in0=EC3, in1=cs_[:, g, tk].unsqueeze(1).to_broadcast([128, 8, 128]), op=ALU.mult), reads=[ECd, csd], writes=[Cdd])
        g4, g4d = pg.next(); g4B, g4Bd = pg.next()
        fns = [(lambda q=q: nc.tensor.matmul(g4[:, 128 * q:128 * q + 128], lhsT=xs[:, 4 * g + q, tk], rhs=identb[:], start=True, stop=True)) for q in range(4)]
        c.group(c.pe, fns, reads=[xsd[g], kd], writes=[g4d])
        c.group(c.pe, [lambda: nc.tensor.matmul(g4B[:, 0:128], lhsT=bs[:, g, tk], rhs=identb[:], start=True, stop=True)], reads=[bsd, kd], writes=[g4Bd])
        Xdt, Xdtd = Xtp.next(); Xd, Xdd = Xdp.next(); de, ded = dep.next(); Btm, Btd = Btp.next()
        c.op(c.dve, lambda: nc.vector.tensor_tensor(out=Xdt[:].rearrange("p (h q) -> p h q", h=8), in0=g4[:].rearrange("p (h q) -> p h q", h=8), in1=dts[:, j, hs_].unsqueeze(2).to_broadcast([128, 8, 64]), op=ALU.mult), reads=[g4d, dtsd], writes=[Xdtd])
        c.op(c.act, lambda: nc.scalar.activation(out=Btm[:], in_=g4B[:, 0:128], func=AF.Copy), reads=[g4Bd], writes=[Btd])
        yield 'A'
        c.op(c.dve, lambda: nc.vector.tensor_tensor(out=de[:], in0=bc3[:, :, 127], in1=cst[:, hs_], op=ALU.subtract), reads=[bcd, cstd], writes=[ded])
        c.op(c.act, lambda: nc.scalar.activation(out=de[:], in_=de[:], func=AF.Exp), reads=[ded], writes=[ded])
        c.op(c.dve, lambda: nc.vector.tensor_tensor(out=Xd[:].rearrange("p (h q) -> p h q", h=8), in0=Xdt[:].rearrange("p (h q) -> p h q", h=8), in1=de[:].unsqueeze(2).to_broadcast([128, 8, 64]), op=ALU.mult), reads=[Xdtd, ded], writes=[Xdd])
        yield 'S'
        g5, g5d = pgB.next()
        fns = []
        for hh in range(8):
            o = g5[64 * (hh % 2):64 * (hh % 2) + 64, 128 * (hh // 2):128 * (hh // 2) + 128]
            fns.append(lambda o=o, hh=hh: nc.tensor.matmul(o, lhsT=Xdt[:, 64 * hh:64 * hh + 64], rhs=M[:, 128 * hh:128 * hh + 128], start=True, stop=first))
            if not first:
                fns.append(lambda o=o, hh=hh: nc.tensor.matmul(o, lhsT=Sb[g][:, 64 * hh:64 * hh + 64], rhs=Cd[:, 128 * hh:128 * hh + 128], start=False, stop=True))
        c.group(c.pe, fns, reads=[Xdtd, Md, Sbd[g], Cdd], writes=[g5d])
        yield 'B'
        YG, YGd = YGp.next()
        g6, g6d = pgB.next()
        for q in range(4):
            ch = 4 * g + q
            yt, ytd = ytp.next()
            c.op(c.dve, lambda: nc.vector.scalar_tensor_tensor(out=yt[:], in0=xs[:, ch, tk], scalar=dns[:, ch, 0:1], in1=g5[:, 128 * q:128 * q + 128], op0=ALU.mult, op1=ALU.add), reads=[xsd[g], g5d, kd], writes=[ytd])
            c.op(c.dve, lambda: nc.vector.tensor_tensor(out=YG[:, q, :], in0=yt[:], in1=zs[:, ch, tk], op=ALU.mult), reads=[ytd, zsd[g]], writes=[YGd])
            sq, sqd = sqp.next()
            c.op(c.act, lambda: nc.scalar.activation(out=sq[:], in_=YG[:, q, :], func=AF.Square), reads=[YGd], writes=[sqd])
            c.group(c.pe, [lambda: nc.tensor.matmul(g6[:, 0:128], lhsT=ones_b[:], rhs=sq[:], start=(q == 0), stop=(q == 3), skip_group_check=True)], reads=[sqd, od], writes=[g6d])
            yield 'B'
        rs, rsd = rsp.next()
        c.op(c.act, lambda: nc.scalar.activation(out=rs[:], in_=g6[:, 0:128], func=AF.Ln, scale=1.0 / 512, bias=epsg[:]), reads=[g6d, od], writes=[rsd])
        c.op(c.act, lambda: nc.scalar.activation(out=rs[:], in_=rs[:], func=AF.Exp, scale=-0.5), reads=[rsd], writes=[rsd])
        for q in range(4):
            ch = 4 * g + q
            c.op(c.dve, lambda: nc.vector.scalar_tensor_tensor(out=yo[:, ch, tk], in0=YG[:, q, :], scalar=dns[:, ch, 1:2], in1=rs[:], op0=ALU.mult, op1=ALU.mult), reads=[YGd, rsd, kd], writes=[yod[g]])
        yield 'B'
        g7, g7d = pgB.next()
        c.group(c.pe, [lambda: nc.tensor.matmul(g7[:], lhsT=Btm[:], rhs=Xd[:], start=True, stop=True)], reads=[Btd, Xdd], writes=[g7d])
        if first:
            c.op(c.dve, lambda: nc.vector.tensor_copy(out=Sf[g][:], in_=g7[:]), reads=[g7d], writes=[Sd[g]])
        else:
            ts_, tsd = tmpS.next()
            c.op(c.dve, lambda: nc.vector.tensor_tensor(out=ts_[:].rearrange("p (h q) -> p h q", h=8), in0=Sf[g][:].rearrange("p (h q) -> p h q", h=8), in1=EC3[:, :, 127:128].to_broadcast([128, 8, 64]), op=ALU.mult), reads=[Sd[g], ECd], writes=[tsd])
            c.op(c.dve, lambda: nc.vector.tensor_tensor(out=Sf[g][:], in0=ts_[:], in1=g7[:], op=ALU.add), reads=[tsd, g7d], writes=[Sd[g]])
        c.op(c.act, lambda: nc.scalar.activation(out=Sb[g][:], in_=Sf[g][:], func=AF.Copy), reads=[Sd[g]], writes=[Sbd[g]])

    wv = Weave()
    for sc in range(min(S // SC, lim[0])):
        s0 = sc * SC
        xs, xsd = xsp.next(); bs, bsd = bsp.next(); cs_, csd = csp.next(); zs, zsd = zsp.next(); dts, dtsd = dtp.next(); yo, yod = yop.next()
        for q in range(4): c.dma(c.sp, xs[:, 4 * q:4 * q + 4, :], XCv[:, 4 * q:4 * q + 4, s0:s0 + SC], reads=xcd[4 * q:4 * q + 4], writes=[xsd[q]])
        c.dma(c.sp, bs[:], XCv[:, 16:20, s0:s0 + SC], reads=xcd[16:20], writes=[bsd])
        c.dma(c.sp, cs_[:], XCv[:, 20:24, s0:s0 + SC], reads=xcd[20:24], writes=[csd])
        for q in range(4): c.dma(c.sp, zs[:, 4 * q:4 * q + 4, :], ZTv[:, 4 * q:4 * q + 4, s0:s0 + SC], reads=zd[4 * q:4 * q + 4], writes=[zsd[q]])
        c.dma(c.sp, dts[:], DTv[:, 4 * sc:4 * sc + 4, :], reads=[dtd], writes=[dtsd])
        for j in range(lim[1]):
            first = (sc == 0 and j == 0)
            tk = slice(j * 128, (j + 1) * 128)
            dta, dtad = dtap.next(); cst, cstd = cstp.next()
            c.op(c.dve, lambda: nc.vector.tensor_tensor(out=dta[:], in0=dts[:, j, :], in1=Abc[:], op=ALU.mult), reads=[dtsd, kd], writes=[dtad])
            g1, g1d = pg.next()
            c.group(c.pe, [lambda: nc.tensor.matmul(g1[:, 0:32], lhsT=tri, rhs=dta[:], start=True, stop=True)], reads=[dtad, kd], writes=[g1d])
            c.op(c.act, lambda: nc.scalar.activation(out=cst[:], in_=g1[:, 0:32], func=AF.Copy), reads=[g1d], writes=[cstd])
            for g in range(lim[2]):
                wv.push(ssd_body(j, g, first, tk, dta, dtad, cst, cstd, xs, xsd, bs, bsd, cs_, csd, zs, zsd, dts, dtsd, yo, yod))
        wv.flush()
        for q in range(4): ywrite(yo[:, 4 * q:4 * q + 4, :], 4 * q, 4, s0, SC, [yod[q]])
        ydone(sc)


import math
SCALE = 128 ** -0.5
NCOLN = 1024 + 6 * 256 + 32

def emit_nsa(nc, c, htile, ywrite, ydone, stop=9, lim=(2, 32), ngrp=5):
    dt = c.dram
    pos_d = dt("pos", [1, S], I32); invf_d = dt("invf", [128, 1]); sgn_d = dt("sgn", [128, 1])
    pm_d = dt("pm", [128, 128], BF16); identb_d = dt("identb", [128, 128], BF16)
    pe_d = dt("peT", [128, 2, 32]); w1_d = dt("w1", [2, 4096, 128]); w2_d = dt("w2", [2, 128, 128])
    cmask_d = dt("cmask", [32, 128, 2, 128], BF16); c2s_d = dt("c2s", [128, 2, 64])
    tkk_d = dt("tkk", [32, 128, 2, 64]); ex_d = dt("ex", [64, 32 * 128], BF16)
    tri_d = dt("tri2", [128, 2, 128], BF16)
    gsel_d = dt("gsel", [24, 6, 512]); ones24_d = dt("ones24", [24, 128])
    QT = dt("QT", [1024, S], BF16, "Internal")
    FT = dt("FT", [1024, S], BF16, "Internal")
    VT = dt("VT", [S, 512], BF16, "Internal")
    GT = dt("GT", [32, S], F32, "Internal")
    qd = [D() for _ in range(8)]; fd = [D() for _ in range(8)]; vtd = D(); gtd = D()
    P = c.psb
    cosf = P("cosf", [128, S], F32); sinf = P("sinf", [128, S], F32)
    invf = P("invf_s", [128, 1], F32); sgn = P("sgn_s", [128, 1], F32); pm = P("pm_s", [128, 128], BF16); identb = P("identb_s", [128, 128], BF16)
    kd = D(); tabd = D()
    for (o, i) in ((invf, invf_d), (sgn, sgn_d), (pm, pm_d), (identb, identb_d)): c.dma(c.sp, o[:], i, writes=[kd])
    posi = c.sb("posi", [128, S], I32); u = c.sb("u", [128, S], F32); kf = c.sb("kf", [128, S], F32); ki = c.sb("ki", [128, S], I32)
    pd_ = D(); ud = D(); kfd = D()
    c.dma(c.sp, posi[:], pos_d[0, :].partition_broadcast(128), writes=[pd_])
    c.op(c.dve, lambda: nc.vector.tensor_copy(out=u[:], in_=posi[:]), reads=[pd_], writes=[ud])
    c.op(c.dve, lambda: nc.vector.tensor_scalar(out=u[:], in0=u[:], scalar1=invf[:, 0:1], scalar2=float(1.0 / (2 * math.pi)), op0=ALU.mult, op1=ALU.mult), reads=[ud, kd], writes=[ud])
    def wrap(dst, src, sd_):
        c.op(c.dve, lambda: nc.vector.tensor_copy(out=ki[:], in_=src), reads=[sd_], writes=[kfd])
        c.op(c.dve, lambda: nc.vector.tensor_copy(out=kf[:], in_=ki[:]), reads=[kfd], writes=[kfd])
        c.op(c.dve, lambda: nc.vector.tensor_tensor(out=dst, in0=src, in1=kf[:], op=ALU.subtract), reads=[sd_, kfd], writes=[tabd])
        c.op(c.dve, lambda: nc.vector.tensor_scalar(out=kf[:], in0=dst, scalar1=0.5, scalar2=None, op0=ALU.is_gt), reads=[tabd], writes=[kfd])
        c.op(c.dve, lambda: nc.vector.tensor_tensor(out=dst, in0=dst, in1=kf[:], op=ALU.subtract), reads=[tabd, kfd], writes=[tabd])
        c.op(c.dve, lambda: nc.vector.tensor_scalar(out=kf[:], in0=dst, scalar1=-0.5, scalar2=None, op0=ALU.is_lt), reads=[tabd], writes=[kfd])
        c.op(c.dve, lambda: nc.vector.tensor_tensor(out=dst, in0=dst, in1=kf[:], op=ALU.add), reads=[tabd, kfd], writes=[tabd])
    wrap(sinf[:], u[:], ud)
    c.op(c.dve, lambda: nc.vector.tensor_scalar(out=u[:], in0=u[:], scalar1=0.25, scalar2=None, op0=ALU.add), reads=[ud, tabd], writes=[ud])
    wrap(cosf[:], u[:], ud)
    c.op(c.act, lambda: nc.scalar.activation(out=sinf[:], in_=sinf[:], func=AF.Sin, scale=float(2 * math.pi)), reads=[tabd], writes=[tabd])
    c.op(c.act, lambda: nc.scalar.activation(out=cosf[:], in_=cosf[:], func=AF.Sin, scale=float(2 * math.pi)), reads=[tabd], writes=[tabd])
    c.op(c.dve, lambda: nc.vector.tensor_scalar(out=sinf[:], in0=sinf[:], scalar1=sgn[:, 0:1], scalar2=None, op0=ALU.mult), reads=[tabd, kd], writes=[tabd])
    if stop == 0:
        dt('hT', [DM, S]); dt('w_in', [DM, NCOLN]); dt('nrm', [128, 16])
        tabo = dt('tabo', [128, 2, S], F32, 'ExternalOutput'); tod = D()
        c.dma(c.sp, tabo[:, 0, :], cosf[:], reads=[tabd], acc=[tod]); c.dma(c.sp, tabo[:, 1, :], sinf[:], reads=[tabd], acc=[tod])
        c.finish([tod]); c.barrier(); return nc
    c.new_stage()
    a1 = A1(nc, c, NCOLN, htile)
    stg = a1.stg; pp = a1.pp
    xbp = Pool(c, "xbr", 2, [128, 512], BF16); t1p = Pool(c, "t1r", 2, [128, 512], F32); t2p = Pool(c, "t2r", 2, [128, 512], F32)
    import os
    RM = 3
    def rope_epi(dst, dd):
        def epi(col, t0, t1, ps, pd):
            n = t1 - t0
            if RM == 0:
                st, sd = stg.next(); o = st[:, 0:n // 2].bitcast(BF16)
                c.op(c.act, lambda: nc.scalar.activation(out=o, in_=ps, func=AF.Copy), reads=[pd], writes=[sd])
                c.dma(c.sp, dst[col:col + 128, t0:t1], o, reads=[sd], acc=[dd[col // 128]])
                return
            xb, xbd = xbp.next(); t1_, t1d = t1p.next(); t2_, t2d = t2p.next()
            c.op(c.act, lambda: nc.scalar.activation(out=xb[:, 0:n], in_=ps, func=AF.Copy), reads=[pd], writes=[xbd])
            p2, p2d = pp.next()
            c.group(c.pe, [lambda: nc.tensor.matmul(p2[:, 0:n], lhsT=pm[:], rhs=xb[:, 0:n], start=True, stop=True)], reads=[xbd, kd], writes=[p2d])
            ch = col // 128
            st, sd = stg.next(); o = st[:, 0:n // 2].bitcast(BF16)
            if RM == 1:
                c.op(c.act, lambda: nc.scalar.activation(out=o, in_=p2[:, 0:n], func=AF.Copy), reads=[pd, p2d], writes=[sd])
                c.dma(c.sp, dst[col:col + 128, t0:t1], o, reads=[sd], acc=[dd[ch]])
                return
            c.op(c.dve, lambda: nc.vector.tensor_tensor(out=t1_[:, 0:n], in0=ps, in1=cosf[:, t0:t1], op=ALU.mult), reads=[pd, tabd], writes=[t1d])
            if RM == 2:
                c.op(c.act, lambda: nc.scalar.activation(out=o, in_=t1_[:, 0:n], func=AF.Copy), reads=[t1d, p2d], writes=[sd])
                c.dma(c.sp, dst[col:col + 128, t0:t1], o, reads=[sd], acc=[dd[ch]])
                return
            c.op(c.dve, lambda: nc.vector.tensor_tensor(out=t2_[:, 0:n], in0=p2[:, 0:n], in1=sinf[:, t0:t1], op=ALU.mult), reads=[p2d, tabd], writes=[t2d])
            c.op(c.dve, lambda: nc.vector.tensor_tensor(out=o, in0=t1_[:, 0:n], in1=t2_[:, 0:n], op=ALU.add), reads=[t1d, t2d], writes=[sd])
            c.dma(c.sp, dst[col:col + 128, t0:t1], o, reads=[sd], acc=[dd[ch]])
        return epi
    def plain_epi(dst, dd, off):
        def epi(col, t0, t1, ps, pd):
            n = t1 - t0
            st, sd = stg.next(); o = st[:, 0:n // 2].bitcast(BF16)
            c.op(c.act, lambda: nc.scalar.activation(out=o, in_=ps, func=AF.Copy), reads=[pd], writes=[sd])
            c.dma(c.sp, dst[off + col:off + col + 128, t0:t1], o, reads=[sd], acc=[dd[(off + col) // 128]])
        return epi
    def epi_v(col, t0, nw, ps, pd):
        st, sd = stg.next(); o = st[:, 0:256].bitcast(BF16)
        c.op(c.act, lambda: nc.scalar.activation(out=o, in_=ps, func=AF.Copy), reads=[pd], writes=[sd])
        c.dma(c.sp, VT[t0:t0 + 128, :], o, reads=[sd], acc=[vtd])
    def epi_g(col, t0, t1, ps, pd):
        st, sd = stg.next()
        c.op(c.act, lambda: nc.scalar.activation(out=st[0:32, 0:t1 - t0], in_=ps, func=AF.Sigmoid), reads=[pd], writes=[sd])
        c.dma(c.sp, GT[:, t0:t1], st[0:32, 0:t1 - t0], reads=[sd], acc=[gtd])
    rope_f = rope_epi(FT, fd)
    a1.run([dict(n0=0, n1=1024, epi=rope_epi(QT, qd)),
            dict(n0=1024, n1=1536, epi=plain_epi(FT, fd, 0)),
            dict(n0=1536, n1=2048, epi=lambda col, t0, t1, ps, pd: rope_f(col + 512, t0, t1, ps, pd)),
            dict(n0=2048, n1=2560, tm=True, epi=epi_v),
            dict(n0=2560, n1=2592, epi=epi_g)][0:ngrp])
    if stop == 1:
        c.finish(qd + fd + [vtd, gtd]); return nc
    c.new_stage()
    NC_ = 255
    KC = [P(f"KC{g}", [128, 256], BF16) for g in range(2)]; VC = [P(f"VC{g}", [128, 2, 128], BF16) for g in range(2)]
    KCd = [D(), D()]; VCd = [D(), D()]
    w1s = c.sb("w1s", [128, 32, 128], BF16); w2s = c.sb("w2s", [128, 128], BF16); pes = c.sb("pes", [128, 2, 32], F32); peb = c.sb("peb", [128, 2, 32], BF16)
    srcp = Pool(c, "csrc", 2, [128, S], BF16); b1 = c.sb("b1", [128, 2], F32); gTt = c.sb("gTt", [128, 256], BF16); xb2 = c.sb("xb2", [128, 256], BF16)
    t1c = c.sb("t1c", [128, 256], F32); t2c = c.sb("t2c", [128, 256], F32)
    pp2 = Pool(c, "pp2", 4, [128, 512], F32, psum=True)
    w1d = D(); w2d = D(); ped = D(); b1d = D(); gTd = D(); xb2d = D(); tcd = D()
    c.dma(c.sp, pes[:], pe_d, writes=[ped])
    c.op(c.dve, lambda: nc.vector.tensor_copy(out=peb[:], in_=pes[:]), reads=[ped], writes=[ped])
    for g in range(2):
        c.op(c.dve, lambda: nc.vector.memset(KC[g][:], 0.0), writes=[KCd[g]])
        c.op(c.dve, lambda: nc.vector.memset(VC[g][:], 0.0), writes=[VCd[g]])
    FTv = FT.rearrange("(c p) t -> p c t", p=128)
    for kv in range(2):
        c.dma(c.pool, w1s[:], w1_d[kv].rearrange("(l d) f -> d l f", d=128), writes=[w1d])
        c.dma(c.pool, w2s[:], w2_d[kv], writes=[w2d])
        pb, pbd = pp2.next()
        c.group(c.pe, [(lambda l=l: nc.tensor.matmul(pb[:, 0:1], lhsT=w1s[:, l, :], rhs=peb[:, kv, l:l + 1], start=(l == 0), stop=(l == 31))) for l in range(32)], reads=[w1d, ped], writes=[pbd])
        c.op(c.act, lambda: nc.scalar.activation(out=b1[:, kv:kv + 1], in_=pb[:, 0:1], func=AF.Copy), reads=[pbd], writes=[b1d])
        for g in range(2):
            src, srcd = srcp.next()
            ch = 2 * kv + g
            c.dma(c.sp, src[:], FTv[:, ch, :], reads=[fd[ch]], writes=[srcd])
            ph, phd = pp2.next()
            c.group(c.pe, [(lambda l=l: nc.tensor.matmul(ph[:, 0:NC_], lhsT=w1s[:, l, :], rhs=src[:, l:l + 16 * (NC_ - 1) + 1:16], start=(l == 0), stop=(l == 31))) for l in range(32)], reads=[w1d, srcd], writes=[phd])
            c.op(c.act, lambda: nc.scalar.activation(out=gTt[:, 0:NC_], in_=ph[:, 0:NC_], func=AF.Gelu_apprx_tanh, bias=b1[:, kv:kv + 1]), reads=[phd, b1d], writes=[gTd])
            if kv == 0:
                pk, pkd = pp2.next()
                c.group(c.pe, [lambda: nc.tensor.matmul(pk[:, 0:NC_], lhsT=w2s[:], rhs=gTt[:, 0:NC_], start=True, stop=True)], reads=[w2d, gTd], writes=[pkd])
                c.op(c.act, lambda: nc.scalar.activation(out=xb2[:, 0:NC_], in_=pk[:, 0:NC_], func=AF.Copy), reads=[pkd], writes=[xb2d])
                p2, p2d = pp2.next()
                c.group(c.pe, [lambda: nc.tensor.matmul(p2[:, 0:NC_], lhsT=pm[:], rhs=xb2[:, 0:NC_], start=True, stop=True)], reads=[xb2d, kd], writes=[p2d])
                cs_ = cosf[:, 31:31 + 16 * (NC_ - 1) + 1:16]; sn_ = sinf[:, 31:31 + 16 * (NC_ - 1) + 1:16]
                c.op(c.dve, lambda: nc.vector.tensor_tensor(out=t1c[:, 0:NC_], in0=pk[:, 0:NC_], in1=cs_, op=ALU.mult), reads=[pkd, tabd], writes=[tcd])
                c.op(c.dve, lambda: nc.vector.tensor_tensor(out=t2c[:, 0:NC_], in0=p2[:, 0:NC_], in1=sn_, op=ALU.mult), reads=[p2d, tabd, tcd], writes=[tcd])
                c.op(c.dve, lambda: nc.vector.tensor_tensor(out=KC[g][:, 0:NC_], in0=t1c[:, 0:NC_], in1=t2c[:, 0:NC_], op=ALU.add), reads=[tcd], writes=[KCd[g]])
            else:
                for nch in range(2):
                    nn = 128 if nch == 0 else NC_ - 128
                    pv, pvd = pp2.next()
                    c.group(c.pe, [lambda: nc.tensor.matmul(pv[0:nn, 0:128], lhsT=gTt[:, 128 * nch:128 * nch + nn], rhs=w2s[:], start=True, stop=True)], reads=[w2d, gTd], writes=[pvd])
                    c.op(c.act, lambda: nc.scalar.activation(out=VC[g][0:nn, nch, :], in_=pv[0:nn, 0:128], func=AF.Copy), reads=[pvd], writes=[VCd[g]])
    if stop == 2:
        c.barrier(); return nc
    c.new_stage()
    KsT_ = [c.sb(f"KsT{g}", [128, S], BF16) for g in range(2)]; KwT_ = [c.sb(f"KwT{g}", [128, S], BF16) for g in range(2)]
    Vs_ = [c.sb(f"Vs{g}", [128, 32, 128], BF16) for g in range(2)]; Vw_ = [c.sb(f"Vw{g}", [128, 32, 128], BF16) for g in range(2)]
    kvd_ = [D(), D()]
    ex = c.sb("ex", [64, 32 * 128], BF16); tri2 = c.sb("tri2", [128, 2, 128], BF16); c2s = c.sb("c2s", [128, 2, 64], F32)
    gsel = c.sb("gsel", [24, 6, 512], F32); ones24 = c.sb("ones24", [24, 128], F32); ones_b = c.sb("ones_b", [128, 128], BF16)
    k3 = D()
    for (o, i) in ((ex, ex_d), (tri2, tri_d), (c2s, c2s_d), (gsel, gsel_d), (ones24, ones24_d)): c.dma(c.sp, o[:], i, writes=[k3])
    c.op(c.dve, lambda: nc.vector.memset(ones_b[:], 1.0), writes=[k3])
    qsp = Pool(c, "qs", 2, [128, 4, 128], BF16, nd=0); cmp_ = Pool(c, "cm", 2, [128, 2, 128], BF16); tkp = Pool(c, "tk", 2, [128, 2, 64], F32)
    gtp = Pool(c, "gt", 2, [24, 128], F32)
    Ecp = Pool(c, "Ec", 2, [128, 2, 512], F32); Ecbp = Pool(c, "Ecb", 2, [128, 2, 512], BF16)
    Ep = Pool(c, "E", 6, [128, 512], BF16); Emp = Pool(c, "Em", 3, [128, 512], BF16)
    rdp = Pool(c, "rd", 2, [128, 512], F32); tp_ = Pool(c, "tt", 2, [128, 512], F32); accp = Pool(c, "acc", 2, [128, 512], F32)
    impp = Pool(c, "imp", 2, [128, 64], F32); imp2p = Pool(c, "imp2", 2, [128, 64], F32); m8p = Pool(c, "m8", 2, [128, 8], F32)
    selp = Pool(c, "sel", 2, [128, 64], BF16); selTp = Pool(c, "selT", 2, [64, 128], BF16); Rgp = Pool(c, "Rg", 2, [24, 512], F32)
    mdp = Pool(c, "md", 2, [128, 128], BF16); yop = Pool(c, "yo", 2, [128, 512], BF16)
    ps = Pool(c, "ps3", 3, [128, 512], F32, psum=True); psacc = Pool(c, "psacc", 2, [128, 512], F32, psum=True); psA = Pool(c, "psA", 3, [128, 512], F32, psum=True)
    VTv = VT.rearrange("(c p) e -> p c e", p=128)
    QTv = QT.rearrange("(h p) t -> p h t", p=128)
    def finish_branch(o_ps, o_d, den_ps, den_d, gl, k, acc, accd, gt, gtd_, first, pspool=None):
        rd, rdd = rdp.next(); tt_, ttd = tp_.next(); Rg, Rgd = Rgp.next()
        c.op(c.dve, lambda: nc.vector.tensor_scalar(out=rd[:], in0=den_ps[:], scalar1=1e-18, scalar2=None, op0=ALU.max), reads=[den_d], writes=[rdd])
        c.op(c.act, lambda: nc.scalar.activation(out=rd[:], in_=rd[:], func=AF.Ln), reads=[rdd], writes=[rdd])
        c.op(c.act, lambda: nc.scalar.activation(out=rd[:], in_=rd[:], func=AF.Exp, scale=-1.0), reads=[rdd], writes=[rdd])
        c.op(c.dve, lambda: nc.vector.tensor_tensor(out=tt_[:], in0=o_ps[:], in1=rd[:], op=ALU.mult), reads=[o_d, rdd], writes=[ttd])
        c.op(c.pool, lambda: nc.gpsimd.tensor_tensor(out=Rg[:].rearrange("k (h t) -> k h t", h=4), in0=gsel[:, 3 * gl + k, :].rearrange("k (h t) -> k h t", h=4), in1=gt[:].unsqueeze(1).to_broadcast([24, 4, 128]), op=ALU.mult), reads=[gtd_, k3], writes=[Rgd])
        gb, gbd = (pspool or ps).next()
        c.group(c.pe, [lambda: nc.tensor.matmul(gb[:], lhsT=ones24[:], rhs=Rg[:], start=True, stop=True)], reads=[Rgd, k3], writes=[gbd])
        if first:
            c.op(c.dve, lambda: nc.vector.tensor_tensor(out=acc[:], in0=tt_[:], in1=gb[:], op=ALU.mult), reads=[ttd, gbd], writes=[accd])
        else:
            c.op(c.dve, lambda: nc.vector.tensor_tensor(out=tt_[:], in0=tt_[:], in1=gb[:], op=ALU.mult), reads=[ttd, gbd], writes=[ttd])
            c.op(c.pool, lambda: nc.gpsimd.tensor_tensor(out=acc[:], in0=acc[:], in1=tt_[:], op=ALU.add), reads=[ttd, accd], writes=[accd])
        return rd, rdd
    BRS = '012'
    for gl in range(2):
        kvd = kvd_[gl]
        c.dma(c.sp, KsT_[gl][:], FTv[:, 4 + gl, :], reads=[fd[4 + gl]], writes=[kvd])
        c.dma(c.sp, KwT_[gl][:], FTv[:, 6 + gl, :], reads=[fd[6 + gl]], writes=[kvd])
        c.dma(c.sp, Vs_[gl][:], VTv[:, :, 128 * gl:128 * gl + 128], reads=[vtd], writes=[kvd])
        c.dma(c.sp, Vw_[gl][:], VTv[:, :, 256 + 128 * gl:256 + 128 * gl + 128], reads=[vtd], writes=[kvd])
    def nsa_body(qb, gl):
        kvd = kvd_[gl]; KsT = KsT_[gl]; KwT = KwT_[gl]; Vs = Vs_[gl]; Vw = Vw_[gl]
        t0 = 128 * qb
        qs, qsd = qsp.next(); cm, cmd = cmp_.next(); tk, tkd = tkp.next(); gt, gtd_ = gtp.next()
        c.dma(c.sp, qs[:], QTv[:, 4 * gl:4 * gl + 4, t0:t0 + 128], reads=qd[4 * gl:4 * gl + 4], writes=[qsd])
        c.dma(c.sp, cm[:], cmask_d[qb], writes=[cmd])
        c.dma(c.sp, tk[:], tkk_d[qb], writes=[tkd])
        c.dma(c.sp, gt[:], GT[0:24, t0:t0 + 128], reads=[gtd], writes=[gtd_])
        q2 = qs[:].rearrange("p h t -> p (h t)")
        acc, accd = accp.next()
        yield 'A'
        Ec, Ecd = Ecp.next(); Ecb, Ecbd = Ecbp.next()
        for nch in range(2):
            sp_, spd = psA.next()
            c.group(c.pe, [lambda: nc.tensor.matmul(sp_[:], lhsT=KC[gl][:, 128 * nch:128 * nch + 128], rhs=q2, start=True, stop=True)], reads=[KCd[gl], qsd], writes=[spd])
            E, Ed = Ep.next()
            c.op(c.act, lambda: nc.scalar.activation(out=E[:], in_=sp_[:], func=AF.Exp, scale=SCALE), reads=[spd], writes=[Ed])
            c.op(c.dve, lambda: nc.vector.tensor_tensor(out=Ec[:, nch, :].rearrange("p (h t) -> p h t", h=4), in0=E[:].rearrange("p (h t) -> p h t", h=4), in1=cm[:, nch, :].unsqueeze(1).to_broadcast([128, 4, 128]), op=ALU.mult), reads=[Ed, cmd], writes=[Ecd])
            yield 'A'
        c.op(c.act, lambda: nc.scalar.activation(out=Ecb[:], in_=Ec[:], func=AF.Copy), reads=[Ecd], writes=[Ecbd])
        oc, ocd = psA.next(); dc, dcd = psA.next()
        c.group(c.pe, [(lambda n_=n_: nc.tensor.matmul(oc[:], lhsT=VC[gl][:, n_, :], rhs=Ecb[:, n_, :], start=(n_ == 0), stop=(n_ == 1))) for n_ in range(2)], reads=[VCd[gl], Ecbd], writes=[ocd])
        c.group(c.pe, [(lambda n_=n_: nc.tensor.matmul(dc[:], lhsT=ones_b[:], rhs=Ecb[:, n_, :], start=(n_ == 0), stop=(n_ == 1))) for n_ in range(2)], reads=[k3, Ecbd], writes=[dcd])
        yield 'A'
        inited = ['0' in BRS]
        if '0' not in BRS:
            rd, rdd = rdp.next()
            c.op(c.dve, lambda: nc.vector.tensor_scalar(out=rd[:], in0=dc[:], scalar1=1e-30, scalar2=None, op0=ALU.max), reads=[dcd], writes=[rdd])
            c.op(c.dve, lambda: nc.vector.reciprocal(out=rd[:], in_=rd[:]), reads=[rdd], writes=[rdd])
        else:
            rd, rdd = finish_branch(oc, ocd, dc, dcd, gl, 0, acc, accd, gt, gtd_, True, psA)
        c.op(c.dve, lambda: nc.vector.tensor_tensor(out=Ec[:], in0=Ec[:], in1=rd[:].unsqueeze(1).to_broadcast([128, 2, 512]), op=ALU.mult), reads=[Ecd, rdd], writes=[Ecd])
        ip, ipd = psA.next()
        fns = []
        for n_ in range(2):
            for h in range(4):
                fns.append(lambda n_=n_, h=h: nc.tensor.matmul(ip[:, 0:64], lhsT=Ec[:, n_, 128 * h:128 * h + 128], rhs=c2s[:, n_, :], start=(n_ == 0 and h == 0), stop=(n_ == 1 and h == 3)))
        c.group(c.pe, fns, reads=[Ecd, k3], writes=[ipd])
        imp, impd = impp.next(); imp2, imp2d = imp2p.next(); m8, m8d = m8p.next(); sel, seld = selp.next(); selT, selTd = selTp.next()
        c.op(c.dve, lambda: nc.vector.tensor_tensor(out=imp[:], in0=ip[:, 0:64], in1=tk[:, 0, :], op=ALU.mult), reads=[ipd, tkd], writes=[impd])
        c.op(c.dve, lambda: nc.vector.tensor_tensor(out=imp[:], in0=imp[:], in1=tk[:, 1, :], op=ALU.add), reads=[impd, tkd], writes=[impd])
        yield 'A'
        c.op(c.dve, lambda: nc.vector.max(out=m8[:], in_=imp[:]), reads=[impd], writes=[m8d])
        c.op(c.dve, lambda: nc.vector.match_replace(out=imp2[:], in_to_replace=m8[:], in_values=imp[:], imm_value=-2e30), reads=[impd, m8d], writes=[imp2d])
        c.op(c.dve, lambda: nc.vector.max(out=m8[:], in_=imp2[:]), reads=[imp2d], writes=[m8d])
        c.op(c.dve, lambda: nc.vector.tensor_scalar(out=sel[:], in0=imp[:], scalar1=m8[:, 7:8], scalar2=None, op0=ALU.is_ge), reads=[impd, m8d], writes=[seld])
        yield 'A'
        stp, stpd = psA.next()
        c.group(c.pe, [lambda: nc.tensor.matmul(stp[0:64, 0:128], lhsT=sel[:], rhs=identb[:], start=True, stop=True)], reads=[seld, kd], writes=[stpd])
        c.op(c.act, lambda: nc.scalar.activation(out=selT[:], in_=stp[0:64, 0:128], func=AF.Copy), reads=[stpd], writes=[selTd])
        yield 'S'
        for br in (1, 2):
            K_ = KsT if br == 1 else KwT; V_ = Vs if br == 1 else Vw
            kcs = list(range(0, qb + 1)) if br == 1 else list(range(max(0, qb - 4), qb + 1))
            ob, obd = psacc.next(); db, dbd = psacc.next()
            def stA(ki_, kc):
                sp_, spd = ps.next()
                c.group(c.pe, [lambda: nc.tensor.matmul(sp_[:], lhsT=K_[:, 128 * kc:128 * kc + 128], rhs=q2, start=True, stop=True)], reads=[kvd, qsd], writes=[spd])
                E, Ed = Ep.next()
                c.op(c.act, lambda: nc.scalar.activation(out=E[:], in_=sp_[:], func=AF.Exp, scale=SCALE), reads=[spd], writes=[Ed])
                st = dict(ki=ki_, kc=kc, E=E, Ed=Ed)
                if br == 1:
                    mp, mpd = ps.next()
                    c.group(c.pe, [lambda: nc.tensor.matmul(mp[:, 0:128], lhsT=ex[:, 128 * kc:128 * kc + 128], rhs=selT[:], start=True, stop=True)], reads=[selTd, k3], writes=[mpd])
                    st.update(mp=mp, mpd=mpd)
                return st
            def stB(st):
                ki_, kc, E, Ed = st['ki'], st['kc'], st['E'], st['Ed']
                if br == 1:
                    mp, mpd = st['mp'], st['mpd']
                    Em, Emd = Emp.next()
                    if kc == qb:
                        md, mdd = mdp.next()
                        c.op(c.dve, lambda: nc.vector.tensor_tensor(out=md[:], in0=mp[:, 0:128], in1=tri2[:, 0, :], op=ALU.mult), reads=[mpd, k3], writes=[mdd])
                        msk, mskd = md[:], mdd
                    else:
                        msk, mskd = mp[:, 0:128], mpd
                    c.op(c.dve, lambda: nc.vector.tensor_tensor(out=Em[:].rearrange("p (h t) -> p h t", h=4), in0=E[:].rearrange("p (h t) -> p h t", h=4), in1=msk.unsqueeze(1).to_broadcast([128, 4, 128]), op=ALU.mult), reads=[Ed, mskd], writes=[Emd])
                else:
                    if kc == qb or kc == qb - 4:
                        Em, Emd = Emp.next()
                        msk = tri2[:, 0 if kc == qb else 1, :]
                        c.op(c.dve, lambda: nc.vector.tensor_tensor(out=Em[:].rearrange("p (h t) -> p h t", h=4), in0=E[:].rearrange("p (h t) -> p h t", h=4), in1=msk.unsqueeze(1).to_broadcast([128, 4, 128]), op=ALU.mult), reads=[Ed, k3], writes=[Emd])
                    else:
                        Em, Emd = E, Ed
                st_ = (ki_ == 0); sp2 = (ki_ == len(kcs) - 1)
                c.group(c.pe, [lambda: nc.tensor.matmul(ob[:], lhsT=V_[:, kc, :], rhs=Em[:], start=st_, stop=sp2, skip_group_check=True)], reads=[kvd, Emd], writes=[obd])
                c.group(c.pe, [lambda: nc.tensor.matmul(db[:], lhsT=ones_b[:], rhs=Em[:], start=st_, stop=sp2, skip_group_check=True)], reads=[k3, Emd], writes=[dbd])
            pend = None
            for ki_, kc in enumerate(kcs):
                cur = stA(ki_, kc)
                if pend is not None: stB(pend)
                pend = cur
                yield 'B'
            stB(pend)
            if str(br) in BRS:
                finish_branch(ob, obd, db, dbd, gl, br, acc, accd, gt, gtd_, not inited[0]); inited[0] = True
        yo, yod = yop.next()
        c.op(c.act, lambda: nc.scalar.activation(out=yo[:], in_=acc[:], func=AF.Copy), reads=[accd], writes=[yod])
        ywrite(yo[:].rearrange("p (h t) -> p h t", h=4), 4 * gl, 4, t0, 128, [yod])
        if qb % 4 == 3 and gl == 1: ydone(qb // 4)
    wv = Weave()
    for qb in range(32):
        for gl in range(2):
            wv.push(nsa_body(qb, gl))
    wv.flush()
import ml_dtypes
_BF = ml_dtypes.bfloat16
_arr = lambda n: np.ascontiguousarray(np.asarray(n).reshape(-1, 128).T)
_sl = lambda a, n: np.arange(a, a + n)
_PROGS = {}
_CONST = {}
_KINDS = [0, 1, 2, 0]
_INNER = [4096, 2048, 2048, 4096]


def build_all():
    nc = bass.Bass("TRN2", target_bir_lowering=False)
    c = Ctx(nc)
    xT = c.dram("xT", [DM, S]); hT0 = c.dram("hT0", [DM, TPC])
    oT = c.dram("oT", [DM, TPC], F32, "ExternalOutput")
    xv = xT.rearrange("(c p) t -> p c t", p=128); h0v = hT0.rearrange("(c p) t -> p c t", p=128); oTv = oT.rearrange("(c p) t -> p c t", p=128)
    hall = None; had = None; hloc = None; hlocd = None
    fin = D()
    for i in range(4):
        kind = _KINDS[i]; inner = _INNER[i]; final = (i == 3)
        IC2 = inner // 256
        c.pfx = f"L{i}A_"
        ysrc = [c.dram(f"ysrc{k}", [inner // 2, 512], BF16, None) for k in range(8)]; ysd = [D() for _ in range(8)]
        yall = [c.dram(f"yall{k}", [inner, 512], BF16, None) for k in range(8)]; yad = [D() for _ in range(8)]
        ysv = [y.rearrange("(c p) t -> p c t", p=128) for y in ysrc]
        if i == 0:
            htile = lambda ti, q: (xv[:, 4 * q:4 * q + 4, ti * TT:(ti + 1) * TT], [])
        else:
            def htile(ti, q, hall=hall, had=had):
                sh = ti // 4; tl = ti % 4; fh = q // 2
                r0 = sh * 1024 + (q % 2) * 512
                return hall[tl][fh][r0:r0 + 512, :].rearrange("(c p) t -> p c t", p=128), [had[tl][fh]]
        def ywrite(ap, ch0, n, t0, T, reads, ysv=ysv, ysd=ysd):
            k = t0 // 512; tl0 = t0 % 512
            c.dma(c.sp, ysv[k][:, ch0:ch0 + n, tl0:tl0 + T], ap, reads=reads, acc=[ysd[k]])
        def ydone(k, ysrc=ysrc, ysd=ysd, yall=yall, yad=yad):
            c.allgather(ysrc[k], [ysd[k]], yall[k], yad[k])
        [emit_ssd, emit_lru, emit_nsa][kind](nc, c, htile, ywrite, ydone)
        c.new_phase()
        c.pfx = f"L{i}B_"
        if i == 0:
            hin = lambda ti, q: (h0v[:, 4 * q:4 * q + 4, ti * TT:(ti + 1) * TT], [])
        else:
            def hin(ti, q, hloc=hloc, hlocd=hlocd):
                fh = q // 2; r0 = (q % 2) * 512
                return hloc[ti][fh][r0:r0 + 512, :].rearrange("(c p) t -> p c t", p=128), [hlocd[ti][fh]]
        if final:
            hout = lambda ti, q: (oTv[:, 4 * q:4 * q + 4, ti * TT:(ti + 1) * TT], fin)
            hdone = lambda ti: None
        else:
            nloc = [[c.dram(f"hout{t}_{f}", [1024, 512], F32, None) for f in range(2)] for t in range(4)]
            nlocd = [[D() for f in range(2)] for t in range(4)]
            nall = [[c.dram(f"hall{t}_{f}", [2048, 512], F32, None) for f in range(2)] for t in range(4)]
            nalld = [[D() for f in range(2)] for t in range(4)]
            def hout(ti, q, nloc=nloc, nlocd=nlocd):
                fh = q // 2; r0 = (q % 2) * 512
                return nloc[ti][fh][r0:r0 + 512, :].rearrange("(c p) t -> p c t", p=128), nlocd[ti][fh]
            def hdone(ti, nloc=nloc, nlocd=nlocd, nall=nall, nalld=nalld):
                for f in range(2): c.allgather(nloc[ti][f], [nlocd[ti][f]], nall[ti][f], nalld[ti][f])
        emit_B(nc, c, inner, final, hin, yall, yad, hout, hdone)
        if not final:
            hall, had, hloc, hlocd = nall, nalld, nloc, nlocd
        c.new_phase()
    c.finish([fin])
    return nc


def _ssd_consts():
    if 'ssd' in _CONST: return _CONST['ssd']
    s_ = np.arange(128)
    tri = (s_[:, None] <= s_[None, :]).astype(np.float32)
    nm = np.where(s_[:, None] <= s_[None, :], 0.0, -30000.0).astype(np.float32)
    cf32 = np.ascontiguousarray(np.concatenate([tri, np.tile(nm, (1, 8)), np.eye(128, dtype=np.float32), tri], axis=1))
    delta = np.zeros((8, 8, 128), np.float32)
    for k in range(8): delta[k, k, :] = 1
    c8 = np.ascontiguousarray(np.concatenate([delta.reshape(8, 1024), np.ones((8, 128), np.float32)], axis=1))
    _CONST['ssd'] = dict(cf32=cf32, c8=c8, identb=np.eye(128).astype(_BF))
    return _CONST['ssd']


def _ssd_inputs(d, j, li, hf):
    W = d['ssd_in_proj'][j]
    cols = np.concatenate([_sl(hf * 2048, 2048), _sl(4096 + hf * 2048, 2048), _sl(8192 + hf * 512, 512), _sl(8192 + 1024 + hf * 512, 512), _sl(4096 + 6144 + 32 * hf, 32)])
    Wc = np.ascontiguousarray(W[:, cols])
    xbc = np.concatenate([_sl(hf * 2048, 2048), _sl(4096 + hf * 512, 512), _sl(4096 + 1024 + hf * 512, 512)])
    cvx = np.zeros((128, 24, 5), np.float32)
    for k in range(4): cvx[:, :, k] = _arr(d['ssd_conv_w'][j][k, xbc])
    cvx[:, :, 4] = _arr(d['ssd_conv_b'][j][xbc])
    hs = slice(32 * hf, 32 * hf + 32)
    hv = np.zeros((128, 2, 32), np.float32); hv[:, 0, :] = d['ssd_dt_bias'][j][hs][None]; hv[:, 1, :] = d['ssd_a_log'][j][hs][None]
    dn = np.zeros((128, 16, 2), np.float32)
    dn[:, :, 0] = _arr(np.repeat(d['ssd_d'][j][hs], 64)); dn[:, :, 1] = _arr(d['ssd_norm'][j][hf * 2048:(hf + 1) * 2048])
    ins = dict(w_in=Wc, nrm=_arr(d['norm_mix'][li]), cvx=cvx, hv=hv, dn=dn)
    ins.update(_ssd_consts())
    return ins


def _lru_inputs(d, li, hf):
    W = d['lru_in_proj'][0]
    cs = slice(hf * 1024, (hf + 1) * 1024)
    Wc = np.ascontiguousarray(np.concatenate([W[:, cs], W[:, 2048 + hf * 1024: 2048 + (hf + 1) * 1024]], axis=1))
    cv = np.zeros((128, 8, 9), np.float32)
    for k in range(4): cv[:, :, k] = _arr(d['lru_conv_w'][0][k, cs])
    cv[:, :, 4] = _arr(d['lru_conv_b'][0][cs]); cv[:, :, 5] = _arr(d['lru_ba'][0][cs]); cv[:, :, 6] = _arr(d['lru_bx'][0][cs]); cv[:, :, 7] = _arr(d['lru_a_param'][0][cs])
    return dict(w_in=Wc, nrm=_arr(d['norm_mix'][li]), wa=np.ascontiguousarray(d['lru_wa'][0][4 * hf:4 * hf + 4]), wx=np.ascontiguousarray(d['lru_wx'][0][4 * hf:4 * hf + 4]), cv=cv)


def _nsa_consts():
    if 'nsa' in _CONST: return _CONST['nsa']
    k = {}
    half = 16
    inv = (np.float32(500000.0) ** (-np.arange(half, dtype=np.float32) * np.float32(2.0) / np.float32(32))).astype(np.float32)
    invf = np.zeros((128, 1), np.float32); invf[0:16, 0] = inv; invf[16:32, 0] = inv
    k['invf'] = invf
    sgn = np.zeros((128, 1), np.float32); sgn[0:16] = -1; sgn[16:32] = 1
    k['sgn'] = sgn
    pm = np.zeros((128, 128), np.float32)
    for dd in range(16): pm[dd + 16, dd] = 1; pm[dd, dd + 16] = 1
    k['pm'] = pm.astype(_BF); k['identb'] = np.eye(128).astype(_BF)
    tt = np.arange(128)
    cmask = np.zeros((32, 128, 2, 128), np.float32)
    for qb in range(32):
        t = 128 * qb + tt
        for nch in range(2):
            nn = 128 * nch + np.arange(128)
            cmask[qb, :, nch, :] = ((16 * nn[:, None] + 31 <= t[None, :]) & (nn[:, None] < 255))
    k['cmask'] = cmask.astype(_BF)
    c_start = np.arange(255)[:, None] * 16; s_start = np.arange(64)[None, :] * 64
    ov = np.clip(np.minimum(c_start + 32, s_start + 64) - np.maximum(c_start, s_start), 0, None) / 16.0
    c2s = np.zeros((256, 64), np.float32); c2s[:255] = ov
    k['c2s'] = np.ascontiguousarray(c2s.reshape(2, 128, 64).transpose(1, 0, 2))
    tkk = np.zeros((32, 128, 2, 64), np.float32)
    jj = np.arange(64)
    for qb in range(32):
        cur = (128 * qb + tt) // 64
        forced = (jj[None, :] == 0) | (jj[None, :] == cur[:, None]) | (jj[None, :] == cur[:, None] - 1)
        valid = jj[None, :] <= cur[:, None]
        tkk[qb, :, 0, :] = (~forced) & valid
        tkk[qb, :, 1, :] = np.where(valid, np.where(forced, 1e9, 0.0), -1e30)
    k['tkk'] = tkk
    ex = np.zeros((64, 32, 128), np.float32)
    for kc in range(32):
        for p in range(128): ex[2 * kc + p // 64, kc, p] = 1
    k['ex'] = ex.reshape(64, 32 * 128).astype(_BF)
    p = np.arange(128)
    tri2 = np.zeros((128, 2, 128), np.float32); tri2[:, 0, :] = p[:, None] <= tt[None, :]; tri2[:, 1, :] = p[:, None] > tt[None, :]
    k['tri2'] = tri2.astype(_BF)
    gsel = np.zeros((24, 6, 4, 128), np.float32)
    for gl in range(2):
        for kk in range(3):
            for h in range(4): gsel[(4 * gl + h) * 3 + kk, 3 * gl + kk, h, :] = 1
    k['gsel'] = gsel.reshape(24, 6, 512); k['ones24'] = np.ones((24, 128), np.float32)
    _CONST['nsa'] = k
    return k


def _nsa_inputs(d, li, pos_b, gp):
    W = d['nsa_in_proj'][0]
    g0 = 2 * gp
    parts = [_sl(1024 * gp, 1024)]
    for kidx in (0, 1, 2, 4, 3, 5):
        parts.append(_sl(2048 + kidx * 512 + g0 * 128, 256))
    parts.append(_sl(2048 + 6 * 512 + 24 * gp, 24))
    Wc = np.ascontiguousarray(np.concatenate([W[:, np.concatenate(parts)], np.zeros((2048, 8), np.float32)], axis=1))
    ins = dict(w_in=Wc, nrm=_arr(d['norm_mix'][li]), pos=np.ascontiguousarray(np.asarray(pos_b)[None, :]).astype(np.int32),
               peT=np.ascontiguousarray(d['nsa_cmp_pe'][0].transpose(2, 0, 1)), w1=d['nsa_cmp_w1'][0], w2=d['nsa_cmp_w2'][0])
    ins.update(_nsa_consts())
    return ins


def kernel(**inputs):
    d = {k: np.asarray(v) for k, v in inputs.items()}
    x = d['x']
    NB = 4
    if 'all' not in _PROGS: _PROGS['all'] = build_all()
    nc = _PROGS['all']
    maps = []
    for b in range(NB):
        xTb = np.ascontiguousarray(x[b].T)
        for r in range(2):
            ts = slice(r * 2048, (r + 1) * 2048)
            m = dict(xT=xTb, hT0=np.ascontiguousarray(xTb[:, ts]))
            sel = np.zeros((128, 2), np.float32); sel[:, r] = 1.0
            for i in range(4):
                kind, j = i % 3, i // 3
                if kind == 0: a = _ssd_inputs(d, j, i, r)
                elif kind == 1: a = _lru_inputs(d, i, r)
                else: a = _nsa_inputs(d, i, d['positions'][b], r)
                for k_, v_ in a.items(): m[f"L{i}A_{k_}"] = v_
                nrm = np.ascontiguousarray(np.concatenate([_arr(d['norm_ffn'][i]), _arr(d['norm_ple'][i]), _arr(d['norm_final'])], axis=1))
                w_o = [d['ssd_out_proj'][j], d['lru_out_proj'][0], d['nsa_out_proj'][0]][kind]
                bi = dict(pT=np.ascontiguousarray(d['p'][i][b, ts].T), sel=sel, w_o=w_o, w_in=d['w_ffn_in'][i], w_out=d['w_ffn_out'][i],
                          w_g=d['w_ple_gate'][i], w_u=d['w_ple_up'][i], nrm=nrm)
                for k_, v_ in bi.items(): m[f"L{i}B_{k_}"] = v_
            maps.append(m)
    res = run_bass_kernel_spmd(nc, maps, core_ids=list(range(8)))
    out = np.empty((NB, S, DM), np.float32)
    for b in range(NB):
        for r in range(2):
            out[b, r * 2048:(r + 1) * 2048, :] = np.asarray(res.results[2 * b + r]['oT']).T
    return out
```

```python
import math, os


import numpy as np
import concourse.bass as bass
import concourse.mybir as mybir
from concourse.bass_utils import run_bass_kernel_spmd

F32 = mybir.dt.float32; BF16 = mybir.dt.bfloat16; I32 = mybir.dt.int32
AF = mybir.ActivationFunctionType; ALU = mybir.AluOpType
AX = mybir.AxisListType


class D:
    __slots__ = ("w", "r", "excl")
    def __init__(s, excl=False): s.w = []; s.r = []; s.excl = excl


class Eng:
    def __init__(s, ctx, name, e):
        s.e = e; s.name = name; s.sem = ctx.nc.alloc_semaphore("s_" + name); s.cnt = 0; s.seen = {}
        s.dsems = []; s.dcnt = []; s.di = 0


class Ctx:
    def __init__(s, nc, ndma=8):
        s.nc = nc
        s.pe = Eng(s, "pe", nc.tensor); s.act = Eng(s, "act", nc.scalar); s.dve = Eng(s, "dve", nc.vector)
        s.pool = Eng(s, "pool", nc.gpsimd); s.sp = Eng(s, "sp", nc.sync)
        for q in (s.sp, s.pool, s.act):
            n = ndma
            q.dsems = [nc.alloc_semaphore(f"d_{q.name}{i}") for i in range(n)]; q.dcnt = [0] * n
        s.nbank = 0
        import contextlib
        s._cl = contextlib
        s.es = contextlib.ExitStack(); s.pes = contextlib.ExitStack(); s.uid = 0; s.pfx = ""

    def psb(s, name, shape, dtype):
        s.uid += 1
        return s.pes.enter_context(s.nc.sbuf_tensor(f"{name}_{s.uid}", shape, dtype))

    def dram(s, name, shape, dtype=F32, kind="ExternalInput"):
        if kind is None: return s.nc.dram_tensor(s.pfx + name, shape, dtype).ap()
        return s.nc.dram_tensor(s.pfx + name, shape, dtype, kind=kind).ap()

    def new_phase(s):
        s.barrier(); s.es.close(); s.pes.close(); s.es = s._cl.ExitStack(); s.pes = s._cl.ExitStack()

    def allgather(s, src, src_deps, dst, dst_dep):
        q = s.pool
        s._deps(q, src_deps, [dst_dep])
        s.uid += 1
        sem = s.nc.alloc_semaphore(f"cc_{s.uid}")
        s.nc.gpsimd.collective_compute("AllGather", ALU.bypass, replica_groups=[[0, 1], [2, 3], [4, 5], [6, 7]], ins=[src.opt()], outs=[dst.opt()]).then_inc(sem)
        s._mark((sem, 1), src_deps, [dst_dep])

    def sb(s, name, shape, dtype):
        s.uid += 1
        return s.es.enter_context(s.nc.sbuf_tensor(f"{name}_{s.uid}", shape, dtype))

    def ps(s, name, shape, dtype=None):
        s.uid += 1
        return s.es.enter_context(s.nc.psum_tensor(f"{name}_{s.uid}", shape, dtype or F32))

    def barrier(s):
        engs = [s.pe, s.act, s.dve, s.pool, s.sp]
        for e in engs:
            for o in engs:
                if o is not e and o.cnt > 0: s._wait(e, o.sem, o.cnt)
            for q in (s.sp, s.pool, s.act):
                for sem, cnt in zip(q.dsems, q.dcnt):
                    if cnt > 0: s._wait(e, sem, cnt)

    def new_stage(s):
        s.barrier(); s.es.close(); s.es = s._cl.ExitStack()

    def _wait(s, eng, sem, val):
        key = id(sem)
        if eng.seen.get(key, 0) < val:
            eng.e.wait_ge(sem, val); eng.seen[key] = val

    def _deps(s, eng, reads, writes, acc=()):
        for t in reads:
            for w in t.w: s._wait(eng, *w)
            if t.excl:
                for (sem, val) in t.r:
                    if sem is not eng.sem: s._wait(eng, sem, val)
        for t in writes:
            for w in t.w: s._wait(eng, *w)
        for t in list(writes) + list(acc):
            for (sem, val) in t.r:
                if sem is eng.sem: continue
                s._wait(eng, sem, val)

    def _mark(s, tag, reads, writes, acc=()):
        for t in writes: t.w = [tag]; t.r = []
        for t in acc: t.w.append(tag)
        for t in reads: t.r.append(tag)

    def op(s, eng, fn, reads=(), writes=()):
        s._deps(eng, reads, writes)
        inst = fn()
        eng.cnt += 1
        inst.then_inc(eng.sem, 1)
        s._mark((eng.sem, eng.cnt), reads, writes)

    def group(s, eng, fns, reads=(), writes=()):
        s._deps(eng, reads, writes)
        inst = None
        for fn in fns: inst = fn()
        eng.cnt += 1
        inst.then_inc(eng.sem, 1)
        s._mark((eng.sem, eng.cnt), reads, writes)

    def dma(s, q, out, in_, reads=(), writes=(), acc=(), **kw):
        s._deps(q, reads, writes, acc)
        i = q.di; q.di = (q.di + 1) % len(q.dsems)
        sem = q.dsems[i]
        if q.dcnt[i] > 0: s._wait(q, sem, q.dcnt[i])
        q.dcnt[i] += 16
        q.e.dma_start(out=out, in_=in_, **kw).then_inc(sem, 16)
        s._mark((sem, q.dcnt[i]), reads, writes, acc)

    def finish(s, deps):
        for t in deps:
            for w in t.w: s._wait(s.sp, *w)


class Pool:
    def __init__(s, c, name, n, shape, dtype, psum=False, nd=0):
        alloc = c.ps if psum else c.sb
        s.t = [alloc(f"{name}{i}", shape, dtype) for i in range(n)]
        s.d = [(D(psum) if nd == 0 else [D(psum) for _ in range(nd)]) for _ in range(n)]; s.i = 0; s.n = n
    def next(s):
        i = s.i; s.i = (s.i + 1) % s.n
        return s.t[i], s.d[i]


class WStream:
    def __init__(s, c, nbuf, nelem, name="wbuf", live=1):
        s.c = c; s.nelem = nelem; s.ahead = nbuf - live
        s.t = [c.sb(f"{name}{i}", [128, nelem], BF16) for i in range(nbuf)]
        s.d = [D() for _ in range(nbuf)]
        s.plan = []; s.issued = 0; s.pos = 0
    def _issue(s):
        i = s.issued
        if i >= len(s.plan): return
        src, kc, nw = s.plan[i]
        b = i % len(s.t)
        dst = s.t[b][:, 0:kc * nw].rearrange("p (k n) -> p k n", k=kc)
        s.c.dma(s.c.pool, dst, src, writes=[s.d[b]])
        s.issued += 1
    def next(s):
        while s.issued <= min(s.pos + s.ahead, len(s.plan) - 1): s._issue()
        src, kc, nw = s.plan[s.pos]
        b = s.pos % len(s.t); s.pos += 1
        return s.t[b][:, 0:kc * nw].rearrange("p (k n) -> p k n", k=kc), s.d[b]


def wtiles(W, K, n0, n1, nw):
    kc = K // 128
    Wv = W.rearrange("(k p) n -> p k n", p=128)
    return [(Wv[:, :, a:min(a + nw, n1)], kc, min(a + nw, n1) - a) for a in range(n0, n1, nw)]


def gemm(c, ws, pp, spec, rhs, rhs_deps, T, epi, mchunk=128):
    nc = c.nc
    col = 0
    for (src, kc, nw) in spec:
        wt, wd = ws.next()
        for m0 in range(0, nw, mchunk):
            m1 = min(m0 + mchunk, nw)
            for t0 in range(0, T, 512):
                t1 = min(t0 + 512, T)
                pt, pd = pp.next()
                out = pt[0:m1 - m0, 0:t1 - t0]
                fns = [(lambda k=k: nc.tensor.matmul(out, lhsT=wt[:, k, m0:m1], rhs=rhs(k, t0, t1), start=(k == 0), stop=(k == kc - 1))) for k in range(kc)]
                deps = [wd]
                for k in range(kc): deps += rhs_deps(k, t0)
                c.group(c.pe, fns, reads=deps, writes=[pd])
                epi(col + m0, t0, t1, out, pd)
        col += nw


DM = 2048; FF = 5632; PLE = 256; TPC = 2048; TT = 512; EPS = 1e-6

class Weave:
    def __init__(s): s.prev = None
    def push(s, g):
        a_done = False; b_done = (s.prev is None)
        while not (a_done and b_done):
            if not a_done:
                if next(g) == 'S': a_done = True
            if not b_done:
                try: next(s.prev)
                except StopIteration: b_done = True
        s.prev = g
    def flush(s):
        if s.prev is not None:
            for _ in s.prev: pass
        s.prev = None


def rmsnorm_tile(c, nc, pp, h, hd, wcol, v, vd, ones, sqp, misc, T=TT, out_f32=None):
    pt, pd = pp.next()
    for ch in range(16):
        sq, sd = sqp.next()
        c.op(c.act, lambda: nc.scalar.activation(out=sq[:, 0:T], in_=h[:, ch, :], func=AF.Square), reads=[hd[ch]], writes=[sd])
        c.group(c.pe, [lambda: nc.tensor.matmul(pt[:, 0:T], lhsT=ones[:], rhs=sq[:, 0:T], start=(ch == 0), stop=(ch == 15), skip_group_check=True)], reads=[sd], writes=[pd])
    rs, rd = misc.next()
    c.op(c.act, lambda: nc.scalar.activation(out=rs[:, 0:T], in_=pt[:, 0:T], func=AF.Ln, scale=1.0 / DM, bias=c.eps[:]), reads=[pd], writes=[rd])
    c.op(c.act, lambda: nc.scalar.activation(out=rs[:, 0:T], in_=rs[:, 0:T], func=AF.Exp, scale=-0.5), reads=[rd], writes=[rd])
    for ch in range(16):
        o = v[:, ch, :] if out_f32 is None else out_f32[:, ch, :]
        c.op(c.dve, lambda: nc.vector.scalar_tensor_tensor(out=o, in0=h[:, ch, :], scalar=wcol[:, ch:ch + 1], in1=rs[:, 0:T], op0=ALU.mult, op1=ALU.mult),
             reads=[hd[ch], rd], writes=[vd[ch]])


def emit_B(nc, c, inner, final, hin, yall, yad, hout, hdone):
    TB = 1024; NS = 2
    IC = inner // 128
    dt = c.dram
    pT = dt("pT", [PLE, TPC]); sel_d = dt("sel", [128, 2])
    w_o = dt("w_o", [inner, DM]); w_in = dt("w_in", [DM, 2 * FF]); w_out = dt("w_out", [FF, DM])
    w_g = dt("w_g", [DM, DM]); w_u = dt("w_u", [PLE, DM])
    nrm = dt("nrm", [128, 48])
    ones = c.sb("ones", [128, 128], BF16); c.eps = c.sb("eps", [128, 1], F32)
    nw = c.sb("nw", [128, 48], F32)
    cd = D()
    c.op(c.dve, lambda: nc.vector.memset(ones[:], 1.0), writes=[cd])
    c.op(c.dve, lambda: nc.vector.memset(c.eps[:], EPS), writes=[cd])
    c.dma(c.sp, nw[:], nrm, writes=[cd])
    sel = c.sb("sel", [128, 2], F32)
    c.dma(c.sp, sel[:], sel_d, writes=[cd])
    h = c.sb("h", [128, 16, TB], F32); hd = [[D() for _ in range(16)] for _ in range(NS)]
    big = c.sb("big", [128, 32, TB], BF16); bd = [[D() for _ in range(32)] for _ in range(NS)]
    v = c.sb("v", [128, 16, TB], BF16); vd = [[D() for _ in range(16)] for _ in range(NS)]
    pb = c.sb("pb", [128, 2, TB], BF16); pbd = [D(), D()]
    pp = Pool(c, "ps", 8, [128, 512], F32, psum=True)
    sqp = Pool(c, "sq", 3, [128, 512], BF16)
    misc = Pool(c, "misc", 4, [128, 512], F32)
    sgp = Pool(c, "sg", 4, [128, 512], BF16)
    ws = WStream(c, 2, 6144)
    NT = TPC // TB
    g_o = wtiles(w_o, inner, 0, DM, 128 if IC > 16 else 256)
    halves = [(0, 3072), (3072, FF)]
    g_in = []; g_out = []
    for (a_, b_) in halves:
        gi = []
        for j in range(a_, b_, 256):
            gi += wtiles(w_in, DM, j, j + 256, 256) + wtiles(w_in, DM, FF + j, FF + j + 256, 256)
        g_in.append(gi)
        g_out.append(wtiles(w_out[a_:b_, :], b_ - a_, 0, DM, 256))
    g_g = wtiles(w_g, DM, 0, DM, 256)
    g_u = wtiles(w_u, PLE, 0, DM, 2048)
    for _ in range(NT): ws.plan += g_o + g_in[0] + g_out[0] + g_in[1] + g_out[1] + g_u + g_g
    pTv = pT.rearrange("(c p) t -> p c t", p=128)
    sb_ = lambda s_: slice(s_ * 512, (s_ + 1) * 512)
    for ti in range(NT):
        for s_ in range(NS):
            u = NS * ti + s_
            for q in range(4):
                hap, hdeps = hin(u, q)
                c.dma(c.sp, h[:, 4 * q:4 * q + 4, sb_(s_)], hap, reads=hdeps, writes=hd[s_][4 * q:4 * q + 4])
            y0v = yall[u].rearrange("(c p) t -> p c t", p=128); y1v = yall[4 + u].rearrange("(c p) t -> p c t", p=128)
            for q in range(0, IC, 8):
                vq = q % 16
                yt_ = v[:, vq:vq + 8, sb_(s_)]; ytd_ = vd[s_][vq:vq + 8]
                c.dma(c.sp, big[:, q:q + 8, sb_(s_)], y0v[:, q:q + 8, :], reads=[yad[u]], writes=bd[s_][q:q + 8])
                c.dma(c.sp, yt_, y1v[:, q:q + 8, :], reads=[yad[4 + u]], writes=ytd_)
                c.op(c.dve, lambda: nc.vector.tensor_scalar(out=yt_, in0=yt_, scalar1=sel[:, 1:2], scalar2=None, op0=ALU.mult), reads=ytd_ + [cd], writes=ytd_)
                c.op(c.dve, lambda: nc.vector.scalar_tensor_tensor(out=big[:, q:q + 8, sb_(s_)], in0=big[:, q:q + 8, sb_(s_)], scalar=sel[:, 0:1], in1=yt_, op0=ALU.mult, op1=ALU.add), reads=bd[s_][q:q + 8] + ytd_ + [cd], writes=bd[s_][q:q + 8])
        c.dma(c.pool, pb[:], pTv[:, :, ti * TB:(ti + 1) * TB], writes=pbd)
        def epi_add(col, t0, t1, ps, pd):
            ch = col // 128; s_ = t0 // 512
            c.op(c.dve, lambda: nc.vector.tensor_tensor(out=h[:, ch, t0:t1], in0=h[:, ch, t0:t1], in1=ps, op=ALU.add), reads=[pd, hd[s_][ch]], writes=[hd[s_][ch]])
        gemm(c, ws, pp, g_o, lambda k, t0, t1: big[:, k, t0:t1], lambda k, t0: [bd[t0 // 512][k]], TB, epi_add)
        for s_ in range(NS):
            rmsnorm_tile(c, nc, pp, h[:, :, sb_(s_)], hd[s_], nw[:, 0:16], v[:, :, sb_(s_)], vd[s_], ones, sqp, misc, T=512)
        for hp in range(2):
            st = {}
            def epi_ffn(col, t0, t1, ps, pd):
                j = col % 512; grp = col // 512; isup = j >= 256; ch = grp * 2 + (j % 256) // 128; s_ = t0 // 512
                if not isup:
                    sg, sd = sgp.next()
                    c.op(c.act, lambda: nc.scalar.activation(out=sg[:, 0:t1 - t0], in_=ps, func=AF.Silu), reads=[pd], writes=[sd])
                    st[(ch, t0)] = (sg, sd)
                else:
                    sg, sd = st.pop((ch, t0))
                    c.op(c.dve, lambda: nc.vector.tensor_tensor(out=big[:, ch, t0:t1], in0=sg[:, 0:t1 - t0], in1=ps, op=ALU.mult), reads=[pd, sd], writes=[bd[s_][ch]])
            gemm(c, ws, pp, g_in[hp], lambda k, t0, t1: v[:, k, t0:t1], lambda k, t0: [vd[t0 // 512][k]], TB, epi_ffn)
            gemm(c, ws, pp, g_out[hp], lambda k, t0, t1: big[:, k, t0:t1], lambda k, t0: [bd[t0 // 512][k]], TB, epi_add)
        for s_ in range(NS):
            rmsnorm_tile(c, nc, pp, h[:, :, sb_(s_)], hd[s_], nw[:, 16:32], v[:, :, sb_(s_)], vd[s_], ones, sqp, misc, T=512)
        def epi_up(col, t0, t1, ps, pd):
            ch = col // 128; s_ = t0 // 512
            c.op(c.act, lambda: nc.scalar.activation(out=big[:, ch, t0:t1], in_=ps, func=AF.Copy), reads=[pd], writes=[bd[s_][ch]])
        gemm(c, ws, pp, g_u, lambda k, t0, t1: pb[:, k, t0:t1], lambda k, t0: [pbd[k]], TB, epi_up)
        def epi_gate(col, t0, t1, ps, pd):
            ch = col // 128; s_ = t0 // 512
            sg, sd = misc.next()
            c.op(c.act, lambda: nc.scalar.activation(out=sg[:, 0:t1 - t0], in_=ps, func=AF.Sigmoid), reads=[pd], writes=[sd])
            c.op(c.dve, lambda: nc.vector.tensor_tensor(out=sg[:, 0:t1 - t0], in0=sg[:, 0:t1 - t0], in1=big[:, ch, t0:t1], op=ALU.mult), reads=[sd, bd[s_][ch]], writes=[sd])
            c.op(c.dve, lambda: nc.vector.tensor_tensor(out=h[:, ch, t0:t1], in0=h[:, ch, t0:t1], in1=sg[:, 0:t1 - t0], op=ALU.add), reads=[sd, hd[s_][ch]], writes=[hd[s_][ch]])
        gemm(c, ws, pp, g_g, lambda k, t0, t1: v[:, k, t0:t1], lambda k, t0: [vd[t0 // 512][k]], TB, epi_gate)
        for s_ in range(NS):
            u = NS * ti + s_
            if final:
                rmsnorm_tile(c, nc, pp, h[:, :, sb_(s_)], hd[s_], nw[:, 32:48], v[:, :, sb_(s_)], hd[s_], ones, sqp, misc, T=512, out_f32=h[:, :, sb_(s_)])
            for q in range(4):
                oap, odp = hout(u, q)
                c.dma(c.sp, oap, h[:, 4 * q:4 * q + 4, sb_(s_)], reads=hd[s_][4 * q:4 * q + 4], acc=[odp])
            hdone(u)


S = 4096; TT = 512


class A1:
    def __init__(s, nc, c, ncols, htile, wbuf_elems=16 * 512):
        s.nc = nc; s.c = c; s.htile = htile
        dt = c.dram
        s.dt = dt
        s.w = dt("w_in", [DM, ncols]); s.nrm = dt("nrm", [128, 16])
        s.ones = c.sb("ones", [128, 128], BF16); c.eps = c.sb("eps", [128, 1], F32)
        s.nw = c.sb("nw", [128, 16], F32)
        s.cd = D()
        c.op(c.dve, lambda: nc.vector.memset(s.ones[:], 1.0), writes=[s.cd])
        c.op(c.dve, lambda: nc.vector.memset(c.eps[:], EPS), writes=[s.cd])
        c.dma(c.sp, s.nw[:], s.nrm, writes=[s.cd])
        s.TA = 1024
        s.h = c.sb("h", [128, 16, s.TA], F32); s.hd = [D() for _ in range(16)]
        s.v = c.sb("v", [128, 16, s.TA], BF16); s.vd = [D() for _ in range(16)]
        s.pp = Pool(c, "ps", 8, [128, 512], F32, psum=True)
        s.sqp = Pool(c, "sq", 3, [128, TT], BF16)
        s.misc = Pool(c, "misc", 4, [128, TT], F32)
        s.stg = Pool(c, "stg", 4, [128, 512], F32)
        s.ws = WStream(c, 2, wbuf_elems)

    def run(s, groups):
        nc, c = s.nc, s.c
        plan = []
        for g in groups: plan += wtiles(s.w, DM, g["n0"], g["n1"], 512)
        TA = s.TA
        NT = S // TA
        for _ in range(NT): s.ws.plan += plan
        for ti in range(NT):
            tb = ti * TA
            for sb in range(TA // TT):
                for q in range(4):
                    hap, hdeps = s.htile((TA // TT) * ti + sb, q)
                    dst = s.h[:, 4 * q:4 * q + 4, sb * TT:(sb + 1) * TT]
                    if sb == 0: c.dma(c.sp, dst, hap, reads=hdeps, writes=s.hd[4 * q:4 * q + 4])
                    else: c.dma(c.sp, dst, hap, reads=hdeps, acc=s.hd[4 * q:4 * q + 4])
            for sb in range(TA // TT):
                rmsnorm_tile(c, nc, s.pp, s.h[:, :, sb * TT:(sb + 1) * TT], s.hd, s.nw[:, 0:16], s.v[:, :, sb * TT:(sb + 1) * TT], s.vd, s.ones, s.sqp, s.misc, T=TT)
            for g in groups:
                spec = wtiles(s.w, DM, g["n0"], g["n1"], 512)
                if not g.get("tm"):
                    gemm(c, s.ws, s.pp, spec, lambda k, t0, t1: s.v[:, k, t0:t1], lambda k, t0: [s.vd[k]], TA,
                         lambda col, t0, t1, ps, pd, g=g: g["epi"](col, tb + t0, tb + t1, ps, pd))
                else:
                    col = 0
                    for (src, kc, nw) in spec:
                        wt, wd = s.ws.next()
                        for t0 in range(0, TA, 128):
                            pt, pd = s.pp.next()
                            out = pt[:, 0:nw]
                            fns = [(lambda k=k: nc.tensor.matmul(out, lhsT=s.v[:, k, t0:t0 + 128], rhs=wt[:, k, :], start=(k == 0), stop=(k == kc - 1))) for k in range(kc)]
                            c.group(c.pe, fns, reads=[wd] + s.vd, writes=[pd])
                            g["epi"](col, tb + t0, nw, out, pd)
                        col += nw


def emit_lru(nc, c, htile, ywrite, ydone):
    cvs = c.psb("cvs", [128, 8, 9], F32)
    cA = c.psb("cA", [128, 8], F32); cA2 = c.psb("cA2", [128, 8], F32)
    a1 = A1(nc, c, 2048, htile)
    dt = a1.dt
    wa = dt("wa", [4, 256, 256]); wx = dt("wx", [4, 256, 256])
    cv = dt("cv", [128, 8, 9])
    GT = dt("GT", [1024, S], BF16, "Internal")
    XT = dt("XT", [1024, S], F32, "Internal")
    gd = [D() for _ in range(8)]; xd = [D() for _ in range(8)]
    cvd = D()
    c.dma(c.sp, cvs[:], cv, writes=[cvd])
    c.op(c.act, lambda: nc.scalar.activation(out=cA[:], in_=cvs[:, :, 7], func=AF.Softplus, scale=-1.0), reads=[cvd], writes=[cvd])
    c.op(c.dve, lambda: nc.vector.tensor_scalar(out=cA2[:], in0=cA[:], scalar1=-16.0, scalar2=None, op0=ALU.mult), reads=[cvd], writes=[cvd])
    c.op(c.dve, lambda: nc.vector.tensor_scalar(out=cA[:], in0=cA[:], scalar1=-8.0, scalar2=None, op0=ALU.mult), reads=[cvd], writes=[cvd])
    stg = a1.stg
    def epi_g(col, t0, t1, ps, pd):
        ch = col // 128
        st, sd = stg.next(); o = st[:].bitcast(BF16)[:, 0:t1 - t0] if False else st[:, 0:(t1 - t0) // 2].bitcast(BF16)
        c.op(c.act, lambda: nc.scalar.activation(out=o, in_=ps, func=AF.Gelu_apprx_tanh), reads=[pd], writes=[sd])
        c.dma(c.sp, GT[col:col + 128, t0:t1], o, reads=[sd], writes=[gd[ch]])
    def epi_x(col, t0, t1, ps, pd):
        ch = col // 128
        st, sd = stg.next()
        c.op(c.dve, lambda: nc.vector.tensor_copy(out=st[:, 0:t1 - t0], in_=ps), reads=[pd], writes=[sd])
        c.dma(c.sp, XT[col:col + 128, t0:t1], st[:, 0:t1 - t0], reads=[sd], writes=[xd[ch]])
    a1.run([dict(n0=0, n1=1024, epi=epi_g), dict(n0=1024, n1=2048, epi=lambda col, t0, t1, ps, pd: epi_x(col, t0, t1, ps, pd))])
    c.new_stage()
    pp = Pool(c, "pp_l2", 8, [128, 512], F32, psum=True)
    TL = 512
    ws2 = WStream(c, 4, 2 * 256, name="wb2_", live=2)
    for tl in range(S // TL):
        for kb in range(4):
            ws2.plan += [(wa[kb].rearrange("(k p) n -> p k n", p=128), 2, 256), (wx[kb].rearrange("(k p) n -> p k n", p=128), 2, 256)]
    xrp = Pool(c, "xr", 2, [128, 2, TL + 3], F32); xcp = Pool(c, "xc", 2, [128, 2, TL], F32)
    xbp = Pool(c, "xb", 2, [128, 2, TL], BF16); gp = Pool(c, "gg", 2, [128, 2, TL], BF16)
    ap_ = Pool(c, "aa", 2, [128, 2, TL], F32); bp = Pool(c, "bb", 2, [128, 2, TL], F32)
    hp = Pool(c, "hs", 2, [128, 2, TL], F32); yp = Pool(c, "yy", 2, [128, 2, TL], BF16)
    tp = Pool(c, "tmp", 4, [128, 512], F32)
    carry = c.sb("carry", [128, 8], F32); cyd = [D() for _ in range(8)]
    XTv = XT.rearrange("(c p) t -> p c t", p=128); GTv = GT.rearrange("(c p) t -> p c t", p=128)
    for tl in range(S // TL):
        for kb in range(4):
            t0 = tl * TL
            xr, xrd = xrp.next(); xc, xcd = xcp.next(); xb, xbd = xbp.next(); gg, ggd = gp.next()
            aa, aad = ap_.next(); bb, bbd = bp.next(); hs, hsd = hp.next(); yy, yyd = yp.next()
            if tl == 0:
                c.op(c.dve, lambda: nc.vector.memset(xr[:, :, 0:3], 0.0), writes=[xrd])
                c.dma(c.sp, xr[:, :, 3:], XTv[:, 2 * kb:2 * kb + 2, 0:TL], reads=xd[2 * kb:2 * kb + 2], writes=[xrd])
            else:
                c.dma(c.sp, xr[:], XTv[:, 2 * kb:2 * kb + 2, t0 - 3:t0 + TL], reads=xd[2 * kb:2 * kb + 2], writes=[xrd])
            c.dma(c.sp, gg[:], GTv[:, 2 * kb:2 * kb + 2, t0:t0 + TL], reads=gd[2 * kb:2 * kb + 2], writes=[ggd])
            for d in range(2):
                ch = 2 * kb + d
                c.op(c.dve, lambda: nc.vector.tensor_scalar(out=xc[:, d, :], in0=xr[:, d, 0:TL], scalar1=cvs[:, ch, 0:1], scalar2=cvs[:, ch, 4:5], op0=ALU.mult, op1=ALU.add), reads=[xrd, cvd], writes=[xcd])
                for k in range(1, 4):
                    c.op(c.dve, lambda: nc.vector.scalar_tensor_tensor(out=xc[:, d, :], in0=xr[:, d, k:k + TL], scalar=cvs[:, ch, k:k + 1], in1=xc[:, d, :], op0=ALU.mult, op1=ALU.add), reads=[xrd, xcd], writes=[xcd])
                c.op(c.act, lambda: nc.scalar.activation(out=xb[:, d, :], in_=xc[:, d, :], func=AF.Copy), reads=[xcd], writes=[xbd])
            wat, wad = ws2.next(); wxt, wxd = ws2.next()
            for d in range(2):
                ch = 2 * kb + d
                for b0 in range(0, TL, 512):
                    pa, pad = pp.next(); px, pxd = pp.next()
                    c.group(c.pe, [(lambda k=k: nc.tensor.matmul(pa[:], lhsT=wat[:, k, d * 128:(d + 1) * 128], rhs=xb[:, k, b0:b0 + 512], start=(k == 0), stop=(k == 1))) for k in range(2)], reads=[wad, xbd], writes=[pad])
                    c.group(c.pe, [(lambda k=k: nc.tensor.matmul(px[:], lhsT=wxt[:, k, d * 128:(d + 1) * 128], rhs=xb[:, k, b0:b0 + 512], start=(k == 0), stop=(k == 1))) for k in range(2)], reads=[wxd, xbd], writes=[pxd])
                    r, rd = tp.next(); a2, a2d = tp.next()
                    asl = aa[:, d, b0:b0 + 512]; bsl = bb[:, d, b0:b0 + 512]
                    c.op(c.act, lambda: nc.scalar.activation(out=r[:], in_=pa[:], func=AF.Sigmoid, bias=cvs[:, ch, 5:6]), reads=[pad, cvd], writes=[rd])
                    c.op(c.act, lambda: nc.scalar.activation(out=asl, in_=r[:], func=AF.Exp, scale=cA[:, ch:ch + 1]), reads=[rd, cvd], writes=[aad])
                    c.op(c.act, lambda: nc.scalar.activation(out=a2[:], in_=r[:], func=AF.Exp, scale=cA2[:, ch:ch + 1]), reads=[rd, cvd], writes=[a2d])
                    c.op(c.act, lambda: nc.scalar.activation(out=a2[:], in_=a2[:], func=AF.Sqrt, scale=-1.0, bias=1.0), reads=[a2d], writes=[a2d])
                    c.op(c.act, lambda: nc.scalar.activation(out=r[:], in_=px[:], func=AF.Sigmoid, bias=cvs[:, ch, 6:7]), reads=[pxd, cvd, aad], writes=[rd])
                    c.op(c.dve, lambda: nc.vector.tensor_tensor(out=bsl, in0=r[:], in1=xc[:, d, b0:b0 + 512], op=ALU.mult), reads=[rd, xcd], writes=[bbd])
                    c.op(c.dve, lambda: nc.vector.tensor_tensor(out=bsl, in0=bsl, in1=a2[:], op=ALU.mult), reads=[a2d, bbd], writes=[bbd])
                init = 0.0 if tl == 0 else carry[:, ch:ch + 1]
                c.op(c.dve, lambda: nc.vector.tensor_tensor_scan(out=hs[:, d, :], data0=aa[:, d, :], data1=bb[:, d, :], initial=init, op0=ALU.mult, op1=ALU.add), reads=[aad, bbd, cyd[ch]], writes=[hsd])
                c.op(c.dve, lambda: nc.vector.tensor_copy(out=carry[:, ch:ch + 1], in_=hs[:, d, TL - 1:TL]), reads=[hsd], writes=[cyd[ch]])
                c.op(c.dve, lambda: nc.vector.tensor_tensor(out=yy[:, d, :], in0=hs[:, d, :], in1=gg[:, d, :], op=ALU.mult), reads=[hsd, ggd], writes=[yyd])
            ywrite(yy[:], 2 * kb, 2, t0, TL, [yyd])
        ydone(tl)


class Cut(Exception): pass

def emit_ssd(nc, c, htile, ywrite, ydone, stop=9, lim=(8, 4, 4), cut=99):
    def CUT(k): pass
    NCOL = 2048 + 3072 + 32
    dt = c.dram
    cvx = dt("cvx", [128, 24, 5]); hv = dt("hv", [128, 2, 32]); dn = dt("dn", [128, 16, 2])
    cf32 = dt("cf32", [128, 128 + 1024 + 128 + 128]); c8 = dt("c8", [8, 1024 + 128]); identb_d = dt("identb", [128, 128], BF16)
    ZT = dt("ZT", [2048, S], BF16, "Internal")
    XT = dt("XT", [3072, S], F32, "Internal")
    XC = dt("XC", [3072, S], BF16, "Internal")
    DT = dt("DT", [S, 32], F32, "Internal")
    zd = [D() for _ in range(16)]; xd = [D() for _ in range(24)]; xcd = [D() for _ in range(24)]; dtd = D()
    P = c.psb
    cvs = P("cvs", [128, 24, 5], F32); hvs = P("hvs", [128, 2, 32], F32); dns = P("dns", [128, 16, 2], F32)
    cf = P("cf", [128, 1408], F32); c8s = P("c8s", [8, 1152], F32); identb = P("identb_s", [128, 128], BF16)
    Abc = P("Abc", [128, 32], F32)
    a1 = A1(nc, c, NCOL, htile)
    kd = D()
    for (o, i) in ((cvs, cvx), (hvs, hv), (dns, dn), (cf, cf32), (c8s, c8), (identb, identb_d)):
        c.dma(c.sp, o[:], i, writes=[kd])
    tri = cf[:, 0:128]; nm8 = cf[:, 128:1152]; identf = cf[:, 1152:1280]; mask01 = cf[:, 1280:1408]
    delta = c8s[:, 0:1024]; ones8 = c8s[:, 1024:1152]
    c.op(c.act, lambda: nc.scalar.activation(out=Abc[:], in_=hvs[:, 1, :], func=AF.Exp), reads=[kd], writes=[kd])
    c.op(c.dve, lambda: nc.vector.tensor_scalar(out=Abc[:], in0=Abc[:], scalar1=-1.0, scalar2=None, op0=ALU.mult), reads=[kd], writes=[kd])
    stg = a1.stg
    def epi_z(col, t0, t1, ps, pd):
        st, sd = stg.next(); o = st[:, 0:(t1 - t0) // 2].bitcast(BF16)
        c.op(c.act, lambda: nc.scalar.activation(out=o, in_=ps, func=AF.Silu), reads=[pd], writes=[sd])
        c.dma(c.sp, ZT[col:col + 128, t0:t1], o, reads=[sd], writes=[zd[col // 128]])
    xrp1 = Pool(c, "xr1", 2, [128, 515], F32); xcp1 = Pool(c, "xc1", 2, [128, 512], F32)
    halo = c.sb("halo", [128, 24, 3], F32); halod = [D() for _ in range(24)]
    def epi_x(col, t0, t1, ps, pd):
        ch = col // 128
        xr, xrd = xrp1.next(); xc, xcd_ = xcp1.next()
        if t0 == 0:
            c.op(c.dve, lambda: nc.vector.memset(xr[:, 0:3], 0.0), writes=[xrd])
        else:
            c.op(c.dve, lambda: nc.vector.tensor_copy(out=xr[:, 0:3], in_=halo[:, ch, :]), reads=[halod[ch]], writes=[xrd])
        c.op(c.act, lambda: nc.scalar.activation(out=xr[:, 3:515], in_=ps, func=AF.Copy), reads=[pd, xrd], writes=[xrd])
        c.op(c.dve, lambda: nc.vector.tensor_copy(out=halo[:, ch, :], in_=xr[:, 512:515]), reads=[xrd], writes=[halod[ch]])
        c.op(c.dve, lambda: nc.vector.tensor_scalar(out=xc[:], in0=xr[:, 0:512], scalar1=cvs[:, ch, 0:1], scalar2=cvs[:, ch, 4:5], op0=ALU.mult, op1=ALU.add), reads=[xrd, kd], writes=[xcd_])
        for k in range(1, 4):
            c.op(c.dve, lambda: nc.vector.scalar_tensor_tensor(out=xc[:], in0=xr[:, k:k + 512], scalar=cvs[:, ch, k:k + 1], in1=xc[:], op0=ALU.mult, op1=ALU.add), reads=[xrd, xcd_], writes=[xcd_])
        st, sd = stg.next(); o = st[:, 0:256].bitcast(BF16)
        c.op(c.act, lambda: nc.scalar.activation(out=o, in_=xc[:], func=AF.Silu), reads=[xcd_], writes=[sd])
        c.dma(c.sp, XC[col:col + 128, t0:t1], o, reads=[sd], acc=[xcd[ch]])
    def epi_dt(col, t0, nw, ps, pd):
        st, sd = stg.next()
        c.op(c.dve, lambda: nc.vector.tensor_tensor(out=st[:, 0:32], in0=ps, in1=hvs[:, 0, :], op=ALU.add), reads=[pd, kd], writes=[sd])
        c.op(c.act, lambda: nc.scalar.activation(out=st[:, 0:32], in_=st[:, 0:32], func=AF.Softplus), reads=[sd], writes=[sd])
        c.dma(c.sp, DT[t0:t0 + 128, :], st[:, 0:32], reads=[sd], writes=[dtd])
    a1.run([dict(n0=0, n1=2048, epi=epi_z), dict(n0=2048, n1=5120, epi=epi_x), dict(n0=5120, n1=5152, tm=True, epi=epi_dt)])
    if stop == 1:
        c.finish(zd + xd + [dtd]); return nc
    c.new_stage()
    XCv = XC.rearrange("(c p) t -> p c t", p=128); ZTv = ZT.rearrange("(c p) t -> p c t", p=128)
    if stop == 2:
        c.finish(xcd); return nc
    c.new_stage()
    SC = 512
    xsp = Pool(c, "xs", 2, [128, 16, SC], BF16, nd=4); bsp = Pool(c, "bs", 2, [128, 4, SC], BF16); csp = Pool(c, "cs", 2, [128, 4, SC], BF16)
    zsp = Pool(c, "zs", 2, [128, 16, SC], BF16, nd=4); dtp = Pool(c, "dts", 2, [128, 4, 32], F32); yop = Pool(c, "yo", 2, [128, 16, SC], BF16, nd=4)
    pbc = Pool(c, "pbc", 1, [128, 1024], F32, psum=True); pg = Pool(c, "pgA", 3, [128, 512], F32, psum=True); pgB = Pool(c, "pgB", 3, [128, 512], F32, psum=True)
    dtap = Pool(c, "dta", 2, [128, 32], F32); cstp = Pool(c, "cst", 2, [128, 32], F32)
    Rp = Pool(c, "R", 2, [8, 1024], F32); difp = Pool(c, "dif", 2, [128, 1024], F32); Lmp = Pool(c, "Lm", 2, [128, 1024], BF16)
    ECp = Pool(c, "EC", 2, [128, 1024], F32); CBp = Pool(c, "CBm", 2, [128, 128], BF16); Mp = Pool(c, "M", 2, [128, 1024], BF16)
    Cdp = Pool(c, "Cd", 2, [128, 1024], BF16); Xtp = Pool(c, "Xdt", 2, [128, 512], BF16); Xdp = Pool(c, "Xd", 2, [128, 512], BF16)
    dep = Pool(c, "de", 2, [128, 8], F32); Btp = Pool(c, "Btm", 2, [128, 128], BF16)
    YGp = Pool(c, "YG", 2, [128, 4, 128], F32); sqp = Pool(c, "sq2", 2, [128, 128], BF16); rsp = Pool(c, "rs", 2, [128, 128], F32)
    ytp = Pool(c, "yt", 3, [128, 128], F32); tmpS = Pool(c, "tmpS", 2, [128, 512], F32)
    Sf = [c.sb(f"Sf{g}", [128, 512], F32) for g in range(4)]; Sb = [c.sb(f"Sb{g}", [128, 512], BF16) for g in range(4)]
    Sd = [D() for _ in range(4)]; Sbd = [D() for _ in range(4)]
    ones_b = c.sb("ones_b", [128, 128], BF16); epsg = c.sb("epsg", [128, 1], F32); od = D()
    c.op(c.dve, lambda: nc.vector.memset(ones_b[:], 1.0), writes=[od])
    c.op(c.dve, lambda: nc.vector.memset(epsg[:], EPS), writes=[od])
    DTv = DT.rearrange("(j p) h -> p j h", p=128)
    def ssd_body(j, g, first, tk, dta, dtad, cst, cstd, xs, xsd, bs, bsd, cs_, csd, zs, zsd, dts, dtsd, yo, yod):
        hs_ = slice(8 * g, 8 * g + 8)
        g2, g2d = pg.next()
        c.group(c.pe, [lambda: nc.tensor.matmul(g2[0:8, 0:128], lhsT=dta[:, hs_], rhs=tri, start=True, stop=True)], reads=[dtad, kd], writes=[g2d])
        R, Rd = Rp.next()
        c.op(c.dve, lambda: nc.vector.tensor_tensor(out=R[:].rearrange("k (h l) -> k h l", h=8), in0=g2[0:8, 0:128].unsqueeze(1).to_broadcast([8, 8, 128]),
                                                      in1=delta.rearrange("k (h l) -> k h l", h=8), op=ALU.mult), reads=[g2d, kd], writes=[Rd])
        yield 'A'
        bc, bcd = pbc.next()
        c.group(c.pe, [(lambda hh=hh: nc.tensor.matmul(bc[:, 512 * hh:512 * hh + 512], lhsT=ones8, rhs=R[:, 512 * hh:512 * hh + 512], start=True, stop=True)) for hh in range(2)], reads=[Rd, kd], writes=[bcd])
        yield 'A'
        bc3 = bc[:].rearrange("p (h l) -> p h l", h=8)
        dif, difd = difp.next(); Lm, Lmd = Lmp.next(); EC, ECd = ECp.next()
        c.op(c.dve, lambda: nc.vector.tensor_tensor(out=dif[:].rearrange("p (h l) -> p h l", h=8), in0=bc3, in1=cst[:, hs_].unsqueeze(2).to_broadcast([128, 8, 128]), op=ALU.subtract), reads=[bcd, cstd], writes=[difd])
        c.op(c.pool, lambda: nc.gpsimd.tensor_tensor(out=dif[:], in0=dif[:], in1=nm8, op=ALU.add), reads=[difd, kd], writes=[difd])
        yield 'A'
        c.op(c.act, lambda: nc.scalar.activation(out=Lm[:], in_=dif[:], func=AF.Exp), reads=[difd], writes=[Lmd])
        c.op(c.act, lambda: nc.scalar.activation(out=EC[:], in_=bc[:], func=AF.Exp), reads=[bcd], writes=[ECd])
        yield 'A'
        EC3 = EC[:].rearrange("p (h l) -> p h l", h=8)
        g3, g3d = pg.next()
        c.group(c.pe, [lambda: nc.tensor.matmul(g3[:, 0:128], lhsT=bs[:, g, tk], rhs=cs_[:, g, tk], start=True, stop=True)], reads=[bsd, csd], writes=[g3d])
        CBm, CBd = CBp.next()
        c.op(c.dve, lambda: nc.vector.tensor_tensor(out=CBm[:], in0=g3[:, 0:128], in1=mask01, op=ALU.mult), reads=[g3d, kd], writes=[CBd])
        M, Md = Mp.next()
        c.op(c.dve, lambda: nc.vector.tensor_tensor(out=M[:].rearrange("p (h l) -> p h l", h=8), in0=Lm[:].rearrange("p (h l) -> p h l", h=8), in1=CBm[:].unsqueeze(1).to_broadcast([128, 8, 128]), op=ALU.mult), reads=[Lmd, CBd], writes=[Md])
        yield 'A'
        Cd, Cdd = Cdp.next()
        if not first:
            c.op(c.pool, lambda: nc.gpsimd.tensor_tensor(out=Cd[:].rearrange("p (h l) -> p h l", h=8), in0=EC3, in1=cs_[:, g, tk].unsqueeze(1).to_broadcast([128, 8, 128]), op=ALU.mult), reads=[ECd, csd], writes=[Cdd])
        g4, g4d = pg.next(); g4B, g4Bd = pg.next()
        fns = [(lambda q=q: nc.tensor.matmul(g4[:, 128 * q:128 * q + 128], lhsT=xs[:, 4 * g + q, tk], rhs=identb[:], start=True, stop=True)) for q in range(4)]
        c.group(c.pe, fns, reads=[xsd[g], kd], writes=[g4d])
        c.group(c.pe, [lambda: nc.tensor.matmul(g4B[:, 0:128], lhsT=bs[:, g, tk], rhs=identb[:], start=True, stop=True)], reads=[bsd, kd], writes=[g4Bd])
        Xdt, Xdtd = Xtp.next(); Xd, Xdd = Xdp.next(); de, ded = dep.next(); Btm, Btd = Btp.next()
        c.op(c.dve, lambda: nc.vector.tensor_tensor(out=Xdt[:].rearrange("p (h q) -> p h q", h=8), in0=g4[:].rearrange("p (h q) -> p h q", h=8), in1=dts[:, j, hs_].unsqueeze(2).to_broadcast([128, 8, 64]), op=ALU.mult), reads=[g4d, dtsd], writes=[Xdtd])
        c.op(c.act, lambda: nc.scalar.activation(out=Btm[:], in_=g4B[:, 0:128], func=AF.Copy), reads=[g4Bd], writes=[Btd])
        yield 'A'
        c.op(c.dve, lambda: nc.vector.tensor_tensor(out=de[:], in0=bc3[:, :, 127], in1=cst[:, hs_], op=ALU.subtract), reads=[bcd, cstd], writes=[ded])
        c.op(c.act, lambda: nc.scalar.activation(out=de[:], in_=de[:], func=AF.Exp), reads=[ded], writes=[ded])
        c.op(c.dve, lambda: nc.vector.tensor_tensor(out=Xd[:].rearrange("p (h q) -> p h q", h=8), in0=Xdt[:].rearrange("p (h q) -> p h q", h=8), in1=de[:].unsqueeze(2).to_broadcast([128, 8, 64]), op=ALU.mult), reads=[Xdtd, ded], writes=[Xdd])
        yield 'S'
        g5, g5d = pgB.next()
        fns = []
        for hh in range(8):
            o = g5[64 * (hh % 2):64 * (hh % 2) + 64, 128 * (hh // 2):128 * (hh // 2) + 128]
            fns.append(lambda o=o, hh=hh: nc.tensor.matmul(o, lhsT=Xdt[:, 64 * hh:64 * hh + 64], rhs=M[:, 128 * hh:128 * hh + 128], start=True, stop=first))
            if not first:
                fns.append(lambda o=o, hh=hh: nc.tensor.matmul(o, lhsT=Sb[g][:, 64 * hh:64 * hh + 64], rhs=Cd[:, 128 * hh:128 * hh + 128], start=False, stop=True))
        c.group(c.pe, fns, reads=[Xdtd, Md, Sbd[g], Cdd], writes=[g5d])
        yield 'B'
        YG, YGd = YGp.next()
        g6, g6d = pgB.next()
        for q in range(4):
            ch = 4 * g + q
            yt, ytd = ytp.next()
            c.op(c.dve, lambda: nc.vector.scalar_tensor_tensor(out=yt[:], in0=xs[:, ch, tk], scalar=dns[:, ch, 0:1], in1=g5[:, 128 * q:128 * q + 128], op0=ALU.mult, op1=ALU.add), reads=[xsd[g], g5d, kd], writes=[ytd])
            c.op(c.dve, lambda: nc.vector.tensor_tensor(out=YG[:, q, :], in0=yt[:], in1=zs[:, ch, tk], op=ALU.mult), reads=[ytd, zsd[g]], writes=[YGd])
            sq, sqd = sqp.next()
            c.op(c.act, lambda: nc.scalar.activation(out=sq[:], in_=YG[:, q, :], func=AF.Square), reads=[YGd], writes=[sqd])
            c.group(c.pe, [lambda: nc.tensor.matmul(g6[:, 0:128], lhsT=ones_b[:], rhs=sq[:], start=(q == 0), stop=(q == 3), skip_group_check=True)], reads=[sqd, od], writes=[g6d])
            yield 'B'
        rs, rsd = rsp.next()
        c.op(c.act, lambda: nc.scalar.activation(out=rs[:], in_=g6[:, 0:128], func=AF.Ln, scale=1.0 / 512, bias=epsg[:]), reads=[g6d, od], writes=[rsd])
        c.op(c.act, lambda: nc.scalar.activation(out=rs[:], in_=rs[:], func=AF.Exp, scale=-0.5), reads=[rsd], writes=[rsd])
        for q in range(4):
            ch = 4 * g + q
            c.op(c.dve, lambda: nc.vector.scalar_tensor_tensor(out=yo[:, ch, tk], in0=YG[:, q, :], scalar=dns[:, ch, 1:2], in1=rs[:], op0=ALU.mult, op1=ALU.mult), reads=[YGd, rsd, kd], writes=[yod[g]])
        yield 'B'
        g7, g7d = pgB.next()
        c.group(c.pe, [lambda: nc.tensor.matmul(g7[:], lhsT=Btm[:], rhs=Xd[:], start=True, stop=True)], reads=[Btd, Xdd], writes=[g7d])
        if first:
            c.op(c.dve, lambda: nc.vector.tensor_copy(out=Sf[g][:], in_=g7[:]), reads=[g7d], writes=[Sd[g]])
        else:
            ts_, tsd = tmpS.next()
            c.op(c.dve, lambda: nc.vector.tensor_tensor(out=ts_[:].rearrange("p (h q) -> p h q", h=8), in0=Sf[g][:].rearrange("p (h q) -> p h q", h=8), in1=EC3[:, :, 127:128].to_broadcast([128, 8, 64]), op=ALU.mult), reads=[Sd[g], ECd], writes=[tsd])
            c.op(c.dve, lambda: nc.vector.tensor_tensor(out=Sf[g][:], in0=ts_[:], in1=g7[:], op=ALU.add), reads=[tsd, g7d], writes=[Sd[g]])
        c.op(c.act, lambda: nc.scalar.activation(out=Sb[g][:], in_=Sf[g][:], func=AF.Copy), reads=[Sd[g]], writes=[Sbd[g]])

    wv = Weave()
    for sc in range(min(S // SC, lim[0])):
        s0 = sc * SC
        xs, xsd = xsp.next(); bs, bsd = bsp.next(); cs_, csd = csp.next(); zs, zsd = zsp.next(); dts, dtsd = dtp.next(); yo, yod = yop.next()
        for q in range(4): c.dma(c.sp, xs[:, 4 * q:4 * q + 4, :], XCv[:, 4 * q:4 * q + 4, s0:s0 + SC], reads=xcd[4 * q:4 * q + 4], writes=[xsd[q]])
        c.dma(c.sp, bs[:], XCv[:, 16:20, s0:s0 + SC], reads=xcd[16:20], writes=[bsd])
        c.dma(c.sp, cs_[:], XCv[:, 20:24, s0:s0 + SC], reads=xcd[20:24], writes=[csd])
        for q in range(4): c.dma(c.sp, zs[:, 4 * q:4 * q + 4, :], ZTv[:, 4 * q:4 * q + 4, s0:s0 + SC], reads=zd[4 * q:4 * q + 4], writes=[zsd[q]])
        c.dma(c.sp, dts[:], DTv[:, 4 * sc:4 * sc + 4, :], reads=[dtd], writes=[dtsd])
        for j in range(lim[1]):
            first = (sc == 0 and j == 0)
            tk = slice(j * 128, (j + 1) * 128)
            dta, dtad = dtap.next(); cst, cstd = cstp.next()
            c.op(c.dve, lambda: nc.vector.tensor_tensor(out=dta[:], in0=dts[:, j, :], in1=Abc[:], op=ALU.mult), reads=[dtsd, kd], writes=[dtad])
            g1, g1d = pg.next()
            c.group(c.pe, [lambda: nc.tensor.matmul(g1[:, 0:32], lhsT=tri, rhs=dta[:], start=True, stop=True)], reads=[dtad, kd], writes=[g1d])
            c.op(c.act, lambda: nc.scalar.activation(out=cst[:], in_=g1[:, 0:32], func=AF.Copy), reads=[g1d], writes=[cstd])
            for g in range(lim[2]):
                wv.push(ssd_body(j, g, first, tk, dta, dtad, cst, cstd, xs, xsd, bs, bsd, cs_, csd, zs, zsd, dts, dtsd, yo, yod))
        wv.flush()
        for q in range(4): ywrite(yo[:, 4 * q:4 * q + 4, :], 4 * q, 4, s0, SC, [yod[q]])
        ydone(sc)


import math
SCALE = 128 ** -0.5
NCOLN = 1024 + 6 * 256 + 32

def emit_nsa(nc, c, htile, ywrite, ydone, stop=9, lim=(2, 32), ngrp=5):
    dt = c.dram
    pos_d = dt("pos", [1, S], I32); invf_d = dt("invf", [128, 1]); sgn_d = dt("sgn", [128, 1])
    pm_d = dt("pm", [128, 128], BF16); identb_d = dt("identb", [128, 128], BF16)
    pe_d = dt("peT", [128, 2, 32]); w1_d = dt("w1", [2, 4096, 128]); w2_d = dt("w2", [2, 128, 128])
    cmask_d = dt("cmask", [32, 128, 2, 128], BF16); c2s_d = dt("c2s", [128, 2, 64])
    tkk_d = dt("tkk", [32, 128, 2, 64]); ex_d = dt("ex", [64, 32 * 128], BF16)
    tri_d = dt("tri2", [128, 2, 128], BF16)
    gsel_d = dt("gsel", [24, 6, 512]); ones24_d = dt("ones24", [24, 128])
    QT = dt("QT", [1024, S], BF16, "Internal")
    FT = dt("FT", [1024, S], BF16, "Internal")
    VT = dt("VT", [S, 512], BF16, "Internal")
    GT = dt("GT", [32, S], F32, "Internal")
    qd = [D() for _ in range(8)]; fd = [D() for _ in range(8)]; vtd = D(); gtd = D()
    P = c.psb
    cosf = P("cosf", [128, S], F32); sinf = P("sinf", [128, S], F32)
    invf = P("invf_s", [128, 1], F32); sgn = P("sgn_s", [128, 1], F32); pm = P("pm_s", [128, 128], BF16); identb = P("identb_s", [128, 128], BF16)
    kd = D(); tabd = D()
    for (o, i) in ((invf, invf_d), (sgn, sgn_d), (pm, pm_d), (identb, identb_d)): c.dma(c.sp, o[:], i, writes=[kd])
    posi = c.sb("posi", [128, S], I32); u = c.sb("u", [128, S], F32); kf = c.sb("kf", [128, S], F32); ki = c.sb("ki", [128, S], I32)
    pd_ = D(); ud = D(); kfd = D()
    c.dma(c.sp, posi[:], pos_d[0, :].partition_broadcast(128), writes=[pd_])
    c.op(c.dve, lambda: nc.vector.tensor_copy(out=u[:], in_=posi[:]), reads=[pd_], writes=[ud])
    c.op(c.dve, lambda: nc.vector.tensor_scalar(out=u[:], in0=u[:], scalar1=invf[:, 0:1], scalar2=float(1.0 / (2 * math.pi)), op0=ALU.mult, op1=ALU.mult), reads=[ud, kd], writes=[ud])
    def wrap(dst, src, sd_):
        c.op(c.dve, lambda: nc.vector.tensor_copy(out=ki[:], in_=src), reads=[sd_], writes=[kfd])
        c.op(c.dve, lambda: nc.vector.tensor_copy(out=kf[:], in_=ki[:]), reads=[kfd], writes=[kfd])
        c.op(c.dve, lambda: nc.vector.tensor_tensor(out=dst, in0=src, in1=kf[:], op=ALU.subtract), reads=[sd_, kfd], writes=[tabd])
        c.op(c.dve, lambda: nc.vector.tensor_scalar(out=kf[:], in0=dst, scalar1=0.5, scalar2=None, op0=ALU.is_gt), reads=[tabd], writes=[kfd])
        c.op(c.dve, lambda: nc.vector.tensor_tensor(out=dst, in0=dst, in1=kf[:], op=ALU.subtract), reads=[tabd, kfd], writes=[tabd])
        c.op(c.dve, lambda: nc.vector.tensor_scalar(out=kf[:], in0=dst, scalar1=-0.5, scalar2=None, op0=ALU.is_lt), reads=[tabd], writes=[kfd])
        c.op(c.dve, lambda: nc.vector.tensor_tensor(out=dst, in0=dst, in1=kf[:], op=ALU.add), reads=[tabd, kfd], writes=[tabd])
    wrap(sinf[:], u[:], ud)
    c.op(c.dve, lambda: nc.vector.tensor_scalar(out=u[:], in0=u[:], scalar1=0.25, scalar2=None, op0=ALU.add), reads=[ud, tabd], writes=[ud])
    wrap(cosf[:], u[:], ud)
    c.op(c.act, lambda: nc.scalar.activation(out=sinf[:], in_=sinf[:], func=AF.Sin, scale=float(2 * math.pi)), reads=[tabd], writes=[tabd])
    c.op(c.act, lambda: nc.scalar.activation(out=cosf[:], in_=cosf[:], func=AF.Sin, scale=float(2 * math.pi)), reads=[tabd], writes=[tabd])
    c.op(c.dve, lambda: nc.vector.tensor_scalar(out=sinf[:], in0=sinf[:], scalar1=sgn[:, 0:1], scalar2=None, op0=ALU.mult), reads=[tabd, kd], writes=[tabd])
    if stop == 0:
        dt('hT', [DM, S]); dt('w_in', [DM, NCOLN]); dt('nrm', [128, 16])
        tabo = dt('tabo', [128, 2, S], F32, 'ExternalOutput'); tod = D()
        c.dma(c.sp, tabo[:, 0, :], cosf[:], reads=[tabd], acc=[tod]); c.dma(c.sp, tabo[:, 1, :], sinf[:], reads=[tabd], acc=[tod])
        c.finish([tod]); c.barrier(); return nc
    c.new_stage()
    a1 = A1(nc, c, NCOLN, htile)
    stg = a1.stg; pp = a1.pp
    xbp = Pool(c, "xbr", 2, [128, 512], BF16); t1p = Pool(c, "t1r", 2, [128, 512], F32); t2p = Pool(c, "t2r", 2, [128, 512], F32)
    import os
    RM = 3
    def rope_epi(dst, dd):
        def epi(col, t0, t1, ps, pd):
            n = t1 - t0
            if RM == 0:
                st, sd = stg.next(); o = st[:, 0:n // 2].bitcast(BF16)
                c.op(c.act, lambda: nc.scalar.activation(out=o, in_=ps, func=AF.Copy), reads=[pd], writes=[sd])
                c.dma(c.sp, dst[col:col + 128, t0:t1], o, reads=[sd], acc=[dd[col // 128]])
                return
            xb, xbd = xbp.next(); t1_, t1d = t1p.next(); t2_, t2d = t2p.next()
            c.op(c.act, lambda: nc.scalar.activation(out=xb[:, 0:n], in_=ps, func=AF.Copy), reads=[pd], writes=[xbd])
            p2, p2d = pp.next()
            c.group(c.pe, [lambda: nc.tensor.matmul(p2[:, 0:n], lhsT=pm[:], rhs=xb[:, 0:n], start=True, stop=True)], reads=[xbd, kd], writes=[p2d])
            ch = col // 128
            st, sd = stg.next(); o = st[:, 0:n // 2].bitcast(BF16)
            if RM == 1:
                c.op(c.act, lambda: nc.scalar.activation(out=o, in_=p2[:, 0:n], func=AF.Copy), reads=[pd, p2d], writes=[sd])
                c.dma(c.sp, dst[col:col + 128, t0:t1], o, reads=[sd], acc=[dd[ch]])
                return
            c.op(c.dve, lambda: nc.vector.tensor_tensor(out=t1_[:, 0:n], in0=ps, in1=cosf[:, t0:t1], op=ALU.mult), reads=[pd, tabd], writes=[t1d])
            if RM == 2:
                c.op(c.act, lambda: nc.scalar.activation(out=o, in_=t1_[:, 0:n], func=AF.Copy), reads=[t1d, p2d], writes=[sd])
                c.dma(c.sp, dst[col:col + 128, t0:t1], o, reads=[sd], acc=[dd[ch]])
                return
            c.op(c.dve, lambda: nc.vector.tensor_tensor(out=t2_[:, 0:n], in0=p2[:, 0:n], in1=sinf[:, t0:t1], op=ALU.mult), reads=[p2d, tabd], writes=[t2d])
            c.op(c.dve, lambda: nc.vector.tensor_tensor(out=o, in0=t1_[:, 0:n], in1=t2_[:, 0:n], op=ALU.add), reads=[t1d, t2d], writes=[sd])
            c.dma(c.sp, dst[col:col + 128, t0:t1], o, reads=[sd], acc=[dd[ch]])
        return epi
    def plain_epi(dst, dd, off):
        def epi(col, t0, t1, ps, pd):
            n = t1 - t0
            st, sd = stg.next(); o = st[:, 0:n // 2].bitcast(BF16)
            c.op(c.act, lambda: nc.scalar.activation(out=o, in_=ps, func=AF.Copy), reads=[pd], writes=[sd])
            c.dma(c.sp, dst[off + col:off + col + 128, t0:t1], o, reads=[sd], acc=[dd[(off + col) // 128]])
        return epi
    def epi_v(col, t0, nw, ps, pd):
        st, sd = stg.next(); o = st[:, 0:256].bitcast(BF16)
        c.op(c.act, lambda: nc.scalar.activation(out=o, in_=ps, func=AF.Copy), reads=[pd], writes=[sd])
        c.dma(c.sp, VT[t0:t0 + 128, :], o, reads=[sd], acc=[vtd])
    def epi_g(col, t0, t1, ps, pd):
        st, sd = stg.next()
        c.op(c.act, lambda: nc.scalar.activation(out=st[0:32, 0:t1 - t0], in_=ps, func=AF.Sigmoid), reads=[pd], writes=[sd])
        c.dma(c.sp, GT[:, t0:t1], st[0:32, 0:t1 - t0], reads=[sd], acc=[gtd])
    rope_f = rope_epi(FT, fd)
    a1.run([dict(n0=0, n1=1024, epi=rope_epi(QT, qd)),
            dict(n0=1024, n1=1536, epi=plain_epi(FT, fd, 0)),
            dict(n0=1536, n1=2048, epi=lambda col, t0, t1, ps, pd: rope_f(col + 512, t0, t1, ps, pd)),
            dict(n0=2048, n1=2560, tm=True, epi=epi_v),
            dict(n0=2560, n1=2592, epi=epi_g)][0:ngrp])
    if stop == 1:
        c.finish(qd + fd + [vtd, gtd]); return nc
    c.new_stage()
    NC_ = 255
    KC = [P(f"KC{g}", [128, 256], BF16) for g in range(2)]; VC = [P(f"VC{g}", [128, 2, 128], BF16) for g in range(2)]
    KCd = [D(), D()]; VCd = [D(), D()]
    w1s = c.sb("w1s", [128, 32, 128], BF16); w2s = c.sb("w2s", [128, 128], BF16); pes = c.sb("pes", [128, 2, 32], F32); peb = c.sb("peb", [128, 2, 32], BF16)
    srcp = Pool(c, "csrc", 2, [128, S], BF16); b1 = c.sb("b1", [128, 2], F32); gTt = c.sb("gTt", [128, 256], BF16); xb2 = c.sb("xb2", [128, 256], BF16)
    t1c = c.sb("t1c", [128, 256], F32); t2c = c.sb("t2c", [128, 256], F32)
    pp2 = Pool(c, "pp2", 4, [128, 512], F32, psum=True)
    w1d = D(); w2d = D(); ped = D(); b1d = D(); gTd = D(); xb2d = D(); tcd = D()
    c.dma(c.sp, pes[:], pe_d, writes=[ped])
    c.op(c.dve, lambda: nc.vector.tensor_copy(out=peb[:], in_=pes[:]), reads=[ped], writes=[ped])
    for g in range(2):
        c.op(c.dve, lambda: nc.vector.memset(KC[g][:], 0.0), writes=[KCd[g]])
        c.op(c.dve, lambda: nc.vector.memset(VC[g][:], 0.0), writes=[VCd[g]])
    FTv = FT.rearrange("(c p) t -> p c t", p=128)
    for kv in range(2):
        c.dma(c.pool, w1s[:], w1_d[kv].rearrange("(l d) f -> d l f", d=128), writes=[w1d])
        c.dma(c.pool, w2s[:], w2_d[kv], writes=[w2d])
        pb, pbd = pp2.next()
        c.group(c.pe, [(lambda l=l: nc.tensor.matmul(pb[:, 0:1], lhsT=w1s[:, l, :], rhs=peb[:, kv, l:l + 1], start=(l == 0), stop=(l == 31))) for l in range(32)], reads=[w1d, ped], writes=[pbd])
        c.op(c.act, lambda: nc.scalar.activation(out=b1[:, kv:kv + 1], in_=pb[:, 0:1], func=AF.Copy), reads=[pbd], writes=[b1d])
        for g in range(2):
            src, srcd = srcp.next()
            ch = 2 * kv + g
            c.dma(c.sp, src[:], FTv[:, ch, :], reads=[fd[ch]], writes=[srcd])
            ph, phd = pp2.next()
            c.group(c.pe, [(lambda l=l: nc.tensor.matmul(ph[:, 0:NC_], lhsT=w1s[:, l, :], rhs=src[:, l:l + 16 * (NC_ - 1) + 1:16], start=(l == 0), stop=(l == 31))) for l in range(32)], reads=[w1d, srcd], writes=[phd])
            c.op(c.act, lambda: nc.scalar.activation(out=gTt[:, 0:NC_], in_=ph[:, 0:NC_], func=AF.Gelu_apprx_tanh, bias=b1[:, kv:kv + 1]), reads=[phd, b1d], writes=[gTd])
            if kv == 0:
                pk, pkd = pp2.next()
                c.group(c.pe, [lambda: nc.tensor.matmul(pk[:, 0:NC_], lhsT=w2s[:], rhs=gTt[:, 0:NC_], start=True, stop=True)], reads=[w2d, gTd], writes=[pkd])
                c.op(c.act, lambda: nc.scalar.activation(out=xb2[:, 0:NC_], in_=pk[:, 0:NC_], func=AF.Copy), reads=[pkd], writes=[xb2d])
                p2, p2d = pp2.next()
                c.group(c.pe, [lambda: nc.tensor.matmul(p2[:, 0:NC_], lhsT=pm[:], rhs=xb2[:, 0:NC_], start=True, stop=True)], reads=[xb2d, kd], writes=[p2d])
                cs_ = cosf[:, 31:31 + 16 * (NC_ - 1) + 1:16]; sn_ = sinf[:, 31:31 + 16 * (NC_ - 1) + 1:16]
                c.op(c.dve, lambda: nc.vector.tensor_tensor(out=t1c[:, 0:NC_], in0=pk[:, 0:NC_], in1=cs_, op=ALU.mult), reads=[pkd, tabd], writes=[tcd])
                c.op(c.dve, lambda: nc.vector.tensor_tensor(out=t2c[:, 0:NC_], in0=p2[:, 0:NC_], in1=sn_, op=ALU.mult), reads=[p2d, tabd, tcd], writes=[tcd])
                c.op(c.dve, lambda: nc.vector.tensor_tensor(out=KC[g][:, 0:NC_], in0=t1c[:, 0:NC_], in1=t2c[:, 0:NC_], op=ALU.add), reads=[tcd], writes=[KCd[g]])
            else:
                for nch in range(2):
                    nn = 128 if nch == 0 else NC_ - 128
                    pv, pvd = pp2.next()
                    c.group(c.pe, [lambda: nc.tensor.matmul(pv[0:nn, 0:128], lhsT=gTt[:, 128 * nch:128 * nch + nn], rhs=w2s[:], start=True, stop=True)], reads=[w2d, gTd], writes=[pvd])
                    c.op(c.act, lambda: nc.scalar.activation(out=VC[g][0:nn, nch, :], in_=pv[0:nn, 0:128], func=AF.Copy), reads=[pvd], writes=[VCd[g]])
    if stop == 2:
        c.barrier(); return nc
    c.new_stage()
    KsT_ = [c.sb(f"KsT{g}", [128, S], BF16) for g in range(2)]; KwT_ = [c.sb(f"KwT{g}", [128, S], BF16) for g in range(2)]
    Vs_ = [c.sb(f"Vs{g}", [128, 32, 128], BF16) for g in range(2)]; Vw_ = [c.sb(f"Vw{g}", [128, 32, 128], BF16) for g in range(2)]
    kvd_ = [D(), D()]
    ex = c.sb("ex", [64, 32 * 128], BF16); tri2 = c.sb("tri2", [128, 2, 128], BF16); c2s = c.sb("c2s", [128, 2, 64], F32)
    gsel = c.sb("gsel", [24, 6, 512], F32); ones24 = c.sb("ones24", [24, 128], F32); ones_b = c.sb("ones_b", [128, 128], BF16)
    k3 = D()
    for (o, i) in ((ex, ex_d), (tri2, tri_d), (c2s, c2s_d), (gsel, gsel_d), (ones24, ones24_d)): c.dma(c.sp, o[:], i, writes=[k3])
    c.op(c.dve, lambda: nc.vector.memset(ones_b[:], 1.0), writes=[k3])
    qsp = Pool(c, "qs", 2, [128, 4, 128], BF16, nd=0); cmp_ = Pool(c, "cm", 2, [128, 2, 128], BF16); tkp = Pool(c, "tk", 2, [128, 2, 64], F32)
    gtp = Pool(c, "gt", 2, [24, 128], F32)
    Ecp = Pool(c, "Ec", 2, [128, 2, 512], F32); Ecbp = Pool(c, "Ecb", 2, [128, 2, 512], BF16)
    Ep = Pool(c, "E", 6, [128, 512], BF16); Emp = Pool(c, "Em", 3, [128, 512], BF16)
    rdp = Pool(c, "rd", 2, [128, 512], F32); tp_ = Pool(c, "tt", 2, [128, 512], F32); accp = Pool(c, "acc", 2, [128, 512], F32)
    impp = Pool(c, "imp", 2, [128, 64], F32); imp2p = Pool(c, "imp2", 2, [128, 64], F32); m8p = Pool(c, "m8", 2, [128, 8], F32)
    selp = Pool(c, "sel", 2, [128, 64], BF16); selTp = Pool(c, "selT", 2, [64, 128], BF16); Rgp = Pool(c, "Rg", 2, [24, 512], F32)
    mdp = Pool(c, "md", 2, [128, 128], BF16); yop = Pool(c, "yo", 2, [128, 512], BF16)
    ps = Pool(c, "ps3", 3, [128, 512], F32, psum=True); psacc = Pool(c, "psacc", 2, [128, 512], F32, psum=True); psA = Pool(c, "psA", 3, [128, 512], F32, psum=True)
    VTv = VT.rearrange("(c p) e -> p c e", p=128)
    QTv = QT.rearrange("(h p) t -> p h t", p=128)
    def finish_branch(o_ps, o_d, den_ps, den_d, gl, k, acc, accd, gt, gtd_, first, pspool=None):
        rd, rdd = rdp.next(); tt_, ttd = tp_.next(); Rg, Rgd = Rgp.next()
        c.op(c.dve, lambda: nc.vector.tensor_scalar(out=rd[:], in0=den_ps[:], scalar1=1e-18, scalar2=None, op0=ALU.max), reads=[den_d], writes=[rdd])
        c.op(c.act, lambda: nc.scalar.activation(out=rd[:], in_=rd[:], func=AF.Ln), reads=[rdd], writes=[rdd])
        c.op(c.act, lambda: nc.scalar.activation(out=rd[:], in_=rd[:], func=AF.Exp, scale=-1.0), reads=[rdd], writes=[rdd])
        c.op(c.dve, lambda: nc.vector.tensor_tensor(out=tt_[:], in0=o_ps[:], in1=rd[:], op=ALU.mult), reads=[o_d, rdd], writes=[ttd])
        c.op(c.pool, lambda: nc.gpsimd.tensor_tensor(out=Rg[:].rearrange("k (h t) -> k h t", h=4), in0=gsel[:, 3 * gl + k, :].rearrange("k (h t) -> k h t", h=4), in1=gt[:].unsqueeze(1).to_broadcast([24, 4, 128]), op=ALU.mult), reads=[gtd_, k3], writes=[Rgd])
        gb, gbd = (pspool or ps).next()
        c.group(c.pe, [lambda: nc.tensor.matmul(gb[:], lhsT=ones24[:], rhs=Rg[:], start=True, stop=True)], reads=[Rgd, k3], writes=[gbd])
        if first:
            c.op(c.dve, lambda: nc.vector.tensor_tensor(out=acc[:], in0=tt_[:], in1=gb[:], op=ALU.mult), reads=[ttd, gbd], writes=[accd])
        else:
            c.op(c.dve, lambda: nc.vector.tensor_tensor(out=tt_[:], in0=tt_[:], in1=gb[:], op=ALU.mult), reads=[ttd, gbd], writes=[ttd])
            c.op(c.pool, lambda: nc.gpsimd.tensor_tensor(out=acc[:], in0=acc[:], in1=tt_[:], op=ALU.add), reads=[ttd, accd], writes=[accd])
        return rd, rdd
    BRS = '012'
    for gl in range(2):
        kvd = kvd_[gl]
        c.dma(c.sp, KsT_[gl][:], FTv[:, 4 + gl, :], reads=[fd[4 + gl]], writes=[kvd])
        c.dma(c.sp, KwT_[gl][:], FTv[:, 6 + gl, :], reads=[fd[6 + gl]], writes=[kvd])
        c.dma(c.sp, Vs_[gl][:], VTv[:, :, 128 * gl:128 * gl + 128], reads=[vtd], writes=[kvd])
        c.dma(c.sp, Vw_[gl][:], VTv[:, :, 256 + 128 * gl:256 + 128 * gl + 128], reads=[vtd], writes=[kvd])
    def nsa_body(qb, gl):
        kvd = kvd_[gl]; KsT = KsT_[gl]; KwT = KwT_[gl]; Vs = Vs_[gl]; Vw = Vw_[gl]
        t0 = 128 * qb
        qs, qsd = qsp.next(); cm, cmd = cmp_.next(); tk, tkd = tkp.next(); gt, gtd_ = gtp.next()
        c.dma(c.sp, qs[:], QTv[:, 4 * gl:4 * gl + 4, t0:t0 + 128], reads=qd[4 * gl:4 * gl + 4], writes=[qsd])
        c.dma(c.sp, cm[:], cmask_d[qb], writes=[cmd])
        c.dma(c.sp, tk[:], tkk_d[qb], writes=[tkd])
        c.dma(c.sp, gt[:], GT[0:24, t0:t0 + 128], reads=[gtd], writes=[gtd_])
        q2 = qs[:].rearrange("p h t -> p (h t)")
        acc, accd = accp.next()
        yield 'A'
        Ec, Ecd = Ecp.next(); Ecb, Ecbd = Ecbp.next()
        for nch in range(2):
            sp_, spd = psA.next()
            c.group(c.pe, [lambda: nc.tensor.matmul(sp_[:], lhsT=KC[gl][:, 128 * nch:128 * nch + 128], rhs=q2, start=True, stop=True)], reads=[KCd[gl], qsd], writes=[spd])
            E, Ed = Ep.next()
            c.op(c.act, lambda: nc.scalar.activation(out=E[:], in_=sp_[:], func=AF.Exp, scale=SCALE), reads=[spd], writes=[Ed])
            c.op(c.dve, lambda: nc.vector.tensor_tensor(out=Ec[:, nch, :].rearrange("p (h t) -> p h t", h=4), in0=E[:].rearrange("p (h t) -> p h t", h=4), in1=cm[:, nch, :].unsqueeze(1).to_broadcast([128, 4, 128]), op=ALU.mult), reads=[Ed, cmd], writes=[Ecd])
            yield 'A'
        c.op(c.act, lambda: nc.scalar.activation(out=Ecb[:], in_=Ec[:], func=AF.Copy), reads=[Ecd], writes=[Ecbd])
        oc, ocd = psA.next(); dc, dcd = psA.next()
        c.group(c.pe, [(lambda n_=n_: nc.tensor.matmul(oc[:], lhsT=VC[gl][:, n_, :], rhs=Ecb[:, n_, :], start=(n_ == 0), stop=(n_ == 1))) for n_ in range(2)], reads=[VCd[gl], Ecbd], writes=[ocd])
        c.group(c.pe, [(lambda n_=n_: nc.tensor.matmul(dc[:], lhsT=ones_b[:], rhs=Ecb[:, n_, :], start=(n_ == 0), stop=(n_ == 1))) for n_ in range(2)], reads=[k3, Ecbd], writes=[dcd])
        yield 'A'
        inited = ['0' in BRS]
        if '0' not in BRS:
            rd, rdd = rdp.next()
            c.op(c.dve, lambda: nc.vector.tensor_scalar(out=rd[:], in0=dc[:], scalar1=1e-30, scalar2=None, op0=ALU.max), reads=[dcd], writes=[rdd])
            c.op(c.dve, lambda: nc.vector.reciprocal(out=rd[:], in_=rd[:]), reads=[rdd], writes=[rdd])
        else:
            rd, rdd = finish_branch(oc, ocd, dc, dcd, gl, 0, acc, accd, gt, gtd_, True, psA)
        c.op(c.dve, lambda: nc.vector.tensor_tensor(out=Ec[:], in0=Ec[:], in1=rd[:].unsqueeze(1).to_broadcast([128, 2, 512]), op=ALU.mult), reads=[Ecd, rdd], writes=[Ecd])
        ip, ipd = psA.next()
        fns = []
        for n_ in range(2):
            for h in range(4):
                fns.append(lambda n_=n_, h=h: nc.tensor.matmul(ip[:, 0:64], lhsT=Ec[:, n_, 128 * h:128 * h + 128], rhs=c2s[:, n_, :], start=(n_ == 0 and h == 0), stop=(n_ == 1 and h == 3)))
        c.group(c.pe, fns, reads=[Ecd, k3], writes=[ipd])
        imp, impd = impp.next(); imp2, imp2d = imp2p.next(); m8, m8d = m8p.next(); sel, seld = selp.next(); selT, selTd = selTp.next()
        c.op(c.dve, lambda: nc.vector.tensor_tensor(out=imp[:], in0=ip[:, 0:64], in1=tk[:, 0, :], op=ALU.mult), reads=[ipd, tkd], writes=[impd])
        c.op(c.dve, lambda: nc.vector.tensor_tensor(out=imp[:], in0=imp[:], in1=tk[:, 1, :], op=ALU.add), reads=[impd, tkd], writes=[impd])
        yield 'A'
        c.op(c.dve, lambda: nc.vector.max(out=m8[:], in_=imp[:]), reads=[impd], writes=[m8d])
        c.op(c.dve, lambda: nc.vector.match_replace(out=imp2[:], in_to_replace=m8[:], in_values=imp[:], imm_value=-2e30), reads=[impd, m8d], writes=[imp2d])
        c.op(c.dve, lambda: nc.vector.max(out=m8[:], in_=imp2[:]), reads=[imp2d], writes=[m8d])
        c.op(c.dve, lambda: nc.vector.tensor_scalar(out=sel[:], in0=imp[:], scalar1=m8[:, 7:8], scalar2=None, op0=ALU.is_ge), reads=[impd, m8d], writes=[seld])
        yield 'A'
        stp, stpd = psA.next()
        c.group(c.pe, [lambda: nc.tensor.matmul(stp[0:64, 0:128], lhsT=sel[:], rhs=identb[:], start=True, stop=True)], reads=[seld, kd], writes=[stpd])
        c.op(c.act, lambda: nc.scalar.activation(out=selT[:], in_=stp[0:64, 0:128], func=AF.Copy), reads=[stpd], writes=[selTd])
        yield 'S'
        for br in (1, 2):
            K_ = KsT if br == 1 else KwT; V_ = Vs if br == 1 else Vw
            kcs = list(range(0, qb + 1)) if br == 1 else list(range(max(0, qb - 4), qb + 1))
            ob, obd = psacc.next(); db, dbd = psacc.next()
            def stA(ki_, kc):
                sp_, spd = ps.next()
                c.group(c.pe, [lambda: nc.tensor.matmul(sp_[:], lhsT=K_[:, 128 * kc:128 * kc + 128], rhs=q2, start=True, stop=True)], reads=[kvd, qsd], writes=[spd])
                E, Ed = Ep.next()
                c.op(c.act, lambda: nc.scalar.activation(out=E[:], in_=sp_[:], func=AF.Exp, scale=SCALE), reads=[spd], writes=[Ed])
                st = dict(ki=ki_, kc=kc, E=E, Ed=Ed)
                if br == 1:
                    mp, mpd = ps.next()
                    c.group(c.pe, [lambda: nc.tensor.matmul(mp[:, 0:128], lhsT=ex[:, 128 * kc:128 * kc + 128], rhs=selT[:], start=True, stop=True)], reads=[selTd, k3], writes=[mpd])
                    st.update(mp=mp, mpd=mpd)
                return st
            def stB(st):
                ki_, kc, E, Ed = st['ki'], st['kc'], st['E'], st['Ed']
                if br == 1:
                    mp, mpd = st['mp'], st['mpd']
                    Em, Emd = Emp.next()
                    if kc == qb:
                        md, mdd = mdp.next()
                        c.op(c.dve, lambda: nc.vector.tensor_tensor(out=md[:], in0=mp[:, 0:128], in1=tri2[:, 0, :], op=ALU.mult), reads=[mpd, k3], writes=[mdd])
                        msk, mskd = md[:], mdd
                    else:
                        msk, mskd = mp[:, 0:128], mpd
                    c.op(c.dve, lambda: nc.vector.tensor_tensor(out=Em[:].rearrange("p (h t) -> p h t", h=4), in0=E[:].rearrange("p (h t) -> p h t", h=4), in1=msk.unsqueeze(1).to_broadcast([128, 4, 128]), op=ALU.mult), reads=[Ed, mskd], writes=[Emd])
                else:
                    if kc == qb or kc == qb - 4:
                        Em, Emd = Emp.next()
                        msk = tri2[:, 0 if kc == qb else 1, :]
                        c.op(c.dve, lambda: nc.vector.tensor_tensor(out=Em[:].rearrange("p (h t) -> p h t", h=4), in0=E[:].rearrange("p (h t) -> p h t", h=4), in1=msk.unsqueeze(1).to_broadcast([128, 4, 128]), op=ALU.mult), reads=[Ed, k3], writes=[Emd])
                    else:
                        Em, Emd = E, Ed
                st_ = (ki_ == 0); sp2 = (ki_ == len(kcs) - 1)
                c.group(c.pe, [lambda: nc.tensor.matmul(ob[:], lhsT=V_[:, kc, :], rhs=Em[:], start=st_, stop=sp2, skip_group_check=True)], reads=[kvd, Emd], writes=[obd])
                c.group(c.pe, [lambda: nc.tensor.matmul(db[:], lhsT=ones_b[:], rhs=Em[:], start=st_, stop=sp2, skip_group_check=True)], reads=[k3, Emd], writes=[dbd])
            pend = None
            for ki_, kc in enumerate(kcs):
                cur = stA(ki_, kc)
                if pend is not None: stB(pend)
                pend = cur
                yield 'B'
            stB(pend)
            if str(br) in BRS:
                finish_branch(ob, obd, db, dbd, gl, br, acc, accd, gt, gtd_, not inited[0]); inited[0] = True
        yo, yod = yop.next()
        c.op(c.act, lambda: nc.scalar.activation(out=yo[:], in_=acc[:], func=AF.Copy), reads=[accd], writes=[yod])
        ywrite(yo[:].rearrange("p (h t) -> p h t", h=4), 4 * gl, 4, t0, 128, [yod])
        if qb % 4 == 3 and gl == 1: ydone(qb // 4)
    wv = Weave()
    for qb in range(32):
        for gl in range(2):
            wv.push(nsa_body(qb, gl))
    wv.flush()
import ml_dtypes
_BF = ml_dtypes.bfloat16
_arr = lambda n: np.ascontiguousarray(np.asarray(n).reshape(-1, 128).T)
_sl = lambda a, n: np.arange(a, a + n)
_PROGS = {}
_CONST = {}
_KINDS = [0, 1, 2, 0]
_INNER = [4096, 2048, 2048, 4096]


def build_all():
    nc = bass.Bass("TRN2", target_bir_lowering=False)
    c = Ctx(nc)
    xT = c.dram("xT", [DM, S]); hT0 = c.dram("hT0", [DM, TPC])
    oT = c.dram("oT", [DM, TPC], F32, "ExternalOutput")
    xv = xT.rearrange("(c p) t -> p c t", p=128); h0v = hT0.rearrange("(c p) t -> p c t", p=128); oTv = oT.rearrange("(c p) t -> p c t", p=128)
    hall = None; had = None; hloc = None; hlocd = None
    fin = D()
    for i in range(4):
        kind = _KINDS[i]; inner = _INNER[i]; final = (i == 3)
        IC2 = inner // 256
        c.pfx = f"L{i}A_"
        ysrc = [c.dram(f"ysrc{k}", [inner // 2, 512], BF16, None) for k in range(8)]; ysd = [D() for _ in range(8)]
        yall = [c.dram(f"yall{k}", [inner, 512], BF16, None) for k in range(8)]; yad = [D() for _ in range(8)]
        ysv = [y.rearrange("(c p) t -> p c t", p=128) for y in ysrc]
        if i == 0:
            htile = lambda ti, q: (xv[:, 4 * q:4 * q + 4, ti * TT:(ti + 1) * TT], [])
        else:
            def htile(ti, q, hall=hall, had=had):
                sh = ti // 4; tl = ti % 4; fh = q // 2
                r0 = sh * 1024 + (q % 2) * 512
                return hall[tl][fh][r0:r0 + 512, :].rearrange("(c p) t -> p c t", p=128), [had[tl][fh]]
        def ywrite(ap, ch0, n, t0, T, reads, ysv=ysv, ysd=ysd):
            k = t0 // 512; tl0 = t0 % 512
            c.dma(c.sp, ysv[k][:, ch0:ch0 + n, tl0:tl0 + T], ap, reads=reads, acc=[ysd[k]])
        def ydone(k, ysrc=ysrc, ysd=ysd, yall=yall, yad=yad):
            c.allgather(ysrc[k], [ysd[k]], yall[k], yad[k])
        [emit_ssd, emit_lru, emit_nsa][kind](nc, c, htile, ywrite, ydone)
        c.new_phase()
        c.pfx = f"L{i}B_"
        if i == 0:
            hin = lambda ti, q: (h0v[:, 4 * q:4 * q + 4, ti * TT:(ti + 1) * TT], [])
        else:
            def hin(ti, q, hloc=hloc, hlocd=hlocd):
                fh = q // 2; r0 = (q % 2) * 512
                return hloc[ti][fh][r0:r0 + 512, :].rearrange("(c p) t -> p c t", p=128), [hlocd[ti][fh]]
        if final:
            hout = lambda ti, q: (oTv[:, 4 * q:4 * q + 4, ti * TT:(ti + 1) * TT], fin)
            hdone = lambda ti: None
        else:
            nloc = [[c.dram(f"hout{t}_{f}", [1024, 512], F32, None) for f in range(2)] for t in range(4)]
            nlocd = [[D() for f in range(2)] for t in range(4)]
            nall = [[c.dram(f"hall{t}_{f}", [2048, 512], F32, None) for f in range(2)] for t in range(4)]
            nalld = [[D() for f in range(2)] for t in range(4)]
            def hout(ti, q, nloc=nloc, nlocd=nlocd):
                fh = q // 2; r0 = (q % 2) * 512
                return nloc[ti][fh][r0:r0 + 512, :].rearrange("(c p) t -> p c t", p=128), nlocd[ti][fh]
            def hdone(ti, nloc=nloc, nlocd=nlocd, nall=nall, nalld=nalld):
                for f in range(2): c.allgather(nloc[ti][f], [nlocd[ti][f]], nall[ti][f], nalld[ti][f])
        emit_B(nc, c, inner, final, hin, yall, yad, hout, hdone)
        if not final:
            hall, had, hloc, hlocd = nall, nalld, nloc, nlocd
        c.new_phase()
    c.finish([fin])
    return nc


def _ssd_consts():
    if 'ssd' in _CONST: return _CONST['ssd']
    s_ = np.arange(128)
    tri = (s_[:, None] <= s_[None, :]).astype(np.float32)
    nm = np.where(s_[:, None] <= s_[None, :], 0.0, -30000.0).astype(np.float32)
    cf32 = np.ascontiguousarray(np.concatenate([tri, np.tile(nm, (1, 8)), np.eye(128, dtype=np.float32), tri], axis=1))
    delta = np.zeros((8, 8, 128), np.float32)
    for k in range(8): delta[k, k, :] = 1
    c8 = np.ascontiguousarray(np.concatenate([delta.reshape(8, 1024), np.ones((8, 128), np.float32)], axis=1))
    _CONST['ssd'] = dict(cf32=cf32, c8=c8, identb=np.eye(128).astype(_BF))
    return _CONST['ssd']


def _ssd_inputs(d, j, li, hf):
    W = d['ssd_in_proj'][j]
    cols = np.concatenate([_sl(hf * 2048, 2048), _sl(4096 + hf * 2048, 2048), _sl(8192 + hf * 512, 512), _sl(8192 + 1024 + hf * 512, 512), _sl(4096 + 6144 + 32 * hf, 32)])
    Wc = np.ascontiguousarray(W[:, cols])
    xbc = np.concatenate([_sl(hf * 2048, 2048), _sl(4096 + hf * 512, 512), _sl(4096 + 1024 + hf * 512, 512)])
    cvx = np.zeros((128, 24, 5), np.float32)
    for k in range(4): cvx[:, :, k] = _arr(d['ssd_conv_w'][j][k, xbc])
    cvx[:, :, 4] = _arr(d['ssd_conv_b'][j][xbc])
    hs = slice(32 * hf, 32 * hf + 32)
    hv = np.zeros((128, 2, 32), np.float32); hv[:, 0, :] = d['ssd_dt_bias'][j][hs][None]; hv[:, 1, :] = d['ssd_a_log'][j][hs][None]
    dn = np.zeros((128, 16, 2), np.float32)
    dn[:, :, 0] = _arr(np.repeat(d['ssd_d'][j][hs], 64)); dn[:, :, 1] = _arr(d['ssd_norm'][j][hf * 2048:(hf + 1) * 2048])
    ins = dict(w_in=Wc, nrm=_arr(d['norm_mix'][li]), cvx=cvx, hv=hv, dn=dn)
    ins.update(_ssd_consts())
    return ins


def _lru_inputs(d, li, hf):
    W = d['lru_in_proj'][0]
    cs = slice(hf * 1024, (hf + 1) * 1024)
    Wc = np.ascontiguousarray(np.concatenate([W[:, cs], W[:, 2048 + hf * 1024: 2048 + (hf + 1) * 1024]], axis=1))
    cv = np.zeros((128, 8, 9), np.float32)
    for k in range(4): cv[:, :, k] = _arr(d['lru_conv_w'][0][k, cs])
    cv[:, :, 4] = _arr(d['lru_conv_b'][0][cs]); cv[:, :, 5] = _arr(d['lru_ba'][0][cs]); cv[:, :, 6] = _arr(d['lru_bx'][0][cs]); cv[:, :, 7] = _arr(d['lru_a_param'][0][cs])
    return dict(w_in=Wc, nrm=_arr(d['norm_mix'][li]), wa=np.ascontiguousarray(d['lru_wa'][0][4 * hf:4 * hf + 4]), wx=np.ascontiguousarray(d['lru_wx'][0][4 * hf:4 * hf + 4]), cv=cv)


def _nsa_consts():
    if 'nsa' in _CONST: return _CONST['nsa']
    k = {}
    half = 16
    inv = (np.float32(500000.0) ** (-np.arange(half, dtype=np.float32) * np.float32(2.0) / np.float32(32))).astype(np.float32)
    invf = np.zeros((128, 1), np.float32); invf[0:16, 0] = inv; invf[16:32, 0] = inv
    k['invf'] = invf
    sgn = np.zeros((128, 1), np.float32); sgn[0:16] = -1; sgn[16:32] = 1
    k['sgn'] = sgn
    pm = np.zeros((128, 128), np.float32)
    for dd in range(16): pm[dd + 16, dd] = 1; pm[dd, dd + 16] = 1
    k['pm'] = pm.astype(_BF); k['identb'] = np.eye(128).astype(_BF)
    tt = np.arange(128)
    cmask = np.zeros((32, 128, 2, 128), np.float32)
    for qb in range(32):
        t = 128 * qb + tt
        for nch in range(2):
            nn = 128 * nch + np.arange(128)
            cmask[qb, :, nch, :] = ((16 * nn[:, None] + 31 <= t[None, :]) & (nn[:, None] < 255))
    k['cmask'] = cmask.astype(_BF)
    c_start = np.arange(255)[:, None] * 16; s_start = np.arange(64)[None, :] * 64
    ov = np.clip(np.minimum(c_start + 32, s_start + 64) - np.maximum(c_start, s_start), 0, None) / 16.0
    c2s = np.zeros((256, 64), np.float32); c2s[:255] = ov
    k['c2s'] = np.ascontiguousarray(c2s.reshape(2, 128, 64).transpose(1, 0, 2))
    tkk = np.zeros((32, 128, 2, 64), np.float32)
    jj = np.arange(64)
    for qb in range(32):
        cur = (128 * qb + tt) // 64
        forced = (jj[None, :] == 0) | (jj[None, :] == cur[:, None]) | (jj[None, :] == cur[:, None] - 1)
        valid = jj[None, :] <= cur[:, None]
        tkk[qb, :, 0, :] = (~forced) & valid
        tkk[qb, :, 1, :] = np.where(valid, np.where(forced, 1e9, 0.0), -1e30)
    k['tkk'] = tkk
    ex = np.zeros((64, 32, 128), np.float32)
    for kc in range(32):
        for p in range(128): ex[2 * kc + p // 64, kc, p] = 1
    k['ex'] = ex.reshape(64, 32 * 128).astype(_BF)
    p = np.arange(128)
    tri2 = np.zeros((128, 2, 128), np.float32); tri2[:, 0, :] = p[:, None] <= tt[None, :]; tri2[:, 1, :] = p[:, None] > tt[None, :]
    k['tri2'] = tri2.astype(_BF)
    gsel = np.zeros((24, 6, 4, 128), np.float32)
    for gl in range(2):
        for kk in range(3):
            for h in range(4): gsel[(4 * gl + h) * 3 + kk, 3 * gl + kk, h, :] = 1
    k['gsel'] = gsel.reshape(24, 6, 512); k['ones24'] = np.ones((24, 128), np.float32)
    _CONST['nsa'] = k
    return k


def _nsa_inputs(d, li, pos_b, gp):
    W = d['nsa_in_proj'][0]
    g0 = 2 * gp
    parts = [_sl(1024 * gp, 1024)]
    for kidx in (0, 1, 2, 4, 3, 5):
        parts.append(_sl(2048 + kidx * 512 + g0 * 128, 256))
    parts.append(_sl(2048 + 6 * 512 + 24 * gp, 24))
    Wc = np.ascontiguousarray(np.concatenate([W[:, np.concatenate(parts)], np.zeros((2048, 8), np.float32)], axis=1))
    ins = dict(w_in=Wc, nrm=_arr(d['norm_mix'][li]), pos=np.ascontiguousarray(np.asarray(pos_b)[None, :]).astype(np.int32),
               peT=np.ascontiguousarray(d['nsa_cmp_pe'][0].transpose(2, 0, 1)), w1=d['nsa_cmp_w1'][0], w2=d['nsa_cmp_w2'][0])
    ins.update(_nsa_consts())
    return ins


def kernel(**inputs):
    d = {k: np.asarray(v) for k, v in inputs.items()}
    x = d['x']
    NB = 4
    if 'all' not in _PROGS: _PROGS['all'] = build_all()
    nc = _PROGS['all']
    maps = []
    for b in range(NB):
        xTb = np.ascontiguousarray(x[b].T)
        for r in range(2):
            ts = slice(r * 2048, (r + 1) * 2048)
            m = dict(xT=xTb, hT0=np.ascontiguousarray(xTb[:, ts]))
            sel = np.zeros((128, 2), np.float32); sel[:, r] = 1.0
            for i in range(4):
                kind, j = i % 3, i // 3
                if kind == 0: a = _ssd_inputs(d, j, i, r)
                elif kind == 1: a = _lru_inputs(d, i, r)
                else: a = _nsa_inputs(d, i, d['positions'][b], r)
                for k_, v_ in a.items(): m[f"L{i}A_{k_}"] = v_
                nrm = np.ascontiguousarray(np.concatenate([_arr(d['norm_ffn'][i]), _arr(d['norm_ple'][i]), _arr(d['norm_final'])], axis=1))
                w_o = [d['ssd_out_proj'][j], d['lru_out_proj'][0], d['nsa_out_proj'][0]][kind]
                bi = dict(pT=np.ascontiguousarray(d['p'][i][b, ts].T), sel=sel, w_o=w_o, w_in=d['w_ffn_in'][i], w_out=d['w_ffn_out'][i],
                          w_g=d['w_ple_gate'][i], w_u=d['w_ple_up'][i], nrm=nrm)
                for k_, v_ in bi.items(): m[f"L{i}B_{k_}"] = v_
            maps.append(m)
    res = run_bass_kernel_spmd(nc, maps, core_ids=list(range(8)))
    out = np.empty((NB, S, DM), np.float32)
    for b in range(NB):
        for r in range(2):
            out[b, r * 2048:(r + 1) * 2048, :] = np.asarray(res.results[2 * b + r]['oT']).T
    return out
```
